# Optimizing a Trainium2 kernel written in Bass

```python
import math
import jax, jax.numpy as jnp
from jax import lax
import numpy as np

D_MODEL = 1024
BATCH = 32
SEQ = 2048
DEPTH = 2

PLE_DIM = 256
MIX_WIDTH = D_MODEL
N_GROUPS_MIX = 4
GROUP_WIDTH = MIX_WIDTH // N_GROUPS_MIX
BLOCK = 128

MLA_HEADS = 4
MLA_Q_RANK = 256
MLA_KV_RANK = 128
MLA_NOPE = 64
MLA_ROPE = 32
MLA_V = GROUP_WIDTH // MLA_HEADS
ROPE_THETA = 10000.0

HY_WIDTH = GROUP_WIDTH
HY_ORDER = 2
HY_EMB = 33
HY_BANDS = (HY_EMB - 1) // 2
HY_FILTER_HIDDEN = 64
HY_FAST_DECAY = 0.3
HY_SLOW_DECAY = 1.5
HY_TARGET = 1e-2

SWA_HEADS = 4
SWA_KV_HEADS = 2
SWA_HEAD_DIM = GROUP_WIDTH // SWA_HEADS
SWA_WINDOW = 128

SSD_D_INNER = GROUP_WIDTH
SSD_HEAD_DIM = 64
SSD_HEADS = SSD_D_INNER // SSD_HEAD_DIM
SSD_GROUPS = 2
SSD_STATE = 128
SSD_CHUNK = BLOCK

D_FF = 2816
SHORT_CONV = 3

LN_EPS = 1e-5
RMS_EPS = 1e-6
NEG_INF = -1e30
ALPHA = (2.0 * DEPTH) ** 0.25
BETA = (8.0 * DEPTH) ** -0.25

IN_SPLITS = [MLA_Q_RANK, MLA_KV_RANK, MLA_ROPE,
             (HY_ORDER + 1) * HY_WIDTH,
             SWA_HEADS * SWA_HEAD_DIM, SWA_KV_HEADS * SWA_HEAD_DIM, SWA_KV_HEADS * SWA_HEAD_DIM,
             SSD_D_INNER, SSD_D_INNER + 2 * SSD_GROUPS * SSD_STATE, 2 * SSD_HEADS]
IN_WIDTH = sum(IN_SPLITS)
IN_OFFSETS = [int(v) for v in np.cumsum(IN_SPLITS)[:-1]]

kernel_name = "hybrid_parallel_group_encoder"


def layer_norm(x, g, b):
    xf = x.astype(jnp.float32)
    mu = jnp.mean(xf, -1, keepdims=True)
    var = jnp.mean(jnp.square(xf - mu), -1, keepdims=True)
    return ((xf - mu) * lax.rsqrt(var + LN_EPS) * g + b).astype(x.dtype)


def rms_norm(x, g):
    xf = x.astype(jnp.float32)
    return (xf * lax.rsqrt(jnp.mean(xf * xf, -1, keepdims=True) + RMS_EPS) * g).astype(x.dtype)


def group_rms_norm(y, g, n_groups):
    bsz, s, w = y.shape
    yf = y.astype(jnp.float32).reshape(bsz, s, n_groups, w // n_groups)
    yf = yf * lax.rsqrt(jnp.mean(yf * yf, -1, keepdims=True) + RMS_EPS)
    return (yf.reshape(bsz, s, w) * g).astype(y.dtype)


def dwconv_centred(x, w, b):
    k = w.shape[0]
    s = x.shape[1]
    half = (k - 1) // 2
    xp = jnp.pad(x, ((0, 0), (half, half), (0, 0)))
    out = xp[:, 0:s] * w[0]
    for j in range(1, k):
        out = out + xp[:, j:j + s] * w[j]
    return out + b


def rotary_tables(seq_len):
    inv_freq = ROPE_THETA ** (-jnp.arange(0, MLA_ROPE, 2, dtype=jnp.float32) / MLA_ROPE)
    ang = jnp.arange(seq_len, dtype=jnp.float32)[:, None] * inv_freq[None, :]
    return jnp.cos(ang), jnp.sin(ang)


def apply_rope(x, cos, sin):
    extra = x.ndim - 3
    c = cos.reshape(cos.shape[:1] + (1,) * extra + cos.shape[1:])
    sn = sin.reshape(sin.shape[:1] + (1,) * extra + sin.shape[1:])
    x1, x2 = jnp.split(x.astype(jnp.float32), 2, axis=-1)
    return jnp.concatenate([x1 * c - x2 * sn, x2 * c + x1 * sn], -1).astype(x.dtype)


def alibi_slopes(n):
    start = 2.0 ** (-8.0 / n)
    return start ** jnp.arange(1, n + 1, dtype=jnp.float32)


def mla_mixer(cq, ckv, kr, gq, gkv, w_uq, w_ukv, cos, sin):
    bsz, s, _ = cq.shape
    q = (rms_norm(cq, gq) @ w_uq).reshape(bsz, s, MLA_HEADS, MLA_NOPE + MLA_ROPE)
    q_nope = q[..., :MLA_NOPE]
    q_rope = apply_rope(q[..., MLA_NOPE:], cos, sin)
    kv = (rms_norm(ckv, gkv) @ w_ukv).reshape(bsz, s, MLA_HEADS, MLA_NOPE + MLA_V)
    k_nope = kv[..., :MLA_NOPE]
    v = kv[..., MLA_NOPE:]
    k_rope = apply_rope(kr, cos, sin)
    scale = (MLA_NOPE + MLA_ROPE) ** -0.5
    nb = s // BLOCK
    qn_b = q_nope.reshape(bsz, nb, BLOCK, MLA_HEADS, MLA_NOPE).transpose(1, 0, 2, 3, 4)
    qr_b = q_rope.reshape(bsz, nb, BLOCK, MLA_HEADS, MLA_ROPE).transpose(1, 0, 2, 3, 4)

    def attend(blk):
        qn, qr = blk
        sc = (jnp.einsum('bqhd,bkhd->bhqk', qn, k_nope, preferred_element_type=jnp.float32)
              + jnp.einsum('bqhr,bkr->bhqk', qr, k_rope, preferred_element_type=jnp.float32)) * scale
        pr = jax.nn.softmax(sc, axis=-1).astype(v.dtype)
        return jnp.einsum('bhqk,bkhd->bqhd', pr, v)

    o = lax.map(attend, (qn_b, qr_b))
    return o.transpose(1, 0, 2, 3, 4).reshape(bsz, s, MLA_HEADS * MLA_V)


def hyena_filters(seq_len, w1, b1, freq, w2, b2, w3):
    f32 = jnp.float32
    t = jnp.linspace(0.0, 1.0, seq_len, dtype=f32)[:, None]
    ang = 2.0 * math.pi * jnp.arange(seq_len, dtype=f32)[:, None] / seq_len
    bands = jnp.linspace(1e-4, HY_BANDS - 1, HY_BANDS, dtype=f32)[None, :]
    feat = jnp.concatenate([t, jnp.cos(bands * ang), -jnp.sin(bands * ang)], -1)
    fr = freq.astype(f32)
    h = jnp.sin(fr * (feat @ w1.astype(f32) + b1.astype(f32)))
    h = jnp.sin(fr * (h @ w2.astype(f32) + b2.astype(f32)))
    h = (h @ w3.astype(f32)).reshape(seq_len, HY_ORDER, 2, HY_WIDTH)
    max_decay = math.log(HY_TARGET) / HY_FAST_DECAY
    min_decay = math.log(HY_TARGET) / HY_SLOW_DECAY
    deltas = jnp.linspace(min_decay, max_decay, HY_WIDTH, dtype=f32)
    decay = jnp.exp(-t * jnp.abs(deltas)[None, :])
    return h * decay[:, None, None, :]


def bidir_fft_conv(z, h_fwd, h_bwd):
    s = z.shape[1]
    k = jnp.concatenate([h_fwd, jnp.zeros((1, h_fwd.shape[1]), h_fwd.dtype), h_bwd[:0:-1]], 0)
    kf = jnp.fft.rfft(k, axis=0)
    zf = jnp.fft.rfft(z, n=2 * s, axis=1)
    return jnp.fft.irfft(zf * kf[None], n=2 * s, axis=1)[:, :s]


def hyena_mixer(u, conv_w, conv_b, w1, b1, freq, w2, b2, w3, filt_bias):
    s = u.shape[1]
    uc = dwconv_centred(u, conv_w, conv_b).astype(jnp.float32)
    v, x1, x2 = jnp.split(uc, 3, axis=-1)
    h = hyena_filters(s, w1, b1, freq, w2, b2, w3)
    z = v
    for n, gate in enumerate((x1, x2)):
        z = gate * (bidir_fft_conv(z, h[:, n, 0], h[:, n, 1]) + filt_bias[n].astype(jnp.float32) * z)
    return z.astype(u.dtype)


def swa_mixer(q, k, v, sink):
    bsz, s, _ = q.shape
    nb = s // BLOCK
    grp = SWA_HEADS // SWA_KV_HEADS
    hd = SWA_HEAD_DIM
    q = q.reshape(bsz, nb, BLOCK, SWA_KV_HEADS, grp, hd)
    pad = ((0, 0), (BLOCK, BLOCK), (0, 0), (0, 0))
    kp = jnp.pad(k.reshape(bsz, s, SWA_KV_HEADS, hd), pad).reshape(bsz, nb + 2, BLOCK, SWA_KV_HEADS, hd)
    vp = jnp.pad(v.reshape(bsz, s, SWA_KV_HEADS, hd), pad).reshape(bsz, nb + 2, BLOCK, SWA_KV_HEADS, hd)
    kw = jnp.concatenate([kp[:, :nb], kp[:, 1:nb + 1], kp[:, 2:]], axis=2)
    vw = jnp.concatenate([vp[:, :nb], vp[:, 1:nb + 1], vp[:, 2:]], axis=2)
    sc = jnp.einsum('bnqkgd,bnjkd->bnkgqj', q, kw, preferred_element_type=jnp.float32) * (hd ** -0.5)
    blk = jnp.arange(nb)[:, None] * BLOCK
    qpos = blk + jnp.arange(BLOCK)[None, :]
    kpos = blk - BLOCK + jnp.arange(3 * BLOCK)[None, :]
    dist = jnp.abs(qpos[:, :, None] - kpos[:, None, :])
    valid = (dist <= SWA_WINDOW) & ((kpos >= 0) & (kpos < s))[:, None, :]
    slopes = alibi_slopes(SWA_HEADS).reshape(SWA_KV_HEADS, grp)
    bias = -slopes[None, :, :, None, None] * dist.astype(jnp.float32)[:, None, None]
    sc = jnp.where(valid[:, None, None], sc + bias, NEG_INF)
    sk = sink.astype(jnp.float32).reshape(SWA_KV_HEADS, grp)[:, :, None]
    m = jnp.maximum(jnp.max(sc, -1), sk)
    e = jnp.exp(sc - m[..., None])
    denom = jnp.sum(e, -1) + jnp.exp(sk - m)
    pr = (e / denom[..., None]).astype(v.dtype)
    o = jnp.einsum('bnkgqj,bnjkd->bnqkgd', pr, vw)
    return o.reshape(bsz, s, SWA_HEADS * hd)


def ssd_chunked(x, a, bm, cm):
    b, s, h, p = x.shape
    c = s // SSD_CHUNK
    qn = SSD_CHUNK
    g = SSD_GROUPS
    r = h // g
    n = SSD_STATE
    X = x.reshape(b, c, qn, g, r, p)
    A = a.reshape(b, c, qn, g, r).transpose(0, 3, 4, 1, 2)
    Bc = bm.reshape(b, c, qn, g, n)
    Cc = cm.reshape(b, c, qn, g, n)
    a_cs = jnp.cumsum(A, axis=-1)
    seg = a_cs[..., :, None] - a_cs[..., None, :]
    tril = jnp.tril(jnp.ones((qn, qn), dtype=bool))
    lmat = jnp.exp(jnp.where(tril, seg, -jnp.inf))
    cb = jnp.einsum('bctgn,bcsgn->bgcts', Cc, Bc)
    y_diag = jnp.einsum('bgcts,bgrcts,bcsgrp->bctgrp', cb, lmat, X)
    decay_states = jnp.exp(a_cs[..., -1:] - a_cs)
    states = jnp.einsum('bcsgn,bgrcs,bcsgrp->cbgrpn', Bc, decay_states, X)
    chunk_decay = jnp.exp(a_cs[..., -1]).transpose(3, 0, 1, 2)

    def step(carry, inp):
        st, dec = inp
        return carry * dec[..., None, None] + st, carry

    _, prev = lax.scan(step, jnp.zeros(states.shape[1:], jnp.float32), (states, chunk_decay))
    y_off = jnp.einsum('bctgn,cbgrpn,bgrct->bctgrp', Cc, prev, jnp.exp(a_cs))
    return (y_diag + y_off).reshape(b, s, h, p)


def ssd_mixer(xbc, dt_raw, conv_w, conv_b, dt_bias, a_log, d_skip):
    bsz, s, _ = xbc.shape
    f32 = jnp.float32
    xbc = jax.nn.silu(dwconv_centred(xbc, conv_w, conv_b).astype(f32))
    xs, bm, cm = jnp.split(xbc, [SSD_D_INNER, SSD_D_INNER + SSD_GROUPS * SSD_STATE], axis=-1)
    xs = xs.reshape(bsz, s, SSD_HEADS, SSD_HEAD_DIM)
    bm = bm.reshape(bsz, s, SSD_GROUPS, SSD_STATE)
    cm = cm.reshape(bsz, s, SSD_GROUPS, SSD_STATE)
    dt = jax.nn.softplus(dt_raw.astype(f32).reshape(bsz, s, 2, SSD_HEADS) + dt_bias.astype(f32))
    a = -jnp.exp(a_log.astype(f32))
    y_f = ssd_chunked(xs * dt[:, :, 0, :, None], dt[:, :, 0] * a[0], bm, cm)
    flip = lambda t: jnp.flip(t, axis=1)
    y_b = flip(ssd_chunked(flip(xs * dt[:, :, 1, :, None]), flip(dt[:, :, 1] * a[1]), flip(bm), flip(cm)))
    dsum = (d_skip[0] + d_skip[1]).astype(f32)[:, None]
    y = y_f + y_b + dsum * xs
    return y.reshape(bsz, s, SSD_D_INNER)


def setup_inputs(seed: int = 0) -> dict:
    key = jax.random.key(seed)
    ks = iter(jax.random.split(key, 64))
    f32 = jnp.float32
    L = DEPTH
    D = D_MODEL

    def nrm(shape, scale):
        return jax.random.normal(next(ks), shape, f32) * scale

    def gain(shape):
        return 1.0 + nrm(shape, 0.02)

    x = nrm((BATCH, SEQ, D), 1.0)
    p = nrm((DEPTH, BATCH, SEQ, PLE_DIM), 1.0)
    emb_ln_g = gain((D,))
    emb_ln_b = nrm((D,), 0.02)
    col_scale = np.ones(IN_WIDTH, np.float32)
    sv0 = IN_OFFSETS[5]
    col_scale[sv0:sv0 + SWA_KV_HEADS * SWA_HEAD_DIM] = BETA
    w_in = nrm((L, D, IN_WIDTH), D ** -0.5) * jnp.asarray(col_scale)
    mla_q_norm = gain((L, MLA_Q_RANK))
    mla_kv_norm = gain((L, MLA_KV_RANK))
    mla_w_uq = nrm((L, MLA_Q_RANK, MLA_HEADS * (MLA_NOPE + MLA_ROPE)), MLA_Q_RANK ** -0.5)
    ukv_scale = jnp.concatenate([jnp.ones((MLA_NOPE,), f32), jnp.full((MLA_V,), BETA, f32)])
    mla_w_ukv = (nrm((L, MLA_KV_RANK, MLA_HEADS, MLA_NOPE + MLA_V), MLA_KV_RANK ** -0.5)
                 * ukv_scale).reshape(L, MLA_KV_RANK, MLA_HEADS * (MLA_NOPE + MLA_V))
    hy_conv_w = nrm((L, SHORT_CONV, (HY_ORDER + 1) * HY_WIDTH), SHORT_CONV ** -0.5)
    hy_conv_b = nrm((L, (HY_ORDER + 1) * HY_WIDTH), 0.02)
    hy_f_w1 = nrm((L, HY_EMB, HY_FILTER_HIDDEN), HY_EMB ** -0.5)
    hy_f_b1 = nrm((L, HY_FILTER_HIDDEN), 0.1)
    hy_f_freq = 1.0 + nrm((L, HY_FILTER_HIDDEN), 0.1)
    hy_f_w2 = nrm((L, HY_FILTER_HIDDEN, HY_FILTER_HIDDEN), HY_FILTER_HIDDEN ** -0.5)
    hy_f_b2 = nrm((L, HY_FILTER_HIDDEN), 0.1)
    hy_f_w3 = nrm((L, HY_FILTER_HIDDEN, HY_ORDER * 2 * HY_WIDTH), 0.1 * HY_FILTER_HIDDEN ** -0.5)
    hy_bias = nrm((L, HY_ORDER, HY_WIDTH), 0.5)
    swa_sink = nrm((L, SWA_HEADS), 0.5)
    ssd_conv_w = nrm((L, SHORT_CONV, SSD_D_INNER + 2 * SSD_GROUPS * SSD_STATE), SHORT_CONV ** -0.5)
    ssd_conv_b = nrm((L, SSD_D_INNER + 2 * SSD_GROUPS * SSD_STATE), 0.02)
    dt0 = jnp.exp(jax.random.uniform(next(ks), (L, 2, SSD_HEADS), f32, math.log(1e-3), math.log(1e-1)))
    ssd_dt_bias = dt0 + jnp.log(-jnp.expm1(-dt0))
    ssd_a_log = jnp.log(jax.random.uniform(next(ks), (L, 2, SSD_HEADS), f32, 1.0, 16.0))
    ssd_d = 1.0 + nrm((L, 2, SSD_HEADS), 0.1)
    mix_norm_g = gain((L, MIX_WIDTH))
    w_out = nrm((L, MIX_WIDTH, D), BETA * MIX_WIDTH ** -0.5)
    ln1_g = gain((L, D))
    ln1_b = nrm((L, D), 0.02)
    ffn_w_gate = nrm((L, D, D_FF), BETA * D ** -0.5)
    ffn_w_up = nrm((L, D, D_FF), BETA * D ** -0.5)
    ffn_conv_w = nrm((L, SHORT_CONV, D_FF), SHORT_CONV ** -0.5)
    ffn_conv_b = nrm((L, D_FF), 0.02)
    ffn_w_down = nrm((L, D_FF, D), BETA * D_FF ** -0.5)
    ln2_g = gain((L, D))
    ln2_b = nrm((L, D), 0.02)
    ple_w_proj = nrm((L, PLE_DIM, D), BETA * PLE_DIM ** -0.5)
    ple_w_gate = nrm((L, D, D), D ** -0.5)
    ple_b_gate = nrm((L, D), 0.02)
    ln3_g = gain((L, D))
    ln3_b = nrm((L, D), 0.02)
    return {"x": x, "p": p, "emb_ln_g": emb_ln_g, "emb_ln_b": emb_ln_b, "w_in": w_in,
            "mla_q_norm": mla_q_norm, "mla_kv_norm": mla_kv_norm, "mla_w_uq": mla_w_uq, "mla_w_ukv": mla_w_ukv,
            "hy_conv_w": hy_conv_w, "hy_conv_b": hy_conv_b, "hy_f_w1": hy_f_w1, "hy_f_b1": hy_f_b1,
            "hy_f_freq": hy_f_freq, "hy_f_w2": hy_f_w2, "hy_f_b2": hy_f_b2, "hy_f_w3": hy_f_w3, "hy_bias": hy_bias,
            "swa_sink": swa_sink, "ssd_conv_w": ssd_conv_w, "ssd_conv_b": ssd_conv_b, "ssd_dt_bias": ssd_dt_bias,
            "ssd_a_log": ssd_a_log, "ssd_d": ssd_d, "mix_norm_g": mix_norm_g, "w_out": w_out,
            "ln1_g": ln1_g, "ln1_b": ln1_b, "ffn_w_gate": ffn_w_gate, "ffn_w_up": ffn_w_up,
            "ffn_conv_w": ffn_conv_w, "ffn_conv_b": ffn_conv_b, "ffn_w_down": ffn_w_down,
            "ln2_g": ln2_g, "ln2_b": ln2_b, "ple_w_proj": ple_w_proj, "ple_w_gate": ple_w_gate,
            "ple_b_gate": ple_b_gate, "ln3_g": ln3_g, "ln3_b": ln3_b}


def reference(x, p, emb_ln_g, emb_ln_b, w_in, mla_q_norm, mla_kv_norm, mla_w_uq, mla_w_ukv,
              hy_conv_w, hy_conv_b, hy_f_w1, hy_f_b1, hy_f_freq, hy_f_w2, hy_f_b2, hy_f_w3, hy_bias,
              swa_sink, ssd_conv_w, ssd_conv_b, ssd_dt_bias, ssd_a_log, ssd_d, mix_norm_g, w_out,
              ln1_g, ln1_b, ffn_w_gate, ffn_w_up, ffn_conv_w, ffn_conv_b, ffn_w_down,
              ln2_g, ln2_b, ple_w_proj, ple_w_gate, ple_b_gate, ln3_g, ln3_b):
    s = x.shape[1]
    cos, sin = rotary_tables(s)
    h = layer_norm(x, emb_ln_g, emb_ln_b)
    for i in range(DEPTH):
        u = h @ w_in[i]
        (cq, ckv, kr, hy_u, sq, sk, sv, ssd_z, ssd_xbc, ssd_dt) = jnp.split(u, IN_OFFSETS, axis=-1)
        y_a = mla_mixer(cq, ckv, kr, mla_q_norm[i], mla_kv_norm[i], mla_w_uq[i], mla_w_ukv[i], cos, sin)
        y_b = hyena_mixer(hy_u, hy_conv_w[i], hy_conv_b[i], hy_f_w1[i], hy_f_b1[i], hy_f_freq[i],
                          hy_f_w2[i], hy_f_b2[i], hy_f_w3[i], hy_bias[i])
        y_c = swa_mixer(sq, sk, sv, swa_sink[i])
        y_d = ssd_mixer(ssd_xbc, ssd_dt, ssd_conv_w[i], ssd_conv_b[i], ssd_dt_bias[i], ssd_a_log[i],
                        ssd_d[i]).astype(h.dtype) * jax.nn.silu(ssd_z)
        y = jnp.concatenate([y_a, y_b.astype(h.dtype), y_c, y_d], axis=-1)
        y = group_rms_norm(y, mix_norm_g[i], N_GROUPS_MIX)
        h = layer_norm(ALPHA * h + y @ w_out[i], ln1_g[i], ln1_b[i])
        gate = dwconv_centred(h @ ffn_w_gate[i], ffn_conv_w[i], ffn_conv_b[i])
        f = (jax.nn.silu(gate) * (h @ ffn_w_up[i])) @ ffn_w_down[i]
        h = layer_norm(ALPHA * h + f, ln2_g[i], ln2_b[i])
        e = (p[i] @ ple_w_proj[i]) * jax.nn.sigmoid(h @ ple_w_gate[i] + ple_b_gate[i])
        h = layer_norm(ALPHA * h + e, ln3_g[i], ln3_b[i])
    return h
```

```python
import contextlib
import math
import numpy as np
import ml_dtypes
import concourse.bass as bass
import concourse.mybir as mybir
from concourse.bass_utils import run_bass_kernel_spmd

F32 = mybir.dt.float32
BF16 = mybir.dt.bfloat16
AF = mybir.ActivationFunctionType
ALU = mybir.AluOpType
AX = mybir.AxisListType

S = 2048
D = 1024
NT = S // 128
DFF = 2816
NFC = DFF // 128
INW = 2728
ALPHA = (2.0 * 2) ** 0.25
LN_EPS = 1e-5
RMS_EPS = 1e-6
O_CQ, O_CKV, O_KR, O_HY, O_SQ, O_SK, O_SV, O_Z, O_XBC, O_DT = 0, 256, 384, 416, 1184, 1440, 1568, 1696, 1952, 2720

COMPUTE = ("pe", "act", "dve", "pool")
NDMASEM = 12


class _Cap:
    def __getattr__(self, name):
        def f(*a, **k):
            self.rec = (name, a, k)
            return self
        return f


class Prog:
    def __init__(self, nc):
        self.nc = nc
        self.ops = {e: [] for e in ("pe", "act", "dve", "pool", "sp")}
        self.cnt = {e: 0 for e in COMPUTE}
        self.dq_eng = {"sp": "sp", "act": "act", "pool": "pool"}
        self.dq_n = {q: 0 for q in self.dq_eng}
        self.dq_semcnt = {q: [0] * NDMASEM for q in self.dq_eng}
        self.last_w = {}
        self.readers = {}
        self.seen = {e: {} for e in self.ops}
        self.marks = []

    def mark(self, name):
        self.marks.append((name, dict(self.cnt)))

    def _deps(self, eng, reads, writes):
        deps = {}

        def add(tok):
            if tok is None:
                return
            s, v = tok
            if deps.get(s, 0) < v:
                deps[s] = v

        for r in reads:
            add(self.last_w.get(r))
        for w in writes:
            add(self.last_w.get(w))
            for t in self.readers.get(w, ()):
                add(t)
        out = []
        for s, v in deps.items():
            if s == "pe" and eng == "pe":
                continue
            if self.seen[eng].get(s, 0) >= v:
                continue
            self.seen[eng][s] = v
            out.append((s, v))
        return out

    def _commit(self, tok, reads, writes):
        for r in reads:
            self.readers.setdefault(r, []).append(tok)
        for w in writes:
            self.last_w[w] = tok
            self.readers[w] = []

    def op(self, eng, fn, reads=(), writes=()):
        cap = _Cap()
        fn(cap)
        name, a, k = cap.rec
        fn = lambda e, name=name, a=a, k=k: getattr(e, name)(*a, **k)
        waits = self._deps(eng, reads, writes)
        self.cnt[eng] += 1
        tok = (eng, self.cnt[eng])
        self.ops[eng].append((fn, waits, (eng, 1)))
        self._commit(tok, reads, writes)

    def dma(self, q, out, in_, reads=(), writes=(), **kw):
        eng = self.dq_eng[q]
        n = self.dq_n[q]
        self.dq_n[q] += 1
        si = n % NDMASEM
        sname = f"d_{q}_{si}"
        waits = self._deps(eng, reads, writes)
        prev = self.dq_semcnt[q][si]
        if prev > 0 and self.seen[eng].get(sname, 0) < prev:
            self.seen[eng][sname] = prev
            waits.append((sname, prev))
        self.dq_semcnt[q][si] += 16
        tok = (sname, self.dq_semcnt[q][si])

        def fn(e, out=out, in_=in_, kw=kw):
            return e.dma_start(out=out, in_=in_, **kw)

        self.ops[eng].append((fn, waits, (sname, 16)))
        self._commit(tok, reads, writes)
        return tok

    def barrier(self):
        toks = [(e, self.cnt[e]) for e in COMPUTE if self.cnt[e] > 0]
        for q in self.dq_eng:
            for i in range(NDMASEM):
                if self.dq_semcnt[q][i] > 0:
                    toks.append((f"d_{q}_{i}", self.dq_semcnt[q][i]))
        for eng in self.ops:
            waits = []
            for s, v in toks:
                if s == eng and eng == "pe":
                    continue
                if self.seen[eng].get(s, 0) < v:
                    self.seen[eng][s] = v
                    waits.append((s, v))
            if waits:
                self.ops[eng].append((None, waits, None))
        self.last_w = {}
        self.readers = {}

    def emit(self):
        nc = self.nc
        semnames = list(COMPUTE) + [f"d_{q}_{i}" for q in self.dq_eng for i in range(NDMASEM)]
        sems = {}
        with contextlib.ExitStack() as st:
            for s in semnames:
                sems[s] = st.enter_context(nc.semaphore(s))
            block = st.enter_context(nc.Block())

            def run(engname):
                def body(e):
                    for fn, waits, inc in self.ops[engname]:
                        for s, v in waits:
                            e.wait_ge(sems[s], v)
                        if fn is not None:
                            fn(e).then_inc(sems[inc[0]], inc[1])
                return body

            block.tensor(run("pe"))
            block.scalar(run("act"))
            block.vector(run("dve"))
            block.gpsimd(run("pool"))
            block.sync(run("sp"))


class Arena:
    def __init__(self, tensor, nbytes):
        self.t = tensor
        self.nbytes = nbytes
        self.off = 0

    def reset(self):
        self.off = 0

    def alloc(self, shape, dtype, parts=128):
        esz = 4 if dtype == F32 else 2
        n = int(np.prod(shape))
        nb = n * esz
        self.off = (self.off + 31) // 32 * 32
        assert self.off + nb <= self.nbytes, f"arena overflow {self.off + nb} > {self.nbytes}"
        a = self.t[0:parts, self.off // 2:(self.off + nb) // 2]
        self.off += nb
        if dtype == F32:
            a = a.bitcast(F32)
        if len(shape) == 2:
            a = a.rearrange("p (a b) -> p a b", a=shape[0])
        elif len(shape) == 3:
            a = a.rearrange("p (a b c) -> p a b c", a=shape[0], b=shape[1])
        elif len(shape) == 4:
            a = a.rearrange("p (a b c d) -> p a b c d", a=shape[0], b=shape[1], c=shape[2])
        return a


def _bc(ap1d, parts=128):
    return ap1d.partition_broadcast(parts)


WNAMES = ["emb_ln_g", "emb_ln_b", "w_in", "mla_q_norm", "mla_kv_norm", "mla_w_uq", "mla_w_ukv",
          "hy_conv_w", "hy_conv_b", "hy_f_w1", "hy_f_b1", "hy_f_freq", "hy_f_w2", "hy_f_b2", "hy_f_w3", "hy_bias",
          "swa_sink", "ssd_conv_w", "ssd_conv_b", "ssd_dt_bias", "ssd_a_log", "ssd_d", "mix_norm_g", "w_out",
          "ln1_g", "ln1_b", "ffn_w_gate", "ffn_w_up", "ffn_conv_w", "ffn_conv_b", "ffn_w_down",
          "ln2_g", "ln2_b", "ple_w_proj", "ple_w_gate", "ple_b_gate", "ln3_g", "ln3_b"]

WSHAPES = {
    "emb_ln_g": [D], "emb_ln_b": [D], "w_in": [2, D, INW], "mla_q_norm": [2, 256], "mla_kv_norm": [2, 128],
    "mla_w_uq": [2, 256, 384], "mla_w_ukv": [2, 128, 512], "hy_conv_w": [2, 3, 768], "hy_conv_b": [2, 768],
    "hy_f_w1": [2, 33, 64], "hy_f_b1": [2, 64], "hy_f_freq": [2, 64], "hy_f_w2": [2, 64, 64], "hy_f_b2": [2, 64],
    "hy_f_w3": [2, 64, 1024], "hy_bias": [2, 2, 256], "swa_sink": [2, 4], "ssd_conv_w": [2, 3, 768],
    "ssd_conv_b": [2, 768], "ssd_dt_bias": [2, 2, 4], "ssd_a_log": [2, 2, 4], "ssd_d": [2, 2, 4],
    "mix_norm_g": [2, D], "w_out": [2, D, D], "ln1_g": [2, D], "ln1_b": [2, D], "ffn_w_gate": [2, D, DFF],
    "ffn_w_up": [2, D, DFF], "ffn_conv_w": [2, 3, DFF], "ffn_conv_b": [2, DFF], "ffn_w_down": [2, DFF, D],
    "ln2_g": [2, D], "ln2_b": [2, D], "ple_w_proj": [2, 256, D], "ple_w_gate": [2, D, D], "ple_b_gate": [2, D],
    "ln3_g": [2, D], "ln3_b": [2, D],
}


def build(nseq=4, nlayer=2, dbg=None):
    dbg = dbg or {}
    nc = bass.Bass("TRN2", target_bir_lowering=False)
    P = Prog(nc)
    dram = {}
    x_d = nc.dram_tensor("x", [nseq, S, D], F32, kind="ExternalInput").ap()
    p_d = nc.dram_tensor("p", [2, nseq, S, 256], F32, kind="ExternalInput").ap()
    for n in WNAMES:
        dram[n] = nc.dram_tensor(n, WSHAPES[n], F32, kind="ExternalInput").ap()
    out_d = nc.dram_tensor("out", [nseq, S, D], F32, kind="ExternalOutput").ap()
    ydbg_d = None
    if dbg.get("ydbg"):
        ydbg_d = nc.dram_tensor("ydbg", [nlayer, nseq, S, D], F32, kind="ExternalInput").ap()
    rope_d = nc.dram_tensor("rope_cs", [S, 32], F32, kind="ExternalInput").ap()
    swa_eb_d = nc.dram_tensor("swa_eb", [128, 3, 4, 128], F32, kind="ExternalInput").ap()
    ssd_masks_d = nc.dram_tensor("ssd_masks", [128, 5, 128], F32, kind="ExternalInput").ap()
    dft_f_d = nc.dram_tensor("dft_f", [2, 16, 128, 16, 128], BF16, kind="ExternalInput").ap()
    dft_i_d = nc.dram_tensor("dft_i", [2, 16, 128, 16, 128], BF16, kind="ExternalInput").ap()
    hy_featT_d = nc.dram_tensor("hy_featT", [33, S], F32, kind="ExternalInput").ap()
    hy_decay_d = nc.dram_tensor("hy_decay", [S, 256], F32, kind="ExternalInput").ap()
    PQ_b = nc.dram_tensor("PQ_b", [2, 2, 16, 128, 512], BF16, kind="Internal").ap()
    ydump_d = None
    if dbg.get("ydump"):
        ydump_d = nc.dram_tensor("ydump", [S, D], F32, kind="ExternalOutput").ap()
    groups = dbg.get("groups", "abcd")

    Hf = nc.dram_tensor("Hf", [S, D], F32, kind="Internal").ap()
    Wi_b = nc.dram_tensor("Wi_b", [2, D, INW], BF16, kind="Internal").ap()
    Wo_b = nc.dram_tensor("Wo_b", [2, D, D], BF16, kind="Internal").ap()
    Wg_b = nc.dram_tensor("Wg_b", [2, NFC, 128, 8, 128], BF16, kind="Internal").ap()
    Wu_b = nc.dram_tensor("Wu_b", [2, NFC, 128, 8, 128], BF16, kind="Internal").ap()
    Wd_b = nc.dram_tensor("Wd_b", [2, DFF, D], BF16, kind="Internal").ap()
    Wpp_b = nc.dram_tensor("Wpp_b", [2, 256, D], BF16, kind="Internal").ap()
    Wpg_b = nc.dram_tensor("Wpg_b", [2, D, D], BF16, kind="Internal").ap()
    Wuq_b = nc.dram_tensor("Wuq_b", [2, 256, 384], BF16, kind="Internal").ap()
    Wukv_b = nc.dram_tensor("Wukv_b", [2, 128, 512], BF16, kind="Internal").ap()

    HT = nc.alloc_sbuf_tensor("HT", [128, 8, S], BF16)
    ident = nc.alloc_sbuf_tensor("ident", [128, 128], BF16)
    cst = nc.alloc_sbuf_tensor("cst", [128, 8], F32)
    lng = nc.alloc_sbuf_tensor("lng", [128, D], F32)
    lnb = nc.alloc_sbuf_tensor("lnb", [128, D], F32)
    ARENA_BYTES = dbg.get("arena", 160 * 1024)
    arena_t = nc.alloc_sbuf_tensor("arena", [128, ARENA_BYTES // 2], BF16)
    A = Arena(arena_t, ARENA_BYTES)
    PS = [nc.alloc_psum_tensor(f"ps{i}", [128, 2, 512], F32) for i in range(4)]

    def psk(i, j):
        return ("ps", i, j)

    P.op("pool", lambda e: e.memset(ident[:], 1.0), writes=["ident"])
    P.op("pool", lambda e: e.affine_select(out=ident[:], in_=ident[:], pattern=[[-1, 128]],
                                           compare_op=ALU.is_equal, fill=0.0, base=0, channel_multiplier=1),
         reads=["ident"], writes=["ident"])
    for i, v in enumerate([LN_EPS, RMS_EPS, 1.0, 0.0, -math.pi]):
        P.op("pool", lambda e, i=i, v=v: e.memset(cst[:, i:i + 1], v), writes=[("cst", i)])

    def cast_rows(dst, src, nrows, key):
        for r0 in range(0, nrows, 256):
            r1 = min(nrows, r0 + 256)
            P.dma("pool", dst[r0:r1], src[r0:r1], writes=[key])

    for l in range(nlayer):
        cast_rows(Wi_b[l], dram["w_in"][l], D, ("Wi_b", l))
        cast_rows(Wo_b[l], dram["w_out"][l], D, ("Wo_b", l))
        cast_rows(Wd_b[l], dram["ffn_w_down"][l], DFF, ("Wd_b", l))
        cast_rows(Wpp_b[l], dram["ple_w_proj"][l], 256, ("Wpp_b", l))
        cast_rows(Wpg_b[l], dram["ple_w_gate"][l], D, ("Wpg_b", l))
        cast_rows(Wuq_b[l], dram["mla_w_uq"][l], 256, ("Wuq_b", l))
        cast_rows(Wukv_b[l], dram["mla_w_ukv"][l], 128, ("Wukv_b", l))
        for ch in range(NFC):
            for (dst, src, nm) in ((Wg_b, dram["ffn_w_gate"], "Wg_b"), (Wu_b, dram["ffn_w_up"], "Wu_b")):
                P.dma("pool", dst[l, ch], src[l][:, ch * 128:(ch + 1) * 128].rearrange("(c p) n -> p c n", p=128),
                      writes=[(nm, l)])

    def load_ln_params(gap, bap):
        P.dma("sp", lng[:], _bc(gap), writes=["lng"])
        P.dma("sp", lnb[:], _bc(bap), writes=["lnb"])

    def ln_tile(z, tt, st, zk, dst_dram, tagk):
        stk = ("st", zk)
        for hseg in range(2):
            P.op("dve", lambda e, hseg=hseg: e.bn_stats(out=st[:, hseg * 6:(hseg + 1) * 6],
                                                         in_=z[:, hseg * 512:(hseg + 1) * 512]),
                 reads=[zk], writes=[(stk, hseg)])
        P.op("dve", lambda e: e.bn_aggr(out=st[:, 12:14], in_=st[:, 0:12]),
             reads=[(stk, 0), (stk, 1)], writes=[(stk, 2)])
        P.op("act", lambda e: e.activation(out=st[:, 14:15], in_=st[:, 13:14], func=AF.Sqrt,
                                           bias=cst[:, 0:1], scale=1.0),
             reads=[(stk, 2), ("cst", 0)], writes=[(stk, 3)])
        P.op("dve", lambda e: e.reciprocal(out=st[:, 14:15], in_=st[:, 14:15]), reads=[(stk, 3)], writes=[(stk, 3)])
        P.op("dve", lambda e: e.scalar_tensor_tensor(out=st[:, 15:16], in0=st[:, 12:13], scalar=-1.0,
                                                     in1=st[:, 14:15], op0=ALU.mult, op1=ALU.mult),
             reads=[(stk, 2), (stk, 3)], writes=[(stk, 4)])
        P.op("act", lambda e: e.activation(out=z, in_=z, func=AF.Identity, bias=st[:, 15:16], scale=st[:, 14:15]),
             reads=[zk, (stk, 3), (stk, 4)], writes=[zk])
        P.op("pool", lambda e: e.tensor_tensor(out=z, in0=z, in1=lng[:], op=ALU.mult), reads=[zk, "lng"], writes=[zk])
        P.op("pool", lambda e: e.tensor_tensor(out=z, in0=z, in1=lnb[:], op=ALU.add), reads=[zk, "lnb"], writes=[zk])
        P.dma("sp", dst_dram, z, reads=[zk], writes=[tagk])
        return

    def to_HT(z, zk, tt, hb, hbk, psi):
        P.op("act", lambda e: e.activation(out=hb, in_=z, func=AF.Identity, bias=cst[:, 3:4], scale=1.0), reads=[zk], writes=[hbk])
        pst = PS[psi[0]][:, psi[1], :].bitcast(BF16).rearrange("p (a b) -> p a b", a=8)
        for c in range(8):
            P.op("pe", lambda e, c=c: e.transpose(out=pst[:, c, :], in_=hb[:, c * 128:(c + 1) * 128], identity=ident[:]),
                 reads=[hbk, "ident"], writes=[psk(*psi)])
        P.op("dve", lambda e: e.tensor_copy(out=HT[:, :, tt * 128:(tt + 1) * 128], in_=pst),
             reads=[psk(*psi)], writes=[("HT", tt)])

    def run_pipe(tiles, stages):
        n, K = len(tiles), len(stages)
        for step in range(n + K - 1):
            for k in range(K - 1, -1, -1):
                i = step - k
                if 0 <= i < n and stages[k] is not None:
                    stages[k](tiles[i])

    def ln_stages(zt, stt, hbs, dst_fn, do_ht=True, psi_fn=lambda tt: (2 + tt % 2, 0)):
        NBz, NBh = len(zt), len(hbs)

        def zk_(tt):
            return ("z", tt % NBz)

        def L1(tt):
            z, st, zk = zt[tt % NBz], stt[tt % NBz], zk_(tt)
            stk = ("st", zk)
            for hseg in range(2):
                P.op("dve", lambda e: e.bn_stats(out=st[:, hseg * 6:(hseg + 1) * 6], in_=z[:, hseg * 512:(hseg + 1) * 512]),
                     reads=[zk], writes=[(stk, hseg)])
            P.op("dve", lambda e: e.bn_aggr(out=st[:, 12:14], in_=st[:, 0:12]), reads=[(stk, 0), (stk, 1)], writes=[(stk, 2)])

        def L234(tt):
            z, st, zk = zt[tt % NBz], stt[tt % NBz], zk_(tt)
            stk = ("st", zk)
            P.op("act", lambda e: e.activation(out=st[:, 14:15], in_=st[:, 13:14], func=AF.Sqrt, bias=cst[:, 0:1], scale=1.0),
                 reads=[(stk, 2), ("cst", 0)], writes=[(stk, 3)])
            P.op("dve", lambda e: e.reciprocal(out=st[:, 14:15], in_=st[:, 14:15]), reads=[(stk, 3)], writes=[(stk, 3)])
            P.op("dve", lambda e: e.scalar_tensor_tensor(out=st[:, 15:16], in0=st[:, 12:13], scalar=-1.0, in1=st[:, 14:15],
                                                         op0=ALU.mult, op1=ALU.mult), reads=[(stk, 2), (stk, 3)], writes=[(stk, 4)])
            P.op("act", lambda e: e.activation(out=z, in_=z, func=AF.Identity, bias=st[:, 15:16], scale=st[:, 14:15]),
                 reads=[zk, (stk, 3), (stk, 4)], writes=[zk])

        def L5(tt):
            z, zk = zt[tt % NBz], zk_(tt)
            P.op("dve", lambda e: e.tensor_tensor(out=z, in0=z, in1=lng[:], op=ALU.mult), reads=[zk, "lng"], writes=[zk])
            P.op("pool", lambda e: e.tensor_tensor(out=z, in0=z, in1=lnb[:], op=ALU.add), reads=[zk, "lnb"], writes=[zk])

        def L6(tt):
            z, zk = zt[tt % NBz], zk_(tt)
            dst, dk = dst_fn(tt)
            P.dma("sp", dst, z, reads=[zk], writes=[dk])
            if do_ht:
                hb, hbk = hbs[tt % NBh], ("hb", tt % NBh)
                P.op("act", lambda e: e.activation(out=hb, in_=z, func=AF.Identity, bias=cst[:, 3:4], scale=1.0), reads=[zk], writes=[hbk])

        def L7(tt):
            hb, hbk = hbs[tt % NBh], ("hb", tt % NBh)
            psi = psi_fn(tt)
            pst = PS[psi[0]][:, psi[1], :].bitcast(BF16).rearrange("p (a b) -> p a b", a=8)
            for c in range(8):
                P.op("pe", lambda e: e.transpose(out=pst[:, c, :], in_=hb[:, c * 128:(c + 1) * 128], identity=ident[:]),
                     reads=[hbk, "ident"], writes=[psk(*psi)])

        def L8(tt):
            psi = psi_fn(tt)
            pst = PS[psi[0]][:, psi[1], :].bitcast(BF16).rearrange("p (a b) -> p a b", a=8)
            P.op("act", lambda e: e.activation(out=HT[:, :, tt * 128:(tt + 1) * 128], in_=pst, func=AF.Identity, bias=cst[:, 3:4], scale=1.0),
                 reads=[psk(*psi)], writes=[("HT", tt)])

        if do_ht:
            return [L1, L234, L5, L6, L7, L8]
        return [L1, L234, L5, L6]

    def stage_embed(b):
        P.mark("stage_embed")
        P.barrier()
        A.reset()
        zt = [A.alloc([D], F32) for _ in range(7)]
        hb = [A.alloc([D], BF16) for _ in range(3)]
        stt = [A.alloc([16], F32) for _ in range(7)]
        load_ln_params(dram["emb_ln_g"], dram["emb_ln_b"])

        def F0(tt):
            P.dma("sp", zt[tt % 7], x_d[b, tt * 128:(tt + 1) * 128, :], writes=[("z", tt % 7)])
        run_pipe(list(range(NT)), [F0, None] + ln_stages(zt, stt, hb, lambda tt: (Hf[tt * 128:(tt + 1) * 128, :], ("Hf", tt))))

    def emit_y(YT, gmix, g, tt, y, yk, scr, scrk, ybf, ybk, psi):
        sq = scr[:, 0:256]
        st = scr[:, 256:260]
        yks = yk if isinstance(yk, list) else [yk]
        P.op("act", lambda e: e.activation(out=sq, in_=y, func=AF.Square), reads=yks, writes=[scrk])
        P.op("dve", lambda e: e.reduce_sum(out=st[:, 0:1], in_=sq, axis=AX.X), reads=[scrk], writes=[scrk])
        P.op("act", lambda e: e.activation(out=st[:, 1:2], in_=st[:, 0:1], func=AF.Sqrt, bias=cst[:, 1:2],
                                           scale=1.0 / 256.0), reads=[scrk, ("cst", 1)], writes=[scrk])
        P.op("dve", lambda e: e.reciprocal(out=st[:, 1:2], in_=st[:, 1:2]), reads=[scrk], writes=[scrk])
        P.op("dve", lambda e: e.scalar_tensor_tensor(out=ybf, in0=y, scalar=st[:, 1:2],
                                                     in1=gmix[:, g * 256:(g + 1) * 256], op0=ALU.mult, op1=ALU.mult),
             reads=yks + [scrk, "gmix"], writes=[ybk])
        pst = PS[psi[0]][:, psi[1], :].bitcast(BF16).rearrange("p (a b) -> p a b", a=8)
        for c in range(2):
            P.op("pe", lambda e, c=c: e.transpose(out=pst[:, c, :], in_=ybf[:, c * 128:(c + 1) * 128], identity=ident[:]),
                 reads=[ybk, "ident"], writes=[psk(*psi)])
        P.op("dve", lambda e: e.tensor_copy(out=YT[:, 2 * g:2 * g + 2, tt * 128:(tt + 1) * 128], in_=pst[:, 0:2, :]),
             reads=[psk(*psi)], writes=[("YT", g, tt)])

    def stage_outproj(l, YT):
        P.mark("stage_outproj")
        Wo = A.alloc([8, D], BF16)
        for c in range(8):
            P.dma("sp", Wo[:, c, :], Wo_b[l, c * 128:(c + 1) * 128, :], reads=[("Wo_b", l)], writes=[("Wo", c)])
        zt = [A.alloc([D], F32) for _ in range(7)]
        hb = [A.alloc([D], BF16) for _ in range(3)]
        stt = [A.alloc([16], F32) for _ in range(7)]
        load_ln_params(dram["ln1_g"][l], dram["ln1_b"][l])

        def F0(tt):
            P.dma("sp", zt[tt % 7], Hf[tt * 128:(tt + 1) * 128, :], reads=[("Hf", tt)], writes=[("z", tt % 7)])

        def F2(tt):
            pi = tt % 2
            for half in range(2):
                for c in range(8):
                    P.op("pe", lambda e: e.matmul(PS[pi][:, half, :], lhsT=YT[:, c, tt * 128:(tt + 1) * 128],
                                                  rhs=Wo[:, c, half * 512:(half + 1) * 512], start=(c == 0), stop=(c == 7)),
                         reads=[("YT", c // 2, tt), ("Wo", c)], writes=[psk(pi, half)])

        lns = ln_stages(zt, stt, hb, lambda tt: (Hf[tt * 128:(tt + 1) * 128, :], ("Hf", tt)))

        def F3(tt):
            pi = tt % 2
            z, zk = zt[tt % 7], ("z", tt % 7)
            P.op("dve", lambda e: e.scalar_tensor_tensor(out=z, in0=z, scalar=ALPHA, in1=PS[pi][:, :, :].rearrange("p a b -> p (a b)"),
                                                         op0=ALU.mult, op1=ALU.add), reads=[zk, psk(pi, 0), psk(pi, 1)], writes=[zk])
            lns[0](tt)
        run_pipe(list(range(NT)), [F0, F2, F3] + lns[1:])

    def stage_ffn(l):
        P.mark("stage_ffn")
        P.barrier()
        A.reset()
        HW = S // 2
        actT = A.alloc([NFC, HW], BF16)
        Wd = A.alloc([NFC, D], BF16)
        wgu = [A.alloc([2, 8, 128], BF16) for _ in range(3)]
        G = [A.alloc([HW + 2], F32) for _ in range(2)]
        T1 = [A.alloc([HW], F32) for _ in range(2)]
        halo_h = A.alloc([8, 2], BF16)
        cw = A.alloc([NFC, 4], F32)
        NBZ = 7
        zt = [A.alloc([D], F32) for _ in range(NBZ)]
        hb = [A.alloc([D], BF16) for _ in range(3)]
        stt = [A.alloc([16], F32) for _ in range(NBZ)]
        for k in range(3):
            P.dma("sp", cw[:, :, k:k + 1], dram["ffn_conv_w"][l, k].rearrange("(c p o) -> p c o", p=128, o=1), writes=["cw"],
                  allow_slow_non_contiguous=True)
        P.dma("sp", cw[:, :, 3:4], dram["ffn_conv_b"][l].rearrange("(c p o) -> p c o", p=128, o=1), writes=["cw"],
              allow_slow_non_contiguous=True)
        load_ln_params(dram["ln2_g"][l], dram["ln2_b"][l])
        P.op("dve", lambda e: e.tensor_copy(out=halo_h, in_=HT[:, :, HW - 1:HW + 1]),
             reads=[("HT", NT // 2 - 1), ("HT", NT // 2)], writes=["halo_h"])
        it = 0
        for half in range(2):
            P.mark("ffn_ph1")
            t0 = half * HW
            tts = list(range(half * NT // 2, (half + 1) * NT // 2))
            for ch in range(NFC):
                wb = wgu[it % 3]
                wk = ("wgu", it % 3)
                gi = it % 2
                it += 1
                P.dma("sp", wb[:, 0], Wg_b[l, ch], reads=[("Wg_b", l)], writes=[(wk, 0)])
                P.dma("sp", wb[:, 1], Wu_b[l, ch], reads=[("Wu_b", l)], writes=[(wk, 1)])
                if half == 0 and 2 <= ch < 2 + 11:
                    c0 = (ch - 2) * 2
                    P.dma("sp", Wd[:, c0:c0 + 2, :], Wd_b[l, c0 * 128:(c0 + 2) * 128, :].rearrange("(c p) n -> p c n", p=128),
                          reads=[("Wd_b", l)], writes=[("Wd", c0), ("Wd", c0 + 1)])
                pg, pu = PS[gi * 2], PS[gi * 2 + 1]
                for (pp, wi, pidx) in ((pg, 0, gi * 2), (pu, 1, gi * 2 + 1)):
                    for nb in range(2):
                        for c in range(8):
                            P.op("pe", lambda e, pp=pp, wi=wi, nb=nb, c=c, wb=wb: e.matmul(
                                pp[:, nb, :], lhsT=wb[:, wi, c, :], rhs=HT[:, c, t0 + nb * 512:t0 + (nb + 1) * 512],
                                start=(c == 0), stop=(c == 7)),
                                reads=[(wk, wi)] + [("HT", t0 // 128 + nb * 4 + j) for j in range(4)],
                                writes=[psk(pidx, nb)])
                Gt = G[gi]
                gk = ("G", gi)
                P.op("act", lambda e, Gt=Gt, pg=pg: e.activation(out=Gt[:, 1:HW + 1], in_=pg[:, :, :].rearrange("p a b -> p (a b)"),
                                                                 func=AF.Identity, bias=cst[:, 3:4], scale=1.0),
                     reads=[psk(gi * 2, 0), psk(gi * 2, 1)], writes=[gk])
                hcol = 1 if half == 0 else 0
                for c in range(8):
                    P.op("pe", lambda e, pg=pg, c=c, wb=wb, hcol=hcol: e.matmul(
                        pg[:, 0, 0:1], lhsT=wb[:, 0, c, :], rhs=halo_h[:, c, hcol:hcol + 1],
                        start=(c == 0), stop=(c == 7)), reads=[(wk, 0), "halo_h", gk], writes=[psk(gi * 2, 0)])
                if half == 0:
                    P.op("pool", lambda e, Gt=Gt: e.memset(Gt[:, 0:1], 0.0), writes=[(gk, "l")])
                    P.op("act", lambda e, Gt=Gt, pg=pg: e.activation(out=Gt[:, HW + 1:HW + 2], in_=pg[:, 0, 0:1], func=AF.Identity, bias=cst[:, 3:4], scale=1.0),
                         reads=[psk(gi * 2, 0)], writes=[(gk, "r")])
                else:
                    P.op("pool", lambda e, Gt=Gt: e.memset(Gt[:, HW + 1:HW + 2], 0.0), writes=[(gk, "r")])
                    P.op("act", lambda e, Gt=Gt, pg=pg: e.activation(out=Gt[:, 0:1], in_=pg[:, 0, 0:1], func=AF.Identity, bias=cst[:, 3:4], scale=1.0),
                         reads=[psk(gi * 2, 0)], writes=[(gk, "l")])
                T = T1[gi]
                tk = ("T1", gi)
                P.op("dve", lambda e, T=T, Gt=Gt, ch=ch: e.tensor_scalar(
                    out=T, in0=Gt[:, 1:HW + 1], scalar1=cw[:, ch, 1:2], scalar2=cw[:, ch, 3:4], op0=ALU.mult, op1=ALU.add),
                    reads=[gk, "cw"], writes=[tk])
                P.op("dve", lambda e, T=T, Gt=Gt, ch=ch: e.scalar_tensor_tensor(
                    out=T, in0=Gt[:, 0:HW], scalar=cw[:, ch, 0:1], in1=T, op0=ALU.mult, op1=ALU.add),
                    reads=[gk, (gk, "l"), tk, "cw"], writes=[tk])
                P.op("dve", lambda e, T=T, Gt=Gt, ch=ch: e.scalar_tensor_tensor(
                    out=T, in0=Gt[:, 2:HW + 2], scalar=cw[:, ch, 2:3], in1=T, op0=ALU.mult, op1=ALU.add),
                    reads=[gk, (gk, "r"), tk, "cw"], writes=[tk])
                P.op("act", lambda e, T=T: e.activation(out=T, in_=T, func=AF.Silu), reads=[tk], writes=[tk])
                P.op("dve", lambda e, T=T, pu=pu, ch=ch: e.tensor_tensor(
                    out=actT[:, ch, :], in0=T, in1=pu[:, :, :].rearrange("p a b -> p (a b)"), op=ALU.mult),
                    reads=[tk, psk(gi * 2 + 1, 0), psk(gi * 2 + 1, 1)], writes=[("actT", ch)])
            P.mark("ffn_ph2")
            def F0(tt):
                P.dma("sp", zt[tt % NBZ], Hf[tt * 128:(tt + 1) * 128, :], reads=[("Hf", tt)], writes=[("z", tt % NBZ)])

            def F2(tt, tts=tts):
                pi = tt % 2
                tl = tt - tts[0]
                for hf in range(2):
                    for ch in range(NFC):
                        P.op("pe", lambda e: e.matmul(PS[pi][:, hf, :], lhsT=actT[:, ch, tl * 128:(tl + 1) * 128],
                                                      rhs=Wd[:, ch, hf * 512:(hf + 1) * 512], start=(ch == 0), stop=(ch == NFC - 1)),
                             reads=[("actT", ch), ("Wd", ch)], writes=[psk(pi, hf)])

            lns = ln_stages(zt, stt, hb, lambda tt: (Hf[tt * 128:(tt + 1) * 128, :], ("Hf", tt)))

            def F3(tt):
                pi = tt % 2
                z, zk = zt[tt % NBZ], ("z", tt % NBZ)
                P.op("dve", lambda e: e.scalar_tensor_tensor(out=z, in0=z, scalar=ALPHA, in1=PS[pi][:, :, :].rearrange("p a b -> p (a b)"),
                                                             op0=ALU.mult, op1=ALU.add), reads=[zk, psk(pi, 0), psk(pi, 1)], writes=[zk])
                lns[0](tt)
            run_pipe(tts, [F0, F2, F3] + lns[1:])

    def stage_ple(l, b, last):
        P.mark("stage_ple")
        P.barrier()
        A.reset()
        Wpg = A.alloc([8, D], BF16)
        Wpp = A.alloc([2, D], BF16)
        bg = A.alloc([D], F32)
        NBZ, NBS = 8, 5
        pt = [A.alloc([256], F32) for _ in range(3)]
        pb = [A.alloc([256], BF16) for _ in range(3)]
        pT = [A.alloc([2, 128], BF16) for _ in range(6)]
        sg = [A.alloc([D], F32) for _ in range(NBS)]
        zt = [A.alloc([D], F32) for _ in range(NBZ)]
        hb = [A.alloc([D], BF16) for _ in range(3)]
        stt = [A.alloc([16], F32) for _ in range(NBZ)]
        for c in range(8):
            P.dma("sp", Wpg[:, c, :], Wpg_b[l, c * 128:(c + 1) * 128, :], reads=[("Wpg_b", l)], writes=[("Wpg", c)])
        P.dma("sp", Wpp, Wpp_b[l].rearrange("(c p) n -> p c n", p=128), reads=[("Wpp_b", l)], writes=["Wpp"])
        P.dma("sp", bg, _bc(dram["ple_b_gate"][l]), writes=["bg"])
        load_ln_params(dram["ln3_g"][l], dram["ln3_b"][l])

        def G0(tt):
            P.dma("sp", pt[tt % 3], p_d[l, b, tt * 128:(tt + 1) * 128, :], writes=[("pt", tt % 3)])

        def G1(tt):
            P.op("act", lambda e: e.activation(out=pb[tt % 3], in_=pt[tt % 3], func=AF.Identity, bias=cst[:, 3:4], scale=1.0),
                 reads=[("pt", tt % 3)], writes=[("pb", tt % 3)])

        def G2(tt):
            i2 = tt % 2
            pst = PS[3][:, 1, :].bitcast(BF16).rearrange("p (a b) -> p a b", a=8)
            for c in range(2):
                P.op("pe", lambda e: e.transpose(out=pst[:, c, :], in_=pb[tt % 3][:, c * 128:(c + 1) * 128], identity=ident[:]),
                     reads=[("pb", tt % 3), "ident"], writes=[psk(3, 1)])
            for hf in range(2):
                for c in range(8):
                    P.op("pe", lambda e: e.matmul(PS[i2][:, hf, :], lhsT=HT[:, c, tt * 128:(tt + 1) * 128], rhs=Wpg[:, c, hf * 512:(hf + 1) * 512],
                                                  start=(c == 0), stop=(c == 7)), reads=[("HT", tt), ("Wpg", c)], writes=[psk(i2, hf)])

        def G3(tt):
            i2 = tt % 2
            pst = PS[3][:, 1, :].bitcast(BF16).rearrange("p (a b) -> p a b", a=8)
            P.op("dve", lambda e: e.tensor_copy(out=pT[tt % 6], in_=pst[:, 0:2, :]), reads=[psk(3, 1)], writes=[("pT", tt % 6)])
            s_, sk = sg[tt % NBS], ("sg", tt % NBS)
            P.op("dve", lambda e: e.tensor_tensor(out=s_, in0=PS[i2][:, :, :].rearrange("p a b -> p (a b)"), in1=bg, op=ALU.add),
                 reads=[psk(i2, 0), psk(i2, 1), "bg"], writes=[sk])

        def G4(tt):
            s_, sk = sg[tt % NBS], ("sg", tt % NBS)
            P.op("act", lambda e: e.activation(out=s_, in_=s_, func=AF.Sigmoid), reads=[sk], writes=[sk])
            i2 = tt % 2
            for hf in range(2):
                for c in range(2):
                    P.op("pe", lambda e: e.matmul(PS[2][:, hf, :], lhsT=pT[tt % 6][:, c, :], rhs=Wpp[:, c, hf * 512:(hf + 1) * 512],
                                                  start=(c == 0), stop=(c == 1)), reads=[("pT", tt % 6), "Wpp"], writes=[psk(2, hf)])
            P.dma("sp", zt[tt % NBZ], Hf[tt * 128:(tt + 1) * 128, :], reads=[("Hf", tt)], writes=[("z", tt % NBZ)])

        dstf = (lambda tt: (out_d[b, tt * 128:(tt + 1) * 128, :], ("out", b, tt))) if last else \
               (lambda tt: (Hf[tt * 128:(tt + 1) * 128, :], ("Hf", tt)))
        lns = ln_stages(zt, stt, hb, dstf, do_ht=not last, psi_fn=lambda tt: (3, 0))

        def G5(tt):
            i2 = tt % 2
            s_, sk = sg[tt % NBS], ("sg", tt % NBS)
            z, zk = zt[tt % NBZ], ("z", tt % NBZ)
            P.op("dve", lambda e: e.tensor_tensor(out=s_, in0=s_, in1=PS[2][:, :, :].rearrange("p a b -> p (a b)"), op=ALU.mult),
                 reads=[psk(2, 0), psk(2, 1), sk], writes=[sk])
            P.op("dve", lambda e: e.scalar_tensor_tensor(out=z, in0=z, scalar=ALPHA, in1=s_, op0=ALU.mult, op1=ALU.add),
                 reads=[zk, sk], writes=[zk])

        def G6(tt):
            lns[0](tt)
        run_pipe(list(range(NT)), [G0, G1, G2, G3, G4, G5, G6] + lns[1:])

    def stage_mix_dbg(l, b):
        P.barrier()
        A.reset()
        YT = A.alloc([8, S], BF16)
        gmix = A.alloc([D], F32)
        P.dma("sp", gmix, _bc(dram["mix_norm_g"][l]), writes=["gmix"])
        mark = A.off
        yt = [A.alloc([D], F32) for _ in range(2)]
        scr = [A.alloc([260], F32) for _ in range(2)]
        ybf = [A.alloc([256], BF16) for _ in range(2)]
        for tt in range(NT):
            i2 = tt % 2
            P.dma("sp", yt[i2], ydbg_d[l, b, tt * 128:(tt + 1) * 128, :], writes=[("yt", i2)])
            for g in range(4):
                emit_y(YT, gmix, g, tt, yt[i2][:, g * 256:(g + 1) * 256], ("yt", i2), scr[i2], ("scr", i2),
                       ybf[i2], ("ybf", i2), (3, 1))
        P.barrier()
        A.off = mark
        return YT

    def y_from_dbg(l, b, g, YT, gmix):
        mark = A.off
        yt = [A.alloc([256], F32) for _ in range(2)]
        scr = [A.alloc([260], F32) for _ in range(2)]
        ybf = [A.alloc([256], BF16) for _ in range(2)]
        for tt in range(NT):
            i2 = tt % 2
            P.dma("sp", yt[i2], ydbg_d[l, b, tt * 128:(tt + 1) * 128, g * 256:(g + 1) * 256], writes=[("yt", i2)])
            emit_y(YT, gmix, g, tt, yt[i2], ("yt", i2), scr[i2], ("scr", i2), ybf[i2], ("ybf", i2), (3, 1))
        P.barrier()
        A.off = mark

    def finish_group(g, YT, gmix, yG, ykeyfn):
        P.mark("finish_group")
        scr = [A.alloc([260], F32) for _ in range(2)]
        ybf = [A.alloc([256], BF16) for _ in range(2)]
        for tt in range(NT):
            i2 = tt % 2
            if ydump_d is not None:
                P.dma("sp", ydump_d[tt * 128:(tt + 1) * 128, g * 256:(g + 1) * 256], yG[:, tt, :], reads=ykeyfn(tt),
                      writes=[("ydump", g, tt)])
            emit_y(YT, gmix, g, tt, yG[:, tt, :], ykeyfn(tt), scr[i2], ("scr", i2), ybf[i2], ("ybf", i2), (3, 1))

    def mix_mla(l, b, YT, gmix):
        P.mark("mix_mla")
        mark = A.off
        Wa = A.alloc([8, 416], BF16)
        Wuq = A.alloc([2, 384], BF16)
        Wukv = A.alloc([512], BF16)
        gq = A.alloc([4], F32)
        CQT = A.alloc([3, S], BF16)
        SQ = A.alloc([3, S], BF16)
        QT = A.alloc([4, S], BF16)
        KT = A.alloc([4, S], BF16)
        V1 = A.alloc([NT, 4, 66], BF16)
        rope = A.alloc([NT, 32], F32)
        ones = A.alloc([2], BF16)
        yA = A.alloc([NT, 256], F32)
        Qb = [A.alloc([4, 96], BF16) for _ in range(2)]
        Kb = [A.alloc([4, 96], BF16) for _ in range(2)]
        R = [A.alloc([5, 32], F32) for _ in range(2)]
        Ro = [A.alloc([5, 32], F32) for _ in range(2)]
        T4 = [A.alloc([4, 5, 16], F32) for _ in range(2)]
        st = [A.alloc([4], F32) for _ in range(2)]
        PT = [A.alloc([512], BF16) for _ in range(3)]
        rc = [A.alloc([4], F32) for _ in range(2)]
        P.dma("sp", Wa, Wi_b[l][:, 0:416].rearrange("(c p) n -> p c n", p=128), reads=[("Wi_b", l)], writes=["Wa"])
        P.dma("sp", Wuq, Wuq_b[l].rearrange("(c p) n -> p c n", p=128), reads=[("Wuq_b", l)], writes=["Wuq"])
        P.dma("sp", Wukv, Wukv_b[l], reads=[("Wukv_b", l)], writes=["Wukv"])
        P.dma("sp", gq[:, 0:2], dram["mla_q_norm"][l].rearrange("(c p) -> p c", p=128), writes=["gq"],
              allow_slow_non_contiguous=True)
        P.dma("sp", gq[:, 2:3], dram["mla_kv_norm"][l].rearrange("(c p) -> p c", p=128), writes=["gq"],
              allow_slow_non_contiguous=True)
        P.dma("sp", rope, rope_d.rearrange("(t p) c -> p t c", p=128), writes=["rope"])
        P.op("pool", lambda e: e.memset(ones, 1.0), writes=["ones"])
        P.op("pool", lambda e: e.memset(V1[:, :, :, 64:65], 1.0), writes=["V1one"])
        it = 0
        for nb in range(4):
            for ci in range(3):
                pi, pb_ = (it % 4) // 2, it % 2
                it += 1
                for c in range(8):
                    P.op("pe", lambda e: e.matmul(PS[pi][:, pb_, :], lhsT=Wa[:, c, ci * 128:(ci + 1) * 128],
                                                  rhs=HT[:, c, nb * 512:(nb + 1) * 512], start=(c == 0), stop=(c == 7)),
                         reads=["Wa"] + [("HT", nb * 4 + j) for j in range(4)], writes=[psk(pi, pb_)])
                P.op("act", lambda e: e.activation(out=CQT[:, ci, nb * 512:(nb + 1) * 512], in_=PS[pi][:, pb_, :],
                                                   func=AF.Identity, bias=cst[:, 3:4], scale=gq[:, ci:ci + 1]),
                     reads=[psk(pi, pb_), "gq", ("cst", 3)], writes=[("CQT", nb)])
                P.op("act", lambda e: e.activation(out=SQ[:, ci, nb * 512:(nb + 1) * 512], in_=PS[pi][:, pb_, :],
                                                   func=AF.Square), reads=[psk(pi, pb_)], writes=[("SQ", nb)])
        P.mark("mla_A2")
        SCALE = 96.0 ** -0.5
        for tt in range(NT):
            i2 = tt % 2
            tsl = slice(tt * 128, (tt + 1) * 128)
            nb = tt // 4
            b0 = PS[i2][:, 0, :]
            PSq, PSkr, PSss, PSkv = b0[:, 0:384], b0[:, 384:416], b0[:, 416:418], PS[i2][:, 1, :]
            for c in range(2):
                P.op("pe", lambda e: e.matmul(PSq, lhsT=CQT[:, c, tsl], rhs=Wuq[:, c, :], start=(c == 0), stop=(c == 1)),
                     reads=[("CQT", nb), "Wuq"], writes=[psk(i2, 0)])
            for c in range(8):
                P.op("pe", lambda e: e.matmul(PSkr, lhsT=HT[:, c, tsl], rhs=Wa[:, c, 384:416], start=(c == 0), stop=(c == 7)),
                     reads=[("HT", tt), "Wa"], writes=[psk(i2, 0)])
            for c in range(2):
                P.op("pe", lambda e: e.matmul(PSss[:, 0:1], lhsT=SQ[:, c, tsl], rhs=ones[:, 0:1], start=(c == 0), stop=(c == 1)),
                     reads=[("SQ", nb), "ones"], writes=[psk(i2, 0)])
            P.op("pe", lambda e: e.matmul(PSss[:, 1:2], lhsT=SQ[:, 2, tsl], rhs=ones[:, 0:1], start=True, stop=True),
                 reads=[("SQ", nb), "ones"], writes=[psk(i2, 0)])
            P.op("pe", lambda e: e.matmul(PSkv, lhsT=CQT[:, 2, tsl], rhs=Wukv, start=True, stop=True),
                 reads=[("CQT", nb), "Wukv"], writes=[psk(i2, 1)])
            sk_ = ("mst", i2)
            s_ = st[i2]
            P.op("act", lambda e: e.activation(out=s_[:, 0:1], in_=PSss[:, 0:1], func=AF.Sqrt, bias=cst[:, 1:2], scale=1.0 / 256),
                 reads=[psk(i2, 0), ("cst", 1)], writes=[sk_])
            P.op("act", lambda e: e.activation(out=s_[:, 1:2], in_=PSss[:, 1:2], func=AF.Sqrt, bias=cst[:, 1:2], scale=1.0 / 128),
                 reads=[psk(i2, 0), ("cst", 1)], writes=[sk_])
            P.op("dve", lambda e: e.reciprocal(out=s_[:, 0:2], in_=s_[:, 0:2]), reads=[sk_], writes=[sk_])
            q3 = PSq.rearrange("p (h d) -> p h d", h=4)
            kv3 = PSkv.rearrange("p (h d) -> p h d", h=4)
            qk, kk, rk = ("Qb", i2), ("Kb", i2), ("R", i2)
            P.op("act", lambda e: e.activation(out=Qb[i2][:, :, 0:64], in_=q3[:, :, 0:64], func=AF.Identity,
                                               bias=cst[:, 3:4], scale=s_[:, 0:1]), reads=[psk(i2, 0), sk_, ("cst", 3)], writes=[qk])
            P.op("act", lambda e: e.activation(out=R[i2][:, 0:4, :], in_=q3[:, :, 64:96], func=AF.Identity,
                                               bias=cst[:, 3:4], scale=s_[:, 0:1]), reads=[psk(i2, 0), sk_, ("cst", 3)], writes=[rk])
            P.op("act", lambda e: e.activation(out=R[i2][:, 4, :], in_=PSkr, func=AF.Identity, bias=cst[:, 3:4], scale=1.0), reads=[psk(i2, 0)], writes=[rk])
            P.op("act", lambda e: e.activation(out=Kb[i2][:, :, 0:64], in_=kv3[:, :, 0:64], func=AF.Identity,
                                               bias=cst[:, 3:4], scale=s_[:, 1:2]), reads=[psk(i2, 1), sk_, ("cst", 3)], writes=[kk])
            P.op("act", lambda e: e.activation(out=V1[:, tt, :, 0:64], in_=kv3[:, :, 64:128], func=AF.Identity,
                                               bias=cst[:, 3:4], scale=s_[:, 1:2]), reads=[psk(i2, 1), sk_, ("cst", 3)],
                 writes=[("V1", tt)])
            cosb = rope[:, tt, 0:16].unsqueeze(1).broadcast_to([128, 5, 16])
            sinb = rope[:, tt, 16:32].unsqueeze(1).broadcast_to([128, 5, 16])
            Rr, T_ = R[i2], T4[i2]
            tk_ = ("T4", i2)
            P.op("pool", lambda e: e.tensor_tensor(out=T_[:, 0], in0=Rr[:, :, 0:16], in1=cosb, op=ALU.mult), reads=[rk, "rope"], writes=[(tk_, 0)])
            P.op("pool", lambda e: e.tensor_tensor(out=T_[:, 1], in0=Rr[:, :, 16:32], in1=sinb, op=ALU.mult), reads=[rk, "rope"], writes=[(tk_, 1)])
            P.op("pool", lambda e: e.tensor_tensor(out=T_[:, 2], in0=Rr[:, :, 16:32], in1=cosb, op=ALU.mult), reads=[rk, "rope"], writes=[(tk_, 2)])
            P.op("pool", lambda e: e.tensor_tensor(out=T_[:, 3], in0=Rr[:, :, 0:16], in1=sinb, op=ALU.mult), reads=[rk, "rope"], writes=[(tk_, 3)])
            rok = ("Ro", i2)
            P.op("dve", lambda e: e.tensor_tensor(out=Ro[i2][:, :, 0:16], in0=T_[:, 0], in1=T_[:, 1], op=ALU.subtract),
                 reads=[(tk_, 0), (tk_, 1)], writes=[(rok, 0)])
            P.op("dve", lambda e: e.tensor_tensor(out=Ro[i2][:, :, 16:32], in0=T_[:, 2], in1=T_[:, 3], op=ALU.add),
                 reads=[(tk_, 2), (tk_, 3)], writes=[(rok, 1)])
            P.op("dve", lambda e: e.tensor_copy(out=Qb[i2][:, :, 64:96], in_=Ro[i2][:, 0:4, :]), reads=[(rok, 0), (rok, 1)], writes=[(qk, "r")])
            P.op("dve", lambda e: e.tensor_copy(out=Kb[i2][:, :, 64:96], in_=Ro[i2][:, 4:5, :].broadcast_to([128, 4, 32])),
                 reads=[(rok, 0), (rok, 1)], writes=[(kk, "r")])
            pst = PS[2 + i2][:, 0, :].bitcast(BF16).rearrange("p (a b) -> p a b", a=8)
            for h in range(4):
                P.op("pe", lambda e: e.transpose(out=pst[0:96, h, :], in_=Qb[i2][:, h, :], identity=ident[:]),
                     reads=[qk, (qk, "r"), "ident"], writes=[psk(2 + i2, 0)])
            for h in range(4):
                P.op("pe", lambda e: e.transpose(out=pst[0:96, 4 + h, :], in_=Kb[i2][:, h, :], identity=ident[:]),
                     reads=[kk, (kk, "r"), "ident"], writes=[psk(2 + i2, 0)])
            P.op("dve", lambda e: e.tensor_copy(out=QT[0:96, :, tsl], in_=pst[0:96, 0:4, :]), reads=[psk(2 + i2, 0)], writes=[("QT", tt)])
            P.op("dve", lambda e: e.tensor_copy(out=KT[0:96, :, tsl], in_=pst[0:96, 4:8, :]), reads=[psk(2 + i2, 0)], writes=[("KT", tt)])
        P.mark("mla_A3")
        items = [(h, qb, kt) for h in range(4) for qb in range(4) for kt in range(NT)]

        def emit_S(i):
            h, qb, kt = items[i]
            si = i % 4
            PSs = PS[si // 2][:, si % 2, :]
            P.op("pe", lambda e: e.matmul(PSs, lhsT=KT[0:96, h, kt * 128:(kt + 1) * 128],
                                          rhs=QT[0:96, h, qb * 512:(qb + 1) * 512], start=True, stop=True),
                 reads=[("KT", kt)] + [("QT", qb * 4 + j) for j in range(4)], writes=[psk(si // 2, si % 2)])

        emit_S(0)
        for i, (h, qb, kt) in enumerate(items):
            grp = i // NT
            oi = grp % 2
            PSo = PS[2 + oi][:, 0, 0:260].rearrange("p (j d) -> p j d", j=4)
            ok_ = psk(2 + oi, 0)
            if kt == 0:
                P.op("dve", lambda e: e.memset(PSo, 0.0), writes=[ok_])
            if i + 1 < len(items):
                emit_S(i + 1)
            si = i % 4
            PSs = PS[si // 2][:, si % 2, :]
            pt_ = PT[i % 3]
            ptk = ("PT", i % 3)
            P.op("act", lambda e: e.activation(out=pt_, in_=PSs, func=AF.Exp, scale=SCALE), reads=[psk(si // 2, si % 2)], writes=[ptk])
            for j in range(4):
                P.op("pe", lambda e: e.matmul(PSo[:, j, :], lhsT=pt_[:, j * 128:(j + 1) * 128], rhs=V1[:, kt, h, 0:65],
                                              start=False, stop=False, skip_group_check=True),
                     reads=[ptk, ("V1", kt), "V1one"], writes=[ok_])
            if kt == NT - 1:
                rck = ("rc", oi)
                P.op("dve", lambda e: e.reciprocal(out=rc[oi], in_=PSo[:, :, 64]), reads=[ok_], writes=[rck])
                P.op("dve", lambda e: e.tensor_tensor(out=yA[:, qb * 4:(qb + 1) * 4, h * 64:(h + 1) * 64], in0=PSo[:, :, 0:64],
                                                      in1=rc[oi].unsqueeze(2).broadcast_to([128, 4, 64]), op=ALU.mult),
                     reads=[ok_, rck], writes=[("yA", qb, h)])
        finish_group(0, YT, gmix, yA, lambda tt: [("yA", tt // 4, h) for h in range(4)])
        P.barrier()
        A.off = mark

    def mix_swa(l, b, YT, gmix):
        P.mark("mix_swa")
        mark = A.off
        Wc = A.alloc([8, 512], BF16)
        Wk2 = A.alloc([8, 2, 2, 64], BF16) if False else A.alloc([8, 256], BF16)
        SQT = A.alloc([2, NT, 256], BF16)
        SKT = A.alloc([2, S], BF16)
        V1s = A.alloc([NT, 2, 66], BF16)
        P.op("pool", lambda e: e.memset(SQT, 0.0), writes=["SQTz"])
        EBt = A.alloc([3, 4, 128], F32)
        esink = A.alloc([4], F32)
        yC = A.alloc([NT, 256], F32)
        E = [A.alloc([3, 256], F32) for _ in range(2)]
        Eb = [A.alloc([3, 256], BF16) for _ in range(2)]
        rc = [A.alloc([4], F32) for _ in range(2)]
        P.dma("sp", Wc, Wi_b[l][:, O_SQ:O_SQ + 512].rearrange("(c p) n -> p c n", p=128), reads=[("Wi_b", l)], writes=["Wc"])
        P.dma("sp", EBt, swa_eb_d, writes=["EBt"])
        P.dma("sp", esink, _bc(dram["swa_sink"][l]), writes=["esink"])
        P.op("act", lambda e: e.activation(out=esink, in_=esink, func=AF.Exp), reads=["esink"], writes=["esink"])
        P.op("pool", lambda e: e.memset(V1s[:, :, :, 64:65], 1.0), writes=["V1sone"])
        Wk2v = Wk2.rearrange("p c (k d e) -> p c k d e", k=2, d=2)
        for dup in range(2):
            P.op("pool", lambda e: e.tensor_copy(out=Wk2v[:, :, :, dup, :],
                                                 in_=Wc[:, :, 256:384].rearrange("p c (k e) -> p c k e", k=2)),
                 reads=["Wc"], writes=[("Wk2", dup)])
        it = 0
        for nb in range(4 if dbg.get("swa_stage", 9) >= 1 else 0):
            for (dst, is_k) in ((SQT, 0), (SKT, 1)):
                for p_ in range(2):
                    pi, pb_ = (it % 4) // 2, it % 2
                    it += 1
                    for c in range(8):
                        lw = Wk2[:, c, p_ * 128:(p_ + 1) * 128] if is_k else Wc[:, c, p_ * 128:(p_ + 1) * 128]
                        P.op("pe", lambda e: e.matmul(PS[pi][:, pb_, :], lhsT=lw, rhs=HT[:, c, nb * 512:(nb + 1) * 512],
                                                      start=(c == 0), stop=(c == 7)),
                             reads=["Wc", ("Wk2", 0), ("Wk2", 1)] + [("HT", nb * 4 + j) for j in range(4)], writes=[psk(pi, pb_)])
                    if is_k:
                        P.op("act", lambda e: e.activation(out=dst[:, p_, nb * 512:(nb + 1) * 512], in_=PS[pi][:, pb_, :],
                                                           func=AF.Identity, bias=cst[:, 3:4], scale=1.0),
                             reads=[psk(pi, pb_)], writes=[("SQK", is_k, nb)])
                    else:
                        for g in range(2):
                            P.op("act", lambda e: e.activation(
                                out=SQT[g * 64:(g + 1) * 64, p_, nb * 4:(nb + 1) * 4, g * 128:(g + 1) * 128],
                                in_=PS[pi][g * 64:(g + 1) * 64, pb_, :].rearrange("p (t q) -> p t q", t=4),
                                func=AF.Identity, bias=cst[g * 64:(g + 1) * 64, 3:4], scale=1.0),
                                reads=[psk(pi, pb_), "SQTz"], writes=[("SQK", is_k, nb)])
        P.mark("swa_V")
        for tt in range(NT if dbg.get("swa_stage", 9) >= 2 else 0):
            i2 = tt % 2
            pv = PS[2 + i2][:, 1, 0:128]
            for c in range(8):
                P.op("pe", lambda e: e.matmul(pv, lhsT=HT[:, c, tt * 128:(tt + 1) * 128], rhs=Wc[:, c, 384:512],
                                              start=(c == 0), stop=(c == 7)), reads=["Wc", ("HT", tt)], writes=[psk(2 + i2, 1)])
            P.op("act", lambda e: e.activation(out=V1s[:, tt, :, 0:64], in_=pv.rearrange("p (k d) -> p k d", k=2), func=AF.Identity,
                                               bias=cst[:, 3:4], scale=1.0),
                 reads=[psk(2 + i2, 1), ("cst", 3)], writes=[("V1s", tt)])
        P.mark("swa_attn")
        for n in range(NT if dbg.get("swa_n") is None else dbg["swa_n"]):
            i2 = n % 2
            rels = [r for r in range(3) if 0 <= n + r - 1 < NT]
            PSo = PS[2 + i2][:, 0, 0:260].rearrange("p (h d) -> p h d", h=4)
            ok_ = psk(2 + i2, 0)
            for p_ in range(2):
                PSs = PS[p_][:, :, :].rearrange("p a b -> p (a b)")
                e_, eb_ = E[p_], Eb[p_]
                for r in rels:
                    kt = n + r - 1
                    bankk = psk(p_, 0 if r < 2 else 1)
                    P.op("pe", lambda e: e.matmul(PSs[:, r * 256:(r + 1) * 256],
                                                  lhsT=SKT[:, p_, kt * 128:(kt + 1) * 128],
                                                  rhs=SQT[:, p_, n, :], start=True, stop=True),
                         reads=[("SQK", 0, n // 4), ("SQK", 1, kt // 4)], writes=[bankk])
                    P.op("act", lambda e: e.activation(out=e_[:, r, :], in_=PSs[:, r * 256:(r + 1) * 256], func=AF.Exp, scale=0.125),
                         reads=[bankk], writes=[("E", p_, r)])
                r0, r1 = rels[0], rels[-1] + 1
                P.op("pool", lambda e: e.tensor_tensor(out=eb_[:, r0:r1, :].rearrange("p r (g q) -> p r g q", g=2),
                                                       in0=e_[:, r0:r1, :].rearrange("p r (g q) -> p r g q", g=2),
                                                       in1=EBt[:, r0:r1, 2 * p_:2 * p_ + 2, :], op=ALU.mult),
                     reads=[("E", p_, r) for r in rels] + ["EBt"], writes=[("Eb", p_)])
                for g in range(2):
                    h = 2 * p_ + g
                    for r in rels:
                        kt = n + r - 1
                        P.op("pe", lambda e: e.matmul(PSo[:, h, :], lhsT=eb_[:, r, g * 128:(g + 1) * 128], rhs=V1s[:, kt, p_, 0:65],
                                                      start=(r == rels[0]), stop=(r == rels[-1])),
                             reads=[("Eb", p_), ("V1s", kt), "V1sone"], writes=[ok_])
            rck = ("rc", i2)
            P.op("dve", lambda e: e.tensor_tensor(out=rc[i2], in0=PSo[:, :, 64], in1=esink, op=ALU.add), reads=[ok_, "esink"], writes=[rck])
            P.op("dve", lambda e: e.reciprocal(out=rc[i2], in_=rc[i2]), reads=[rck], writes=[rck])
            P.op("dve", lambda e: e.tensor_tensor(out=yC[:, n, :].rearrange("p (h d) -> p h d", h=4), in0=PSo[:, :, 0:64],
                                                  in1=rc[i2].unsqueeze(2).broadcast_to([128, 4, 64]), op=ALU.mult),
                 reads=[ok_, rck], writes=[("yC", n)])
        finish_group(2, YT, gmix, yC, lambda tt: [("yC", tt)])
        P.barrier()
        A.off = mark

    def mix_ssd(l, b, YT, gmix):
        P.mark("mix_ssd")
        mark = A.off
        Wd_ = A.alloc([8, 1032], BF16)
        XBCT = A.alloc([6, S], BF16)
        X = A.alloc([NT, 256], BF16)
        Bt = A.alloc([NT, 256], BF16)
        Zs = A.alloc([NT, 256], BF16)
        yD = A.alloc([NT, 256], F32)
        msk = A.alloc([5, 128], F32)
        cwx = A.alloc([6, 4], F32)
        prm = A.alloc([20], F32)
        dsk2 = A.alloc([8], F32)
        dta = A.alloc([NT, 16], F32)
        P.dma("sp", Wd_, Wi_b[l][:, O_Z:O_Z + 1032].rearrange("(c p) n -> p c n", p=128), reads=[("Wi_b", l)], writes=["Wd_"])
        P.dma("sp", msk, ssd_masks_d, writes=["msk"])
        for k in range(3):
            P.dma("sp", cwx[:, :, k:k + 1], dram["ssd_conv_w"][l, k].rearrange("(c p o) -> p c o", p=128, o=1), writes=["cwx"],
                  allow_slow_non_contiguous=True)
        P.dma("sp", cwx[:, :, 3:4], dram["ssd_conv_b"][l].rearrange("(c p o) -> p c o", p=128, o=1), writes=["cwx"],
              allow_slow_non_contiguous=True)
        P.dma("sp", prm[:, 0:8], _bc(dram["ssd_dt_bias"][l].rearrange("a b -> (a b)")), writes=["prm0"])
        P.dma("sp", prm[:, 8:16], _bc(dram["ssd_a_log"][l].rearrange("a b -> (a b)")), writes=["prm1"])
        P.dma("sp", dsk2, _bc(dram["ssd_d"][l].rearrange("a b -> (a b)")), writes=["dsk2"])
        P.op("act", lambda e: e.activation(out=prm[:, 8:16], in_=prm[:, 8:16], func=AF.Exp), reads=["prm1"], writes=["prm1"])
        P.op("dve", lambda e: e.tensor_scalar(out=prm[:, 8:16], in0=prm[:, 8:16], scalar1=-1.0, scalar2=None, op0=ALU.mult),
             reads=["prm1"], writes=["prm1"])
        P.op("dve", lambda e: e.tensor_tensor(out=prm[:, 16:20], in0=dsk2[:, 0:4], in1=dsk2[:, 4:8], op=ALU.add),
             reads=["dsk2"], writes=["prm2"])
        mark2 = A.off
        G = A.alloc([S + 2], F32)
        T = A.alloc([S], F32)
        P.op("pool", lambda e: e.memset(G[:, 0:1], 0.0), writes=["Gl"])
        P.op("pool", lambda e: e.memset(G[:, S + 1:S + 2], 0.0), writes=["Gr"])
        it = 0
        for ch in range(6):
            for nb in range(4):
                pi, pb_ = (it % 4) // 2, it % 2
                it += 1
                for c in range(8):
                    P.op("pe", lambda e: e.matmul(PS[pi][:, pb_, :], lhsT=Wd_[:, c, 256 + ch * 128:256 + (ch + 1) * 128],
                                                  rhs=HT[:, c, nb * 512:(nb + 1) * 512], start=(c == 0), stop=(c == 7)),
                         reads=["Wd_"] + [("HT", nb * 4 + j) for j in range(4)], writes=[psk(pi, pb_)])
                P.op("act", lambda e: e.activation(out=G[:, 1 + nb * 512:1 + (nb + 1) * 512], in_=PS[pi][:, pb_, :],
                                                   func=AF.Identity, bias=cst[:, 3:4], scale=1.0),
                     reads=[psk(pi, pb_)], writes=[("G", nb)])
            gks = [("G", nb) for nb in range(4)]
            P.op("dve", lambda e: e.tensor_scalar(out=T, in0=G[:, 1:S + 1], scalar1=cwx[:, ch, 1:2], scalar2=cwx[:, ch, 3:4],
                                                  op0=ALU.mult, op1=ALU.add), reads=gks + ["cwx"], writes=["T"])
            P.op("dve", lambda e: e.scalar_tensor_tensor(out=T, in0=G[:, 0:S], scalar=cwx[:, ch, 0:1], in1=T, op0=ALU.mult, op1=ALU.add),
                 reads=gks + ["Gl", "T", "cwx"], writes=["T"])
            P.op("dve", lambda e: e.scalar_tensor_tensor(out=T, in0=G[:, 2:S + 2], scalar=cwx[:, ch, 2:3], in1=T, op0=ALU.mult, op1=ALU.add),
                 reads=gks + ["Gr", "T", "cwx"], writes=["T"])
            P.op("act", lambda e: e.activation(out=XBCT[:, ch, :], in_=T, func=AF.Silu), reads=["T"], writes=[("XBCT", ch)])
        P.barrier()
        A.off = mark2
        P.mark("ssd_D2")
        zt_ = [A.alloc([256], F32) for _ in range(2)]
        for tt in range(NT):
            i2 = tt % 2
            tsl = slice(tt * 128, (tt + 1) * 128)
            pst = PS[2 + i2][:, 0, :].bitcast(BF16).rearrange("p (a b) -> p a b", a=8)
            for ch in range(4):
                P.op("pe", lambda e: e.transpose(out=pst[:, ch, :], in_=XBCT[:, ch, tsl], identity=ident[:]),
                     reads=[("XBCT", ch), "ident"], writes=[psk(2 + i2, 0)])
            P.op("dve", lambda e: e.tensor_copy(out=X[:, tt, :].rearrange("p (a b) -> p a b", a=2), in_=pst[:, 0:2, :]),
                 reads=[psk(2 + i2, 0)], writes=[("X", tt)])
            P.op("dve", lambda e: e.tensor_copy(out=Bt[:, tt, :].rearrange("p (a b) -> p a b", a=2), in_=pst[:, 2:4, :]),
                 reads=[psk(2 + i2, 0)], writes=[("Bt", tt)])
            pz = PS[i2][:, 0, 0:256]
            pdt = PS[i2][:, 1, 0:8]
            for c in range(8):
                P.op("pe", lambda e: e.matmul(pz, lhsT=HT[:, c, tsl], rhs=Wd_[:, c, 0:256], start=(c == 0), stop=(c == 7)),
                     reads=[("HT", tt), "Wd_"], writes=[psk(i2, 0)])
            for c in range(8):
                P.op("pe", lambda e: e.matmul(pdt, lhsT=HT[:, c, tsl], rhs=Wd_[:, c, 1024:1032], start=(c == 0), stop=(c == 7)),
                     reads=[("HT", tt), "Wd_"], writes=[psk(i2, 1)])
            P.op("act", lambda e: e.activation(out=Zs[:, tt, :], in_=pz, func=AF.Silu), reads=[psk(i2, 0)], writes=[("Zs", tt)])
            dk = ("dta", tt)
            P.op("dve", lambda e: e.tensor_tensor(out=dta[:, tt, 0:8], in0=pdt, in1=prm[:, 0:8], op=ALU.add),
                 reads=[psk(i2, 1), "prm0"], writes=[dk])
            P.op("act", lambda e: e.activation(out=dta[:, tt, 0:8], in_=dta[:, tt, 0:8], func=AF.Exp), reads=[dk], writes=[dk])
            P.op("act", lambda e: e.activation(out=dta[:, tt, 0:8], in_=dta[:, tt, 0:8], func=AF.Ln, bias=cst[:, 2:3], scale=1.0),
                 reads=[dk, ("cst", 2)], writes=[dk])
            P.op("dve", lambda e: e.tensor_tensor(out=dta[:, tt, 8:16], in0=dta[:, tt, 0:8], in1=prm[:, 8:16], op=ALU.mult),
                 reads=[dk, "prm1"], writes=[dk])
        P.mark("ssd_D3")
        carry = A.alloc([4, 64], F32)
        prev = A.alloc([4, 64], BF16)
        Am = [A.alloc([4, 128], F32) for _ in range(2)]
        Lx = [A.alloc([4, 128], F32) for _ in range(2)]
        CBm = [A.alloc([2, 128], F32) for _ in range(2)]
        MT = [A.alloc([4, 128], BF16) for _ in range(2)]
        Xdt = [A.alloc([4, 64], BF16) for _ in range(2)]
        Xdd = [A.alloc([4, 64], BF16) for _ in range(2)]
        ex = [A.alloc([3, 4], F32) for _ in range(2)]
        ytmp = [A.alloc([4, 64], F32) for _ in range(2)]
        for d in range(2):
            order = list(range(NT)) if d == 0 else list(range(NT - 1, -1, -1))
            tri = msk[:, 0, :] if d == 0 else msk[:, 1, :]
            mgl = msk[:, 2, :] if d == 0 else msk[:, 3, :]
            for idx, tt in enumerate(order):
                i2 = idx % 2
                tsl = slice(tt * 128, (tt + 1) * 128)
                first = (idx == 0)
                a_ = dta[:, tt, 8 + d * 4:12 + d * 4]
                dt_ = dta[:, tt, d * 4:d * 4 + 4]
                dk = ("dta", tt)
                pc = PS[0][:, i2, 0:4]
                ptot = PS[0][:, i2, 4:8]
                P.op("pe", lambda e: e.matmul(pc, lhsT=tri, rhs=a_, start=True, stop=True), reads=["msk", dk], writes=[psk(0, i2)])
                P.op("pe", lambda e: e.matmul(ptot, lhsT=msk[:, 4, :], rhs=a_, start=True, stop=True), reads=["msk", dk], writes=[psk(0, i2)])
                exk = ("ex", i2)
                e_ = ex[i2]
                P.op("dve", lambda e: e.tensor_copy(out=e_[:, 1:3, :], in_=PS[0][:, i2, 0:8].rearrange("p (a b) -> p a b", a=2)),
                     reads=[psk(0, i2)], writes=[(exk, 1)])
                P.op("dve", lambda e: e.tensor_tensor(out=e_[:, 0, :], in0=e_[:, 2, :], in1=e_[:, 1, :], op=ALU.subtract),
                     reads=[(exk, 1)], writes=[exk])
                P.op("act", lambda e: e.activation(out=e_, in_=e_, func=AF.Exp), reads=[exk, (exk, 1)], writes=[exk])
                amk = ("Am", i2)
                P.op("pool", lambda e: e.tensor_tensor(out=Am[i2], in0=mgl.unsqueeze(1).broadcast_to([128, 4, 128]),
                                                       in1=a_.unsqueeze(2).broadcast_to([128, 4, 128]), op=ALU.mult),
                     reads=["msk", dk], writes=[amk])
                pseg = PS[1][:, i2, :].rearrange("p (h t) -> p h t", h=4)
                for h in range(4):
                    P.op("pe", lambda e: e.matmul(pseg[:, h, :], lhsT=Am[i2][:, h, :], rhs=tri, start=True, stop=True),
                         reads=[amk, "msk"], writes=[psk(1, i2)])
                lk = ("Lx", i2)
                P.op("act", lambda e: e.activation(out=Lx[i2], in_=pseg, func=AF.Exp), reads=[psk(1, i2)], writes=[lk])
                pcb = PS[2][:, i2, 0:256].rearrange("p (g t) -> p g t", g=2)
                for g in range(2):
                    P.op("pe", lambda e: e.matmul(pcb[:, g, :], lhsT=XBCT[:, 2 + g, tsl], rhs=XBCT[:, 4 + g, tsl], start=True, stop=True),
                         reads=[("XBCT", 2 + g), ("XBCT", 4 + g)], writes=[psk(2, i2)])
                cbk = ("CBm", i2)
                P.op("dve", lambda e: e.tensor_tensor(out=CBm[i2], in0=pcb, in1=tri.unsqueeze(1).broadcast_to([128, 2, 128]), op=ALU.mult),
                     reads=[psk(2, i2), "msk"], writes=[cbk])
                mk_ = ("MT", i2)
                P.op("pool", lambda e: e.tensor_tensor(out=MT[i2].rearrange("p (g r) t -> p g r t", g=2),
                                                       in0=Lx[i2].rearrange("p (g r) t -> p g r t", g=2),
                                                       in1=CBm[i2].unsqueeze(2).broadcast_to([128, 2, 2, 128]), op=ALU.mult),
                     reads=[lk, cbk], writes=[mk_])
                xk, xdk = ("Xdt", i2), ("Xdd", i2)
                X4 = X[:, tt, :].rearrange("p (h d) -> p h d", h=4)
                P.op("dve", lambda e: e.tensor_tensor(out=Xdt[i2], in0=X4, in1=dt_.unsqueeze(2).broadcast_to([128, 4, 64]), op=ALU.mult),
                     reads=[("X", tt), dk], writes=[xk])
                P.op("dve", lambda e: e.tensor_tensor(out=Xdd[i2], in0=Xdt[i2], in1=e_[:, 0, :].unsqueeze(2).broadcast_to([128, 4, 64]), op=ALU.mult),
                     reads=[xk, exk], writes=[xdk])
                pyd = PS[3][:, i2, 0:256].rearrange("p (h d) -> p h d", h=4)
                pyo = PS[3][:, i2, 256:512].rearrange("p (h d) -> p h d", h=4)
                for h in range(4):
                    P.op("pe", lambda e: e.matmul(pyd[:, h, :], lhsT=MT[i2][:, h, :], rhs=Xdt[i2][:, h, :], start=True, stop=True),
                         reads=[mk_, xk], writes=[psk(3, i2)])
                if not first:
                    for h in range(4):
                        P.op("pe", lambda e: e.matmul(pyo[:, h, :], lhsT=XBCT[:, 4 + h // 2, tsl], rhs=prev[:, h, :], start=True, stop=True),
                             reads=[("XBCT", 4 + h // 2), "prev"], writes=[psk(3, i2)])
                yk_ = ("yD", tt)
                y4 = yD[:, tt, :].rearrange("p (h d) -> p h d", h=4)
                if d == 0:
                    P.op("dve", lambda e: e.tensor_copy(out=y4, in_=pyd), reads=[psk(3, i2)], writes=[yk_])
                else:
                    P.op("dve", lambda e: e.tensor_tensor(out=y4, in0=y4, in1=pyd, op=ALU.add), reads=[psk(3, i2), yk_], writes=[yk_])
                if not first:
                    tk_ = ("ytmp", i2)
                    P.op("dve", lambda e: e.tensor_tensor(out=ytmp[i2], in0=pyo, in1=e_[:, 1, :].unsqueeze(2).broadcast_to([128, 4, 64]), op=ALU.mult),
                         reads=[psk(3, i2), exk], writes=[tk_])
                    P.op("pool", lambda e: e.tensor_tensor(out=y4, in0=y4, in1=ytmp[i2], op=ALU.add), reads=[tk_, yk_], writes=[yk_])
                pst_ = PS[2][:, i2, 256:512].rearrange("p (h d) -> p h d", h=4)
                for h in range(4):
                    P.op("pe", lambda e: e.matmul(pst_[:, h, :], lhsT=Bt[:, tt, (h // 2) * 128:(h // 2 + 1) * 128], rhs=Xdd[i2][:, h, :],
                                                  start=True, stop=True), reads=[("Bt", tt), xdk], writes=[psk(2, i2)])
                if first:
                    P.op("dve", lambda e: e.tensor_copy(out=carry, in_=pst_), reads=[psk(2, i2)], writes=["carry"])
                else:
                    P.op("dve", lambda e: e.tensor_tensor(out=carry, in0=carry, in1=e_[:, 2, :].unsqueeze(2).broadcast_to([128, 4, 64]), op=ALU.mult),
                         reads=["carry", exk], writes=["carry"])
                    P.op("dve", lambda e: e.tensor_tensor(out=carry, in0=carry, in1=pst_, op=ALU.add), reads=["carry", psk(2, i2)], writes=["carry"])
                P.op("act", lambda e: e.activation(out=prev, in_=carry, func=AF.Identity, bias=cst[:, 3:4], scale=1.0),
                     reads=["carry"], writes=["prev"])
        P.mark("ssd_D4")
        for tt in range(NT):
            i2 = tt % 2
            yk_ = ("yD", tt)
            y4 = yD[:, tt, :].rearrange("p (h d) -> p h d", h=4)
            X4 = X[:, tt, :].rearrange("p (h d) -> p h d", h=4)
            tk_ = ("ytmp", i2)
            P.op("pool", lambda e: e.tensor_tensor(out=ytmp[i2], in0=X4, in1=prm[:, 16:20].unsqueeze(2).broadcast_to([128, 4, 64]), op=ALU.mult),
                 reads=[("X", tt), "prm2"], writes=[tk_])
            P.op("pool", lambda e: e.tensor_tensor(out=y4, in0=y4, in1=ytmp[i2], op=ALU.add), reads=[tk_, yk_], writes=[yk_])
            P.op("dve", lambda e: e.tensor_tensor(out=yD[:, tt, :], in0=yD[:, tt, :], in1=Zs[:, tt, :], op=ALU.mult), reads=[yk_, ("Zs", tt)], writes=[yk_])
        finish_group(3, YT, gmix, yD, lambda tt: [("yD", tt)])
        P.barrier()
        A.off = mark

    def hyena_prologue(l):
        P.mark("hyena_prologue")
        P.barrier()
        A.reset()
        featT = A.alloc([S], F32)
        dec = A.alloc([NT, 256], F32)
        w1 = A.alloc([64], F32)
        w2 = A.alloc([64], F32)
        w3 = A.alloc([1024], F32)
        pr = A.alloc([6], F32)
        h1 = A.alloc([S], F32)
        h2 = A.alloc([S], F32)
        tmp = [A.alloc([512], F32) for _ in range(2)]
        tmpf = [A.alloc([512], F32) for _ in range(2)]
        tmpi = [A.alloc([512], F32).bitcast(mybir.dt.int32) for _ in range(2)]
        sd = A.alloc([2, 2, NT, 256], BF16)
        hfd = [A.alloc([2, 256], F32) for _ in range(2)]
        hbd = [A.alloc([2, 256], F32) for _ in range(2)]
        slab = [A.alloc([2, 16, 128], BF16) for _ in range(2)]
        pqs = [A.alloc([512], BF16) for _ in range(2)]
        P.dma("sp", featT[0:33, :], hy_featT_d, writes=["featT"])
        P.dma("sp", dec, hy_decay_d.rearrange("(t p) c -> p t c", p=128), writes=["dec"])
        P.dma("sp", w1[0:33, :], dram["hy_f_w1"][l], writes=["w1"])
        P.dma("sp", w2[0:64, :], dram["hy_f_w2"][l], writes=["w2"])
        P.dma("sp", w3[0:64, :], dram["hy_f_w3"][l], writes=["w3"])
        for i, nm in enumerate(["hy_f_b1", "hy_f_freq", "hy_f_b2"]):
            P.dma("sp", pr[0:64, i:i + 1], dram[nm][l].rearrange("(p o) -> p o", o=1), writes=["pr"], allow_slow_non_contiguous=True)

        def sin_layer(dst, wT, kdim, src, srck, bcol):
            for nb in range(4):
                i2 = nb % 2
                ps = PS[0][0:64, i2, :]
                P.op("pe", lambda e: e.matmul(ps, lhsT=wT[0:kdim, :], rhs=src[0:kdim, nb * 512:(nb + 1) * 512], start=True, stop=True),
                     reads=[srck, "w1", "w2"], writes=[psk(0, i2)])
                t_ = tmp[i2][0:64, :]
                tk = ("tmp", i2)
                P.op("dve", lambda e: e.tensor_scalar(out=t_, in0=ps, scalar1=pr[0:64, bcol:bcol + 1], scalar2=pr[0:64, 1:2],
                                                      op0=ALU.add, op1=ALU.mult), reads=[psk(0, i2), "pr"], writes=[tk])
                ti_ = tmpi[i2][0:64, :]
                tf_ = tmpf[i2][0:64, :]
                P.op("dve", lambda e: e.tensor_scalar(out=t_, in0=t_, scalar1=1.0 / (2 * math.pi), scalar2=None, op0=ALU.mult),
                     reads=[tk], writes=[tk])
                P.op("dve", lambda e: e.tensor_copy(out=ti_, in_=t_), reads=[tk], writes=[(tk, "i")])
                P.op("dve", lambda e: e.tensor_copy(out=tf_, in_=ti_), reads=[(tk, "i")], writes=[(tk, "f")])
                P.op("dve", lambda e: e.tensor_tensor(out=t_, in0=t_, in1=tf_, op=ALU.subtract), reads=[tk, (tk, "f")], writes=[tk])
                P.op("dve", lambda e: e.tensor_scalar(out=tf_, in0=t_, scalar1=0.5, scalar2=None, op0=ALU.is_gt), reads=[tk], writes=[(tk, "f")])
                P.op("dve", lambda e: e.tensor_tensor(out=t_, in0=t_, in1=tf_, op=ALU.subtract), reads=[tk, (tk, "f")], writes=[tk])
                P.op("dve", lambda e: e.tensor_scalar(out=tf_, in0=t_, scalar1=-0.5, scalar2=None, op0=ALU.is_lt), reads=[tk], writes=[(tk, "f")])
                P.op("dve", lambda e: e.tensor_tensor(out=t_, in0=t_, in1=tf_, op=ALU.add), reads=[tk, (tk, "f")], writes=[tk])
                P.op("act", lambda e: e.activation(out=dst[0:64, nb * 512:(nb + 1) * 512], in_=t_, func=AF.Sin,
                                                   bias=cst[0:64, 3:4], scale=2 * math.pi), reads=[tk, ("cst", 3)], writes=[(dst.tensor.name, id(dst))])
            return (dst.tensor.name, id(dst))

        k1 = sin_layer(h1, w1, 33, featT, "featT", 0)
        k2 = sin_layer(h2, w2, 64, h1, k1, 2)
        for tt in range(NT):
            i2 = tt % 2
            for n in range(2):
                P.op("pe", lambda e: e.matmul(PS[1][:, n, :], lhsT=h2[0:64, tt * 128:(tt + 1) * 128], rhs=w3[0:64, n * 512:(n + 1) * 512],
                                              start=True, stop=True), reads=[k2, "w3"], writes=[psk(1, n)])
            ps4 = PS[1][:, :, :].rearrange("p n (d c) -> p n d c", d=2)
            dbc = dec[:, tt, :].unsqueeze(1).broadcast_to([128, 2, 256])
            P.op("dve", lambda e: e.tensor_tensor(out=hfd[i2], in0=ps4[:, :, 0, :], in1=dbc, op=ALU.mult),
                 reads=[psk(1, 0), psk(1, 1), "dec"], writes=[("hfd", i2)])
            P.op("dve", lambda e: e.tensor_tensor(out=hbd[i2], in0=ps4[:, :, 1, :], in1=dbc, op=ALU.mult),
                 reads=[psk(1, 0), psk(1, 1), "dec"], writes=[("hbd", i2)])
            if tt == 0:
                P.op("dve", lambda e: e.memset(hbd[i2][0:1, :, :], 0.0), reads=[("hbd", i2)], writes=[("hbd", i2)])
            P.op("pool", lambda e: e.tensor_tensor(out=sd[:, :, 0, tt, :], in0=hfd[i2], in1=hbd[i2], op=ALU.add),
                 reads=[("hfd", i2), ("hbd", i2)], writes=[("sd", tt)])
            P.op("pool", lambda e: e.tensor_tensor(out=sd[:, :, 1, tt, :], in0=hbd[i2], in1=hfd[i2], op=ALU.subtract),
                 reads=[("hfd", i2), ("hbd", i2)], writes=[("sd", tt)])
        sdk = [("sd", tt) for tt in range(NT)]
        it = 0
        for fc in range(16):
            sb = slab[fc % 2]
            sk_ = ("slab", fc % 2)
            for m in range(2):
                P.dma("sp", sb[:, m], dft_f_d[m, fc], writes=[(sk_, m)])
            for n in range(2):
                i2 = it % 2
                it += 1
                for m in range(2):
                    for tc in range(16):
                        P.op("pe", lambda e: e.matmul(PS[2][:, i2, m * 256:(m + 1) * 256], lhsT=sb[:, m, tc, :], rhs=sd[:, n, m, tc, :],
                                                      start=(tc == 0), stop=(tc == 15)), reads=[(sk_, m)] + sdk, writes=[psk(2, i2)])
                P.op("act", lambda e: e.activation(out=pqs[i2], in_=PS[2][:, i2, :], func=AF.Identity, bias=cst[:, 3:4], scale=1.0),
                     reads=[psk(2, i2)], writes=[("pqs", i2)])
                P.dma("sp", PQ_b[l, n, fc], pqs[i2], reads=[("pqs", i2)], writes=[("PQ_b", l, n, fc)])

    def mix_hyena(l, b, YT, gmix):
        P.mark("mix_hyena")
        mark = A.off
        V0 = A.alloc([NT, 256], BF16)
        X12 = A.alloc([2, NT, 256], BF16)
        cwx = A.alloc([6, 4], F32)
        hbias = A.alloc([2, 256], F32)
        yB = A.alloc([NT, 256], F32)
        for k in range(3):
            P.dma("sp", cwx[:, :, k:k + 1], dram["hy_conv_w"][l, k].rearrange("(c p o) -> p c o", p=128, o=1), writes=["cwx"],
                  allow_slow_non_contiguous=True)
        P.dma("sp", cwx[:, :, 3:4], dram["hy_conv_b"][l].rearrange("(c p o) -> p c o", p=128, o=1), writes=["cwx"],
              allow_slow_non_contiguous=True)
        P.dma("sp", hbias, _bc(dram["hy_bias"][l].rearrange("a b -> (a b)")), writes=["hbias"])
        mark2 = A.off
        Wb = A.alloc([8, 768], BF16)
        P.dma("sp", Wb, Wi_b[l][:, O_HY:O_HY + 768].rearrange("(c p) n -> p c n", p=128), reads=[("Wi_b", l)], writes=["Wb"])
        UCT = A.alloc([6, S], BF16)
        G = A.alloc([S + 2], F32)
        T = A.alloc([S], F32)
        P.op("pool", lambda e: e.memset(G[:, 0:1], 0.0), writes=["Gl"])
        P.op("pool", lambda e: e.memset(G[:, S + 1:S + 2], 0.0), writes=["Gr"])
        it = 0
        for ch in range(6):
            for nb in range(4):
                pi, pb_ = (it % 4) // 2, it % 2
                it += 1
                for c in range(8):
                    P.op("pe", lambda e: e.matmul(PS[pi][:, pb_, :], lhsT=Wb[:, c, ch * 128:(ch + 1) * 128],
                                                  rhs=HT[:, c, nb * 512:(nb + 1) * 512], start=(c == 0), stop=(c == 7)),
                         reads=["Wb"] + [("HT", nb * 4 + j) for j in range(4)], writes=[psk(pi, pb_)])
                P.op("act", lambda e: e.activation(out=G[:, 1 + nb * 512:1 + (nb + 1) * 512], in_=PS[pi][:, pb_, :],
                                                   func=AF.Identity, bias=cst[:, 3:4], scale=1.0),
                     reads=[psk(pi, pb_)], writes=[("G", nb)])
            gks = [("G", nb) for nb in range(4)]
            P.op("dve", lambda e: e.tensor_scalar(out=T, in0=G[:, 1:S + 1], scalar1=cwx[:, ch, 1:2], scalar2=cwx[:, ch, 3:4],
                                                  op0=ALU.mult, op1=ALU.add), reads=gks + ["cwx"], writes=["T"])
            P.op("dve", lambda e: e.scalar_tensor_tensor(out=T, in0=G[:, 0:S], scalar=cwx[:, ch, 0:1], in1=T, op0=ALU.mult, op1=ALU.add),
                 reads=gks + ["Gl", "T", "cwx"], writes=["T"])
            P.op("dve", lambda e: e.scalar_tensor_tensor(out=UCT[:, ch, :], in0=G[:, 2:S + 2], scalar=cwx[:, ch, 2:3], in1=T, op0=ALU.mult, op1=ALU.add),
                 reads=gks + ["Gr", "T", "cwx"], writes=[("UCT", ch)])
        for tt in range(NT):
            i2 = tt % 2
            tsl = slice(tt * 128, (tt + 1) * 128)
            pst = PS[2 + i2][:, 0, :].bitcast(BF16).rearrange("p (a b) -> p a b", a=8)
            for ch in range(6):
                P.op("pe", lambda e: e.transpose(out=pst[:, ch, :], in_=UCT[:, ch, tsl], identity=ident[:]),
                     reads=[("UCT", ch), "ident"], writes=[psk(2 + i2, 0)])
            P.op("dve", lambda e: e.tensor_copy(out=V0[:, tt, :].rearrange("p (a b) -> p a b", a=2), in_=pst[:, 0:2, :]),
                 reads=[psk(2 + i2, 0)], writes=[("z0", tt)])
            P.op("dve", lambda e: e.tensor_copy(out=X12[:, :, tt, :].rearrange("p n (a b) -> p n a b", a=2),
                                                in_=pst[:, 2:6, :].rearrange("p (n a) b -> p n a b", n=2)),
                 reads=[psk(2 + i2, 0)], writes=[("X12", tt)])
        P.barrier()
        P.mark("hy_conv")
        A.off = mark2
        Z1 = A.alloc([NT, 256], BF16)
        PQ = A.alloc([16, 512], BF16)
        Yc = A.alloc([2, 16, 256], BF16)
        slab = [A.alloc([2, 16, 128], BF16) for _ in range(2)]
        AB = [A.alloc([512], F32) for _ in range(2)]
        tq = [A.alloc([4, 256], F32) for _ in range(2)]
        te = [A.alloc([256], F32) for _ in range(2)]
        sit = 0
        for n in range(2):
            zin = V0 if n == 0 else Z1
            zkf = (lambda tt: ("z0", tt)) if n == 0 else (lambda tt: ("z1", tt))
            P.dma("sp", PQ, PQ_b[l, n].rearrange("fc p x -> p fc x"), reads=[("PQ_b", l, n, fc) for fc in range(16)], writes=["PQ"])
            zks = [zkf(tt) for tt in range(NT)]
            for fc in range(16):
                sb = slab[sit % 2]
                sk_ = ("slab", sit % 2)
                sit += 1
                i2 = fc % 2
                for m in range(2):
                    P.dma("sp", sb[:, m], dft_f_d[m, fc], writes=[(sk_, m)])
                for m in range(2):
                    for tc in range(16):
                        P.op("pe", lambda e: e.matmul(PS[0][:, i2, m * 256:(m + 1) * 256], lhsT=sb[:, m, tc, :], rhs=zin[:, tc, :],
                                                      start=(tc == 0), stop=(tc == 15)), reads=[(sk_, m)] + zks, writes=[psk(0, i2)])
                ab = AB[i2]
                abk = ("AB", i2)
                P.op("act", lambda e: e.activation(out=ab, in_=PS[0][:, i2, :], func=AF.Identity, bias=cst[:, 3:4], scale=1.0),
                     reads=[psk(0, i2)], writes=[abk])
                q_ = tq[i2]
                qk_ = ("tq", i2)
                Aa, Bb = ab[:, 0:256], ab[:, 256:512]
                Pp, Qq = PQ[:, fc, 0:256], PQ[:, fc, 256:512]
                P.op("dve", lambda e: e.tensor_tensor(out=q_[:, 0, :], in0=Aa, in1=Pp, op=ALU.mult), reads=[abk, "PQ"], writes=[(qk_, 0)])
                P.op("pool", lambda e: e.tensor_tensor(out=q_[:, 1, :], in0=Bb, in1=Qq, op=ALU.mult), reads=[abk, "PQ"], writes=[(qk_, 1)])
                P.op("dve", lambda e: e.tensor_tensor(out=q_[:, 2, :], in0=Aa, in1=Qq, op=ALU.mult), reads=[abk, "PQ"], writes=[(qk_, 2)])
                P.op("pool", lambda e: e.tensor_tensor(out=q_[:, 3, :], in0=Bb, in1=Pp, op=ALU.mult), reads=[abk, "PQ"], writes=[(qk_, 3)])
                P.op("dve", lambda e: e.tensor_tensor(out=Yc[:, 0, fc, :], in0=q_[:, 0, :], in1=q_[:, 1, :], op=ALU.add),
                     reads=[(qk_, 0), (qk_, 1)], writes=[("Yc", fc)])
                P.op("pool", lambda e: e.tensor_tensor(out=Yc[:, 1, fc, :], in0=q_[:, 2, :], in1=q_[:, 3, :], op=ALU.subtract),
                     reads=[(qk_, 2), (qk_, 3)], writes=[("Yc", fc)])
            yks = [("Yc", fc) for fc in range(16)]
            for tcl in range(16):
                sb = slab[sit % 2]
                sk_ = ("slab", sit % 2)
                sit += 1
                i2 = tcl % 2
                for m in range(2):
                    P.dma("sp", sb[:, m], dft_i_d[m, tcl], writes=[(sk_, m)])
                py = PS[1][:, i2, 0:256]
                for m in range(2):
                    for fc in range(16):
                        P.op("pe", lambda e: e.matmul(py, lhsT=sb[:, m, fc, :], rhs=Yc[:, m, fc, :],
                                                      start=(m == 0 and fc == 0), stop=(m == 1 and fc == 15)),
                             reads=[(sk_, m)] + yks, writes=[psk(1, i2)])
                t_ = te[i2]
                tk = ("te", i2)
                P.op("pool", lambda e: e.tensor_tensor(out=t_, in0=zin[:, tcl, :], in1=hbias[:, n, :], op=ALU.mult),
                     reads=[zkf(tcl), "hbias"], writes=[tk])
                P.op("dve", lambda e: e.tensor_tensor(out=t_, in0=t_, in1=py, op=ALU.add), reads=[tk, psk(1, i2)], writes=[tk])
                if n == 0:
                    P.op("pool", lambda e: e.tensor_tensor(out=Z1[:, tcl, :], in0=t_, in1=X12[:, 0, tcl, :], op=ALU.mult),
                         reads=[tk, ("X12", tcl)], writes=[("z1", tcl)])
                else:
                    P.op("pool", lambda e: e.tensor_tensor(out=yB[:, tcl, :], in0=t_, in1=X12[:, 1, tcl, :], op=ALU.mult),
                         reads=[tk, ("X12", tcl)], writes=[("yB", tcl)])
        finish_group(1, YT, gmix, yB, lambda tt: [("yB", tt)])
        P.barrier()
        A.off = mark

    def stage_mix(l, b):
        P.barrier()
        A.reset()
        YT = A.alloc([8, S], BF16)
        gmix = A.alloc([D], F32)
        P.dma("sp", gmix, _bc(dram["mix_norm_g"][l]), writes=["gmix"])
        fns = {"a": mix_mla, "b": mix_hyena, "c": mix_swa, "d": mix_ssd}
        for g, nm in enumerate("abcd"):
            if nm in groups and nm in fns:
                fns[nm](l, b, YT, gmix)
            else:
                y_from_dbg(l, b, g, YT, gmix)
        return YT

    stop = dbg.get("stop")
    if "b" in groups:
        for l in range(nlayer):
            hyena_prologue(l)

    def dump_and_stop():
        P.barrier()
        P.dma("sp", out_d[0], Hf, writes=["outdump"])
        P.barrier()
        P.emit()
        return nc

    for b in range(nseq):
        stage_embed(b)
        if stop == "embed":
            return dump_and_stop()
        for l in range(nlayer):
            YT = stage_mix(l, b)
            stage_outproj(l, YT)
            if stop == "outproj":
                return dump_and_stop()
            stage_ffn(l)
            if stop == "ffn":
                return dump_and_stop()
            stage_ple(l, b, last=(l == nlayer - 1))
            if stop == "ple":
                return dump_and_stop()
    P.barrier()
    P.mark("end")
    P.emit()
    nc._marks = P.marks
    return nc


def host_consts():
    c = {}
    inv = 10000.0 ** (-np.arange(0, 32, 2, dtype=np.float32) / 32.0)
    ang = np.arange(S, dtype=np.float32)[:, None] * inv[None, :].astype(np.float32)
    c["rope_cs"] = np.concatenate([np.cos(ang), np.sin(ang)], axis=1).astype(np.float32)
    j = np.arange(128)[:, None, None, None]
    r = np.arange(3)[None, :, None, None]
    q = np.arange(128)[None, None, None, :]
    dist = np.abs(q - j - (r - 1) * 128).astype(np.float32)
    slopes = ((2.0 ** (-8.0 / 4)) ** np.arange(1, 5, dtype=np.float32))[None, None, :, None]
    c["swa_eb"] = np.where(dist <= 128, np.exp(-slopes * dist), 0.0).astype(np.float32)
    u = np.arange(128)[:, None]
    t = np.arange(128)[None, :]
    c["ssd_masks"] = np.stack([(u <= t), (u >= t), (u > t), (u < t), np.ones((128, 128), bool)], axis=1).astype(np.float32)
    th = 2.0 * np.pi / 4096.0
    f = np.arange(S, dtype=np.float64)[:, None] + 0.5
    t = np.arange(S, dtype=np.float64)[None, :]
    ang = th * f * t
    mats = [np.cos(ang), np.sin(ang)]
    dff = np.empty((2, 16, 128, 16, 128), dtype=ml_dtypes.bfloat16)
    dfi = np.empty((2, 16, 128, 16, 128), dtype=ml_dtypes.bfloat16)
    for m in range(2):
        M = mats[m].reshape(16, 128, 16, 128)
        dff[m] = M.transpose(0, 3, 2, 1).astype(np.float32)
        sgn = 1.0 if m == 0 else -1.0
        dfi[m] = (sgn / 2048.0 * M).transpose(2, 1, 0, 3).astype(np.float32)
    c["dft_f"] = dff
    c["dft_i"] = dfi
    tl = np.linspace(0.0, 1.0, S, dtype=np.float32)[:, None]
    ang2 = (2.0 * math.pi * np.arange(S, dtype=np.float32)[:, None] / S).astype(np.float32)
    bands = np.linspace(1e-4, 15.0, 16, dtype=np.float32)[None, :]
    feat = np.concatenate([tl, np.cos(bands * ang2), -np.sin(bands * ang2)], -1).astype(np.float32)
    c["hy_featT"] = np.ascontiguousarray(feat.T)
    max_decay = math.log(1e-2) / 0.3
    min_decay = math.log(1e-2) / 1.5
    deltas = np.linspace(min_decay, max_decay, 256, dtype=np.float32)
    c["hy_decay"] = np.exp(-tl * np.abs(deltas)[None, :]).astype(np.float32)
    return c


def kernel(**inputs):
    ncores = 8
    nseq = 32 // ncores
    nc = build(nseq=nseq, nlayer=2)
    in_maps = []
    consts = host_consts()
    for c in range(ncores):
        m = {"x": np.ascontiguousarray(inputs["x"][c * nseq:(c + 1) * nseq]),
             "p": np.ascontiguousarray(inputs["p"][:, c * nseq:(c + 1) * nseq])}
        for n in WNAMES:
            m[n] = np.ascontiguousarray(inputs[n])
        m.update(consts)
        in_maps.append(m)
    res = run_bass_kernel_spmd(nc, in_maps, core_ids=list(range(ncores)))
    return np.concatenate([r["out"] for r in res.results], axis=0)
```

```python
import contextlib
import math
import numpy as np
import ml_dtypes
import concourse.bass as bass
import concourse.mybir as mybir
from concourse.bass_utils import run_bass_kernel_spmd

F32 = mybir.dt.float32
BF16 = mybir.dt.bfloat16
AF = mybir.ActivationFunctionType
ALU = mybir.AluOpType
AX = mybir.AxisListType

S = 2048
D = 1024
NT = S // 128
DFF = 2816
NFC = DFF // 128
INW = 2728
ALPHA = (2.0 * 2) ** 0.25
LN_EPS = 1e-5
RMS_EPS = 1e-6
O_CQ, O_CKV, O_KR, O_HY, O_SQ, O_SK, O_SV, O_Z, O_XBC, O_DT = 0, 256, 384, 416, 1184, 1440, 1568, 1696, 1952, 2720

COMPUTE = ("pe", "act", "dve", "pool")
NDMASEM = 12


class _Cap:
    def __getattr__(self, name):
        def f(*a, **k):
            self.rec = (name, a, k)
            return self
        return f


class Prog:
    def __init__(self, nc):
        self.nc = nc
        self.ops = {e: [] for e in ("pe", "act", "dve", "pool", "sp")}
        self.cnt = {e: 0 for e in COMPUTE}
        self.dq_eng = {"sp": "sp", "act": "act", "pool": "pool"}
        self.dq_n = {q: 0 for q in self.dq_eng}
        self.dq_semcnt = {q: [0] * NDMASEM for q in self.dq_eng}
        self.last_w = {}
        self.readers = {}
        self.seen = {e: {} for e in self.ops}
        self.marks = []

    def mark(self, name):
        self.marks.append((name, dict(self.cnt)))

    def _deps(self, eng, reads, writes):
        deps = {}

        def add(tok):
            if tok is None:
                return
            s, v = tok
            if deps.get(s, 0) < v:
                deps[s] = v

        for r in reads:
            add(self.last_w.get(r))
        for w in writes:
            add(self.last_w.get(w))
            for t in self.readers.get(w, ()):
                add(t)
        out = []
        for s, v in deps.items():
            if s == "pe" and eng == "pe":
                continue
            if self.seen[eng].get(s, 0) >= v:
                continue
            self.seen[eng][s] = v
            out.append((s, v))
        return out

    def _commit(self, tok, reads, writes):
        for r in reads:
            self.readers.setdefault(r, []).append(tok)
        for w in writes:
            self.last_w[w] = tok
            self.readers[w] = []

    def op(self, eng, fn, reads=(), writes=()):
        cap = _Cap()
        fn(cap)
        name, a, k = cap.rec
        fn = lambda e, name=name, a=a, k=k: getattr(e, name)(*a, **k)
        waits = self._deps(eng, reads, writes)
        self.cnt[eng] += 1
        tok = (eng, self.cnt[eng])
        self.ops[eng].append((fn, waits, (eng, 1)))
        self._commit(tok, reads, writes)

    def dma(self, q, out, in_, reads=(), writes=(), **kw):
        eng = self.dq_eng[q]
        n = self.dq_n[q]
        self.dq_n[q] += 1
        si = n % NDMASEM
        sname = f"d_{q}_{si}"
        waits = self._deps(eng, reads, writes)
        prev = self.dq_semcnt[q][si]
        if prev > 0 and self.seen[eng].get(sname, 0) < prev:
            self.seen[eng][sname] = prev
            waits.append((sname, prev))
        self.dq_semcnt[q][si] += 16
        tok = (sname, self.dq_semcnt[q][si])

        def fn(e, out=out, in_=in_, kw=kw):
            return e.dma_start(out=out, in_=in_, **kw)

        self.ops[eng].append((fn, waits, (sname, 16)))
        self._commit(tok, reads, writes)
        return tok

    def barrier(self, skip_q=()):
        toks = [(e, self.cnt[e]) for e in COMPUTE if self.cnt[e] > 0]
        for q in self.dq_eng:
            if q in skip_q:
                continue
            for i in range(NDMASEM):
                if self.dq_semcnt[q][i] > 0:
                    toks.append((f"d_{q}_{i}", self.dq_semcnt[q][i]))
        for eng in self.ops:
            waits = []
            for s, v in toks:
                if s == eng and eng == "pe":
                    continue
                if self.seen[eng].get(s, 0) < v:
                    self.seen[eng][s] = v
                    waits.append((s, v))
            if waits:
                self.ops[eng].append((None, waits, None))
        pref = tuple(f"d_{q}_" for q in skip_q)
        self.last_w = {k: v for k, v in self.last_w.items() if pref and v[0].startswith(pref)}
        self.readers = {}

    def emit(self):
        nc = self.nc
        semnames = list(COMPUTE) + [f"d_{q}_{i}" for q in self.dq_eng for i in range(NDMASEM)]
        sems = {}
        with contextlib.ExitStack() as st:
            for s in semnames:
                sems[s] = st.enter_context(nc.semaphore(s))
            block = st.enter_context(nc.Block())

            def run(engname):
                def body(e):
                    for fn, waits, inc in self.ops[engname]:
                        for s, v in waits:
                            e.wait_ge(sems[s], v)
                        if fn is not None:
                            fn(e).then_inc(sems[inc[0]], inc[1])
                return body

            block.tensor(run("pe"))
            block.scalar(run("act"))
            block.vector(run("dve"))
            block.gpsimd(run("pool"))
            block.sync(run("sp"))


class Arena:
    def __init__(self, tensor, nbytes):
        self.t = tensor
        self.nbytes = nbytes
        self.off = 0

    def reset(self):
        self.off = 0

    def alloc(self, shape, dtype, parts=128):
        esz = 4 if dtype == F32 else 2
        n = int(np.prod(shape))
        nb = n * esz
        self.off = (self.off + 31) // 32 * 32
        assert self.off + nb <= self.nbytes, f"arena overflow {self.off + nb} > {self.nbytes}"
        a = self.t[0:parts, self.off // 2:(self.off + nb) // 2]
        self.off += nb
        if dtype == F32:
            a = a.bitcast(F32)
        if len(shape) == 2:
            a = a.rearrange("p (a b) -> p a b", a=shape[0])
        elif len(shape) == 3:
            a = a.rearrange("p (a b c) -> p a b c", a=shape[0], b=shape[1])
        elif len(shape) == 4:
            a = a.rearrange("p (a b c d) -> p a b c d", a=shape[0], b=shape[1], c=shape[2])
        return a


def _bc(ap1d, parts=128):
    return ap1d.partition_broadcast(parts)


WNAMES = ["emb_ln_g", "emb_ln_b", "w_in", "mla_q_norm", "mla_kv_norm", "mla_w_uq", "mla_w_ukv",
          "hy_conv_w", "hy_conv_b", "hy_f_w1", "hy_f_b1", "hy_f_freq", "hy_f_w2", "hy_f_b2", "hy_f_w3", "hy_bias",
          "swa_sink", "ssd_conv_w", "ssd_conv_b", "ssd_dt_bias", "ssd_a_log", "ssd_d", "mix_norm_g", "w_out",
          "ln1_g", "ln1_b", "ffn_w_gate", "ffn_w_up", "ffn_conv_w", "ffn_conv_b", "ffn_w_down",
          "ln2_g", "ln2_b", "ple_w_proj", "ple_w_gate", "ple_b_gate", "ln3_g", "ln3_b"]

WSHAPES = {
    "emb_ln_g": [D], "emb_ln_b": [D], "w_in": [2, D, INW], "mla_q_norm": [2, 256], "mla_kv_norm": [2, 128],
    "mla_w_uq": [2, 256, 384], "mla_w_ukv": [2, 128, 512], "hy_conv_w": [2, 3, 768], "hy_conv_b": [2, 768],
    "hy_f_w1": [2, 33, 64], "hy_f_b1": [2, 64], "hy_f_freq": [2, 64], "hy_f_w2": [2, 64, 64], "hy_f_b2": [2, 64],
    "hy_f_w3": [2, 64, 1024], "hy_bias": [2, 2, 256], "swa_sink": [2, 4], "ssd_conv_w": [2, 3, 768],
    "ssd_conv_b": [2, 768], "ssd_dt_bias": [2, 2, 4], "ssd_a_log": [2, 2, 4], "ssd_d": [2, 2, 4],
    "mix_norm_g": [2, D], "w_out": [2, D, D], "ln1_g": [2, D], "ln1_b": [2, D], "ffn_w_gate": [2, D, DFF],
    "ffn_w_up": [2, D, DFF], "ffn_conv_w": [2, 3, DFF], "ffn_conv_b": [2, DFF], "ffn_w_down": [2, DFF, D],
    "ln2_g": [2, D], "ln2_b": [2, D], "ple_w_proj": [2, 256, D], "ple_w_gate": [2, D, D], "ple_b_gate": [2, D],
    "ln3_g": [2, D], "ln3_b": [2, D],
}


def build(nseq=4, nlayer=2, dbg=None):
    dbg = dbg or {}
    nc = bass.Bass("TRN2", target_bir_lowering=False)
    P = Prog(nc)
    dram = {}
    x_d = nc.dram_tensor("x", [nseq, S, D], F32, kind="ExternalInput").ap()
    p_d = nc.dram_tensor("p", [2, nseq, S, 256], F32, kind="ExternalInput").ap()
    for n in WNAMES:
        dram[n] = nc.dram_tensor(n, WSHAPES[n], F32, kind="ExternalInput").ap()
    out_d = nc.dram_tensor("out", [nseq, S, D], F32, kind="ExternalOutput").ap()
    ydbg_d = None
    if dbg.get("ydbg"):
        ydbg_d = nc.dram_tensor("ydbg", [nlayer, nseq, S, D], F32, kind="ExternalInput").ap()
    rope_d = nc.dram_tensor("rope_cs", [S, 32], F32, kind="ExternalInput").ap()
    swa_eb_d = nc.dram_tensor("swa_eb", [128, 3, 4, 128], F32, kind="ExternalInput").ap()
    ssd_masks_d = nc.dram_tensor("ssd_masks", [128, 5, 128], F32, kind="ExternalInput").ap()
    dft_f_d = nc.dram_tensor("dft_f", [2, 16, 128, 16, 128], BF16, kind="ExternalInput").ap()
    dft_i_d = nc.dram_tensor("dft_i", [2, 16, 128, 16, 128], BF16, kind="ExternalInput").ap()
    hy_featT_d = nc.dram_tensor("hy_featT", [33, S], F32, kind="ExternalInput").ap()
    hy_decay_d = nc.dram_tensor("hy_decay", [S, 256], F32, kind="ExternalInput").ap()
    PQ_b = nc.dram_tensor("PQ_b", [2, 2, 16, 128, 512], BF16, kind="Internal").ap()
    ydump_d = None
    if dbg.get("ydump"):
        ydump_d = nc.dram_tensor("ydump", [S, D], F32, kind="ExternalOutput").ap()
    groups = dbg.get("groups", "abcd")

    Hf = nc.dram_tensor("Hf", [S, D], F32, kind="Internal").ap()
    Wi_b = nc.dram_tensor("Wi_b", [2, D, INW], BF16, kind="Internal").ap()
    Wo_b = nc.dram_tensor("Wo_b", [2, D, D], BF16, kind="Internal").ap()
    Wg_b = nc.dram_tensor("Wg_b", [2, NFC, 128, 8, 128], BF16, kind="Internal").ap()
    Wu_b = nc.dram_tensor("Wu_b", [2, NFC, 128, 8, 128], BF16, kind="Internal").ap()
    Wd_b = nc.dram_tensor("Wd_b", [2, DFF, D], BF16, kind="Internal").ap()
    Wpp_b = nc.dram_tensor("Wpp_b", [2, 256, D], BF16, kind="Internal").ap()
    Wpg_b = nc.dram_tensor("Wpg_b", [2, D, D], BF16, kind="Internal").ap()
    Wuq_b = nc.dram_tensor("Wuq_b", [2, 256, 384], BF16, kind="Internal").ap()
    Wukv_b = nc.dram_tensor("Wukv_b", [2, 128, 512], BF16, kind="Internal").ap()

    HT = nc.alloc_sbuf_tensor("HT", [128, 8, S], BF16)
    ident = nc.alloc_sbuf_tensor("ident", [128, 128], BF16)
    cst = nc.alloc_sbuf_tensor("cst", [128, 8], F32)
    lng = nc.alloc_sbuf_tensor("lng", [128, D], F32)
    lnb = nc.alloc_sbuf_tensor("lnb", [128, D], F32)
    ARENA_BYTES = dbg.get("arena", 160 * 1024)
    arena_t = nc.alloc_sbuf_tensor("arena", [128, ARENA_BYTES // 2], BF16)
    A = Arena(arena_t, ARENA_BYTES)
    PS = [nc.alloc_psum_tensor(f"ps{i}", [128, 2, 512], F32) for i in range(4)]

    def psk(i, j):
        return ("ps", i, j)

    P.op("pool", lambda e: e.memset(ident[:], 1.0), writes=["ident"])
    P.op("pool", lambda e: e.affine_select(out=ident[:], in_=ident[:], pattern=[[-1, 128]],
                                           compare_op=ALU.is_equal, fill=0.0, base=0, channel_multiplier=1),
         reads=["ident"], writes=["ident"])
    for i, v in enumerate([LN_EPS, RMS_EPS, 1.0, 0.0, -math.pi]):
        P.op("pool", lambda e, i=i, v=v: e.memset(cst[:, i:i + 1], v), writes=[("cst", i)])

    def cast_rows(dst, src, nrows, key):
        for r0 in range(0, nrows, 256):
            r1 = min(nrows, r0 + 256)
            P.dma("pool", dst[r0:r1], src[r0:r1], writes=[key])

    for l in range(nlayer):
        cast_rows(Wi_b[l], dram["w_in"][l], D, ("Wi_b", l))
        cast_rows(Wo_b[l], dram["w_out"][l], D, ("Wo_b", l))
        cast_rows(Wd_b[l], dram["ffn_w_down"][l], DFF, ("Wd_b", l))
        cast_rows(Wpp_b[l], dram["ple_w_proj"][l], 256, ("Wpp_b", l))
        cast_rows(Wpg_b[l], dram["ple_w_gate"][l], D, ("Wpg_b", l))
        cast_rows(Wuq_b[l], dram["mla_w_uq"][l], 256, ("Wuq_b", l))
        cast_rows(Wukv_b[l], dram["mla_w_ukv"][l], 128, ("Wukv_b", l))
        for ch in range(NFC):
            for (dst, src, nm) in ((Wg_b, dram["ffn_w_gate"], "Wg_b"), (Wu_b, dram["ffn_w_up"], "Wu_b")):
                P.dma("pool", dst[l, ch], src[l][:, ch * 128:(ch + 1) * 128].rearrange("(c p) n -> p c n", p=128),
                      writes=[(nm, l)])

    def load_ln_params(gap, bap):
        P.dma("sp", lng[:], _bc(gap), writes=["lng"])
        P.dma("sp", lnb[:], _bc(bap), writes=["lnb"])

    def ln_tile(z, tt, st, zk, dst_dram, tagk):
        stk = ("st", zk)
        for hseg in range(2):
            P.op("dve", lambda e, hseg=hseg: e.bn_stats(out=st[:, hseg * 6:(hseg + 1) * 6],
                                                         in_=z[:, hseg * 512:(hseg + 1) * 512]),
                 reads=[zk], writes=[(stk, hseg)])
        P.op("dve", lambda e: e.bn_aggr(out=st[:, 12:14], in_=st[:, 0:12]),
             reads=[(stk, 0), (stk, 1)], writes=[(stk, 2)])
        P.op("act", lambda e: e.activation(out=st[:, 14:15], in_=st[:, 13:14], func=AF.Sqrt,
                                           bias=cst[:, 0:1], scale=1.0),
             reads=[(stk, 2), ("cst", 0)], writes=[(stk, 3)])
        P.op("dve", lambda e: e.reciprocal(out=st[:, 14:15], in_=st[:, 14:15]), reads=[(stk, 3)], writes=[(stk, 3)])
        P.op("dve", lambda e: e.scalar_tensor_tensor(out=st[:, 15:16], in0=st[:, 12:13], scalar=-1.0,
                                                     in1=st[:, 14:15], op0=ALU.mult, op1=ALU.mult),
             reads=[(stk, 2), (stk, 3)], writes=[(stk, 4)])
        P.op("act", lambda e: e.activation(out=z, in_=z, func=AF.Identity, bias=st[:, 15:16], scale=st[:, 14:15]),
             reads=[zk, (stk, 3), (stk, 4)], writes=[zk])
        P.op("pool", lambda e: e.tensor_tensor(out=z, in0=z, in1=lng[:], op=ALU.mult), reads=[zk, "lng"], writes=[zk])
        P.op("pool", lambda e: e.tensor_tensor(out=z, in0=z, in1=lnb[:], op=ALU.add), reads=[zk, "lnb"], writes=[zk])
        P.dma("sp", dst_dram, z, reads=[zk], writes=[tagk])
        return

    def to_HT(z, zk, tt, hb, hbk, psi):
        P.op("act", lambda e: e.activation(out=hb, in_=z, func=AF.Identity, bias=cst[:, 3:4], scale=1.0), reads=[zk], writes=[hbk])
        pst = PS[psi[0]][:, psi[1], :].bitcast(BF16).rearrange("p (a b) -> p a b", a=8)
        for c in range(8):
            P.op("pe", lambda e, c=c: e.transpose(out=pst[:, c, :], in_=hb[:, c * 128:(c + 1) * 128], identity=ident[:]),
                 reads=[hbk, "ident"], writes=[psk(*psi)])
        P.op("dve", lambda e: e.tensor_copy(out=HT[:, :, tt * 128:(tt + 1) * 128], in_=pst),
             reads=[psk(*psi)], writes=[("HT", tt)])

    def run_pipe(tiles, stages):
        n, K = len(tiles), len(stages)
        for step in range(n + K - 1):
            for k in range(K - 1, -1, -1):
                i = step - k
                if 0 <= i < n and stages[k] is not None:
                    stages[k](tiles[i])

    def ln_stages(zt, stt, hbs, dst_fn, do_ht=True, psi_fn=lambda tt: (2 + tt % 2, 0)):
        NBz, NBh = len(zt), len(hbs)

        def zk_(tt):
            return ("z", tt % NBz)

        def L1(tt):
            z, st, zk = zt[tt % NBz], stt[tt % NBz], zk_(tt)
            stk = ("st", zk)
            for hseg in range(2):
                P.op("dve", lambda e: e.bn_stats(out=st[:, hseg * 6:(hseg + 1) * 6], in_=z[:, hseg * 512:(hseg + 1) * 512]),
                     reads=[zk], writes=[(stk, hseg)])
            P.op("dve", lambda e: e.bn_aggr(out=st[:, 12:14], in_=st[:, 0:12]), reads=[(stk, 0), (stk, 1)], writes=[(stk, 2)])

        def L234(tt):
            z, st, zk = zt[tt % NBz], stt[tt % NBz], zk_(tt)
            stk = ("st", zk)
            P.op("act", lambda e: e.activation(out=st[:, 14:15], in_=st[:, 13:14], func=AF.Sqrt, bias=cst[:, 0:1], scale=1.0),
                 reads=[(stk, 2), ("cst", 0)], writes=[(stk, 3)])
            P.op("dve", lambda e: e.reciprocal(out=st[:, 14:15], in_=st[:, 14:15]), reads=[(stk, 3)], writes=[(stk, 3)])
            P.op("dve", lambda e: e.scalar_tensor_tensor(out=st[:, 15:16], in0=st[:, 12:13], scalar=-1.0, in1=st[:, 14:15],
                                                         op0=ALU.mult, op1=ALU.mult), reads=[(stk, 2), (stk, 3)], writes=[(stk, 4)])
            P.op("act", lambda e: e.activation(out=z, in_=z, func=AF.Identity, bias=st[:, 15:16], scale=st[:, 14:15]),
                 reads=[zk, (stk, 3), (stk, 4)], writes=[zk])

        def L5(tt):
            z, zk = zt[tt % NBz], zk_(tt)
            P.op("dve", lambda e: e.tensor_tensor(out=z, in0=z, in1=lng[:], op=ALU.mult), reads=[zk, "lng"], writes=[zk])
            P.op("pool", lambda e: e.tensor_tensor(out=z, in0=z, in1=lnb[:], op=ALU.add), reads=[zk, "lnb"], writes=[zk])

        def L6(tt):
            z, zk = zt[tt % NBz], zk_(tt)
            dst, dk = dst_fn(tt)
            P.dma("sp", dst, z, reads=[zk], writes=[dk])
            if do_ht:
                hb, hbk = hbs[tt % NBh], ("hb", tt % NBh)
                P.op("act", lambda e: e.activation(out=hb, in_=z, func=AF.Identity, bias=cst[:, 3:4], scale=1.0), reads=[zk], writes=[hbk])

        def L7(tt):
            hb, hbk = hbs[tt % NBh], ("hb", tt % NBh)
            psi = psi_fn(tt)
            pst = PS[psi[0]][:, psi[1], :].bitcast(BF16).rearrange("p (a b) -> p a b", a=8)
            for c in range(8):
                P.op("pe", lambda e: e.transpose(out=pst[:, c, :], in_=hb[:, c * 128:(c + 1) * 128], identity=ident[:]),
                     reads=[hbk, "ident"], writes=[psk(*psi)])

        def L8(tt):
            psi = psi_fn(tt)
            pst = PS[psi[0]][:, psi[1], :].bitcast(BF16).rearrange("p (a b) -> p a b", a=8)
            P.op("act", lambda e: e.activation(out=HT[:, :, tt * 128:(tt + 1) * 128], in_=pst, func=AF.Identity, bias=cst[:, 3:4], scale=1.0),
                 reads=[psk(*psi)], writes=[("HT", tt)])

        if do_ht:
            return [L1, L234, L5, L6, L7, L8]
        return [L1, L234, L5, L6]

    def stage_embed(b):
        P.mark("stage_embed")
        P.barrier(skip_q=("pool",) if b == 0 else ())
        A.reset()
        zt = [A.alloc([D], F32) for _ in range(7)]
        hb = [A.alloc([D], BF16) for _ in range(3)]
        stt = [A.alloc([16], F32) for _ in range(7)]
        load_ln_params(dram["emb_ln_g"], dram["emb_ln_b"])

        def F0(tt):
            P.dma("sp", zt[tt % 7], x_d[b, tt * 128:(tt + 1) * 128, :], writes=[("z", tt % 7)])
        run_pipe(list(range(NT)), [F0, None] + ln_stages(zt, stt, hb, lambda tt: (Hf[tt * 128:(tt + 1) * 128, :], ("Hf", tt))))

    def emit_y(YT, gmix, g, tt, y, yk, scr, scrk, ybf, ybk, psi):
        sq = scr[:, 0:256]
        st = scr[:, 256:260]
        yks = yk if isinstance(yk, list) else [yk]
        P.op("act", lambda e: e.activation(out=sq, in_=y, func=AF.Square), reads=yks, writes=[scrk])
        P.op("dve", lambda e: e.reduce_sum(out=st[:, 0:1], in_=sq, axis=AX.X), reads=[scrk], writes=[scrk])
        P.op("act", lambda e: e.activation(out=st[:, 1:2], in_=st[:, 0:1], func=AF.Sqrt, bias=cst[:, 1:2],
                                           scale=1.0 / 256.0), reads=[scrk, ("cst", 1)], writes=[scrk])
        P.op("dve", lambda e: e.reciprocal(out=st[:, 1:2], in_=st[:, 1:2]), reads=[scrk], writes=[scrk])
        P.op("dve", lambda e: e.scalar_tensor_tensor(out=ybf, in0=y, scalar=st[:, 1:2],
                                                     in1=gmix[:, g * 256:(g + 1) * 256], op0=ALU.mult, op1=ALU.mult),
             reads=yks + [scrk, "gmix"], writes=[ybk])
        pst = PS[psi[0]][:, psi[1], :].bitcast(BF16).rearrange("p (a b) -> p a b", a=8)
        for c in range(2):
            P.op("pe", lambda e, c=c: e.transpose(out=pst[:, c, :], in_=ybf[:, c * 128:(c + 1) * 128], identity=ident[:]),
                 reads=[ybk, "ident"], writes=[psk(*psi)])
        P.op("dve", lambda e: e.tensor_copy(out=YT[:, 2 * g:2 * g + 2, tt * 128:(tt + 1) * 128], in_=pst[:, 0:2, :]),
             reads=[psk(*psi)], writes=[("YT", g, tt)])

    def stage_outproj(l, YT):
        P.mark("stage_outproj")
        Wo = A.alloc([8, D], BF16)
        for c in range(8):
            P.dma("sp", Wo[:, c, :], Wo_b[l, c * 128:(c + 1) * 128, :], reads=[("Wo_b", l)], writes=[("Wo", c)])
        zt = [A.alloc([D], F32) for _ in range(7)]
        hb = [A.alloc([D], BF16) for _ in range(3)]
        stt = [A.alloc([16], F32) for _ in range(7)]
        load_ln_params(dram["ln1_g"][l], dram["ln1_b"][l])

        def F0(tt):
            P.dma("sp", zt[tt % 7], Hf[tt * 128:(tt + 1) * 128, :], reads=[("Hf", tt)], writes=[("z", tt % 7)])

        def F2(tt):
            pi = tt % 2
            for half in range(2):
                for c in range(8):
                    P.op("pe", lambda e: e.matmul(PS[pi][:, half, :], lhsT=YT[:, c, tt * 128:(tt + 1) * 128],
                                                  rhs=Wo[:, c, half * 512:(half + 1) * 512], start=(c == 0), stop=(c == 7)),
                         reads=[("YT", c // 2, tt), ("Wo", c)], writes=[psk(pi, half)])

        lns = ln_stages(zt, stt, hb, lambda tt: (Hf[tt * 128:(tt + 1) * 128, :], ("Hf", tt)))

        def F3(tt):
            pi = tt % 2
            z, zk = zt[tt % 7], ("z", tt % 7)
            P.op("dve", lambda e: e.scalar_tensor_tensor(out=z, in0=z, scalar=ALPHA, in1=PS[pi][:, :, :].rearrange("p a b -> p (a b)"),
                                                         op0=ALU.mult, op1=ALU.add), reads=[zk, psk(pi, 0), psk(pi, 1)], writes=[zk])
            lns[0](tt)
        run_pipe(list(range(NT)), [F0, F2, F3] + lns[1:])

    def stage_ffn(l):
        P.mark("stage_ffn")
        P.barrier()
        A.reset()
        HW = S // 2
        actT = A.alloc([NFC, HW], BF16)
        Wd = A.alloc([NFC, D], BF16)
        wgu = [A.alloc([2, 8, 128], BF16) for _ in range(3)]
        G = [A.alloc([HW + 2], F32) for _ in range(2)]
        T1 = [A.alloc([HW], F32) for _ in range(2)]
        halo_h = A.alloc([8, 2], BF16)
        cw = A.alloc([NFC, 4], F32)
        NBZ = 7
        zt = [A.alloc([D], F32) for _ in range(NBZ)]
        hb = [A.alloc([D], BF16) for _ in range(3)]
        stt = [A.alloc([16], F32) for _ in range(NBZ)]
        for k in range(3):
            P.dma("sp", cw[:, :, k:k + 1], dram["ffn_conv_w"][l, k].rearrange("(c p o) -> p c o", p=128, o=1), writes=["cw"],
                  allow_slow_non_contiguous=True)
        P.dma("sp", cw[:, :, 3:4], dram["ffn_conv_b"][l].rearrange("(c p o) -> p c o", p=128, o=1), writes=["cw"],
              allow_slow_non_contiguous=True)
        load_ln_params(dram["ln2_g"][l], dram["ln2_b"][l])
        P.op("dve", lambda e: e.tensor_copy(out=halo_h, in_=HT[:, :, HW - 1:HW + 1]),
             reads=[("HT", NT // 2 - 1), ("HT", NT // 2)], writes=["halo_h"])
        it = 0
        for half in range(2):
            P.mark("ffn_ph1")
            t0 = half * HW
            tts = list(range(half * NT // 2, (half + 1) * NT // 2))
            for ch in range(NFC):
                wb = wgu[it % 3]
                wk = ("wgu", it % 3)
                gi = it % 2
                it += 1
                P.dma("sp", wb[:, 0], Wg_b[l, ch], reads=[("Wg_b", l)], writes=[(wk, 0)])
                P.dma("sp", wb[:, 1], Wu_b[l, ch], reads=[("Wu_b", l)], writes=[(wk, 1)])
                if half == 0 and 2 <= ch < 2 + 11:
                    c0 = (ch - 2) * 2
                    P.dma("pool", Wd[:, c0:c0 + 2, :], Wd_b[l, c0 * 128:(c0 + 2) * 128, :].rearrange("(c p) n -> p c n", p=128),
                          reads=[("Wd_b", l)], writes=[("Wd", c0), ("Wd", c0 + 1)])
                pg, pu = PS[gi * 2], PS[gi * 2 + 1]
                for (pp, wi, pidx) in ((pg, 0, gi * 2), (pu, 1, gi * 2 + 1)):
                    for nb in range(2):
                        for c in range(8):
                            P.op("pe", lambda e, pp=pp, wi=wi, nb=nb, c=c, wb=wb: e.matmul(
                                pp[:, nb, :], lhsT=wb[:, wi, c, :], rhs=HT[:, c, t0 + nb * 512:t0 + (nb + 1) * 512],
                                start=(c == 0), stop=(c == 7)),
                                reads=[(wk, wi)] + [("HT", t0 // 128 + nb * 4 + j) for j in range(4)],
                                writes=[psk(pidx, nb)])
                Gt = G[gi]
                gk = ("G", gi)
                P.op("act", lambda e, Gt=Gt, pg=pg: e.activation(out=Gt[:, 1:HW + 1], in_=pg[:, :, :].rearrange("p a b -> p (a b)"),
                                                                 func=AF.Identity, bias=cst[:, 3:4], scale=1.0),
                     reads=[psk(gi * 2, 0), psk(gi * 2, 1)], writes=[gk])
                hcol = 1 if half == 0 else 0
                for c in range(8):
                    P.op("pe", lambda e, pg=pg, c=c, wb=wb, hcol=hcol: e.matmul(
                        pg[:, 0, 0:1], lhsT=wb[:, 0, c, :], rhs=halo_h[:, c, hcol:hcol + 1],
                        start=(c == 0), stop=(c == 7)), reads=[(wk, 0), "halo_h", gk], writes=[psk(gi * 2, 0)])
                if half == 0:
                    P.op("pool", lambda e, Gt=Gt: e.memset(Gt[:, 0:1], 0.0), writes=[(gk, "l")])
                    P.op("act", lambda e, Gt=Gt, pg=pg: e.activation(out=Gt[:, HW + 1:HW + 2], in_=pg[:, 0, 0:1], func=AF.Identity, bias=cst[:, 3:4], scale=1.0),
                         reads=[psk(gi * 2, 0)], writes=[(gk, "r")])
                else:
                    P.op("pool", lambda e, Gt=Gt: e.memset(Gt[:, HW + 1:HW + 2], 0.0), writes=[(gk, "r")])
                    P.op("act", lambda e, Gt=Gt, pg=pg: e.activation(out=Gt[:, 0:1], in_=pg[:, 0, 0:1], func=AF.Identity, bias=cst[:, 3:4], scale=1.0),
                         reads=[psk(gi * 2, 0)], writes=[(gk, "l")])
                T = T1[gi]
                tk = ("T1", gi)
                P.op("dve", lambda e, T=T, Gt=Gt, ch=ch: e.tensor_scalar(
                    out=T, in0=Gt[:, 1:HW + 1], scalar1=cw[:, ch, 1:2], scalar2=cw[:, ch, 3:4], op0=ALU.mult, op1=ALU.add),
                    reads=[gk, "cw"], writes=[tk])
                P.op("dve", lambda e, T=T, Gt=Gt, ch=ch: e.scalar_tensor_tensor(
                    out=T, in0=Gt[:, 0:HW], scalar=cw[:, ch, 0:1], in1=T, op0=ALU.mult, op1=ALU.add),
                    reads=[gk, (gk, "l"), tk, "cw"], writes=[tk])
                P.op("dve", lambda e, T=T, Gt=Gt, ch=ch: e.scalar_tensor_tensor(
                    out=T, in0=Gt[:, 2:HW + 2], scalar=cw[:, ch, 2:3], in1=T, op0=ALU.mult, op1=ALU.add),
                    reads=[gk, (gk, "r"), tk, "cw"], writes=[tk])
                P.op("act", lambda e, T=T: e.activation(out=T, in_=T, func=AF.Silu), reads=[tk], writes=[tk])
                P.op("dve", lambda e, T=T, pu=pu, ch=ch: e.tensor_tensor(
                    out=actT[:, ch, :], in0=T, in1=pu[:, :, :].rearrange("p a b -> p (a b)"), op=ALU.mult),
                    reads=[tk, psk(gi * 2 + 1, 0), psk(gi * 2 + 1, 1)], writes=[("actT", ch)])
            P.mark("ffn_ph2")
            def F0(tt):
                P.dma("sp", zt[tt % NBZ], Hf[tt * 128:(tt + 1) * 128, :], reads=[("Hf", tt)], writes=[("z", tt % NBZ)])

            def F2(tt, tts=tts):
                pi = tt % 2
                tl = tt - tts[0]
                for hf in range(2):
                    for ch in range(NFC):
                        P.op("pe", lambda e: e.matmul(PS[pi][:, hf, :], lhsT=actT[:, ch, tl * 128:(tl + 1) * 128],
                                                      rhs=Wd[:, ch, hf * 512:(hf + 1) * 512], start=(ch == 0), stop=(ch == NFC - 1)),
                             reads=[("actT", ch), ("Wd", ch)], writes=[psk(pi, hf)])

            lns = ln_stages(zt, stt, hb, lambda tt: (Hf[tt * 128:(tt + 1) * 128, :], ("Hf", tt)))

            def F3(tt):
                pi = tt % 2
                z, zk = zt[tt % NBZ], ("z", tt % NBZ)
                P.op("dve", lambda e: e.scalar_tensor_tensor(out=z, in0=z, scalar=ALPHA, in1=PS[pi][:, :, :].rearrange("p a b -> p (a b)"),
                                                             op0=ALU.mult, op1=ALU.add), reads=[zk, psk(pi, 0), psk(pi, 1)], writes=[zk])
                lns[0](tt)
            run_pipe(tts, [F0, F2, F3] + lns[1:])

    def stage_ple(l, b, last):
        P.mark("stage_ple")
        P.barrier()
        A.reset()
        Wpg = A.alloc([8, D], BF16)
        Wpp = A.alloc([2, D], BF16)
        bg = A.alloc([D], F32)
        NBZ, NBS = 8, 5
        pt = [A.alloc([256], F32) for _ in range(3)]
        pb = [A.alloc([256], BF16) for _ in range(3)]
        pT = [A.alloc([2, 128], BF16) for _ in range(6)]
        sg = [A.alloc([D], F32) for _ in range(NBS)]
        zt = [A.alloc([D], F32) for _ in range(NBZ)]
        hb = [A.alloc([D], BF16) for _ in range(3)]
        stt = [A.alloc([16], F32) for _ in range(NBZ)]
        for c in range(8):
            P.dma("sp", Wpg[:, c, :], Wpg_b[l, c * 128:(c + 1) * 128, :], reads=[("Wpg_b", l)], writes=[("Wpg", c)])
        P.dma("sp", Wpp, Wpp_b[l].rearrange("(c p) n -> p c n", p=128), reads=[("Wpp_b", l)], writes=["Wpp"])
        P.dma("sp", bg, _bc(dram["ple_b_gate"][l]), writes=["bg"])
        load_ln_params(dram["ln3_g"][l], dram["ln3_b"][l])

        def G0(tt):
            P.dma("sp", pt[tt % 3], p_d[l, b, tt * 128:(tt + 1) * 128, :], writes=[("pt", tt % 3)])

        def G1(tt):
            P.op("act", lambda e: e.activation(out=pb[tt % 3], in_=pt[tt % 3], func=AF.Identity, bias=cst[:, 3:4], scale=1.0),
                 reads=[("pt", tt % 3)], writes=[("pb", tt % 3)])

        def G2(tt):
            i2 = tt % 2
            pst = PS[3][:, 1, :].bitcast(BF16).rearrange("p (a b) -> p a b", a=8)
            for c in range(2):
                P.op("pe", lambda e: e.transpose(out=pst[:, c, :], in_=pb[tt % 3][:, c * 128:(c + 1) * 128], identity=ident[:]),
                     reads=[("pb", tt % 3), "ident"], writes=[psk(3, 1)])
            for hf in range(2):
                for c in range(8):
                    P.op("pe", lambda e: e.matmul(PS[i2][:, hf, :], lhsT=HT[:, c, tt * 128:(tt + 1) * 128], rhs=Wpg[:, c, hf * 512:(hf + 1) * 512],
                                                  start=(c == 0), stop=(c == 7)), reads=[("HT", tt), ("Wpg", c)], writes=[psk(i2, hf)])

        def G3(tt):
            i2 = tt % 2
            pst = PS[3][:, 1, :].bitcast(BF16).rearrange("p (a b) -> p a b", a=8)
            P.op("dve", lambda e: e.tensor_copy(out=pT[tt % 6], in_=pst[:, 0:2, :]), reads=[psk(3, 1)], writes=[("pT", tt % 6)])
            s_, sk = sg[tt % NBS], ("sg", tt % NBS)
            P.op("dve", lambda e: e.tensor_tensor(out=s_, in0=PS[i2][:, :, :].rearrange("p a b -> p (a b)"), in1=bg, op=ALU.add),
                 reads=[psk(i2, 0), psk(i2, 1), "bg"], writes=[sk])

        def G4(tt):
            s_, sk = sg[tt % NBS], ("sg", tt % NBS)
            P.op("act", lambda e: e.activation(out=s_, in_=s_, func=AF.Sigmoid), reads=[sk], writes=[sk])
            i2 = tt % 2
            for hf in range(2):
                for c in range(2):
                    P.op("pe", lambda e: e.matmul(PS[2][:, hf, :], lhsT=pT[tt % 6][:, c, :], rhs=Wpp[:, c, hf * 512:(hf + 1) * 512],
                                                  start=(c == 0), stop=(c == 1)), reads=[("pT", tt % 6), "Wpp"], writes=[psk(2, hf)])
            P.dma("sp", zt[tt % NBZ], Hf[tt * 128:(tt + 1) * 128, :], reads=[("Hf", tt)], writes=[("z", tt % NBZ)])

        dstf = (lambda tt: (out_d[b, tt * 128:(tt + 1) * 128, :], ("out", b, tt))) if last else \
               (lambda tt: (Hf[tt * 128:(tt + 1) * 128, :], ("Hf", tt)))
        lns = ln_stages(zt, stt, hb, dstf, do_ht=not last, psi_fn=lambda tt: (3, 0))

        def G5(tt):
            i2 = tt % 2
            s_, sk = sg[tt % NBS], ("sg", tt % NBS)
            z, zk = zt[tt % NBZ], ("z", tt % NBZ)
            P.op("dve", lambda e: e.tensor_tensor(out=s_, in0=s_, in1=PS[2][:, :, :].rearrange("p a b -> p (a b)"), op=ALU.mult),
                 reads=[psk(2, 0), psk(2, 1), sk], writes=[sk])
            P.op("dve", lambda e: e.scalar_tensor_tensor(out=z, in0=z, scalar=ALPHA, in1=s_, op0=ALU.mult, op1=ALU.add),
                 reads=[zk, sk], writes=[zk])

        def G6(tt):
            lns[0](tt)
        run_pipe(list(range(NT)), [G0, G1, G2, G3, G4, G5, G6] + lns[1:])

    def stage_mix_dbg(l, b):
        P.barrier()
        A.reset()
        YT = A.alloc([8, S], BF16)
        gmix = A.alloc([D], F32)
        P.dma("sp", gmix, _bc(dram["mix_norm_g"][l]), writes=["gmix"])
        mark = A.off
        yt = [A.alloc([D], F32) for _ in range(2)]
        scr = [A.alloc([260], F32) for _ in range(2)]
        ybf = [A.alloc([256], BF16) for _ in range(2)]
        for tt in range(NT):
            i2 = tt % 2
            P.dma("sp", yt[i2], ydbg_d[l, b, tt * 128:(tt + 1) * 128, :], writes=[("yt", i2)])
            for g in range(4):
                emit_y(YT, gmix, g, tt, yt[i2][:, g * 256:(g + 1) * 256], ("yt", i2), scr[i2], ("scr", i2),
                       ybf[i2], ("ybf", i2), (3, 1))
        P.barrier()
        A.off = mark
        return YT

    def y_from_dbg(l, b, g, YT, gmix):
        mark = A.off
        yt = [A.alloc([256], F32) for _ in range(2)]
        scr = [A.alloc([260], F32) for _ in range(2)]
        ybf = [A.alloc([256], BF16) for _ in range(2)]
        for tt in range(NT):
            i2 = tt % 2
            P.dma("sp", yt[i2], ydbg_d[l, b, tt * 128:(tt + 1) * 128, g * 256:(g + 1) * 256], writes=[("yt", i2)])
            emit_y(YT, gmix, g, tt, yt[i2], ("yt", i2), scr[i2], ("scr", i2), ybf[i2], ("ybf", i2), (3, 1))
        P.barrier()
        A.off = mark

    def finish_group(g, YT, gmix, yG, ykeyfn):
        P.mark("finish_group")
        scr = [A.alloc([260], F32) for _ in range(2)]
        ybf = [A.alloc([256], BF16) for _ in range(2)]
        for tt in range(NT):
            i2 = tt % 2
            if ydump_d is not None:
                P.dma("sp", ydump_d[tt * 128:(tt + 1) * 128, g * 256:(g + 1) * 256], yG[:, tt, :], reads=ykeyfn(tt),
                      writes=[("ydump", g, tt)])
            emit_y(YT, gmix, g, tt, yG[:, tt, :], ykeyfn(tt), scr[i2], ("scr", i2), ybf[i2], ("ybf", i2), (3, 1))

    def mix_mla(l, b, YT, gmix):
        P.mark("mix_mla")
        mark = A.off
        Wa = A.alloc([8, 416], BF16)
        Wuq = A.alloc([2, 384], BF16)
        Wukv = A.alloc([512], BF16)
        gq = A.alloc([4], F32)
        CQT = A.alloc([3, S], BF16)
        SQ = A.alloc([3, S], BF16)
        QT = A.alloc([4, S], BF16)
        KT = A.alloc([4, S], BF16)
        V1 = A.alloc([NT, 4, 66], BF16)
        rope = A.alloc([NT, 32], F32)
        ones = A.alloc([2], BF16)
        yA = A.alloc([NT, 256], F32)
        Qb = [A.alloc([4, 96], BF16) for _ in range(2)]
        Kb = [A.alloc([4, 96], BF16) for _ in range(2)]
        R = [A.alloc([5, 32], F32) for _ in range(2)]
        Ro = [A.alloc([5, 32], F32) for _ in range(2)]
        T4 = [A.alloc([4, 5, 16], F32) for _ in range(2)]
        st = [A.alloc([4], F32) for _ in range(2)]
        PT = [A.alloc([512], BF16) for _ in range(3)]
        rc = [A.alloc([4], F32) for _ in range(2)]
        P.dma("sp", Wa, Wi_b[l][:, 0:416].rearrange("(c p) n -> p c n", p=128), reads=[("Wi_b", l)], writes=["Wa"])
        P.dma("sp", Wuq, Wuq_b[l].rearrange("(c p) n -> p c n", p=128), reads=[("Wuq_b", l)], writes=["Wuq"])
        P.dma("sp", Wukv, Wukv_b[l], reads=[("Wukv_b", l)], writes=["Wukv"])
        P.dma("sp", gq[:, 0:2], dram["mla_q_norm"][l].rearrange("(c p) -> p c", p=128), writes=["gq"],
              allow_slow_non_contiguous=True)
        P.dma("sp", gq[:, 2:3], dram["mla_kv_norm"][l].rearrange("(c p) -> p c", p=128), writes=["gq"],
              allow_slow_non_contiguous=True)
        P.dma("sp", rope, rope_d.rearrange("(t p) c -> p t c", p=128), writes=["rope"])
        P.op("pool", lambda e: e.memset(ones, 1.0), writes=["ones"])
        P.op("pool", lambda e: e.memset(V1[:, :, :, 64:65], 1.0), writes=["V1one"])
        it = 0
        for nb in range(4):
            for ci in range(3):
                pi, pb_ = (it % 4) // 2, it % 2
                it += 1
                for c in range(8):
                    P.op("pe", lambda e: e.matmul(PS[pi][:, pb_, :], lhsT=Wa[:, c, ci * 128:(ci + 1) * 128],
                                                  rhs=HT[:, c, nb * 512:(nb + 1) * 512], start=(c == 0), stop=(c == 7)),
                         reads=["Wa"] + [("HT", nb * 4 + j) for j in range(4)], writes=[psk(pi, pb_)])
                P.op("act", lambda e: e.activation(out=CQT[:, ci, nb * 512:(nb + 1) * 512], in_=PS[pi][:, pb_, :],
                                                   func=AF.Identity, bias=cst[:, 3:4], scale=gq[:, ci:ci + 1]),
                     reads=[psk(pi, pb_), "gq", ("cst", 3)], writes=[("CQT", nb)])
                P.op("act", lambda e: e.activation(out=SQ[:, ci, nb * 512:(nb + 1) * 512], in_=PS[pi][:, pb_, :],
                                                   func=AF.Square), reads=[psk(pi, pb_)], writes=[("SQ", nb)])
        P.mark("mla_A2")
        SCALE = 96.0 ** -0.5
        for tt in range(NT):
            i2 = tt % 2
            tsl = slice(tt * 128, (tt + 1) * 128)
            nb = tt // 4
            b0 = PS[i2][:, 0, :]
            PSq, PSkr, PSss, PSkv = b0[:, 0:384], b0[:, 384:416], b0[:, 416:418], PS[i2][:, 1, :]
            for c in range(2):
                P.op("pe", lambda e: e.matmul(PSq, lhsT=CQT[:, c, tsl], rhs=Wuq[:, c, :], start=(c == 0), stop=(c == 1)),
                     reads=[("CQT", nb), "Wuq"], writes=[psk(i2, 0)])
            for c in range(8):
                P.op("pe", lambda e: e.matmul(PSkr, lhsT=HT[:, c, tsl], rhs=Wa[:, c, 384:416], start=(c == 0), stop=(c == 7)),
                     reads=[("HT", tt), "Wa"], writes=[psk(i2, 0)])
            for c in range(2):
                P.op("pe", lambda e: e.matmul(PSss[:, 0:1], lhsT=SQ[:, c, tsl], rhs=ones[:, 0:1], start=(c == 0), stop=(c == 1)),
                     reads=[("SQ", nb), "ones"], writes=[psk(i2, 0)])
            P.op("pe", lambda e: e.matmul(PSss[:, 1:2], lhsT=SQ[:, 2, tsl], rhs=ones[:, 0:1], start=True, stop=True),
                 reads=[("SQ", nb), "ones"], writes=[psk(i2, 0)])
            P.op("pe", lambda e: e.matmul(PSkv, lhsT=CQT[:, 2, tsl], rhs=Wukv, start=True, stop=True),
                 reads=[("CQT", nb), "Wukv"], writes=[psk(i2, 1)])
            sk_ = ("mst", i2)
            s_ = st[i2]
            P.op("act", lambda e: e.activation(out=s_[:, 0:1], in_=PSss[:, 0:1], func=AF.Sqrt, bias=cst[:, 1:2], scale=1.0 / 256),
                 reads=[psk(i2, 0), ("cst", 1)], writes=[sk_])
            P.op("act", lambda e: e.activation(out=s_[:, 1:2], in_=PSss[:, 1:2], func=AF.Sqrt, bias=cst[:, 1:2], scale=1.0 / 128),
                 reads=[psk(i2, 0), ("cst", 1)], writes=[sk_])
            P.op("dve", lambda e: e.reciprocal(out=s_[:, 0:2], in_=s_[:, 0:2]), reads=[sk_], writes=[sk_])
            q3 = PSq.rearrange("p (h d) -> p h d", h=4)
            kv3 = PSkv.rearrange("p (h d) -> p h d", h=4)
            qk, kk, rk = ("Qb", i2), ("Kb", i2), ("R", i2)
            P.op("act", lambda e: e.activation(out=Qb[i2][:, :, 0:64], in_=q3[:, :, 0:64], func=AF.Identity,
                                               bias=cst[:, 3:4], scale=s_[:, 0:1]), reads=[psk(i2, 0), sk_, ("cst", 3)], writes=[qk])
            P.op("act", lambda e: e.activation(out=R[i2][:, 0:4, :], in_=q3[:, :, 64:96], func=AF.Identity,
                                               bias=cst[:, 3:4], scale=s_[:, 0:1]), reads=[psk(i2, 0), sk_, ("cst", 3)], writes=[rk])
            P.op("act", lambda e: e.activation(out=R[i2][:, 4, :], in_=PSkr, func=AF.Identity, bias=cst[:, 3:4], scale=1.0), reads=[psk(i2, 0)], writes=[rk])
            P.op("act", lambda e: e.activation(out=Kb[i2][:, :, 0:64], in_=kv3[:, :, 0:64], func=AF.Identity,
                                               bias=cst[:, 3:4], scale=s_[:, 1:2]), reads=[psk(i2, 1), sk_, ("cst", 3)], writes=[kk])
            P.op("act", lambda e: e.activation(out=V1[:, tt, :, 0:64], in_=kv3[:, :, 64:128], func=AF.Identity,
                                               bias=cst[:, 3:4], scale=s_[:, 1:2]), reads=[psk(i2, 1), sk_, ("cst", 3)],
                 writes=[("V1", tt)])
            cosb = rope[:, tt, 0:16].unsqueeze(1).broadcast_to([128, 5, 16])
            sinb = rope[:, tt, 16:32].unsqueeze(1).broadcast_to([128, 5, 16])
            Rr, T_ = R[i2], T4[i2]
            tk_ = ("T4", i2)
            P.op("pool", lambda e: e.tensor_tensor(out=T_[:, 0], in0=Rr[:, :, 0:16], in1=cosb, op=ALU.mult), reads=[rk, "rope"], writes=[(tk_, 0)])
            P.op("pool", lambda e: e.tensor_tensor(out=T_[:, 1], in0=Rr[:, :, 16:32], in1=sinb, op=ALU.mult), reads=[rk, "rope"], writes=[(tk_, 1)])
            P.op("pool", lambda e: e.tensor_tensor(out=T_[:, 2], in0=Rr[:, :, 16:32], in1=cosb, op=ALU.mult), reads=[rk, "rope"], writes=[(tk_, 2)])
            P.op("pool", lambda e: e.tensor_tensor(out=T_[:, 3], in0=Rr[:, :, 0:16], in1=sinb, op=ALU.mult), reads=[rk, "rope"], writes=[(tk_, 3)])
            rok = ("Ro", i2)
            P.op("dve", lambda e: e.tensor_tensor(out=Ro[i2][:, :, 0:16], in0=T_[:, 0], in1=T_[:, 1], op=ALU.subtract),
                 reads=[(tk_, 0), (tk_, 1)], writes=[(rok, 0)])
            P.op("dve", lambda e: e.tensor_tensor(out=Ro[i2][:, :, 16:32], in0=T_[:, 2], in1=T_[:, 3], op=ALU.add),
                 reads=[(tk_, 2), (tk_, 3)], writes=[(rok, 1)])
            P.op("dve", lambda e: e.tensor_copy(out=Qb[i2][:, :, 64:96], in_=Ro[i2][:, 0:4, :]), reads=[(rok, 0), (rok, 1)], writes=[(qk, "r")])
            P.op("dve", lambda e: e.tensor_copy(out=Kb[i2][:, :, 64:96], in_=Ro[i2][:, 4:5, :].broadcast_to([128, 4, 32])),
                 reads=[(rok, 0), (rok, 1)], writes=[(kk, "r")])
            pst = PS[2 + i2][:, 0, :].bitcast(BF16).rearrange("p (a b) -> p a b", a=8)
            for h in range(4):
                P.op("pe", lambda e: e.transpose(out=pst[0:96, h, :], in_=Qb[i2][:, h, :], identity=ident[:]),
                     reads=[qk, (qk, "r"), "ident"], writes=[psk(2 + i2, 0)])
            for h in range(4):
                P.op("pe", lambda e: e.transpose(out=pst[0:96, 4 + h, :], in_=Kb[i2][:, h, :], identity=ident[:]),
                     reads=[kk, (kk, "r"), "ident"], writes=[psk(2 + i2, 0)])
            P.op("dve", lambda e: e.tensor_copy(out=QT[0:96, :, tsl], in_=pst[0:96, 0:4, :]), reads=[psk(2 + i2, 0)], writes=[("QT", tt)])
            P.op("dve", lambda e: e.tensor_copy(out=KT[0:96, :, tsl], in_=pst[0:96, 4:8, :]), reads=[psk(2 + i2, 0)], writes=[("KT", tt)])
        P.mark("mla_A3")
        items = [(h, qb, kt) for h in range(4) for qb in range(4) for kt in range(NT)]

        def emit_S(i):
            h, qb, kt = items[i]
            si = i % 4
            PSs = PS[si // 2][:, si % 2, :]
            P.op("pe", lambda e: e.matmul(PSs, lhsT=KT[0:96, h, kt * 128:(kt + 1) * 128],
                                          rhs=QT[0:96, h, qb * 512:(qb + 1) * 512], start=True, stop=True),
                 reads=[("KT", kt)] + [("QT", qb * 4 + j) for j in range(4)], writes=[psk(si // 2, si % 2)])

        emit_S(0)
        for i, (h, qb, kt) in enumerate(items):
            grp = i // NT
            oi = grp % 2
            PSo = PS[2 + oi][:, 0, 0:260].rearrange("p (j d) -> p j d", j=4)
            ok_ = psk(2 + oi, 0)
            if kt == 0:
                P.op("dve", lambda e: e.memset(PSo, 0.0), writes=[ok_])
            if i + 1 < len(items):
                emit_S(i + 1)
            si = i % 4
            PSs = PS[si // 2][:, si % 2, :]
            pt_ = PT[i % 3]
            ptk = ("PT", i % 3)
            P.op("act", lambda e: e.activation(out=pt_, in_=PSs, func=AF.Exp, scale=SCALE), reads=[psk(si // 2, si % 2)], writes=[ptk])
            for j in range(4):
                P.op("pe", lambda e: e.matmul(PSo[:, j, :], lhsT=pt_[:, j * 128:(j + 1) * 128], rhs=V1[:, kt, h, 0:65],
                                              start=False, stop=False, skip_group_check=True),
                     reads=[ptk, ("V1", kt), "V1one"], writes=[ok_])
            if kt == NT - 1:
                rck = ("rc", oi)
                P.op("dve", lambda e: e.reciprocal(out=rc[oi], in_=PSo[:, :, 64]), reads=[ok_], writes=[rck])
                P.op("dve", lambda e: e.tensor_tensor(out=yA[:, qb * 4:(qb + 1) * 4, h * 64:(h + 1) * 64], in0=PSo[:, :, 0:64],
                                                      in1=rc[oi].unsqueeze(2).broadcast_to([128, 4, 64]), op=ALU.mult),
                     reads=[ok_, rck], writes=[("yA", qb, h)])
        finish_group(0, YT, gmix, yA, lambda tt: [("yA", tt // 4, h) for h in range(4)])
        P.barrier()
        A.off = mark

    def mix_swa(l, b, YT, gmix):
        P.mark("mix_swa")
        mark = A.off
        Wc = A.alloc([8, 512], BF16)
        Wk2 = A.alloc([8, 2, 2, 64], BF16) if False else A.alloc([8, 256], BF16)
        SQT = A.alloc([2, NT, 256], BF16)
        SKT = A.alloc([2, S], BF16)
        V1s = A.alloc([NT, 2, 66], BF16)
        P.op("pool", lambda e: e.memset(SQT, 0.0), writes=["SQTz"])
        EBt = A.alloc([3, 4, 128], F32)
        esink = A.alloc([4], F32)
        yC = A.alloc([NT, 256], F32)
        E = [A.alloc([3, 256], F32) for _ in range(3)]
        Eb = [A.alloc([3, 256], BF16) for _ in range(3)]
        rc = [A.alloc([4], F32) for _ in range(2)]
        P.dma("sp", Wc, Wi_b[l][:, O_SQ:O_SQ + 512].rearrange("(c p) n -> p c n", p=128), reads=[("Wi_b", l)], writes=["Wc"])
        P.dma("sp", EBt, swa_eb_d, writes=["EBt"])
        P.dma("sp", esink, _bc(dram["swa_sink"][l]), writes=["esink"])
        P.op("act", lambda e: e.activation(out=esink, in_=esink, func=AF.Exp), reads=["esink"], writes=["esink"])
        P.op("pool", lambda e: e.memset(V1s[:, :, :, 64:65], 1.0), writes=["V1sone"])
        Wk2v = Wk2.rearrange("p c (k d e) -> p c k d e", k=2, d=2)
        for dup in range(2):
            P.op("pool", lambda e: e.tensor_copy(out=Wk2v[:, :, :, dup, :],
                                                 in_=Wc[:, :, 256:384].rearrange("p c (k e) -> p c k e", k=2)),
                 reads=["Wc"], writes=[("Wk2", dup)])
        it = 0
        for nb in range(4 if dbg.get("swa_stage", 9) >= 1 else 0):
            for (dst, is_k) in ((SQT, 0), (SKT, 1)):
                for p_ in range(2):
                    pi, pb_ = (it % 4) // 2, it % 2
                    it += 1
                    for c in range(8):
                        lw = Wk2[:, c, p_ * 128:(p_ + 1) * 128] if is_k else Wc[:, c, p_ * 128:(p_ + 1) * 128]
                        P.op("pe", lambda e: e.matmul(PS[pi][:, pb_, :], lhsT=lw, rhs=HT[:, c, nb * 512:(nb + 1) * 512],
                                                      start=(c == 0), stop=(c == 7)),
                             reads=["Wc", ("Wk2", 0), ("Wk2", 1)] + [("HT", nb * 4 + j) for j in range(4)], writes=[psk(pi, pb_)])
                    if is_k:
                        P.op("act", lambda e: e.activation(out=dst[:, p_, nb * 512:(nb + 1) * 512], in_=PS[pi][:, pb_, :],
                                                           func=AF.Identity, bias=cst[:, 3:4], scale=1.0),
                             reads=[psk(pi, pb_)], writes=[("SQK", is_k, nb)])
                    else:
                        for g in range(2):
                            P.op("act", lambda e: e.activation(
                                out=SQT[g * 64:(g + 1) * 64, p_, nb * 4:(nb + 1) * 4, g * 128:(g + 1) * 128],
                                in_=PS[pi][g * 64:(g + 1) * 64, pb_, :].rearrange("p (t q) -> p t q", t=4),
                                func=AF.Identity, bias=cst[g * 64:(g + 1) * 64, 3:4], scale=1.0),
                                reads=[psk(pi, pb_), "SQTz"], writes=[("SQK", is_k, nb)])
        P.mark("swa_V")
        for tt in range(NT if dbg.get("swa_stage", 9) >= 2 else 0):
            i2 = tt % 2
            pv = PS[2 + i2][:, 1, 0:128]
            for c in range(8):
                P.op("pe", lambda e: e.matmul(pv, lhsT=HT[:, c, tt * 128:(tt + 1) * 128], rhs=Wc[:, c, 384:512],
                                              start=(c == 0), stop=(c == 7)), reads=["Wc", ("HT", tt)], writes=[psk(2 + i2, 1)])
            P.op("act", lambda e: e.activation(out=V1s[:, tt, :, 0:64], in_=pv.rearrange("p (k d) -> p k d", k=2), func=AF.Identity,
                                               bias=cst[:, 3:4], scale=1.0),
                 reads=[psk(2 + i2, 1), ("cst", 3)], writes=[("V1s", tt)])
        P.mark("swa_attn")
        items = [(n, p_) for n in range(NT if dbg.get("swa_n") is None else dbg["swa_n"]) for p_ in range(2)]
        idx = {it_: i for i, it_ in enumerate(items)}

        def rels_of(n):
            return [r for r in range(3) if 0 <= n + r - 1 < NT]

        def W0(it_):
            n, p_ = it_
            i = idx[it_]
            PSs = PS[i % 2][:, :, :].rearrange("p a b -> p (a b)")
            for r in rels_of(n):
                kt = n + r - 1
                bankk = psk(i % 2, 0 if r < 2 else 1)
                P.op("pe", lambda e: e.matmul(PSs[:, r * 256:(r + 1) * 256], lhsT=SKT[:, p_, kt * 128:(kt + 1) * 128],
                                              rhs=SQT[:, p_, n, :], start=True, stop=True),
                     reads=[("SQK", 0, n // 4), ("SQK", 1, kt // 4)], writes=[bankk])

        def W1(it_):
            n, p_ = it_
            i = idx[it_]
            rels = rels_of(n)
            PSs = PS[i % 2][:, :, :].rearrange("p a b -> p (a b)")
            e_, eb_ = E[i % 3], Eb[i % 3]
            for r in rels:
                bankk = psk(i % 2, 0 if r < 2 else 1)
                P.op("act", lambda e: e.activation(out=e_[:, r, :], in_=PSs[:, r * 256:(r + 1) * 256], func=AF.Exp, scale=0.125),
                     reads=[bankk], writes=[("E", i % 3, r)])
            r0, r1 = rels[0], rels[-1] + 1
            P.op("pool", lambda e: e.tensor_tensor(out=eb_[:, r0:r1, :].rearrange("p r (g q) -> p r g q", g=2),
                                                   in0=e_[:, r0:r1, :].rearrange("p r (g q) -> p r g q", g=2),
                                                   in1=EBt[:, r0:r1, 2 * p_:2 * p_ + 2, :], op=ALU.mult),
                 reads=[("E", i % 3, r) for r in rels] + ["EBt"], writes=[("Eb", i % 3)])

        def W2(it_):
            n, p_ = it_
            i = idx[it_]
            i2 = n % 2
            rels = rels_of(n)
            eb_ = Eb[i % 3]
            PSo = PS[2 + i2][:, 0, 0:260].rearrange("p (h d) -> p h d", h=4)
            ok_ = psk(2 + i2, 0)
            for g in range(2):
                h = 2 * p_ + g
                for r in rels:
                    kt = n + r - 1
                    P.op("pe", lambda e: e.matmul(PSo[:, h, :], lhsT=eb_[:, r, g * 128:(g + 1) * 128], rhs=V1s[:, kt, p_, 0:65],
                                                  start=(r == rels[0]), stop=(r == rels[-1])),
                         reads=[("Eb", i % 3), ("V1s", kt), "V1sone"], writes=[ok_])
            if p_ == 1:
                rck = ("rc", i2)
                P.op("dve", lambda e: e.tensor_tensor(out=rc[i2], in0=PSo[:, :, 64], in1=esink, op=ALU.add), reads=[ok_, "esink"], writes=[rck])
                P.op("dve", lambda e: e.reciprocal(out=rc[i2], in_=rc[i2]), reads=[rck], writes=[rck])
                P.op("dve", lambda e: e.tensor_tensor(out=yC[:, n, :].rearrange("p (h d) -> p h d", h=4), in0=PSo[:, :, 0:64],
                                                      in1=rc[i2].unsqueeze(2).broadcast_to([128, 4, 64]), op=ALU.mult),
                     reads=[ok_, rck], writes=[("yC", n)])
        run_pipe(items, [W0, W1, W2])
        finish_group(2, YT, gmix, yC, lambda tt: [("yC", tt)])
        P.barrier()
        A.off = mark

    def mix_ssd(l, b, YT, gmix):
        P.mark("mix_ssd")
        mark = A.off
        Wd_ = A.alloc([8, 1032], BF16)
        XBCT = A.alloc([6, S], BF16)
        X = A.alloc([NT, 256], BF16)
        Bt = A.alloc([NT, 256], BF16)
        Zs = A.alloc([NT, 256], BF16)
        yD = A.alloc([NT, 256], F32)
        msk = A.alloc([5, 128], F32)
        cwx = A.alloc([6, 4], F32)
        prm = A.alloc([20], F32)
        dsk2 = A.alloc([8], F32)
        dta = A.alloc([NT, 16], F32)
        P.dma("sp", Wd_, Wi_b[l][:, O_Z:O_Z + 1032].rearrange("(c p) n -> p c n", p=128), reads=[("Wi_b", l)], writes=["Wd_"])
        P.dma("sp", msk, ssd_masks_d, writes=["msk"])
        for k in range(3):
            P.dma("sp", cwx[:, :, k:k + 1], dram["ssd_conv_w"][l, k].rearrange("(c p o) -> p c o", p=128, o=1), writes=["cwx"],
                  allow_slow_non_contiguous=True)
        P.dma("sp", cwx[:, :, 3:4], dram["ssd_conv_b"][l].rearrange("(c p o) -> p c o", p=128, o=1), writes=["cwx"],
              allow_slow_non_contiguous=True)
        P.dma("sp", prm[:, 0:8], _bc(dram["ssd_dt_bias"][l].rearrange("a b -> (a b)")), writes=["prm0"])
        P.dma("sp", prm[:, 8:16], _bc(dram["ssd_a_log"][l].rearrange("a b -> (a b)")), writes=["prm1"])
        P.dma("sp", dsk2, _bc(dram["ssd_d"][l].rearrange("a b -> (a b)")), writes=["dsk2"])
        P.op("act", lambda e: e.activation(out=prm[:, 8:16], in_=prm[:, 8:16], func=AF.Exp), reads=["prm1"], writes=["prm1"])
        P.op("dve", lambda e: e.tensor_scalar(out=prm[:, 8:16], in0=prm[:, 8:16], scalar1=-1.0, scalar2=None, op0=ALU.mult),
             reads=["prm1"], writes=["prm1"])
        P.op("dve", lambda e: e.tensor_tensor(out=prm[:, 16:20], in0=dsk2[:, 0:4], in1=dsk2[:, 4:8], op=ALU.add),
             reads=["dsk2"], writes=["prm2"])
        mark2 = A.off
        Gs = [A.alloc([S + 2], F32) for _ in range(2)]
        Ts = [A.alloc([S], F32) for _ in range(2)]
        for gi_ in range(2):
            P.op("pool", lambda e: e.memset(Gs[gi_][:, 0:1], 0.0), writes=[("Gl", gi_)])
            P.op("pool", lambda e: e.memset(Gs[gi_][:, S + 1:S + 2], 0.0), writes=[("Gr", gi_)])
        it = 0
        for ch in range(6):
            G, T = Gs[ch % 2], Ts[ch % 2]
            gq_ = ch % 2
            for nb in range(4):
                pi, pb_ = (it % 4) // 2, it % 2
                it += 1
                for c in range(8):
                    P.op("pe", lambda e: e.matmul(PS[pi][:, pb_, :], lhsT=Wd_[:, c, 256 + ch * 128:256 + (ch + 1) * 128],
                                                  rhs=HT[:, c, nb * 512:(nb + 1) * 512], start=(c == 0), stop=(c == 7)),
                         reads=["Wd_"] + [("HT", nb * 4 + j) for j in range(4)], writes=[psk(pi, pb_)])
                P.op("act", lambda e: e.activation(out=G[:, 1 + nb * 512:1 + (nb + 1) * 512], in_=PS[pi][:, pb_, :],
                                                   func=AF.Identity, bias=cst[:, 3:4], scale=1.0),
                     reads=[psk(pi, pb_)], writes=[("G", gq_, nb)])
            gks = [("G", gq_, nb) for nb in range(4)]
            P.op("dve", lambda e: e.tensor_scalar(out=T, in0=G[:, 1:S + 1], scalar1=cwx[:, ch, 1:2], scalar2=cwx[:, ch, 3:4],
                                                  op0=ALU.mult, op1=ALU.add), reads=gks + ["cwx"], writes=[("T", gq_)])
            P.op("dve", lambda e: e.scalar_tensor_tensor(out=T, in0=G[:, 0:S], scalar=cwx[:, ch, 0:1], in1=T, op0=ALU.mult, op1=ALU.add),
                 reads=gks + [("Gl", gq_), ("T", gq_), "cwx"], writes=[("T", gq_)])
            P.op("dve", lambda e: e.scalar_tensor_tensor(out=T, in0=G[:, 2:S + 2], scalar=cwx[:, ch, 2:3], in1=T, op0=ALU.mult, op1=ALU.add),
                 reads=gks + [("Gr", gq_), ("T", gq_), "cwx"], writes=[("T", gq_)])
            P.op("act", lambda e: e.activation(out=XBCT[:, ch, :], in_=T, func=AF.Silu), reads=[("T", gq_)], writes=[("XBCT", ch)])
        P.barrier()
        A.off = mark2
        P.mark("ssd_D2")
        zt_ = [A.alloc([256], F32) for _ in range(2)]
        for tt in range(NT):
            i2 = tt % 2
            tsl = slice(tt * 128, (tt + 1) * 128)
            pst = PS[2 + i2][:, 0, :].bitcast(BF16).rearrange("p (a b) -> p a b", a=8)
            for ch in range(4):
                P.op("pe", lambda e: e.transpose(out=pst[:, ch, :], in_=XBCT[:, ch, tsl], identity=ident[:]),
                     reads=[("XBCT", ch), "ident"], writes=[psk(2 + i2, 0)])
            P.op("dve", lambda e: e.tensor_copy(out=X[:, tt, :].rearrange("p (a b) -> p a b", a=2), in_=pst[:, 0:2, :]),
                 reads=[psk(2 + i2, 0)], writes=[("X", tt)])
            P.op("dve", lambda e: e.tensor_copy(out=Bt[:, tt, :].rearrange("p (a b) -> p a b", a=2), in_=pst[:, 2:4, :]),
                 reads=[psk(2 + i2, 0)], writes=[("Bt", tt)])
            pz = PS[i2][:, 0, 0:256]
            pdt = PS[i2][:, 1, 0:8]
            for c in range(8):
                P.op("pe", lambda e: e.matmul(pz, lhsT=HT[:, c, tsl], rhs=Wd_[:, c, 0:256], start=(c == 0), stop=(c == 7)),
                     reads=[("HT", tt), "Wd_"], writes=[psk(i2, 0)])
            for c in range(8):
                P.op("pe", lambda e: e.matmul(pdt, lhsT=HT[:, c, tsl], rhs=Wd_[:, c, 1024:1032], start=(c == 0), stop=(c == 7)),
                     reads=[("HT", tt), "Wd_"], writes=[psk(i2, 1)])
            P.op("act", lambda e: e.activation(out=Zs[:, tt, :], in_=pz, func=AF.Silu), reads=[psk(i2, 0)], writes=[("Zs", tt)])
            dk = ("dta", tt)
            P.op("dve", lambda e: e.tensor_tensor(out=dta[:, tt, 0:8], in0=pdt, in1=prm[:, 0:8], op=ALU.add),
                 reads=[psk(i2, 1), "prm0"], writes=[dk])
            P.op("act", lambda e: e.activation(out=dta[:, tt, 0:8], in_=dta[:, tt, 0:8], func=AF.Exp), reads=[dk], writes=[dk])
            P.op("act", lambda e: e.activation(out=dta[:, tt, 0:8], in_=dta[:, tt, 0:8], func=AF.Ln, bias=cst[:, 2:3], scale=1.0),
                 reads=[dk, ("cst", 2)], writes=[dk])
            P.op("dve", lambda e: e.tensor_tensor(out=dta[:, tt, 8:16], in0=dta[:, tt, 0:8], in1=prm[:, 8:16], op=ALU.mult),
                 reads=[dk, "prm1"], writes=[dk])
        P.mark("ssd_D3")
        carry = A.alloc([4, 64], F32)
        prev = A.alloc([4, 64], BF16)
        Am = [A.alloc([4, 128], F32) for _ in range(2)]
        Lx = [A.alloc([4, 128], F32) for _ in range(2)]
        CBm = [A.alloc([2, 128], F32) for _ in range(2)]
        MT = [A.alloc([4, 128], BF16) for _ in range(2)]
        Xdt = [A.alloc([4, 64], BF16) for _ in range(2)]
        Xdd = [A.alloc([4, 64], BF16) for _ in range(2)]
        ex = [A.alloc([3, 4], F32) for _ in range(2)]
        ytmp = [A.alloc([4, 64], F32) for _ in range(2)]
        for d in range(2):
            order = list(range(NT)) if d == 0 else list(range(NT - 1, -1, -1))
            tri = msk[:, 0, :] if d == 0 else msk[:, 1, :]
            mgl = msk[:, 2, :] if d == 0 else msk[:, 3, :]
            for idx, tt in enumerate(order):
                i2 = idx % 2
                tsl = slice(tt * 128, (tt + 1) * 128)
                first = (idx == 0)
                a_ = dta[:, tt, 8 + d * 4:12 + d * 4]
                dt_ = dta[:, tt, d * 4:d * 4 + 4]
                dk = ("dta", tt)
                pc = PS[0][:, i2, 0:4]
                ptot = PS[0][:, i2, 4:8]
                P.op("pe", lambda e: e.matmul(pc, lhsT=tri, rhs=a_, start=True, stop=True), reads=["msk", dk], writes=[psk(0, i2)])
                P.op("pe", lambda e: e.matmul(ptot, lhsT=msk[:, 4, :], rhs=a_, start=True, stop=True), reads=["msk", dk], writes=[psk(0, i2)])
                exk = ("ex", i2)
                e_ = ex[i2]
                P.op("dve", lambda e: e.tensor_copy(out=e_[:, 1:3, :], in_=PS[0][:, i2, 0:8].rearrange("p (a b) -> p a b", a=2)),
                     reads=[psk(0, i2)], writes=[(exk, 1)])
                P.op("dve", lambda e: e.tensor_tensor(out=e_[:, 0, :], in0=e_[:, 2, :], in1=e_[:, 1, :], op=ALU.subtract),
                     reads=[(exk, 1)], writes=[exk])
                P.op("act", lambda e: e.activation(out=e_, in_=e_, func=AF.Exp), reads=[exk, (exk, 1)], writes=[exk])
                amk = ("Am", i2)
                P.op("pool", lambda e: e.tensor_tensor(out=Am[i2], in0=mgl.unsqueeze(1).broadcast_to([128, 4, 128]),
                                                       in1=a_.unsqueeze(2).broadcast_to([128, 4, 128]), op=ALU.mult),
                     reads=["msk", dk], writes=[amk])
                pseg = PS[1][:, i2, :].rearrange("p (h t) -> p h t", h=4)
                for h in range(4):
                    P.op("pe", lambda e: e.matmul(pseg[:, h, :], lhsT=Am[i2][:, h, :], rhs=tri, start=True, stop=True),
                         reads=[amk, "msk"], writes=[psk(1, i2)])
                lk = ("Lx", i2)
                P.op("act", lambda e: e.activation(out=Lx[i2], in_=pseg, func=AF.Exp), reads=[psk(1, i2)], writes=[lk])
                pcb = PS[2][:, i2, 0:256].rearrange("p (g t) -> p g t", g=2)
                for g in range(2):
                    P.op("pe", lambda e: e.matmul(pcb[:, g, :], lhsT=XBCT[:, 2 + g, tsl], rhs=XBCT[:, 4 + g, tsl], start=True, stop=True),
                         reads=[("XBCT", 2 + g), ("XBCT", 4 + g)], writes=[psk(2, i2)])
                cbk = ("CBm", i2)
                P.op("dve", lambda e: e.tensor_tensor(out=CBm[i2], in0=pcb, in1=tri.unsqueeze(1).broadcast_to([128, 2, 128]), op=ALU.mult),
                     reads=[psk(2, i2), "msk"], writes=[cbk])
                mk_ = ("MT", i2)
                P.op("pool", lambda e: e.tensor_tensor(out=MT[i2].rearrange("p (g r) t -> p g r t", g=2),
                                                       in0=Lx[i2].rearrange("p (g r) t -> p g r t", g=2),
                                                       in1=CBm[i2].unsqueeze(2).broadcast_to([128, 2, 2, 128]), op=ALU.mult),
                     reads=[lk, cbk], writes=[mk_])
                xk, xdk = ("Xdt", i2), ("Xdd", i2)
                X4 = X[:, tt, :].rearrange("p (h d) -> p h d", h=4)
                P.op("dve", lambda e: e.tensor_tensor(out=Xdt[i2], in0=X4, in1=dt_.unsqueeze(2).broadcast_to([128, 4, 64]), op=ALU.mult),
                     reads=[("X", tt), dk], writes=[xk])
                P.op("dve", lambda e: e.tensor_tensor(out=Xdd[i2], in0=Xdt[i2], in1=e_[:, 0, :].unsqueeze(2).broadcast_to([128, 4, 64]), op=ALU.mult),
                     reads=[xk, exk], writes=[xdk])
                pyd = PS[3][:, i2, 0:256].rearrange("p (h d) -> p h d", h=4)
                pyo = PS[3][:, i2, 256:512].rearrange("p (h d) -> p h d", h=4)
                for h in range(4):
                    P.op("pe", lambda e: e.matmul(pyd[:, h, :], lhsT=MT[i2][:, h, :], rhs=Xdt[i2][:, h, :], start=True, stop=True),
                         reads=[mk_, xk], writes=[psk(3, i2)])
                if not first:
                    for h in range(4):
                        P.op("pe", lambda e: e.matmul(pyo[:, h, :], lhsT=XBCT[:, 4 + h // 2, tsl], rhs=prev[:, h, :], start=True, stop=True),
                             reads=[("XBCT", 4 + h // 2), "prev"], writes=[psk(3, i2)])
                yk_ = ("yD", tt)
                y4 = yD[:, tt, :].rearrange("p (h d) -> p h d", h=4)
                if d == 0:
                    P.op("dve", lambda e: e.tensor_copy(out=y4, in_=pyd), reads=[psk(3, i2)], writes=[yk_])
                else:
                    P.op("dve", lambda e: e.tensor_tensor(out=y4, in0=y4, in1=pyd, op=ALU.add), reads=[psk(3, i2), yk_], writes=[yk_])
                if not first:
                    tk_ = ("ytmp", i2)
                    P.op("dve", lambda e: e.tensor_tensor(out=ytmp[i2], in0=pyo, in1=e_[:, 1, :].unsqueeze(2).broadcast_to([128, 4, 64]), op=ALU.mult),
                         reads=[psk(3, i2), exk], writes=[tk_])
                    P.op("pool", lambda e: e.tensor_tensor(out=y4, in0=y4, in1=ytmp[i2], op=ALU.add), reads=[tk_, yk_], writes=[yk_])
                pst_ = PS[2][:, i2, 256:512].rearrange("p (h d) -> p h d", h=4)
                for h in range(4):
                    P.op("pe", lambda e: e.matmul(pst_[:, h, :], lhsT=Bt[:, tt, (h // 2) * 128:(h // 2 + 1) * 128], rhs=Xdd[i2][:, h, :],
                                                  start=True, stop=True), reads=[("Bt", tt), xdk], writes=[psk(2, i2)])
                if first:
                    P.op("dve", lambda e: e.tensor_copy(out=carry, in_=pst_), reads=[psk(2, i2)], writes=["carry"])
                else:
                    P.op("dve", lambda e: e.tensor_tensor(out=carry, in0=carry, in1=e_[:, 2, :].unsqueeze(2).broadcast_to([128, 4, 64]), op=ALU.mult),
                         reads=["carry", exk], writes=["carry"])
                    P.op("dve", lambda e: e.tensor_tensor(out=carry, in0=carry, in1=pst_, op=ALU.add), reads=["carry", psk(2, i2)], writes=["carry"])
                P.op("act", lambda e: e.activation(out=prev, in_=carry, func=AF.Identity, bias=cst[:, 3:4], scale=1.0),
                     reads=["carry"], writes=["prev"])
        P.mark("ssd_D4")
        for tt in range(NT):
            i2 = tt % 2
            yk_ = ("yD", tt)
            y4 = yD[:, tt, :].rearrange("p (h d) -> p h d", h=4)
            X4 = X[:, tt, :].rearrange("p (h d) -> p h d", h=4)
            tk_ = ("ytmp", i2)
            P.op("pool", lambda e: e.tensor_tensor(out=ytmp[i2], in0=X4, in1=prm[:, 16:20].unsqueeze(2).broadcast_to([128, 4, 64]), op=ALU.mult),
                 reads=[("X", tt), "prm2"], writes=[tk_])
            P.op("pool", lambda e: e.tensor_tensor(out=y4, in0=y4, in1=ytmp[i2], op=ALU.add), reads=[tk_, yk_], writes=[yk_])
            P.op("dve", lambda e: e.tensor_tensor(out=yD[:, tt, :], in0=yD[:, tt, :], in1=Zs[:, tt, :], op=ALU.mult), reads=[yk_, ("Zs", tt)], writes=[yk_])
        finish_group(3, YT, gmix, yD, lambda tt: [("yD", tt)])
        P.barrier()
        A.off = mark

    def hyena_prologue(l):
        P.mark("hyena_prologue")
        P.barrier(skip_q=("pool",))
        A.reset()
        featT = A.alloc([S], F32)
        dec = A.alloc([NT, 256], F32)
        w1 = A.alloc([64], F32)
        w2 = A.alloc([64], F32)
        w3 = A.alloc([1024], F32)
        pr = A.alloc([6], F32)
        h1 = A.alloc([S], F32)
        h2 = A.alloc([S], F32)
        tmp = [A.alloc([512], F32) for _ in range(2)]
        tmpf = [A.alloc([512], F32) for _ in range(2)]
        tmpi = [A.alloc([512], F32).bitcast(mybir.dt.int32) for _ in range(2)]
        sd = A.alloc([2, 2, NT, 256], BF16)
        hfd = [A.alloc([2, 256], F32) for _ in range(2)]
        hbd = [A.alloc([2, 256], F32) for _ in range(2)]
        slab = [A.alloc([2, 16, 128], BF16) for _ in range(2)]
        pqs = [A.alloc([512], BF16) for _ in range(2)]
        P.dma("sp", featT[0:33, :], hy_featT_d, writes=["featT"])
        P.dma("sp", dec, hy_decay_d.rearrange("(t p) c -> p t c", p=128), writes=["dec"])
        P.dma("sp", w1[0:33, :], dram["hy_f_w1"][l], writes=["w1"])
        P.dma("sp", w2[0:64, :], dram["hy_f_w2"][l], writes=["w2"])
        P.dma("sp", w3[0:64, :], dram["hy_f_w3"][l], writes=["w3"])
        for i, nm in enumerate(["hy_f_b1", "hy_f_freq", "hy_f_b2"]):
            P.dma("sp", pr[0:64, i:i + 1], dram[nm][l].rearrange("(p o) -> p o", o=1), writes=["pr"], allow_slow_non_contiguous=True)

        def sin_layer(dst, wT, kdim, src, srck, bcol):
            for nb in range(4):
                i2 = nb % 2
                ps = PS[0][0:64, i2, :]
                P.op("pe", lambda e: e.matmul(ps, lhsT=wT[0:kdim, :], rhs=src[0:kdim, nb * 512:(nb + 1) * 512], start=True, stop=True),
                     reads=[srck, "w1", "w2"], writes=[psk(0, i2)])
                t_ = tmp[i2][0:64, :]
                tk = ("tmp", i2)
                P.op("dve", lambda e: e.tensor_scalar(out=t_, in0=ps, scalar1=pr[0:64, bcol:bcol + 1], scalar2=pr[0:64, 1:2],
                                                      op0=ALU.add, op1=ALU.mult), reads=[psk(0, i2), "pr"], writes=[tk])
                ti_ = tmpi[i2][0:64, :]
                tf_ = tmpf[i2][0:64, :]
                P.op("dve", lambda e: e.tensor_scalar(out=t_, in0=t_, scalar1=1.0 / (2 * math.pi), scalar2=None, op0=ALU.mult),
                     reads=[tk], writes=[tk])
                P.op("dve", lambda e: e.tensor_copy(out=ti_, in_=t_), reads=[tk], writes=[(tk, "i")])
                P.op("dve", lambda e: e.tensor_copy(out=tf_, in_=ti_), reads=[(tk, "i")], writes=[(tk, "f")])
                P.op("dve", lambda e: e.tensor_tensor(out=t_, in0=t_, in1=tf_, op=ALU.subtract), reads=[tk, (tk, "f")], writes=[tk])
                P.op("dve", lambda e: e.tensor_scalar(out=tf_, in0=t_, scalar1=0.5, scalar2=None, op0=ALU.is_gt), reads=[tk], writes=[(tk, "f")])
                P.op("dve", lambda e: e.tensor_tensor(out=t_, in0=t_, in1=tf_, op=ALU.subtract), reads=[tk, (tk, "f")], writes=[tk])
                P.op("dve", lambda e: e.tensor_scalar(out=tf_, in0=t_, scalar1=-0.5, scalar2=None, op0=ALU.is_lt), reads=[tk], writes=[(tk, "f")])
                P.op("dve", lambda e: e.tensor_tensor(out=t_, in0=t_, in1=tf_, op=ALU.add), reads=[tk, (tk, "f")], writes=[tk])
                P.op("act", lambda e: e.activation(out=dst[0:64, nb * 512:(nb + 1) * 512], in_=t_, func=AF.Sin,
                                                   bias=cst[0:64, 3:4], scale=2 * math.pi), reads=[tk, ("cst", 3)], writes=[(dst.tensor.name, id(dst))])
            return (dst.tensor.name, id(dst))

        k1 = sin_layer(h1, w1, 33, featT, "featT", 0)
        k2 = sin_layer(h2, w2, 64, h1, k1, 2)
        for tt in range(NT):
            i2 = tt % 2
            for n in range(2):
                P.op("pe", lambda e: e.matmul(PS[1][:, n, :], lhsT=h2[0:64, tt * 128:(tt + 1) * 128], rhs=w3[0:64, n * 512:(n + 1) * 512],
                                              start=True, stop=True), reads=[k2, "w3"], writes=[psk(1, n)])
            ps4 = PS[1][:, :, :].rearrange("p n (d c) -> p n d c", d=2)
            dbc = dec[:, tt, :].unsqueeze(1).broadcast_to([128, 2, 256])
            P.op("dve", lambda e: e.tensor_tensor(out=hfd[i2], in0=ps4[:, :, 0, :], in1=dbc, op=ALU.mult),
                 reads=[psk(1, 0), psk(1, 1), "dec"], writes=[("hfd", i2)])
            P.op("dve", lambda e: e.tensor_tensor(out=hbd[i2], in0=ps4[:, :, 1, :], in1=dbc, op=ALU.mult),
                 reads=[psk(1, 0), psk(1, 1), "dec"], writes=[("hbd", i2)])
            if tt == 0:
                P.op("dve", lambda e: e.memset(hbd[i2][0:1, :, :], 0.0), reads=[("hbd", i2)], writes=[("hbd", i2)])
            P.op("pool", lambda e: e.tensor_tensor(out=sd[:, :, 0, tt, :], in0=hfd[i2], in1=hbd[i2], op=ALU.add),
                 reads=[("hfd", i2), ("hbd", i2)], writes=[("sd", tt)])
            P.op("pool", lambda e: e.tensor_tensor(out=sd[:, :, 1, tt, :], in0=hbd[i2], in1=hfd[i2], op=ALU.subtract),
                 reads=[("hfd", i2), ("hbd", i2)], writes=[("sd", tt)])
        sdk = [("sd", tt) for tt in range(NT)]
        it = 0
        for fc in range(16):
            sb = slab[fc % 2]
            sk_ = ("slab", fc % 2)
            for m in range(2):
                P.dma("sp", sb[:, m], dft_f_d[m, fc], writes=[(sk_, m)])
            for n in range(2):
                i2 = it % 2
                it += 1
                for m in range(2):
                    for tc in range(16):
                        P.op("pe", lambda e: e.matmul(PS[2][:, i2, m * 256:(m + 1) * 256], lhsT=sb[:, m, tc, :], rhs=sd[:, n, m, tc, :],
                                                      start=(tc == 0), stop=(tc == 15)), reads=[(sk_, m)] + sdk, writes=[psk(2, i2)])
                P.op("act", lambda e: e.activation(out=pqs[i2], in_=PS[2][:, i2, :], func=AF.Identity, bias=cst[:, 3:4], scale=1.0),
                     reads=[psk(2, i2)], writes=[("pqs", i2)])
                P.dma("sp", PQ_b[l, n, fc], pqs[i2], reads=[("pqs", i2)], writes=[("PQ_b", l, n, fc)])

    def mix_hyena(l, b, YT, gmix):
        P.mark("mix_hyena")
        mark = A.off
        V0 = A.alloc([NT, 256], BF16)
        X12 = A.alloc([2, NT, 256], BF16)
        cwx = A.alloc([6, 4], F32)
        hbias = A.alloc([2, 256], F32)
        yB = A.alloc([NT, 256], F32)
        for k in range(3):
            P.dma("sp", cwx[:, :, k:k + 1], dram["hy_conv_w"][l, k].rearrange("(c p o) -> p c o", p=128, o=1), writes=["cwx"],
                  allow_slow_non_contiguous=True)
        P.dma("sp", cwx[:, :, 3:4], dram["hy_conv_b"][l].rearrange("(c p o) -> p c o", p=128, o=1), writes=["cwx"],
              allow_slow_non_contiguous=True)
        P.dma("sp", hbias, _bc(dram["hy_bias"][l].rearrange("a b -> (a b)")), writes=["hbias"])
        mark2 = A.off
        Wb = A.alloc([8, 768], BF16)
        P.dma("sp", Wb, Wi_b[l][:, O_HY:O_HY + 768].rearrange("(c p) n -> p c n", p=128), reads=[("Wi_b", l)], writes=["Wb"])
        UCT = A.alloc([6, S], BF16)
        Gs = [A.alloc([S + 2], F32) for _ in range(2)]
        Ts = [A.alloc([S], F32) for _ in range(2)]
        for gi_ in range(2):
            P.op("pool", lambda e: e.memset(Gs[gi_][:, 0:1], 0.0), writes=[("Gl", gi_)])
            P.op("pool", lambda e: e.memset(Gs[gi_][:, S + 1:S + 2], 0.0), writes=[("Gr", gi_)])
        it = 0
        for ch in range(6):
            G, T = Gs[ch % 2], Ts[ch % 2]
            gq_ = ch % 2
            for nb in range(4):
                pi, pb_ = (it % 4) // 2, it % 2
                it += 1
                for c in range(8):
                    P.op("pe", lambda e: e.matmul(PS[pi][:, pb_, :], lhsT=Wb[:, c, ch * 128:(ch + 1) * 128],
                                                  rhs=HT[:, c, nb * 512:(nb + 1) * 512], start=(c == 0), stop=(c == 7)),
                         reads=["Wb"] + [("HT", nb * 4 + j) for j in range(4)], writes=[psk(pi, pb_)])
                P.op("act", lambda e: e.activation(out=G[:, 1 + nb * 512:1 + (nb + 1) * 512], in_=PS[pi][:, pb_, :],
                                                   func=AF.Identity, bias=cst[:, 3:4], scale=1.0),
                     reads=[psk(pi, pb_)], writes=[("G", gq_, nb)])
            gks = [("G", gq_, nb) for nb in range(4)]
            P.op("dve", lambda e: e.tensor_scalar(out=T, in0=G[:, 1:S + 1], scalar1=cwx[:, ch, 1:2], scalar2=cwx[:, ch, 3:4],
                                                  op0=ALU.mult, op1=ALU.add), reads=gks + ["cwx"], writes=[("T", gq_)])
            P.op("dve", lambda e: e.scalar_tensor_tensor(out=T, in0=G[:, 0:S], scalar=cwx[:, ch, 0:1], in1=T, op0=ALU.mult, op1=ALU.add),
                 reads=gks + [("Gl", gq_), ("T", gq_), "cwx"], writes=[("T", gq_)])
            P.op("dve", lambda e: e.scalar_tensor_tensor(out=UCT[:, ch, :], in0=G[:, 2:S + 2], scalar=cwx[:, ch, 2:3], in1=T, op0=ALU.mult, op1=ALU.add),
                 reads=gks + [("Gr", gq_), ("T", gq_), "cwx"], writes=[("UCT", ch)])
        for tt in range(NT):
            i2 = tt % 2
            tsl = slice(tt * 128, (tt + 1) * 128)
            pst = PS[2 + i2][:, 0, :].bitcast(BF16).rearrange("p (a b) -> p a b", a=8)
            for ch in range(6):
                P.op("pe", lambda e: e.transpose(out=pst[:, ch, :], in_=UCT[:, ch, tsl], identity=ident[:]),
                     reads=[("UCT", ch), "ident"], writes=[psk(2 + i2, 0)])
            P.op("dve", lambda e: e.tensor_copy(out=V0[:, tt, :].rearrange("p (a b) -> p a b", a=2), in_=pst[:, 0:2, :]),
                 reads=[psk(2 + i2, 0)], writes=[("z0", tt)])
            P.op("dve", lambda e: e.tensor_copy(out=X12[:, :, tt, :].rearrange("p n (a b) -> p n a b", a=2),
                                                in_=pst[:, 2:6, :].rearrange("p (n a) b -> p n a b", n=2)),
                 reads=[psk(2 + i2, 0)], writes=[("X12", tt)])
        P.barrier()
        P.mark("hy_conv")
        A.off = mark2
        Z1 = A.alloc([NT, 256], BF16)
        PQ = A.alloc([16, 512], BF16)
        Yc = A.alloc([2, 16, 256], BF16)
        slab = [A.alloc([2, 16, 128], BF16) for _ in range(2)]
        AB = [A.alloc([512], F32) for _ in range(2)]
        tq = [A.alloc([4, 256], F32) for _ in range(2)]
        te = [A.alloc([256], F32) for _ in range(2)]
        sit = 0
        for n in range(2):
            zin = V0 if n == 0 else Z1
            zkf = (lambda tt: ("z0", tt)) if n == 0 else (lambda tt: ("z1", tt))
            P.dma("sp", PQ, PQ_b[l, n].rearrange("fc p x -> p fc x"), reads=[("PQ_b", l, n, fc) for fc in range(16)], writes=["PQ"])
            zks = [zkf(tt) for tt in range(NT)]
            for fc in range(16):
                sb = slab[sit % 2]
                sk_ = ("slab", sit % 2)
                sit += 1
                i2 = fc % 2
                for m in range(2):
                    P.dma("sp", sb[:, m], dft_f_d[m, fc], writes=[(sk_, m)])
                for m in range(2):
                    for tc in range(16):
                        P.op("pe", lambda e: e.matmul(PS[0][:, i2, m * 256:(m + 1) * 256], lhsT=sb[:, m, tc, :], rhs=zin[:, tc, :],
                                                      start=(tc == 0), stop=(tc == 15)), reads=[(sk_, m)] + zks, writes=[psk(0, i2)])
                ab = AB[i2]
                abk = ("AB", i2)
                P.op("act", lambda e: e.activation(out=ab, in_=PS[0][:, i2, :], func=AF.Identity, bias=cst[:, 3:4], scale=1.0),
                     reads=[psk(0, i2)], writes=[abk])
                q_ = tq[i2]
                qk_ = ("tq", i2)
                Aa, Bb = ab[:, 0:256], ab[:, 256:512]
                Pp, Qq = PQ[:, fc, 0:256], PQ[:, fc, 256:512]
                P.op("dve", lambda e: e.tensor_tensor(out=q_[:, 0, :], in0=Aa, in1=Pp, op=ALU.mult), reads=[abk, "PQ"], writes=[(qk_, 0)])
                P.op("pool", lambda e: e.tensor_tensor(out=q_[:, 1, :], in0=Bb, in1=Qq, op=ALU.mult), reads=[abk, "PQ"], writes=[(qk_, 1)])
                P.op("dve", lambda e: e.tensor_tensor(out=q_[:, 2, :], in0=Aa, in1=Qq, op=ALU.mult), reads=[abk, "PQ"], writes=[(qk_, 2)])
                P.op("pool", lambda e: e.tensor_tensor(out=q_[:, 3, :], in0=Bb, in1=Pp, op=ALU.mult), reads=[abk, "PQ"], writes=[(qk_, 3)])
                P.op("dve", lambda e: e.tensor_tensor(out=Yc[:, 0, fc, :], in0=q_[:, 0, :], in1=q_[:, 1, :], op=ALU.add),
                     reads=[(qk_, 0), (qk_, 1)], writes=[("Yc", fc)])
                P.op("pool", lambda e: e.tensor_tensor(out=Yc[:, 1, fc, :], in0=q_[:, 2, :], in1=q_[:, 3, :], op=ALU.subtract),
                     reads=[(qk_, 2), (qk_, 3)], writes=[("Yc", fc)])
            yks = [("Yc", fc) for fc in range(16)]
            for tcl in range(16):
                sb = slab[sit % 2]
                sk_ = ("slab", sit % 2)
                sit += 1
                i2 = tcl % 2
                for m in range(2):
                    P.dma("sp", sb[:, m], dft_i_d[m, tcl], writes=[(sk_, m)])
                py = PS[1][:, i2, 0:256]
                for m in range(2):
                    for fc in range(16):
                        P.op("pe", lambda e: e.matmul(py, lhsT=sb[:, m, fc, :], rhs=Yc[:, m, fc, :],
                                                      start=(m == 0 and fc == 0), stop=(m == 1 and fc == 15)),
                             reads=[(sk_, m)] + yks, writes=[psk(1, i2)])
                t_ = te[i2]
                tk = ("te", i2)
                P.op("pool", lambda e: e.tensor_tensor(out=t_, in0=zin[:, tcl, :], in1=hbias[:, n, :], op=ALU.mult),
                     reads=[zkf(tcl), "hbias"], writes=[tk])
                P.op("dve", lambda e: e.tensor_tensor(out=t_, in0=t_, in1=py, op=ALU.add), reads=[tk, psk(1, i2)], writes=[tk])
                if n == 0:
                    P.op("pool", lambda e: e.tensor_tensor(out=Z1[:, tcl, :], in0=t_, in1=X12[:, 0, tcl, :], op=ALU.mult),
                         reads=[tk, ("X12", tcl)], writes=[("z1", tcl)])
                else:
                    P.op("pool", lambda e: e.tensor_tensor(out=yB[:, tcl, :], in0=t_, in1=X12[:, 1, tcl, :], op=ALU.mult),
                         reads=[tk, ("X12", tcl)], writes=[("yB", tcl)])
        finish_group(1, YT, gmix, yB, lambda tt: [("yB", tt)])
        P.barrier()
        A.off = mark

    def stage_mix(l, b):
        P.barrier()
        A.reset()
        YT = A.alloc([8, S], BF16)
        gmix = A.alloc([D], F32)
        P.dma("sp", gmix, _bc(dram["mix_norm_g"][l]), writes=["gmix"])
        fns = {"a": mix_mla, "b": mix_hyena, "c": mix_swa, "d": mix_ssd}
        for g, nm in enumerate("abcd"):
            if nm in groups and nm in fns:
                fns[nm](l, b, YT, gmix)
            else:
                y_from_dbg(l, b, g, YT, gmix)
        return YT

    stop = dbg.get("stop")
    if "b" in groups:
        for l in range(nlayer):
            hyena_prologue(l)

    def dump_and_stop():
        P.barrier()
        P.dma("sp", out_d[0], Hf, writes=["outdump"])
        P.barrier()
        P.emit()
        return nc

    for b in range(nseq):
        stage_embed(b)
        if stop == "embed":
            return dump_and_stop()
        for l in range(nlayer):
            YT = stage_mix(l, b)
            stage_outproj(l, YT)
            if stop == "outproj":
                return dump_and_stop()
            stage_ffn(l)
            if stop == "ffn":
                return dump_and_stop()
            stage_ple(l, b, last=(l == nlayer - 1))
            if stop == "ple":
                return dump_and_stop()
    P.barrier()
    P.mark("end")
    P.emit()
    nc._marks = P.marks
    return nc


def host_consts():
    c = {}
    inv = 10000.0 ** (-np.arange(0, 32, 2, dtype=np.float32) / 32.0)
    ang = np.arange(S, dtype=np.float32)[:, None] * inv[None, :].astype(np.float32)
    c["rope_cs"] = np.concatenate([np.cos(ang), np.sin(ang)], axis=1).astype(np.float32)
    j = np.arange(128)[:, None, None, None]
    r = np.arange(3)[None, :, None, None]
    q = np.arange(128)[None, None, None, :]
    dist = np.abs(q - j - (r - 1) * 128).astype(np.float32)
    slopes = ((2.0 ** (-8.0 / 4)) ** np.arange(1, 5, dtype=np.float32))[None, None, :, None]
    c["swa_eb"] = np.where(dist <= 128, np.exp(-slopes * dist), 0.0).astype(np.float32)
    u = np.arange(128)[:, None]
    t = np.arange(128)[None, :]
    c["ssd_masks"] = np.stack([(u <= t), (u >= t), (u > t), (u < t), np.ones((128, 128), bool)], axis=1).astype(np.float32)
    th = 2.0 * np.pi / 4096.0
    f = np.arange(S, dtype=np.float64)[:, None] + 0.5
    t = np.arange(S, dtype=np.float64)[None, :]
    ang = th * f * t
    mats = [np.cos(ang), np.sin(ang)]
    dff = np.empty((2, 16, 128, 16, 128), dtype=ml_dtypes.bfloat16)
    dfi = np.empty((2, 16, 128, 16, 128), dtype=ml_dtypes.bfloat16)
    for m in range(2):
        M = mats[m].reshape(16, 128, 16, 128)
        dff[m] = M.transpose(0, 3, 2, 1).astype(np.float32)
        sgn = 1.0 if m == 0 else -1.0
        dfi[m] = (sgn / 2048.0 * M).transpose(2, 1, 0, 3).astype(np.float32)
    c["dft_f"] = dff
    c["dft_i"] = dfi
    tl = np.linspace(0.0, 1.0, S, dtype=np.float32)[:, None]
    ang2 = (2.0 * math.pi * np.arange(S, dtype=np.float32)[:, None] / S).astype(np.float32)
    bands = np.linspace(1e-4, 15.0, 16, dtype=np.float32)[None, :]
    feat = np.concatenate([tl, np.cos(bands * ang2), -np.sin(bands * ang2)], -1).astype(np.float32)
    c["hy_featT"] = np.ascontiguousarray(feat.T)
    max_decay = math.log(1e-2) / 0.3
    min_decay = math.log(1e-2) / 1.5
    deltas = np.linspace(min_decay, max_decay, 256, dtype=np.float32)
    c["hy_decay"] = np.exp(-tl * np.abs(deltas)[None, :]).astype(np.float32)
    return c


def kernel(**inputs):
    ncores = 8
    nseq = 32 // ncores
    nc = build(nseq=nseq, nlayer=2)
    in_maps = []
    consts = host_consts()
    for c in range(ncores):
        m = {"x": np.ascontiguousarray(inputs["x"][c * nseq:(c + 1) * nseq]),
             "p": np.ascontiguousarray(inputs["p"][:, c * nseq:(c + 1) * nseq])}
        for n in WNAMES:
            m[n] = np.ascontiguousarray(inputs[n])
        m.update(consts)
        in_maps.append(m)
    res = run_bass_kernel_spmd(nc, in_maps, core_ids=list(range(ncores)))
    return np.concatenate([r["out"] for r in res.results], axis=0)
```

```python
import contextlib
import math
import numpy as np
import ml_dtypes
import concourse.bass as bass
import concourse.mybir as mybir
from concourse.bass_utils import run_bass_kernel_spmd

F32 = mybir.dt.float32
BF16 = mybir.dt.bfloat16
AF = mybir.ActivationFunctionType
ALU = mybir.AluOpType
AX = mybir.AxisListType

S = 2048
D = 1024
NT = S // 128
DFF = 2816
NFC = DFF // 128
INW = 2728
ALPHA = (2.0 * 2) ** 0.25
LN_EPS = 1e-5
RMS_EPS = 1e-6
O_CQ, O_CKV, O_KR, O_HY, O_SQ, O_SK, O_SV, O_Z, O_XBC, O_DT = 0, 256, 384, 416, 1184, 1440, 1568, 1696, 1952, 2720

COMPUTE = ("pe", "act", "dve", "pool")
NDMASEM = 12


class _Cap:
    def __getattr__(self, name):
        def f(*a, **k):
            self.rec = (name, a, k)
            return self
        return f


class Prog:
    def __init__(self, nc):
        self.nc = nc
        self.ops = {e: [] for e in ("pe", "act", "dve", "pool", "sp")}
        self.cnt = {e: 0 for e in COMPUTE}
        self.dq_eng = {"sp": "sp", "act": "act", "pool": "pool"}
        self.dq_n = {q: 0 for q in self.dq_eng}
        self.dq_semcnt = {q: [0] * NDMASEM for q in self.dq_eng}
        self.last_w = {}
        self.readers = {}
        self.seen = {e: {} for e in self.ops}
        self.marks = []

    def mark(self, name):
        self.marks.append((name, dict(self.cnt)))

    def _deps(self, eng, reads, writes):
        deps = {}

        def add(tok):
            if tok is None:
                return
            s, v = tok
            if deps.get(s, 0) < v:
                deps[s] = v

        for r in reads:
            add(self.last_w.get(r))
            if isinstance(r, tuple) and r and r[0] == "ps":
                for t in self.readers.get(r, ()):
                    if t[0] != eng:
                        add(t)
        for w in writes:
            add(self.last_w.get(w))
            for t in self.readers.get(w, ()):
                add(t)
        out = []
        for s, v in deps.items():
            if s == "pe" and eng == "pe":
                continue
            if self.seen[eng].get(s, 0) >= v:
                continue
            self.seen[eng][s] = v
            out.append((s, v))
        return out

    def _commit(self, tok, reads, writes):
        for r in reads:
            self.readers.setdefault(r, []).append(tok)
        for w in writes:
            self.last_w[w] = tok
            self.readers[w] = []

    def op(self, eng, fn, reads=(), writes=()):
        cap = _Cap()
        fn(cap)
        name, a, k = cap.rec
        fn = lambda e, name=name, a=a, k=k: getattr(e, name)(*a, **k)
        waits = self._deps(eng, reads, writes)
        self.cnt[eng] += 1
        tok = (eng, self.cnt[eng])
        self.ops[eng].append((fn, waits, (eng, 1)))
        self._commit(tok, reads, writes)

    def dma(self, q, out, in_, reads=(), writes=(), **kw):
        eng = self.dq_eng[q]
        n = self.dq_n[q]
        self.dq_n[q] += 1
        si = n % NDMASEM
        sname = f"d_{q}_{si}"
        waits = self._deps(eng, reads, writes)
        prev = self.dq_semcnt[q][si]
        if prev > 0 and self.seen[eng].get(sname, 0) < prev:
            self.seen[eng][sname] = prev
            waits.append((sname, prev))
        self.dq_semcnt[q][si] += 16
        tok = (sname, self.dq_semcnt[q][si])

        def fn(e, out=out, in_=in_, kw=kw):
            return e.dma_start(out=out, in_=in_, **kw)

        self.ops[eng].append((fn, waits, (sname, 16)))
        self._commit(tok, reads, writes)
        return tok

    def barrier(self, skip_q=()):
        toks = [(e, self.cnt[e]) for e in COMPUTE if self.cnt[e] > 0]
        for q in self.dq_eng:
            if q in skip_q:
                continue
            for i in range(NDMASEM):
                if self.dq_semcnt[q][i] > 0:
                    toks.append((f"d_{q}_{i}", self.dq_semcnt[q][i]))
        for eng in self.ops:
            waits = []
            for s, v in toks:
                if s == eng and eng == "pe":
                    continue
                if self.seen[eng].get(s, 0) < v:
                    self.seen[eng][s] = v
                    waits.append((s, v))
            if waits:
                self.ops[eng].append((None, waits, None))
        pref = tuple(f"d_{q}_" for q in skip_q)
        self.last_w = {k: v for k, v in self.last_w.items() if pref and v[0].startswith(pref)}
        self.readers = {}

    def emit(self):
        nc = self.nc
        semnames = list(COMPUTE) + [f"d_{q}_{i}" for q in self.dq_eng for i in range(NDMASEM)]
        sems = {}
        with contextlib.ExitStack() as st:
            for s in semnames:
                sems[s] = st.enter_context(nc.semaphore(s))
            block = st.enter_context(nc.Block())

            def run(engname):
                def body(e):
                    for fn, waits, inc in self.ops[engname]:
                        for s, v in waits:
                            e.wait_ge(sems[s], v)
                        if fn is not None:
                            fn(e).then_inc(sems[inc[0]], inc[1])
                return body

            block.tensor(run("pe"))
            block.scalar(run("act"))
            block.vector(run("dve"))
            block.gpsimd(run("pool"))
            block.sync(run("sp"))


class Arena:
    def __init__(self, tensor, nbytes):
        self.t = tensor
        self.nbytes = nbytes
        self.off = 0

    def reset(self):
        self.off = 0

    def alloc(self, shape, dtype, parts=128):
        esz = 4 if dtype == F32 else 2
        n = int(np.prod(shape))
        nb = n * esz
        self.off = (self.off + 31) // 32 * 32
        assert self.off + nb <= self.nbytes, f"arena overflow {self.off + nb} > {self.nbytes}"
        a = self.t[0:parts, self.off // 2:(self.off + nb) // 2]
        self.off += nb
        if dtype == F32:
            a = a.bitcast(F32)
        if len(shape) == 2:
            a = a.rearrange("p (a b) -> p a b", a=shape[0])
        elif len(shape) == 3:
            a = a.rearrange("p (a b c) -> p a b c", a=shape[0], b=shape[1])
        elif len(shape) == 4:
            a = a.rearrange("p (a b c d) -> p a b c d", a=shape[0], b=shape[1], c=shape[2])
        return a


def _bc(ap1d, parts=128):
    return ap1d.partition_broadcast(parts)


WNAMES = ["emb_ln_g", "emb_ln_b", "w_in", "mla_q_norm", "mla_kv_norm", "mla_w_uq", "mla_w_ukv",
          "hy_conv_w", "hy_conv_b", "hy_f_w1", "hy_f_b1", "hy_f_freq", "hy_f_w2", "hy_f_b2", "hy_f_w3", "hy_bias",
          "swa_sink", "ssd_conv_w", "ssd_conv_b", "ssd_dt_bias", "ssd_a_log", "ssd_d", "mix_norm_g", "w_out",
          "ln1_g", "ln1_b", "ffn_w_gate", "ffn_w_up", "ffn_conv_w", "ffn_conv_b", "ffn_w_down",
          "ln2_g", "ln2_b", "ple_w_proj", "ple_w_gate", "ple_b_gate", "ln3_g", "ln3_b"]

WSHAPES = {
    "emb_ln_g": [D], "emb_ln_b": [D], "w_in": [2, D, INW], "mla_q_norm": [2, 256], "mla_kv_norm": [2, 128],
    "mla_w_uq": [2, 256, 384], "mla_w_ukv": [2, 128, 512], "hy_conv_w": [2, 3, 768], "hy_conv_b": [2, 768],
    "hy_f_w1": [2, 33, 64], "hy_f_b1": [2, 64], "hy_f_freq": [2, 64], "hy_f_w2": [2, 64, 64], "hy_f_b2": [2, 64],
    "hy_f_w3": [2, 64, 1024], "hy_bias": [2, 2, 256], "swa_sink": [2, 4], "ssd_conv_w": [2, 3, 768],
    "ssd_conv_b": [2, 768], "ssd_dt_bias": [2, 2, 4], "ssd_a_log": [2, 2, 4], "ssd_d": [2, 2, 4],
    "mix_norm_g": [2, D], "w_out": [2, D, D], "ln1_g": [2, D], "ln1_b": [2, D], "ffn_w_gate": [2, D, DFF],
    "ffn_w_up": [2, D, DFF], "ffn_conv_w": [2, 3, DFF], "ffn_conv_b": [2, DFF], "ffn_w_down": [2, DFF, D],
    "ln2_g": [2, D], "ln2_b": [2, D], "ple_w_proj": [2, 256, D], "ple_w_gate": [2, D, D], "ple_b_gate": [2, D],
    "ln3_g": [2, D], "ln3_b": [2, D],
}


def build(nseq=4, nlayer=2, dbg=None):
    dbg = dbg or {}
    nc = bass.Bass("TRN2", target_bir_lowering=False)
    P = Prog(nc)
    dram = {}
    x_d = nc.dram_tensor("x", [nseq, S, D], F32, kind="ExternalInput").ap()
    p_d = nc.dram_tensor("p", [2, nseq, S, 256], F32, kind="ExternalInput").ap()
    for n in WNAMES:
        dram[n] = nc.dram_tensor(n, WSHAPES[n], F32, kind="ExternalInput").ap()
    out_d = nc.dram_tensor("out", [nseq, S, D], F32, kind="ExternalOutput").ap()
    ydbg_d = None
    if dbg.get("ydbg"):
        ydbg_d = nc.dram_tensor("ydbg", [nlayer, nseq, S, D], F32, kind="ExternalInput").ap()
    rope_d = nc.dram_tensor("rope_cs", [S, 32], F32, kind="ExternalInput").ap()
    swa_eb_d = nc.dram_tensor("swa_eb", [128, 3, 4, 128], F32, kind="ExternalInput").ap()
    ssd_masks_d = nc.dram_tensor("ssd_masks", [128, 5, 128], F32, kind="ExternalInput").ap()
    dft_f_d = nc.dram_tensor("dft_f", [2, 16, 128, 16, 128], BF16, kind="ExternalInput").ap()
    dft_i_d = nc.dram_tensor("dft_i", [2, 16, 128, 16, 128], BF16, kind="ExternalInput").ap()
    hy_featT_d = nc.dram_tensor("hy_featT", [33, S], F32, kind="ExternalInput").ap()
    hy_decay_d = nc.dram_tensor("hy_decay", [S, 256], F32, kind="ExternalInput").ap()
    PQ_b = nc.dram_tensor("PQ_b", [2, 2, 16, 128, 512], BF16, kind="Internal").ap()
    ydump_d = None
    if dbg.get("ydump"):
        ydump_d = nc.dram_tensor("ydump", [S, D], F32, kind="ExternalOutput").ap()
    groups = dbg.get("groups", "abcd")

    Hf = nc.dram_tensor("Hf", [S, D], F32, kind="Internal").ap()
    Wi_b = nc.dram_tensor("Wi_b", [2, D, INW], BF16, kind="Internal").ap()
    Wo_b = nc.dram_tensor("Wo_b", [2, D, D], BF16, kind="Internal").ap()
    Wg_b = nc.dram_tensor("Wg_b", [2, NFC, 128, 8, 128], BF16, kind="Internal").ap()
    Wu_b = nc.dram_tensor("Wu_b", [2, NFC, 128, 8, 128], BF16, kind="Internal").ap()
    Wd_b = nc.dram_tensor("Wd_b", [2, DFF, D], BF16, kind="Internal").ap()
    Wpp_b = nc.dram_tensor("Wpp_b", [2, 256, D], BF16, kind="Internal").ap()
    Wpg_b = nc.dram_tensor("Wpg_b", [2, D, D], BF16, kind="Internal").ap()
    Wuq_b = nc.dram_tensor("Wuq_b", [2, 256, 384], BF16, kind="Internal").ap()
    Wukv_b = nc.dram_tensor("Wukv_b", [2, 128, 512], BF16, kind="Internal").ap()

    HT = nc.alloc_sbuf_tensor("HT", [128, 8, S], BF16)
    ident = nc.alloc_sbuf_tensor("ident", [128, 128], BF16)
    ident_f = nc.alloc_sbuf_tensor("ident_f", [128, 128], F32)
    cst = nc.alloc_sbuf_tensor("cst", [128, 8], F32)
    lng = nc.alloc_sbuf_tensor("lng", [128, D], F32)
    lnb = nc.alloc_sbuf_tensor("lnb", [128, D], F32)
    ARENA_BYTES = dbg.get("arena", 160 * 1024)
    arena_t = nc.alloc_sbuf_tensor("arena", [128, ARENA_BYTES // 2], BF16)
    A = Arena(arena_t, ARENA_BYTES)
    PS = [nc.alloc_psum_tensor(f"ps{i}", [128, 2, 512], F32) for i in range(4)]

    def psk(i, j):
        return ("ps", i, j)

    P.op("pool", lambda e: e.memset(ident[:], 1.0), writes=["ident"])
    P.op("pool", lambda e: e.affine_select(out=ident[:], in_=ident[:], pattern=[[-1, 128]],
                                           compare_op=ALU.is_equal, fill=0.0, base=0, channel_multiplier=1),
         reads=["ident"], writes=["ident"])
    P.op("pool", lambda e: e.memset(ident_f[:], 1.0), writes=["ident_f"])
    P.op("pool", lambda e: e.affine_select(out=ident_f[:], in_=ident_f[:], pattern=[[-1, 128]],
                                           compare_op=ALU.is_equal, fill=0.0, base=0, channel_multiplier=1),
         reads=["ident_f"], writes=["ident_f"])
    for i, v in enumerate([LN_EPS, RMS_EPS, 1.0, 0.0, -math.pi]):
        P.op("pool", lambda e, i=i, v=v: e.memset(cst[:, i:i + 1], v), writes=[("cst", i)])

    def cast_rows(dst, src, nrows, key):
        for r0 in range(0, nrows, 256):
            r1 = min(nrows, r0 + 256)
            P.dma("pool", dst[r0:r1], src[r0:r1], writes=[key])

    for l in range(nlayer):
        cast_rows(Wi_b[l], dram["w_in"][l], D, ("Wi_b", l))
        cast_rows(Wo_b[l], dram["w_out"][l], D, ("Wo_b", l))
        cast_rows(Wd_b[l], dram["ffn_w_down"][l], DFF, ("Wd_b", l))
        cast_rows(Wpp_b[l], dram["ple_w_proj"][l], 256, ("Wpp_b", l))
        cast_rows(Wpg_b[l], dram["ple_w_gate"][l], D, ("Wpg_b", l))
        cast_rows(Wuq_b[l], dram["mla_w_uq"][l], 256, ("Wuq_b", l))
        cast_rows(Wukv_b[l], dram["mla_w_ukv"][l], 128, ("Wukv_b", l))
        for ch in range(NFC):
            for (dst, src, nm) in ((Wg_b, dram["ffn_w_gate"], "Wg_b"), (Wu_b, dram["ffn_w_up"], "Wu_b")):
                P.dma("pool", dst[l, ch], src[l][:, ch * 128:(ch + 1) * 128].rearrange("(c p) n -> p c n", p=128),
                      writes=[(nm, l)])

    def load_ln_params(gap, bap):
        P.dma("sp", lng[:], _bc(gap), writes=["lng"])
        P.dma("sp", lnb[:], _bc(bap), writes=["lnb"])

    def ln_tile(z, tt, st, zk, dst_dram, tagk):
        stk = ("st", zk)
        for hseg in range(2):
            P.op("dve", lambda e, hseg=hseg: e.bn_stats(out=st[:, hseg * 6:(hseg + 1) * 6],
                                                         in_=z[:, hseg * 512:(hseg + 1) * 512]),
                 reads=[zk], writes=[(stk, hseg)])
        P.op("dve", lambda e: e.bn_aggr(out=st[:, 12:14], in_=st[:, 0:12]),
             reads=[(stk, 0), (stk, 1)], writes=[(stk, 2)])
        P.op("act", lambda e: e.activation(out=st[:, 14:15], in_=st[:, 13:14], func=AF.Sqrt,
                                           bias=cst[:, 0:1], scale=1.0),
             reads=[(stk, 2), ("cst", 0)], writes=[(stk, 3)])
        P.op("dve", lambda e: e.reciprocal(out=st[:, 14:15], in_=st[:, 14:15]), reads=[(stk, 3)], writes=[(stk, 3)])
        P.op("dve", lambda e: e.scalar_tensor_tensor(out=st[:, 15:16], in0=st[:, 12:13], scalar=-1.0,
                                                     in1=st[:, 14:15], op0=ALU.mult, op1=ALU.mult),
             reads=[(stk, 2), (stk, 3)], writes=[(stk, 4)])
        P.op("act", lambda e: e.activation(out=z, in_=z, func=AF.Identity, bias=st[:, 15:16], scale=st[:, 14:15]),
             reads=[zk, (stk, 3), (stk, 4)], writes=[zk])
        P.op("pool", lambda e: e.tensor_tensor(out=z, in0=z, in1=lng[:], op=ALU.mult), reads=[zk, "lng"], writes=[zk])
        P.op("pool", lambda e: e.tensor_tensor(out=z, in0=z, in1=lnb[:], op=ALU.add), reads=[zk, "lnb"], writes=[zk])
        P.dma("sp", dst_dram, z, reads=[zk], writes=[tagk])
        return

    def to_HT(z, zk, tt, hb, hbk, psi):
        P.op("act", lambda e: e.activation(out=hb, in_=z, func=AF.Identity, bias=cst[:, 3:4], scale=1.0), reads=[zk], writes=[hbk])
        pst = PS[psi[0]][:, psi[1], :].bitcast(BF16).rearrange("p (a b) -> p a b", a=8)
        for c in range(8):
            P.op("pe", lambda e, c=c: e.transpose(out=pst[:, c, :], in_=hb[:, c * 128:(c + 1) * 128], identity=ident[:]),
                 reads=[hbk, "ident"], writes=[psk(*psi)])
        P.op("dve", lambda e: e.tensor_copy(out=HT[:, :, tt * 128:(tt + 1) * 128], in_=pst),
             reads=[psk(*psi)], writes=[("HT", tt)])

    def run_pipe(tiles, stages):
        n, K = len(tiles), len(stages)
        for step in range(n + K - 1):
            for k in range(K - 1, -1, -1):
                i = step - k
                if 0 <= i < n and stages[k] is not None:
                    stages[k](tiles[i])

    def ln_stages(zt, stt, hbs, dst_fn, do_ht=True, psi_fn=lambda tt: (2 + tt % 2, 0)):
        NBz, NBh = len(zt), len(hbs)

        def zk_(tt):
            return ("z", tt % NBz)

        def L1(tt):
            z, st, zk = zt[tt % NBz], stt[tt % NBz], zk_(tt)
            stk = ("st", zk)
            for hseg in range(2):
                P.op("dve", lambda e: e.bn_stats(out=st[:, hseg * 6:(hseg + 1) * 6], in_=z[:, hseg * 512:(hseg + 1) * 512]),
                     reads=[zk], writes=[(stk, hseg)])
            P.op("dve", lambda e: e.bn_aggr(out=st[:, 12:14], in_=st[:, 0:12]), reads=[(stk, 0), (stk, 1)], writes=[(stk, 2)])

        def L234(tt):
            z, st, zk = zt[tt % NBz], stt[tt % NBz], zk_(tt)
            stk = ("st", zk)
            P.op("act", lambda e: e.activation(out=st[:, 14:15], in_=st[:, 13:14], func=AF.Ln, bias=cst[:, 0:1], scale=1.0),
                 reads=[(stk, 2), ("cst", 0)], writes=[(stk, 3)])
            P.op("act", lambda e: e.activation(out=st[:, 14:15], in_=st[:, 14:15], func=AF.Exp, scale=-0.5),
                 reads=[(stk, 3)], writes=[(stk, 3)])
            P.op("dve", lambda e: e.scalar_tensor_tensor(out=st[:, 15:16], in0=st[:, 12:13], scalar=-1.0, in1=st[:, 14:15],
                                                         op0=ALU.mult, op1=ALU.mult), reads=[(stk, 2), (stk, 3)], writes=[(stk, 4)])
            P.op("act", lambda e: e.activation(out=z, in_=z, func=AF.Identity, bias=st[:, 15:16], scale=st[:, 14:15]),
                 reads=[zk, (stk, 3), (stk, 4)], writes=[zk])

        def L5(tt):
            z, zk = zt[tt % NBz], zk_(tt)
            P.op("dve", lambda e: e.tensor_tensor(out=z, in0=z, in1=lng[:], op=ALU.mult), reads=[zk, "lng"], writes=[zk])
            P.op("pool", lambda e: e.tensor_tensor(out=z, in0=z, in1=lnb[:], op=ALU.add), reads=[zk, "lnb"], writes=[zk])

        def L6(tt):
            z, zk = zt[tt % NBz], zk_(tt)
            dst, dk = dst_fn(tt)
            P.dma("sp", dst, z, reads=[zk], writes=[dk])
            if do_ht:
                hb, hbk = hbs[tt % NBh], ("hb", tt % NBh)
                P.op("act", lambda e: e.activation(out=hb, in_=z, func=AF.Identity, bias=cst[:, 3:4], scale=1.0), reads=[zk], writes=[hbk])

        def L7(tt):
            hb, hbk = hbs[tt % NBh], ("hb", tt % NBh)
            psi = psi_fn(tt)
            pst = PS[psi[0]][:, psi[1], :].bitcast(BF16).rearrange("p (a b) -> p a b", a=8)
            for c in range(8):
                P.op("pe", lambda e: e.transpose(out=pst[:, c, :], in_=hb[:, c * 128:(c + 1) * 128], identity=ident[:]),
                     reads=[hbk, "ident"], writes=[psk(*psi)])

        def L8(tt):
            psi = psi_fn(tt)
            pst = PS[psi[0]][:, psi[1], :].bitcast(BF16).rearrange("p (a b) -> p a b", a=8)
            P.op("act", lambda e: e.activation(out=HT[:, :, tt * 128:(tt + 1) * 128], in_=pst, func=AF.Identity, bias=cst[:, 3:4], scale=1.0),
                 reads=[psk(*psi)], writes=[("HT", tt)])

        if do_ht:
            return [L1, L234, L5, L6, L7, L8]
        return [L1, L234, L5, L6]

    def stage_embed(b):
        P.mark("stage_embed")
        P.barrier(skip_q=("pool",) if b == 0 else ())
        A.reset()
        zt = [A.alloc([D], F32) for _ in range(7)]
        hb = [A.alloc([D], BF16) for _ in range(3)]
        stt = [A.alloc([16], F32) for _ in range(7)]
        load_ln_params(dram["emb_ln_g"], dram["emb_ln_b"])

        def F0(tt):
            P.dma("sp", zt[tt % 7], x_d[b, tt * 128:(tt + 1) * 128, :], writes=[("z", tt % 7)])
        run_pipe(list(range(NT)), [F0, None] + ln_stages(zt, stt, hb, lambda tt: (Hf[tt * 128:(tt + 1) * 128, :], ("Hf", tt))))

    def emit_y(YT, gmix, g, tt, y, yk, scr, scrk, ybf, ybk, psi):
        sq = scr[:, 0:256]
        st = scr[:, 256:260]
        yks = yk if isinstance(yk, list) else [yk]
        P.op("act", lambda e: e.activation(out=sq, in_=y, func=AF.Square), reads=yks, writes=[scrk])
        P.op("dve", lambda e: e.reduce_sum(out=st[:, 0:1], in_=sq, axis=AX.X), reads=[scrk], writes=[scrk])
        P.op("act", lambda e: e.activation(out=st[:, 1:2], in_=st[:, 0:1], func=AF.Sqrt, bias=cst[:, 1:2],
                                           scale=1.0 / 256.0), reads=[scrk, ("cst", 1)], writes=[scrk])
        P.op("dve", lambda e: e.reciprocal(out=st[:, 1:2], in_=st[:, 1:2]), reads=[scrk], writes=[scrk])
        P.op("dve", lambda e: e.scalar_tensor_tensor(out=ybf, in0=y, scalar=st[:, 1:2],
                                                     in1=gmix[:, g * 256:(g + 1) * 256], op0=ALU.mult, op1=ALU.mult),
             reads=yks + [scrk, "gmix"], writes=[ybk])
        pst = PS[psi[0]][:, psi[1], :].bitcast(BF16).rearrange("p (a b) -> p a b", a=8)
        for c in range(2):
            P.op("pe", lambda e, c=c: e.transpose(out=pst[:, c, :], in_=ybf[:, c * 128:(c + 1) * 128], identity=ident[:]),
                 reads=[ybk, "ident"], writes=[psk(*psi)])
        P.op("dve", lambda e: e.tensor_copy(out=YT[:, 2 * g:2 * g + 2, tt * 128:(tt + 1) * 128], in_=pst[:, 0:2, :]),
             reads=[psk(*psi)], writes=[("YT", g, tt)])

    def stage_outproj(l, YT):
        P.mark("stage_outproj")
        Wo = A.alloc([8, D], BF16)
        for c in range(8):
            P.dma("sp", Wo[:, c, :], Wo_b[l, c * 128:(c + 1) * 128, :], reads=[("Wo_b", l)], writes=[("Wo", c)])
        zt = [A.alloc([D], F32) for _ in range(7)]
        hb = [A.alloc([D], BF16) for _ in range(3)]
        stt = [A.alloc([16], F32) for _ in range(7)]
        load_ln_params(dram["ln1_g"][l], dram["ln1_b"][l])

        def F0(tt):
            P.dma("sp", zt[tt % 7], Hf[tt * 128:(tt + 1) * 128, :], reads=[("Hf", tt)], writes=[("z", tt % 7)])

        def F2(tt):
            pi = tt % 2
            for half in range(2):
                for c in range(8):
                    P.op("pe", lambda e: e.matmul(PS[pi][:, half, :], lhsT=YT[:, c, tt * 128:(tt + 1) * 128],
                                                  rhs=Wo[:, c, half * 512:(half + 1) * 512], start=(c == 0), stop=(c == 7)),
                         reads=[("YT", c // 2, tt), ("Wo", c)], writes=[psk(pi, half)])

        lns = ln_stages(zt, stt, hb, lambda tt: (Hf[tt * 128:(tt + 1) * 128, :], ("Hf", tt)))

        def F3(tt):
            pi = tt % 2
            z, zk = zt[tt % 7], ("z", tt % 7)
            P.op("dve", lambda e: e.scalar_tensor_tensor(out=z, in0=z, scalar=ALPHA, in1=PS[pi][:, :, :].rearrange("p a b -> p (a b)"),
                                                         op0=ALU.mult, op1=ALU.add), reads=[zk, psk(pi, 0), psk(pi, 1)], writes=[zk])
            lns[0](tt)
        run_pipe(list(range(NT)), [F0, F2, F3] + lns[1:])

    def stage_ffn(l):
        P.mark("stage_ffn")
        P.barrier()
        A.reset()
        HW = S // 2
        actT = A.alloc([NFC, HW], BF16)
        Wd = A.alloc([NFC, D], BF16)
        wgu = [A.alloc([2, 8, 128], BF16) for _ in range(3)]
        G = [A.alloc([HW + 2], F32) for _ in range(2)]
        T1 = [A.alloc([HW], F32) for _ in range(2)]
        halo_h = A.alloc([8, 2], BF16)
        cw = A.alloc([NFC, 4], F32)
        NBZ = 7
        zt = [A.alloc([D], F32) for _ in range(NBZ)]
        hb = [A.alloc([D], BF16) for _ in range(3)]
        stt = [A.alloc([16], F32) for _ in range(NBZ)]
        for k in range(3):
            P.dma("sp", cw[:, :, k:k + 1], dram["ffn_conv_w"][l, k].rearrange("(c p o) -> p c o", p=128, o=1), writes=["cw"],
                  allow_slow_non_contiguous=True)
        P.dma("sp", cw[:, :, 3:4], dram["ffn_conv_b"][l].rearrange("(c p o) -> p c o", p=128, o=1), writes=["cw"],
              allow_slow_non_contiguous=True)
        load_ln_params(dram["ln2_g"][l], dram["ln2_b"][l])
        P.op("dve", lambda e: e.tensor_copy(out=halo_h, in_=HT[:, :, HW - 1:HW + 1]),
             reads=[("HT", NT // 2 - 1), ("HT", NT // 2)], writes=["halo_h"])
        it = 0
        for half in range(2):
            P.mark("ffn_ph1")
            t0 = half * HW
            tts = list(range(half * NT // 2, (half + 1) * NT // 2))
            for ch in range(NFC):
                wb = wgu[it % 3]
                wk = ("wgu", it % 3)
                gi = it % 2
                it += 1
                P.dma("sp", wb[:, 0], Wg_b[l, ch], reads=[("Wg_b", l)], writes=[(wk, 0)])
                P.dma("sp", wb[:, 1], Wu_b[l, ch], reads=[("Wu_b", l)], writes=[(wk, 1)])
                if half == 0 and 2 <= ch < 2 + 11:
                    c0 = (ch - 2) * 2
                    P.dma("pool", Wd[:, c0:c0 + 2, :], Wd_b[l, c0 * 128:(c0 + 2) * 128, :].rearrange("(c p) n -> p c n", p=128),
                          reads=[("Wd_b", l)], writes=[("Wd", c0), ("Wd", c0 + 1)])
                pg, pu = PS[gi * 2], PS[gi * 2 + 1]
                for (pp, wi, pidx) in ((pg, 0, gi * 2), (pu, 1, gi * 2 + 1)):
                    for nb in range(2):
                        for c in range(8):
                            P.op("pe", lambda e, pp=pp, wi=wi, nb=nb, c=c, wb=wb: e.matmul(
                                pp[:, nb, :], lhsT=wb[:, wi, c, :], rhs=HT[:, c, t0 + nb * 512:t0 + (nb + 1) * 512],
                                start=(c == 0), stop=(c == 7)),
                                reads=[(wk, wi)] + [("HT", t0 // 128 + nb * 4 + j) for j in range(4)],
                                writes=[psk(pidx, nb)])
                Gt = G[gi]
                gk = ("G", gi)
                P.op("act", lambda e, Gt=Gt, pg=pg: e.activation(out=Gt[:, 1:HW + 1], in_=pg[:, :, :].rearrange("p a b -> p (a b)"),
                                                                 func=AF.Identity, bias=cst[:, 3:4], scale=1.0),
                     reads=[psk(gi * 2, 0), psk(gi * 2, 1)], writes=[gk])
                hcol = 1 if half == 0 else 0
                for c in range(8):
                    P.op("pe", lambda e, pg=pg, c=c, wb=wb, hcol=hcol: e.matmul(
                        pg[:, 0, 0:1], lhsT=wb[:, 0, c, :], rhs=halo_h[:, c, hcol:hcol + 1],
                        start=(c == 0), stop=(c == 7)), reads=[(wk, 0), "halo_h", gk], writes=[psk(gi * 2, 0)])
                if half == 0:
                    P.op("pool", lambda e, Gt=Gt: e.memset(Gt[:, 0:1], 0.0), writes=[(gk, "l")])
                    P.op("act", lambda e, Gt=Gt, pg=pg: e.activation(out=Gt[:, HW + 1:HW + 2], in_=pg[:, 0, 0:1], func=AF.Identity, bias=cst[:, 3:4], scale=1.0),
                         reads=[psk(gi * 2, 0)], writes=[(gk, "r")])
                else:
                    P.op("pool", lambda e, Gt=Gt: e.memset(Gt[:, HW + 1:HW + 2], 0.0), writes=[(gk, "r")])
                    P.op("act", lambda e, Gt=Gt, pg=pg: e.activation(out=Gt[:, 0:1], in_=pg[:, 0, 0:1], func=AF.Identity, bias=cst[:, 3:4], scale=1.0),
                         reads=[psk(gi * 2, 0)], writes=[(gk, "l")])
                T = T1[gi]
                tk = ("T1", gi)
                P.op("dve", lambda e, T=T, Gt=Gt, ch=ch: e.tensor_scalar(
                    out=T, in0=Gt[:, 1:HW + 1], scalar1=cw[:, ch, 1:2], scalar2=cw[:, ch, 3:4], op0=ALU.mult, op1=ALU.add),
                    reads=[gk, "cw"], writes=[tk])
                P.op("dve", lambda e, T=T, Gt=Gt, ch=ch: e.scalar_tensor_tensor(
                    out=T, in0=Gt[:, 0:HW], scalar=cw[:, ch, 0:1], in1=T, op0=ALU.mult, op1=ALU.add),
                    reads=[gk, (gk, "l"), tk, "cw"], writes=[tk])
                P.op("dve", lambda e, T=T, Gt=Gt, ch=ch: e.scalar_tensor_tensor(
                    out=T, in0=Gt[:, 2:HW + 2], scalar=cw[:, ch, 2:3], in1=T, op0=ALU.mult, op1=ALU.add),
                    reads=[gk, (gk, "r"), tk, "cw"], writes=[tk])
                P.op("act", lambda e, T=T: e.activation(out=T, in_=T, func=AF.Silu), reads=[tk], writes=[tk])
                P.op("dve", lambda e, T=T, pu=pu, ch=ch: e.tensor_tensor(
                    out=actT[:, ch, :], in0=T, in1=pu[:, :, :].rearrange("p a b -> p (a b)"), op=ALU.mult),
                    reads=[tk, psk(gi * 2 + 1, 0), psk(gi * 2 + 1, 1)], writes=[("actT", ch)])
            P.mark("ffn_ph2")
            def F0(tt):
                P.dma("sp", zt[tt % NBZ], Hf[tt * 128:(tt + 1) * 128, :], reads=[("Hf", tt)], writes=[("z", tt % NBZ)])

            def F2(tt, tts=tts):
                pi = tt % 2
                tl = tt - tts[0]
                for hf in range(2):
                    for ch in range(NFC):
                        P.op("pe", lambda e: e.matmul(PS[pi][:, hf, :], lhsT=actT[:, ch, tl * 128:(tl + 1) * 128],
                                                      rhs=Wd[:, ch, hf * 512:(hf + 1) * 512], start=(ch == 0), stop=(ch == NFC - 1)),
                             reads=[("actT", ch), ("Wd", ch)], writes=[psk(pi, hf)])

            lns = ln_stages(zt, stt, hb, lambda tt: (Hf[tt * 128:(tt + 1) * 128, :], ("Hf", tt)))

            def F3(tt):
                pi = tt % 2
                z, zk = zt[tt % NBZ], ("z", tt % NBZ)
                P.op("dve", lambda e: e.scalar_tensor_tensor(out=z, in0=z, scalar=ALPHA, in1=PS[pi][:, :, :].rearrange("p a b -> p (a b)"),
                                                             op0=ALU.mult, op1=ALU.add), reads=[zk, psk(pi, 0), psk(pi, 1)], writes=[zk])
                lns[0](tt)
            run_pipe(tts, [F0, F2, F3] + lns[1:])

    def stage_ple(l, b, last):
        P.mark("stage_ple")
        P.barrier()
        A.reset()
        Wpg = A.alloc([8, D], BF16)
        Wpp = A.alloc([2, D], BF16)
        bg = A.alloc([D], F32)
        NBZ, NBS = 8, 5
        pt = [A.alloc([256], F32) for _ in range(3)]
        pb = [A.alloc([256], BF16) for _ in range(3)]
        pT = [A.alloc([2, 128], BF16) for _ in range(6)]
        sg = [A.alloc([D], F32) for _ in range(NBS)]
        zt = [A.alloc([D], F32) for _ in range(NBZ)]
        hb = [A.alloc([D], BF16) for _ in range(3)]
        stt = [A.alloc([16], F32) for _ in range(NBZ)]
        for c in range(8):
            P.dma("sp", Wpg[:, c, :], Wpg_b[l, c * 128:(c + 1) * 128, :], reads=[("Wpg_b", l)], writes=[("Wpg", c)])
        P.dma("sp", Wpp, Wpp_b[l].rearrange("(c p) n -> p c n", p=128), reads=[("Wpp_b", l)], writes=["Wpp"])
        P.dma("sp", bg, _bc(dram["ple_b_gate"][l]), writes=["bg"])
        load_ln_params(dram["ln3_g"][l], dram["ln3_b"][l])

        def G0(tt):
            P.dma("sp", pt[tt % 3], p_d[l, b, tt * 128:(tt + 1) * 128, :], writes=[("pt", tt % 3)])

        def G1(tt):
            P.op("act", lambda e: e.activation(out=pb[tt % 3], in_=pt[tt % 3], func=AF.Identity, bias=cst[:, 3:4], scale=1.0),
                 reads=[("pt", tt % 3)], writes=[("pb", tt % 3)])

        def G2(tt):
            i2 = tt % 2
            pst = PS[3][:, 1, :].bitcast(BF16).rearrange("p (a b) -> p a b", a=8)
            for c in range(2):
                P.op("pe", lambda e: e.transpose(out=pst[:, c, :], in_=pb[tt % 3][:, c * 128:(c + 1) * 128], identity=ident[:]),
                     reads=[("pb", tt % 3), "ident"], writes=[psk(3, 1)])
            for hf in range(2):
                for c in range(8):
                    P.op("pe", lambda e: e.matmul(PS[i2][:, hf, :], lhsT=HT[:, c, tt * 128:(tt + 1) * 128], rhs=Wpg[:, c, hf * 512:(hf + 1) * 512],
                                                  start=(c == 0), stop=(c == 7)), reads=[("HT", tt), ("Wpg", c)], writes=[psk(i2, hf)])

        def G3(tt):
            i2 = tt % 2
            pst = PS[3][:, 1, :].bitcast(BF16).rearrange("p (a b) -> p a b", a=8)
            P.op("dve", lambda e: e.tensor_copy(out=pT[tt % 6], in_=pst[:, 0:2, :]), reads=[psk(3, 1)], writes=[("pT", tt % 6)])
            s_, sk = sg[tt % NBS], ("sg", tt % NBS)
            P.op("dve", lambda e: e.tensor_tensor(out=s_, in0=PS[i2][:, :, :].rearrange("p a b -> p (a b)"), in1=bg, op=ALU.add),
                 reads=[psk(i2, 0), psk(i2, 1), "bg"], writes=[sk])

        def G4(tt):
            s_, sk = sg[tt % NBS], ("sg", tt % NBS)
            P.op("act", lambda e: e.activation(out=s_, in_=s_, func=AF.Sigmoid), reads=[sk], writes=[sk])
            i2 = tt % 2
            for hf in range(2):
                for c in range(2):
                    P.op("pe", lambda e: e.matmul(PS[2][:, hf, :], lhsT=pT[tt % 6][:, c, :], rhs=Wpp[:, c, hf * 512:(hf + 1) * 512],
                                                  start=(c == 0), stop=(c == 1)), reads=[("pT", tt % 6), "Wpp"], writes=[psk(2, hf)])
            P.dma("sp", zt[tt % NBZ], Hf[tt * 128:(tt + 1) * 128, :], reads=[("Hf", tt)], writes=[("z", tt % NBZ)])

        dstf = (lambda tt: (out_d[b, tt * 128:(tt + 1) * 128, :], ("out", b, tt))) if last else \
               (lambda tt: (Hf[tt * 128:(tt + 1) * 128, :], ("Hf", tt)))
        lns = ln_stages(zt, stt, hb, dstf, do_ht=not last, psi_fn=lambda tt: (3, 0))

        def G5(tt):
            i2 = tt % 2
            s_, sk = sg[tt % NBS], ("sg", tt % NBS)
            z, zk = zt[tt % NBZ], ("z", tt % NBZ)
            P.op("dve", lambda e: e.tensor_tensor(out=s_, in0=s_, in1=PS[2][:, :, :].rearrange("p a b -> p (a b)"), op=ALU.mult),
                 reads=[psk(2, 0), psk(2, 1), sk], writes=[sk])
            P.op("dve", lambda e: e.scalar_tensor_tensor(out=z, in0=z, scalar=ALPHA, in1=s_, op0=ALU.mult, op1=ALU.add),
                 reads=[zk, sk], writes=[zk])

        def G6(tt):
            lns[0](tt)
        run_pipe(list(range(NT)), [G0, G1, G2, G3, G4, G5, G6] + lns[1:])

    def stage_mix_dbg(l, b):
        P.barrier()
        A.reset()
        YT = A.alloc([8, S], BF16)
        gmix = A.alloc([D], F32)
        P.dma("sp", gmix, _bc(dram["mix_norm_g"][l]), writes=["gmix"])
        mark = A.off
        yt = [A.alloc([D], F32) for _ in range(2)]
        scr = [A.alloc([260], F32) for _ in range(2)]
        ybf = [A.alloc([256], BF16) for _ in range(2)]
        for tt in range(NT):
            i2 = tt % 2
            P.dma("sp", yt[i2], ydbg_d[l, b, tt * 128:(tt + 1) * 128, :], writes=[("yt", i2)])
            for g in range(4):
                emit_y(YT, gmix, g, tt, yt[i2][:, g * 256:(g + 1) * 256], ("yt", i2), scr[i2], ("scr", i2),
                       ybf[i2], ("ybf", i2), (3, 1))
        P.barrier()
        A.off = mark
        return YT

    def y_from_dbg(l, b, g, YT, gmix):
        mark = A.off
        yt = [A.alloc([256], F32) for _ in range(2)]
        scr = [A.alloc([260], F32) for _ in range(2)]
        ybf = [A.alloc([256], BF16) for _ in range(2)]
        for tt in range(NT):
            i2 = tt % 2
            P.dma("sp", yt[i2], ydbg_d[l, b, tt * 128:(tt + 1) * 128, g * 256:(g + 1) * 256], writes=[("yt", i2)])
            emit_y(YT, gmix, g, tt, yt[i2], ("yt", i2), scr[i2], ("scr", i2), ybf[i2], ("ybf", i2), (3, 1))
        P.barrier()
        A.off = mark

    def finish_group(g, YT, gmix, yG, ykeyfn):
        P.mark("finish_group")
        scr = [A.alloc([260], F32) for _ in range(2)]
        ybf = [A.alloc([256], BF16) for _ in range(2)]
        for tt in range(NT):
            i2 = tt % 2
            if ydump_d is not None:
                P.dma("sp", ydump_d[tt * 128:(tt + 1) * 128, g * 256:(g + 1) * 256], yG[:, tt, :], reads=ykeyfn(tt),
                      writes=[("ydump", g, tt)])
            emit_y(YT, gmix, g, tt, yG[:, tt, :], ykeyfn(tt), scr[i2], ("scr", i2), ybf[i2], ("ybf", i2), (3, 1))

    def mix_mla(l, b, YT, gmix):
        P.mark("mix_mla")
        mark = A.off
        Wa = A.alloc([8, 416], BF16)
        Wuq = A.alloc([2, 384], BF16)
        Wukv = A.alloc([512], BF16)
        gq = A.alloc([4], F32)
        CQT = A.alloc([3, S], BF16)
        SQ = A.alloc([3, S], BF16)
        QT = A.alloc([4, S], BF16)
        KT = A.alloc([4, S], BF16)
        V1 = A.alloc([NT, 4, 66], BF16)
        rope = A.alloc([NT, 32], F32)
        ones = A.alloc([2], BF16)
        yA = A.alloc([NT, 256], F32)
        Qb = [A.alloc([4, 96], BF16) for _ in range(5)]
        Kb = [A.alloc([4, 96], BF16) for _ in range(5)]
        R = [A.alloc([5, 32], F32) for _ in range(3)]
        Ro = [A.alloc([5, 32], F32) for _ in range(2)]
        T4 = [A.alloc([4, 5, 16], F32) for _ in range(3)]
        st = [A.alloc([4], F32) for _ in range(3)]
        PT = [A.alloc([512], BF16) for _ in range(3)]
        rc = [A.alloc([4], F32) for _ in range(2)]
        OTs = [A.alloc([512], F32) for _ in range(2)]
        P.dma("sp", Wa, Wi_b[l][:, 0:416].rearrange("(c p) n -> p c n", p=128), reads=[("Wi_b", l)], writes=["Wa"])
        P.dma("sp", Wuq, Wuq_b[l].rearrange("(c p) n -> p c n", p=128), reads=[("Wuq_b", l)], writes=["Wuq"])
        P.dma("sp", Wukv, Wukv_b[l], reads=[("Wukv_b", l)], writes=["Wukv"])
        P.dma("sp", gq[:, 0:2], dram["mla_q_norm"][l].rearrange("(c p) -> p c", p=128), writes=["gq"],
              allow_slow_non_contiguous=True)
        P.dma("sp", gq[:, 2:3], dram["mla_kv_norm"][l].rearrange("(c p) -> p c", p=128), writes=["gq"],
              allow_slow_non_contiguous=True)
        P.dma("sp", rope, rope_d.rearrange("(t p) c -> p t c", p=128), writes=["rope"])
        P.op("pool", lambda e: e.memset(ones, 1.0), writes=["ones"])
        P.op("pool", lambda e: e.memset(V1[:, :, :, 64:65], 1.0), writes=["V1one"])
        it = 0
        for nb in range(4):
            for ci in range(3):
                pi, pb_ = (it % 4) // 2, it % 2
                it += 1
                for c in range(8):
                    P.op("pe", lambda e: e.matmul(PS[pi][:, pb_, :], lhsT=Wa[:, c, ci * 128:(ci + 1) * 128],
                                                  rhs=HT[:, c, nb * 512:(nb + 1) * 512], start=(c == 0), stop=(c == 7)),
                         reads=["Wa"] + [("HT", nb * 4 + j) for j in range(4)], writes=[psk(pi, pb_)])
                P.op("act", lambda e: e.activation(out=CQT[:, ci, nb * 512:(nb + 1) * 512], in_=PS[pi][:, pb_, :],
                                                   func=AF.Identity, bias=cst[:, 3:4], scale=gq[:, ci:ci + 1]),
                     reads=[psk(pi, pb_), "gq", ("cst", 3)], writes=[("CQT", nb)])
                P.op("act", lambda e: e.activation(out=SQ[:, ci, nb * 512:(nb + 1) * 512], in_=PS[pi][:, pb_, :],
                                                   func=AF.Square), reads=[psk(pi, pb_)], writes=[("SQ", nb)])
        P.mark("mla_A2")
        SCALE = 96.0 ** -0.5

        def bufs(tt):
            i3 = tt % 3
            b0 = PS[i3][:, 0, :]
            return i3, b0[:, 0:384], b0[:, 384:416], b0[:, 416:418], PS[i3][:, 1, :]

        def M0(tt):
            i3, PSq, PSkr, PSss, PSkv = bufs(tt)
            tsl = slice(tt * 128, (tt + 1) * 128)
            nb = tt // 4
            for c in range(2):
                P.op("pe", lambda e: e.matmul(PSq, lhsT=CQT[:, c, tsl], rhs=Wuq[:, c, :], start=(c == 0), stop=(c == 1)),
                     reads=[("CQT", nb), "Wuq"], writes=[psk(i3, 0)])
            for c in range(8):
                P.op("pe", lambda e: e.matmul(PSkr, lhsT=HT[:, c, tsl], rhs=Wa[:, c, 384:416], start=(c == 0), stop=(c == 7)),
                     reads=[("HT", tt), "Wa"], writes=[psk(i3, 0)])
            for c in range(2):
                P.op("pe", lambda e: e.matmul(PSss[:, 0:1], lhsT=SQ[:, c, tsl], rhs=ones[:, 0:1], start=(c == 0), stop=(c == 1)),
                     reads=[("SQ", nb), "ones"], writes=[psk(i3, 0)])
            P.op("pe", lambda e: e.matmul(PSss[:, 1:2], lhsT=SQ[:, 2, tsl], rhs=ones[:, 0:1], start=True, stop=True),
                 reads=[("SQ", nb), "ones"], writes=[psk(i3, 0)])
            P.op("pe", lambda e: e.matmul(PSkv, lhsT=CQT[:, 2, tsl], rhs=Wukv, start=True, stop=True),
                 reads=[("CQT", nb), "Wukv"], writes=[psk(i3, 1)])

        def M1(tt):
            i3, PSq, PSkr, PSss, PSkv = bufs(tt)
            s_, sk_ = st[tt % 3], ("mst", tt % 3)
            P.op("act", lambda e: e.activation(out=s_[:, 0:1], in_=PSss[:, 0:1], func=AF.Sqrt, bias=cst[:, 1:2], scale=1.0 / 256),
                 reads=[psk(i3, 0), ("cst", 1)], writes=[sk_])
            P.op("act", lambda e: e.activation(out=s_[:, 1:2], in_=PSss[:, 1:2], func=AF.Sqrt, bias=cst[:, 1:2], scale=1.0 / 128),
                 reads=[psk(i3, 0), ("cst", 1)], writes=[sk_])
            P.op("dve", lambda e: e.reciprocal(out=s_[:, 0:2], in_=s_[:, 0:2]), reads=[sk_], writes=[sk_])

        def M2(tt):
            i3, PSq, PSkr, PSss, PSkv = bufs(tt)
            s_, sk_ = st[tt % 3], ("mst", tt % 3)
            q3 = PSq.rearrange("p (h d) -> p h d", h=4)
            kv3 = PSkv.rearrange("p (h d) -> p h d", h=4)
            i5 = tt % 5
            qk, kk, rk = ("Qb", i5), ("Kb", i5), ("R", tt % 3)
            P.op("act", lambda e: e.activation(out=Qb[i5][:, :, 0:64], in_=q3[:, :, 0:64], func=AF.Identity,
                                               bias=cst[:, 3:4], scale=s_[:, 0:1]), reads=[psk(i3, 0), sk_, ("cst", 3)], writes=[qk])
            P.op("act", lambda e: e.activation(out=R[tt % 3][:, 0:4, :], in_=q3[:, :, 64:96], func=AF.Identity,
                                               bias=cst[:, 3:4], scale=s_[:, 0:1]), reads=[psk(i3, 0), sk_, ("cst", 3)], writes=[rk])
            P.op("act", lambda e: e.activation(out=R[tt % 3][:, 4, :], in_=PSkr, func=AF.Identity, bias=cst[:, 3:4], scale=1.0),
                 reads=[psk(i3, 0)], writes=[rk])
            P.op("act", lambda e: e.activation(out=Kb[i5][:, :, 0:64], in_=kv3[:, :, 0:64], func=AF.Identity,
                                               bias=cst[:, 3:4], scale=s_[:, 1:2]), reads=[psk(i3, 1), sk_, ("cst", 3)], writes=[kk])
            P.op("act", lambda e: e.activation(out=V1[:, tt, :, 0:64], in_=kv3[:, :, 64:128], func=AF.Identity,
                                               bias=cst[:, 3:4], scale=s_[:, 1:2]), reads=[psk(i3, 1), sk_, ("cst", 3)],
                 writes=[("V1", tt)])

        def M3(tt):
            cosb = rope[:, tt, 0:16].unsqueeze(1).broadcast_to([128, 5, 16])
            sinb = rope[:, tt, 16:32].unsqueeze(1).broadcast_to([128, 5, 16])
            Rr, T_ = R[tt % 3], T4[tt % 3]
            rk, tk_ = ("R", tt % 3), ("T4", tt % 3)
            P.op("pool", lambda e: e.tensor_tensor(out=T_[:, 0], in0=Rr[:, :, 0:16], in1=cosb, op=ALU.mult), reads=[rk, "rope"], writes=[(tk_, 0)])
            P.op("pool", lambda e: e.tensor_tensor(out=T_[:, 1], in0=Rr[:, :, 16:32], in1=sinb, op=ALU.mult), reads=[rk, "rope"], writes=[(tk_, 1)])
            P.op("pool", lambda e: e.tensor_tensor(out=T_[:, 2], in0=Rr[:, :, 16:32], in1=cosb, op=ALU.mult), reads=[rk, "rope"], writes=[(tk_, 2)])
            P.op("pool", lambda e: e.tensor_tensor(out=T_[:, 3], in0=Rr[:, :, 0:16], in1=sinb, op=ALU.mult), reads=[rk, "rope"], writes=[(tk_, 3)])

        def M4(tt):
            T_, tk_ = T4[tt % 3], ("T4", tt % 3)
            i5 = tt % 5
            qk, kk = ("Qb", i5), ("Kb", i5)
            ro, rok = Ro[tt % 2], ("Ro", tt % 2)
            P.op("dve", lambda e: e.tensor_tensor(out=ro[:, :, 0:16], in0=T_[:, 0], in1=T_[:, 1], op=ALU.subtract),
                 reads=[(tk_, 0), (tk_, 1)], writes=[(rok, 0)])
            P.op("dve", lambda e: e.tensor_tensor(out=ro[:, :, 16:32], in0=T_[:, 2], in1=T_[:, 3], op=ALU.add),
                 reads=[(tk_, 2), (tk_, 3)], writes=[(rok, 1)])
            P.op("dve", lambda e: e.tensor_copy(out=Qb[i5][:, :, 64:96], in_=ro[:, 0:4, :]), reads=[(rok, 0), (rok, 1)], writes=[(qk, "r")])
            P.op("dve", lambda e: e.tensor_copy(out=Kb[i5][:, :, 64:96], in_=ro[:, 4:5, :].broadcast_to([128, 4, 32])),
                 reads=[(rok, 0), (rok, 1)], writes=[(kk, "r")])

        def M5(tt):
            i5, i2 = tt % 5, tt % 2
            qk, kk = ("Qb", i5), ("Kb", i5)
            pst = PS[3][:, i2, :].bitcast(BF16).rearrange("p (a b) -> p a b", a=8)
            for h in range(4):
                P.op("pe", lambda e: e.transpose(out=pst[0:96, h, :], in_=Qb[i5][:, h, :], identity=ident[:]),
                     reads=[qk, (qk, "r"), "ident"], writes=[psk(3, i2)])
            for h in range(4):
                P.op("pe", lambda e: e.transpose(out=pst[0:96, 4 + h, :], in_=Kb[i5][:, h, :], identity=ident[:]),
                     reads=[kk, (kk, "r"), "ident"], writes=[psk(3, i2)])

        def M6(tt):
            i2 = tt % 2
            tsl = slice(tt * 128, (tt + 1) * 128)
            pst = PS[3][:, i2, :].bitcast(BF16).rearrange("p (a b) -> p a b", a=8)
            P.op("dve", lambda e: e.tensor_copy(out=QT[0:96, :, tsl], in_=pst[0:96, 0:4, :]), reads=[psk(3, i2)], writes=[("QT", tt)])
            P.op("act", lambda e: e.activation(out=KT[0:96, :, tsl], in_=pst[0:96, 4:8, :], func=AF.Identity, bias=cst[0:96, 3:4], scale=1.0),
                 reads=[psk(3, i2)], writes=[("KT", tt)])
        run_pipe(list(range(NT)), [M0, M1, M2, M3, M4, M5, M6])
        P.barrier()
        P.mark("mla_A3")
        items = [(h, qb, kt) for h in range(4) for qb in range(4) for kt in range(NT)]

        def emit_S(i):
            h, qb, kt = items[i]
            si = i % 4
            PSs = PS[si // 2][:, si % 2, :]
            P.op("pe", lambda e: e.matmul(PSs, lhsT=KT[0:96, h, kt * 128:(kt + 1) * 128],
                                          rhs=QT[0:96, h, qb * 512:(qb + 1) * 512], start=True, stop=True),
                 reads=[("KT", kt)] + [("QT", qb * 4 + j) for j in range(4)], writes=[psk(si // 2, si % 2)])

        emit_S(0)
        for i, (h, qb, kt) in enumerate(items):
            grp = i // NT
            oi = grp % 2
            POT = PS[2 + oi][0:65, 0, :]
            ok_ = psk(2 + oi, 0)
            if i + 1 < len(items):
                emit_S(i + 1)
            si = i % 4
            PSs = PS[si // 2][:, si % 2, :]
            pt_ = PT[i % 3]
            ptk = ("PT", i % 3)
            P.op("act", lambda e: e.activation(out=pt_, in_=PSs, func=AF.Exp, scale=SCALE), reads=[psk(si // 2, si % 2)], writes=[ptk])
            P.op("pe", lambda e: e.matmul(POT, lhsT=V1[:, kt, h, 0:65], rhs=pt_, start=(kt == 0), stop=(kt == NT - 1)),
                 reads=[ptk, ("V1", kt), "V1one"], writes=[ok_])
            if kt == NT - 1:
                ots, otk = OTs[oi], ("OTs", oi)
                P.op("dve", lambda e: e.tensor_copy(out=ots[0:65, :], in_=POT), reads=[ok_], writes=[otk])
                PTR = PS[2 + oi][:, 1, 0:260].rearrange("p (j d) -> p j d", j=4)
                trk = psk(2 + oi, 1)
                for j in range(4):
                    P.op("pe", lambda e: e.transpose(out=PTR[:, j, :], in_=ots[0:65, j * 128:(j + 1) * 128], identity=ident_f[0:65, 0:65]),
                         reads=[otk, "ident_f"], writes=[trk])
                rck = ("rc", oi)
                P.op("dve", lambda e: e.reciprocal(out=rc[oi], in_=PTR[:, :, 64]), reads=[trk], writes=[rck])
                P.op("dve", lambda e: e.tensor_tensor(out=yA[:, qb * 4:(qb + 1) * 4, h * 64:(h + 1) * 64], in0=PTR[:, :, 0:64],
                                                      in1=rc[oi].unsqueeze(2).broadcast_to([128, 4, 64]), op=ALU.mult),
                     reads=[trk, rck], writes=[("yA", qb, h)])
        finish_group(0, YT, gmix, yA, lambda tt: [("yA", tt // 4, h) for h in range(4)])
        P.barrier()
        A.off = mark

    def mix_swa(l, b, YT, gmix):
        P.mark("mix_swa")
        mark = A.off
        Wc = A.alloc([8, 512], BF16)
        Wk2 = A.alloc([8, 2, 2, 64], BF16) if False else A.alloc([8, 256], BF16)
        SQT = A.alloc([2, NT, 256], BF16)
        SKT = A.alloc([2, S], BF16)
        V1s = A.alloc([NT, 2, 66], BF16)
        P.op("dve", lambda e: e.memset(SQT, 0.0), writes=["SQTz"])
        EBt = A.alloc([3, 4, 128], F32)
        esink = A.alloc([4], F32)
        yC = A.alloc([NT, 256], F32)
        E = [A.alloc([3, 256], F32) for _ in range(3)]
        Eb = [A.alloc([3, 256], BF16) for _ in range(3)]
        rc = [A.alloc([4], F32) for _ in range(2)]
        P.dma("sp", Wc, Wi_b[l][:, O_SQ:O_SQ + 512].rearrange("(c p) n -> p c n", p=128), reads=[("Wi_b", l)], writes=["Wc"])
        P.dma("sp", EBt, swa_eb_d, writes=["EBt"])
        P.dma("sp", esink, _bc(dram["swa_sink"][l]), writes=["esink"])
        P.op("act", lambda e: e.activation(out=esink, in_=esink, func=AF.Exp), reads=["esink"], writes=["esink"])
        P.op("pool", lambda e: e.memset(V1s[:, :, :, 64:65], 1.0), writes=["V1sone"])
        Wk2v = Wk2.rearrange("p c (k d e) -> p c k d e", k=2, d=2)
        for dup in range(2):
            P.op("pool", lambda e: e.tensor_copy(out=Wk2v[:, :, :, dup, :],
                                                 in_=Wc[:, :, 256:384].rearrange("p c (k e) -> p c k e", k=2)),
                 reads=["Wc"], writes=[("Wk2", dup)])
        it = 0
        for nb in range(4 if dbg.get("swa_stage", 9) >= 1 else 0):
            for (dst, is_k) in ((SQT, 0), (SKT, 1)):
                for p_ in range(2):
                    pi, pb_ = (it % 4) // 2, it % 2
                    it += 1
                    for c in range(8):
                        lw = Wk2[:, c, p_ * 128:(p_ + 1) * 128] if is_k else Wc[:, c, p_ * 128:(p_ + 1) * 128]
                        P.op("pe", lambda e: e.matmul(PS[pi][:, pb_, :], lhsT=lw, rhs=HT[:, c, nb * 512:(nb + 1) * 512],
                                                      start=(c == 0), stop=(c == 7)),
                             reads=["Wc", ("Wk2", 0), ("Wk2", 1)] + [("HT", nb * 4 + j) for j in range(4)], writes=[psk(pi, pb_)])
                    if is_k:
                        P.op("act", lambda e: e.activation(out=dst[:, p_, nb * 512:(nb + 1) * 512], in_=PS[pi][:, pb_, :],
                                                           func=AF.Identity, bias=cst[:, 3:4], scale=1.0),
                             reads=[psk(pi, pb_)], writes=[("SQK", is_k, nb)])
                    else:
                        for g in range(2):
                            P.op("act", lambda e: e.activation(
                                out=SQT[g * 64:(g + 1) * 64, p_, nb * 4:(nb + 1) * 4, g * 128:(g + 1) * 128],
                                in_=PS[pi][g * 64:(g + 1) * 64, pb_, :].rearrange("p (t q) -> p t q", t=4),
                                func=AF.Identity, bias=cst[g * 64:(g + 1) * 64, 3:4], scale=1.0),
                                reads=[psk(pi, pb_), "SQTz"], writes=[("SQK", is_k, nb)])
        P.mark("swa_V")
        for tt in range(NT if dbg.get("swa_stage", 9) >= 2 else 0):
            i2 = tt % 2
            pv = PS[2 + i2][:, 1, 0:128]
            for c in range(8):
                P.op("pe", lambda e: e.matmul(pv, lhsT=HT[:, c, tt * 128:(tt + 1) * 128], rhs=Wc[:, c, 384:512],
                                              start=(c == 0), stop=(c == 7)), reads=["Wc", ("HT", tt)], writes=[psk(2 + i2, 1)])
            P.op("act", lambda e: e.activation(out=V1s[:, tt, :, 0:64], in_=pv.rearrange("p (k d) -> p k d", k=2), func=AF.Identity,
                                               bias=cst[:, 3:4], scale=1.0),
                 reads=[psk(2 + i2, 1), ("cst", 3)], writes=[("V1s", tt)])
        P.mark("swa_attn")
        items = [(n, p_) for n in range(NT if dbg.get("swa_n") is None else dbg["swa_n"]) for p_ in range(2)]
        idx = {it_: i for i, it_ in enumerate(items)}

        def rels_of(n):
            return [r for r in range(3) if 0 <= n + r - 1 < NT]

        def W0(it_):
            n, p_ = it_
            i = idx[it_]
            PSs = PS[i % 2][:, :, :].rearrange("p a b -> p (a b)")
            for r in rels_of(n):
                kt = n + r - 1
                bankk = psk(i % 2, 0 if r < 2 else 1)
                P.op("pe", lambda e: e.matmul(PSs[:, r * 256:(r + 1) * 256], lhsT=SKT[:, p_, kt * 128:(kt + 1) * 128],
                                              rhs=SQT[:, p_, n, :], start=True, stop=True),
                     reads=[("SQK", 0, n // 4), ("SQK", 1, kt // 4)], writes=[bankk])

        def W1(it_):
            n, p_ = it_
            i = idx[it_]
            rels = rels_of(n)
            PSs = PS[i % 2][:, :, :].rearrange("p a b -> p (a b)")
            e_, eb_ = E[i % 3], Eb[i % 3]
            for r in rels:
                bankk = psk(i % 2, 0 if r < 2 else 1)
                P.op("act", lambda e: e.activation(out=e_[:, r, :], in_=PSs[:, r * 256:(r + 1) * 256], func=AF.Exp, scale=0.125),
                     reads=[bankk], writes=[("E", i % 3, r)])
            r0, r1 = rels[0], rels[-1] + 1
            P.op("pool", lambda e: e.tensor_tensor(out=eb_[:, r0:r1, :].rearrange("p r (g q) -> p r g q", g=2),
                                                   in0=e_[:, r0:r1, :].rearrange("p r (g q) -> p r g q", g=2),
                                                   in1=EBt[:, r0:r1, 2 * p_:2 * p_ + 2, :], op=ALU.mult),
                 reads=[("E", i % 3, r) for r in rels] + ["EBt"], writes=[("Eb", i % 3)])

        def W2(it_):
            n, p_ = it_
            i = idx[it_]
            i2 = n % 2
            rels = rels_of(n)
            eb_ = Eb[i % 3]
            PSo = PS[2 + i2][:, 0, 0:260].rearrange("p (h d) -> p h d", h=4)
            ok_ = psk(2 + i2, 0)
            for g in range(2):
                h = 2 * p_ + g
                for r in rels:
                    kt = n + r - 1
                    P.op("pe", lambda e: e.matmul(PSo[:, h, :], lhsT=eb_[:, r, g * 128:(g + 1) * 128], rhs=V1s[:, kt, p_, 0:65],
                                                  start=(r == rels[0]), stop=(r == rels[-1])),
                         reads=[("Eb", i % 3), ("V1s", kt), "V1sone"], writes=[ok_])
            if p_ == 1:
                rck = ("rc", i2)
                P.op("dve", lambda e: e.tensor_tensor(out=rc[i2], in0=PSo[:, :, 64], in1=esink, op=ALU.add), reads=[ok_, "esink"], writes=[rck])
                P.op("dve", lambda e: e.reciprocal(out=rc[i2], in_=rc[i2]), reads=[rck], writes=[rck])
                P.op("dve", lambda e: e.tensor_tensor(out=yC[:, n, :].rearrange("p (h d) -> p h d", h=4), in0=PSo[:, :, 0:64],
                                                      in1=rc[i2].unsqueeze(2).broadcast_to([128, 4, 64]), op=ALU.mult),
                     reads=[ok_, rck], writes=[("yC", n)])
        run_pipe(items, [W0, W1, W2])
        finish_group(2, YT, gmix, yC, lambda tt: [("yC", tt)])
        P.barrier()
        A.off = mark

    def mix_ssd(l, b, YT, gmix):
        P.mark("mix_ssd")
        mark = A.off
        Wd_ = A.alloc([8, 1032], BF16)
        XBCT = A.alloc([6, S], BF16)
        X = A.alloc([NT, 256], BF16)
        Bt = A.alloc([NT, 256], BF16)
        Zs = A.alloc([NT, 256], BF16)
        yD = A.alloc([NT, 256], F32)
        msk = A.alloc([5, 128], F32)
        cwx = A.alloc([6, 4], F32)
        prm = A.alloc([20], F32)
        dsk2 = A.alloc([8], F32)
        dta = A.alloc([NT, 16], F32)
        P.dma("sp", Wd_, Wi_b[l][:, O_Z:O_Z + 1032].rearrange("(c p) n -> p c n", p=128), reads=[("Wi_b", l)], writes=["Wd_"])
        P.dma("sp", msk, ssd_masks_d, writes=["msk"])
        for k in range(3):
            P.dma("sp", cwx[:, :, k:k + 1], dram["ssd_conv_w"][l, k].rearrange("(c p o) -> p c o", p=128, o=1), writes=["cwx"],
                  allow_slow_non_contiguous=True)
        P.dma("sp", cwx[:, :, 3:4], dram["ssd_conv_b"][l].rearrange("(c p o) -> p c o", p=128, o=1), writes=["cwx"],
              allow_slow_non_contiguous=True)
        P.dma("sp", prm[:, 0:8], _bc(dram["ssd_dt_bias"][l].rearrange("a b -> (a b)")), writes=["prm0"])
        P.dma("sp", prm[:, 8:16], _bc(dram["ssd_a_log"][l].rearrange("a b -> (a b)")), writes=["prm1"])
        P.dma("sp", dsk2, _bc(dram["ssd_d"][l].rearrange("a b -> (a b)")), writes=["dsk2"])
        P.op("act", lambda e: e.activation(out=prm[:, 8:16], in_=prm[:, 8:16], func=AF.Exp), reads=["prm1"], writes=["prm1"])
        P.op("dve", lambda e: e.tensor_scalar(out=prm[:, 8:16], in0=prm[:, 8:16], scalar1=-1.0, scalar2=None, op0=ALU.mult),
             reads=["prm1"], writes=["prm1"])
        P.op("dve", lambda e: e.tensor_tensor(out=prm[:, 16:20], in0=dsk2[:, 0:4], in1=dsk2[:, 4:8], op=ALU.add),
             reads=["dsk2"], writes=["prm2"])
        mark2 = A.off
        Gs = [A.alloc([S + 2], F32) for _ in range(2)]
        Ts = [A.alloc([S], F32) for _ in range(2)]
        for gi_ in range(2):
            P.op("pool", lambda e: e.memset(Gs[gi_][:, 0:1], 0.0), writes=[("Gl", gi_)])
            P.op("pool", lambda e: e.memset(Gs[gi_][:, S + 1:S + 2], 0.0), writes=[("Gr", gi_)])

        def Sa(ch):
            G, T, gq_ = Gs[ch % 2], Ts[ch % 2], ch % 2
            for nb in range(4):
                it = ch * 4 + nb
                pi, pb_ = (it % 4) // 2, it % 2
                for c in range(8):
                    P.op("pe", lambda e: e.matmul(PS[pi][:, pb_, :], lhsT=Wd_[:, c, 256 + ch * 128:256 + (ch + 1) * 128],
                                                  rhs=HT[:, c, nb * 512:(nb + 1) * 512], start=(c == 0), stop=(c == 7)),
                         reads=["Wd_"] + [("HT", nb * 4 + j) for j in range(4)], writes=[psk(pi, pb_)])
                P.op("act", lambda e: e.activation(out=G[:, 1 + nb * 512:1 + (nb + 1) * 512], in_=PS[pi][:, pb_, :],
                                                   func=AF.Identity, bias=cst[:, 3:4], scale=1.0),
                     reads=[psk(pi, pb_)], writes=[("G", gq_, nb)])
                P.op("act", lambda e: e.activation(out=T[:, nb * 512:(nb + 1) * 512], in_=PS[pi][:, pb_, :],
                                                   func=AF.Identity, bias=cwx[:, ch, 3:4], scale=cwx[:, ch, 1:2]),
                     reads=[psk(pi, pb_), "cwx"], writes=[("T", gq_, nb)])

        def Sb(ch):
            G, T, gq_ = Gs[ch % 2], Ts[ch % 2], ch % 2
            gks = [("G", gq_, nb) for nb in range(4)]
            tks = [("T", gq_, nb) for nb in range(4)]
            P.op("dve", lambda e: e.scalar_tensor_tensor(out=T, in0=G[:, 0:S], scalar=cwx[:, ch, 0:1], in1=T, op0=ALU.mult, op1=ALU.add),
                 reads=gks + tks + [("Gl", gq_), "cwx"], writes=tks)
            P.op("dve", lambda e: e.scalar_tensor_tensor(out=T, in0=G[:, 2:S + 2], scalar=cwx[:, ch, 2:3], in1=T, op0=ALU.mult, op1=ALU.add),
                 reads=gks + tks + [("Gr", gq_), "cwx"], writes=tks)

        def Sc(ch):
            T, gq_ = Ts[ch % 2], ch % 2
            P.op("act", lambda e: e.activation(out=XBCT[:, ch, :], in_=T, func=AF.Silu), reads=[("T", gq_, nb) for nb in range(4)],
                 writes=[("XBCT", ch)])
        run_pipe(list(range(6)), [Sa, Sb, Sc])
        P.barrier()
        A.off = mark2
        P.mark("ssd_D2")
        zt_ = [A.alloc([256], F32) for _ in range(2)]
        for tt in range(NT):
            i2 = tt % 2
            tsl = slice(tt * 128, (tt + 1) * 128)
            pst = PS[2 + i2][:, 0, :].bitcast(BF16).rearrange("p (a b) -> p a b", a=8)
            for ch in range(4):
                P.op("pe", lambda e: e.transpose(out=pst[:, ch, :], in_=XBCT[:, ch, tsl], identity=ident[:]),
                     reads=[("XBCT", ch), "ident"], writes=[psk(2 + i2, 0)])
            P.op("dve", lambda e: e.tensor_copy(out=X[:, tt, :].rearrange("p (a b) -> p a b", a=2), in_=pst[:, 0:2, :]),
                 reads=[psk(2 + i2, 0)], writes=[("X", tt)])
            P.op("dve", lambda e: e.tensor_copy(out=Bt[:, tt, :].rearrange("p (a b) -> p a b", a=2), in_=pst[:, 2:4, :]),
                 reads=[psk(2 + i2, 0)], writes=[("Bt", tt)])
            pz = PS[i2][:, 0, 0:256]
            for c in range(8):
                P.op("pe", lambda e: e.matmul(pz, lhsT=HT[:, c, tsl], rhs=Wd_[:, c, 0:256], start=(c == 0), stop=(c == 7)),
                     reads=[("HT", tt), "Wd_"], writes=[psk(i2, 0)])
            P.op("act", lambda e: e.activation(out=Zs[:, tt, :], in_=pz, func=AF.Silu), reads=[psk(i2, 0)], writes=[("Zs", tt)])
        for tt in range(NT):
            i2 = tt % 2
            tsl = slice(tt * 128, (tt + 1) * 128)
            pdt = PS[i2][:, 1, 0:8]
            for c in range(8):
                P.op("pe", lambda e: e.matmul(pdt, lhsT=HT[:, c, tsl], rhs=Wd_[:, c, 1024:1032], start=(c == 0), stop=(c == 7)),
                     reads=[("HT", tt), "Wd_"], writes=[psk(i2, 1)])
            dk = ("dta", tt)
            P.op("dve", lambda e: e.tensor_tensor(out=dta[:, tt, 0:8], in0=pdt, in1=prm[:, 0:8], op=ALU.add),
                 reads=[psk(i2, 1), "prm0"], writes=[dk])
            P.op("act", lambda e: e.activation(out=dta[:, tt, 0:8], in_=dta[:, tt, 0:8], func=AF.Exp), reads=[dk], writes=[dk])
            P.op("act", lambda e: e.activation(out=dta[:, tt, 0:8], in_=dta[:, tt, 0:8], func=AF.Ln, bias=cst[:, 2:3], scale=1.0),
                 reads=[dk, ("cst", 2)], writes=[dk])
            P.op("dve", lambda e: e.tensor_tensor(out=dta[:, tt, 8:16], in0=dta[:, tt, 0:8], in1=prm[:, 8:16], op=ALU.mult),
                 reads=[dk, "prm1"], writes=[dk])
        P.mark("ssd_D3")
        carry = A.alloc([4, 64], F32)
        prev = A.alloc([4, 64], BF16)
        Am = [A.alloc([4, 128], F32) for _ in range(3)]
        Lx = [A.alloc([4, 128], F32) for _ in range(3)]
        CBm = [A.alloc([2, 128], F32) for _ in range(3)]
        MT = [A.alloc([4, 128], BF16) for _ in range(3)]
        Xdt = [A.alloc([4, 64], BF16) for _ in range(4)]
        Xdd = [A.alloc([4, 64], BF16) for _ in range(4)]
        ex = [A.alloc([3, 4], F32) for _ in range(7)]
        ytmp = [A.alloc([4, 64], F32) for _ in range(2)]
        items = [(d, idx_) for d in range(2) for idx_ in range(NT)]
        inum = {it_: i for i, it_ in enumerate(items)}

        def info(it_):
            d, idx_ = it_
            tt = idx_ if d == 0 else NT - 1 - idx_
            tri = msk[:, 0, :] if d == 0 else msk[:, 1, :]
            mgl = msk[:, 2, :] if d == 0 else msk[:, 3, :]
            return d, idx_, tt, tri, mgl, inum[it_]

        def Q0(it_):
            d, idx_, tt, tri, mgl, i = info(it_)
            i2 = i % 2
            a_ = dta[:, tt, 8 + d * 4:12 + d * 4]
            dk = ("dta", tt)
            P.op("pe", lambda e: e.matmul(PS[0][:, i2, 0:4], lhsT=tri, rhs=a_, start=True, stop=True), reads=["msk", dk], writes=[psk(0, i2)])
            P.op("pe", lambda e: e.matmul(PS[0][:, i2, 4:8], lhsT=msk[:, 4, :], rhs=a_, start=True, stop=True), reads=["msk", dk], writes=[psk(0, i2)])
            P.op("pool", lambda e: e.tensor_tensor(out=Am[i % 3], in0=mgl.unsqueeze(1).broadcast_to([128, 4, 128]),
                                                   in1=a_.unsqueeze(2).broadcast_to([128, 4, 128]), op=ALU.mult),
                 reads=["msk", dk], writes=[("Am", i % 3)])

        def Q1(it_):
            d, idx_, tt, tri, mgl, i = info(it_)
            i2 = i % 2
            tsl = slice(tt * 128, (tt + 1) * 128)
            e_, exk = ex[i % 7], ("ex", i % 7)
            P.op("dve", lambda e: e.tensor_copy(out=e_[:, 1:3, :], in_=PS[0][:, i2, 0:8].rearrange("p (a b) -> p a b", a=2)),
                 reads=[psk(0, i2)], writes=[(exk, 1)])
            P.op("dve", lambda e: e.tensor_tensor(out=e_[:, 0, :], in0=e_[:, 2, :], in1=e_[:, 1, :], op=ALU.subtract),
                 reads=[(exk, 1)], writes=[exk])
            P.op("act", lambda e: e.activation(out=e_, in_=e_, func=AF.Exp), reads=[exk, (exk, 1)], writes=[exk])
            pseg = PS[1][:, i2, :].rearrange("p (h t) -> p h t", h=4)
            for h in range(4):
                P.op("pe", lambda e: e.matmul(pseg[:, h, :], lhsT=Am[i % 3][:, h, :], rhs=tri, start=True, stop=True),
                     reads=[("Am", i % 3), "msk"], writes=[psk(1, i2)])
            pcb = PS[2][:, i2, 0:256].rearrange("p (g t) -> p g t", g=2)
            for g in range(2):
                P.op("pe", lambda e: e.matmul(pcb[:, g, :], lhsT=XBCT[:, 2 + g, tsl], rhs=XBCT[:, 4 + g, tsl], start=True, stop=True),
                     reads=[("XBCT", 2 + g), ("XBCT", 4 + g)], writes=[psk(2, i2)])

        def Q2(it_):
            d, idx_, tt, tri, mgl, i = info(it_)
            i2 = i % 2
            e_, exk = ex[i % 7], ("ex", i % 7)
            dt_ = dta[:, tt, d * 4:d * 4 + 4]
            dk = ("dta", tt)
            pseg = PS[1][:, i2, :].rearrange("p (h t) -> p h t", h=4)
            pcb = PS[2][:, i2, 0:256].rearrange("p (g t) -> p g t", g=2)
            P.op("act", lambda e: e.activation(out=Lx[i % 3], in_=pseg, func=AF.Exp), reads=[psk(1, i2)], writes=[("Lx", i % 3)])
            P.op("dve", lambda e: e.tensor_tensor(out=CBm[i % 3], in0=pcb, in1=tri.unsqueeze(1).broadcast_to([128, 2, 128]), op=ALU.mult),
                 reads=[psk(2, i2), "msk"], writes=[("CBm", i % 3)])
            X4 = X[:, tt, :].rearrange("p (h d) -> p h d", h=4)
            P.op("dve", lambda e: e.tensor_tensor(out=Xdt[i % 4], in0=X4, in1=dt_.unsqueeze(2).broadcast_to([128, 4, 64]), op=ALU.mult),
                 reads=[("X", tt), dk], writes=[("Xdt", i % 4)])
            P.op("dve", lambda e: e.tensor_tensor(out=Xdd[i % 4], in0=Xdt[i % 4], in1=e_[:, 0, :].unsqueeze(2).broadcast_to([128, 4, 64]), op=ALU.mult),
                 reads=[("Xdt", i % 4), exk], writes=[("Xdd", i % 4)])

        def Q3(it_):
            d, idx_, tt, tri, mgl, i = info(it_)
            P.op("pool", lambda e: e.tensor_tensor(out=MT[i % 3].rearrange("p (g r) t -> p g r t", g=2),
                                                   in0=Lx[i % 3].rearrange("p (g r) t -> p g r t", g=2),
                                                   in1=CBm[i % 3].unsqueeze(2).broadcast_to([128, 2, 2, 128]), op=ALU.mult),
                 reads=[("Lx", i % 3), ("CBm", i % 3)], writes=[("MT", i % 3)])

        def Q4(it_):
            d, idx_, tt, tri, mgl, i = info(it_)
            i2 = i % 2
            tsl = slice(tt * 128, (tt + 1) * 128)
            pyd = PS[3][:, i2, 0:256].rearrange("p (h d) -> p h d", h=4)
            pyo = PS[3][:, i2, 256:512].rearrange("p (h d) -> p h d", h=4)
            pst_ = PS[0][:, i2, 256:512].rearrange("p (h d) -> p h d", h=4)
            for h in range(4):
                P.op("pe", lambda e: e.matmul(pyd[:, h, :], lhsT=MT[i % 3][:, h, :], rhs=Xdt[i % 4][:, h, :], start=True, stop=True),
                     reads=[("MT", i % 3), ("Xdt", i % 4)], writes=[psk(3, i2)])
            for h in range(4):
                P.op("pe", lambda e: e.matmul(pst_[:, h, :], lhsT=Bt[:, tt, (h // 2) * 128:(h // 2 + 1) * 128], rhs=Xdd[i % 4][:, h, :],
                                              start=True, stop=True), reads=[("Bt", tt), ("Xdd", i % 4)], writes=[psk(0, i2)])
            if idx_ != 0:
                for h in range(4):
                    P.op("pe", lambda e: e.matmul(pyo[:, h, :], lhsT=XBCT[:, 4 + h // 2, tsl], rhs=prev[:, h, :], start=True, stop=True),
                         reads=[("XBCT", 4 + h // 2), "prev"], writes=[psk(3, i2)])

        def Q5(it_):
            d, idx_, tt, tri, mgl, i = info(it_)
            i2 = i % 2
            e_, exk = ex[i % 7], ("ex", i % 7)
            pyd = PS[3][:, i2, 0:256].rearrange("p (h d) -> p h d", h=4)
            pyo = PS[3][:, i2, 256:512].rearrange("p (h d) -> p h d", h=4)
            pst_ = PS[0][:, i2, 256:512].rearrange("p (h d) -> p h d", h=4)
            first = (idx_ == 0)
            if first:
                P.op("dve", lambda e: e.tensor_copy(out=carry, in_=pst_), reads=[psk(0, i2)], writes=["carry"])
            else:
                P.op("dve", lambda e: e.tensor_tensor(out=carry, in0=carry, in1=e_[:, 2, :].unsqueeze(2).broadcast_to([128, 4, 64]), op=ALU.mult),
                     reads=["carry", exk], writes=["carry"])
                P.op("dve", lambda e: e.tensor_tensor(out=carry, in0=carry, in1=pst_, op=ALU.add), reads=["carry", psk(0, i2)], writes=["carry"])
            P.op("act", lambda e: e.activation(out=prev, in_=carry, func=AF.Identity, bias=cst[:, 3:4], scale=1.0),
                 reads=["carry"], writes=["prev"])
            yk_ = ("yD", tt)
            y4 = yD[:, tt, :].rearrange("p (h d) -> p h d", h=4)
            if d == 0:
                P.op("dve", lambda e: e.tensor_copy(out=y4, in_=pyd), reads=[psk(3, i2)], writes=[yk_])
            else:
                P.op("dve", lambda e: e.tensor_tensor(out=y4, in0=y4, in1=pyd, op=ALU.add), reads=[psk(3, i2), yk_], writes=[yk_])
            if not first:
                tk_ = ("ytmp", i2)
                P.op("dve", lambda e: e.tensor_tensor(out=ytmp[i2], in0=pyo, in1=e_[:, 1, :].unsqueeze(2).broadcast_to([128, 4, 64]), op=ALU.mult),
                     reads=[psk(3, i2), exk], writes=[tk_])
                P.op("pool", lambda e: e.tensor_tensor(out=y4, in0=y4, in1=ytmp[i2], op=ALU.add), reads=[tk_, yk_], writes=[yk_])

        run_pipe(items, [Q0, Q1, Q2, Q3, Q4, Q5])
        P.mark("ssd_D4")
        for tt in range(NT):
            i2 = tt % 2
            yk_ = ("yD", tt)
            y4 = yD[:, tt, :].rearrange("p (h d) -> p h d", h=4)
            X4 = X[:, tt, :].rearrange("p (h d) -> p h d", h=4)
            tk_ = ("ytmp", i2)
            P.op("pool", lambda e: e.tensor_tensor(out=ytmp[i2], in0=X4, in1=prm[:, 16:20].unsqueeze(2).broadcast_to([128, 4, 64]), op=ALU.mult),
                 reads=[("X", tt), "prm2"], writes=[tk_])
            P.op("pool", lambda e: e.tensor_tensor(out=y4, in0=y4, in1=ytmp[i2], op=ALU.add), reads=[tk_, yk_], writes=[yk_])
            P.op("dve", lambda e: e.tensor_tensor(out=yD[:, tt, :], in0=yD[:, tt, :], in1=Zs[:, tt, :], op=ALU.mult), reads=[yk_, ("Zs", tt)], writes=[yk_])
        finish_group(3, YT, gmix, yD, lambda tt: [("yD", tt)])
        P.barrier()
        A.off = mark

    def hyena_prologue(l):
        P.mark("hyena_prologue")
        P.barrier(skip_q=("pool",))
        A.reset()
        featT = A.alloc([S], F32)
        dec = A.alloc([NT, 256], F32)
        w1 = A.alloc([64], F32)
        w2 = A.alloc([64], F32)
        w3 = A.alloc([1024], F32)
        pr = A.alloc([6], F32)
        h1 = A.alloc([S], F32)
        h2 = A.alloc([S], F32)
        tmp = [A.alloc([512], F32) for _ in range(2)]
        tmpf = [A.alloc([512], F32) for _ in range(2)]
        tmpi = [A.alloc([512], F32).bitcast(mybir.dt.int32) for _ in range(2)]
        sd = A.alloc([2, 2, NT, 256], BF16)
        hfd = [A.alloc([2, 256], F32) for _ in range(2)]
        hbd = [A.alloc([2, 256], F32) for _ in range(2)]
        slab = [A.alloc([2, 16, 128], BF16) for _ in range(2)]
        pqs = [A.alloc([512], BF16) for _ in range(2)]
        P.dma("sp", featT[0:33, :], hy_featT_d, writes=["featT"])
        P.dma("sp", dec, hy_decay_d.rearrange("(t p) c -> p t c", p=128), writes=["dec"])
        P.dma("sp", w1[0:33, :], dram["hy_f_w1"][l], writes=["w1"])
        P.dma("sp", w2[0:64, :], dram["hy_f_w2"][l], writes=["w2"])
        P.dma("sp", w3[0:64, :], dram["hy_f_w3"][l], writes=["w3"])
        for i, nm in enumerate(["hy_f_b1", "hy_f_freq", "hy_f_b2"]):
            P.dma("sp", pr[0:64, i:i + 1], dram[nm][l].rearrange("(p o) -> p o", o=1), writes=["pr"], allow_slow_non_contiguous=True)

        def sin_layer(dst, wT, kdim, src, srck, bcol):
            for nb in range(4):
                i2 = nb % 2
                ps = PS[0][0:64, i2, :]
                P.op("pe", lambda e: e.matmul(ps, lhsT=wT[0:kdim, :], rhs=src[0:kdim, nb * 512:(nb + 1) * 512], start=True, stop=True),
                     reads=[srck, "w1", "w2"], writes=[psk(0, i2)])
                t_ = tmp[i2][0:64, :]
                tk = ("tmp", i2)
                P.op("dve", lambda e: e.tensor_scalar(out=t_, in0=ps, scalar1=pr[0:64, bcol:bcol + 1], scalar2=pr[0:64, 1:2],
                                                      op0=ALU.add, op1=ALU.mult), reads=[psk(0, i2), "pr"], writes=[tk])
                ti_ = tmpi[i2][0:64, :]
                tf_ = tmpf[i2][0:64, :]
                P.op("dve", lambda e: e.tensor_scalar(out=t_, in0=t_, scalar1=1.0 / (2 * math.pi), scalar2=None, op0=ALU.mult),
                     reads=[tk], writes=[tk])
                P.op("dve", lambda e: e.tensor_copy(out=ti_, in_=t_), reads=[tk], writes=[(tk, "i")])
                P.op("dve", lambda e: e.tensor_copy(out=tf_, in_=ti_), reads=[(tk, "i")], writes=[(tk, "f")])
                P.op("dve", lambda e: e.tensor_tensor(out=t_, in0=t_, in1=tf_, op=ALU.subtract), reads=[tk, (tk, "f")], writes=[tk])
                P.op("dve", lambda e: e.tensor_scalar(out=tf_, in0=t_, scalar1=0.5, scalar2=None, op0=ALU.is_gt), reads=[tk], writes=[(tk, "f")])
                P.op("dve", lambda e: e.tensor_tensor(out=t_, in0=t_, in1=tf_, op=ALU.subtract), reads=[tk, (tk, "f")], writes=[tk])
                P.op("dve", lambda e: e.tensor_scalar(out=tf_, in0=t_, scalar1=-0.5, scalar2=None, op0=ALU.is_lt), reads=[tk], writes=[(tk, "f")])
                P.op("dve", lambda e: e.tensor_tensor(out=t_, in0=t_, in1=tf_, op=ALU.add), reads=[tk, (tk, "f")], writes=[tk])
                P.op("act", lambda e: e.activation(out=dst[0:64, nb * 512:(nb + 1) * 512], in_=t_, func=AF.Sin,
                                                   bias=cst[0:64, 3:4], scale=2 * math.pi), reads=[tk, ("cst", 3)], writes=[(dst.tensor.name, id(dst))])
            return (dst.tensor.name, id(dst))

        k1 = sin_layer(h1, w1, 33, featT, "featT", 0)
        k2 = sin_layer(h2, w2, 64, h1, k1, 2)
        for tt in range(NT):
            i2 = tt % 2
            for n in range(2):
                P.op("pe", lambda e: e.matmul(PS[1][:, n, :], lhsT=h2[0:64, tt * 128:(tt + 1) * 128], rhs=w3[0:64, n * 512:(n + 1) * 512],
                                              start=True, stop=True), reads=[k2, "w3"], writes=[psk(1, n)])
            ps4 = PS[1][:, :, :].rearrange("p n (d c) -> p n d c", d=2)
            dbc = dec[:, tt, :].unsqueeze(1).broadcast_to([128, 2, 256])
            P.op("dve", lambda e: e.tensor_tensor(out=hfd[i2], in0=ps4[:, :, 0, :], in1=dbc, op=ALU.mult),
                 reads=[psk(1, 0), psk(1, 1), "dec"], writes=[("hfd", i2)])
            P.op("dve", lambda e: e.tensor_tensor(out=hbd[i2], in0=ps4[:, :, 1, :], in1=dbc, op=ALU.mult),
                 reads=[psk(1, 0), psk(1, 1), "dec"], writes=[("hbd", i2)])
            if tt == 0:
                P.op("dve", lambda e: e.memset(hbd[i2][0:1, :, :], 0.0), reads=[("hbd", i2)], writes=[("hbd", i2)])
            P.op("pool", lambda e: e.tensor_tensor(out=sd[:, :, 0, tt, :], in0=hfd[i2], in1=hbd[i2], op=ALU.add),
                 reads=[("hfd", i2), ("hbd", i2)], writes=[("sd", tt)])
            P.op("pool", lambda e: e.tensor_tensor(out=sd[:, :, 1, tt, :], in0=hbd[i2], in1=hfd[i2], op=ALU.subtract),
                 reads=[("hfd", i2), ("hbd", i2)], writes=[("sd", tt)])
        sdk = [("sd", tt) for tt in range(NT)]
        it = 0
        for fc in range(16):
            sb = slab[fc % 2]
            sk_ = ("slab", fc % 2)
            for m in range(2):
                P.dma("sp", sb[:, m], dft_f_d[m, fc], writes=[(sk_, m)])
            for n in range(2):
                i2 = it % 2
                it += 1
                for m in range(2):
                    for tc in range(16):
                        P.op("pe", lambda e: e.matmul(PS[2][:, i2, m * 256:(m + 1) * 256], lhsT=sb[:, m, tc, :], rhs=sd[:, n, m, tc, :],
                                                      start=(tc == 0), stop=(tc == 15)), reads=[(sk_, m)] + sdk, writes=[psk(2, i2)])
                P.op("act", lambda e: e.activation(out=pqs[i2], in_=PS[2][:, i2, :], func=AF.Identity, bias=cst[:, 3:4], scale=1.0),
                     reads=[psk(2, i2)], writes=[("pqs", i2)])
                P.dma("sp", PQ_b[l, n, fc], pqs[i2], reads=[("pqs", i2)], writes=[("PQ_b", l, n, fc)])

    def mix_hyena(l, b, YT, gmix):
        P.mark("mix_hyena")
        mark = A.off
        V0 = A.alloc([NT, 256], BF16)
        X12 = A.alloc([2, NT, 256], BF16)
        cwx = A.alloc([6, 4], F32)
        hbias = A.alloc([2, 256], F32)
        yB = A.alloc([NT, 256], F32)
        for k in range(3):
            P.dma("sp", cwx[:, :, k:k + 1], dram["hy_conv_w"][l, k].rearrange("(c p o) -> p c o", p=128, o=1), writes=["cwx"],
                  allow_slow_non_contiguous=True)
        P.dma("sp", cwx[:, :, 3:4], dram["hy_conv_b"][l].rearrange("(c p o) -> p c o", p=128, o=1), writes=["cwx"],
              allow_slow_non_contiguous=True)
        P.dma("sp", hbias, _bc(dram["hy_bias"][l].rearrange("a b -> (a b)")), writes=["hbias"])
        mark2 = A.off
        Wb = A.alloc([8, 768], BF16)
        P.dma("sp", Wb, Wi_b[l][:, O_HY:O_HY + 768].rearrange("(c p) n -> p c n", p=128), reads=[("Wi_b", l)], writes=["Wb"])
        UCT = A.alloc([6, S], BF16)
        Gs = [A.alloc([S + 2], F32) for _ in range(2)]
        Ts = [A.alloc([S], F32) for _ in range(2)]
        for gi_ in range(2):
            P.op("pool", lambda e: e.memset(Gs[gi_][:, 0:1], 0.0), writes=[("Gl", gi_)])
            P.op("pool", lambda e: e.memset(Gs[gi_][:, S + 1:S + 2], 0.0), writes=[("Gr", gi_)])

        def Sa(ch):
            G, T, gq_ = Gs[ch % 2], Ts[ch % 2], ch % 2
            for nb in range(4):
                it = ch * 4 + nb
                pi, pb_ = (it % 4) // 2, it % 2
                for c in range(8):
                    P.op("pe", lambda e: e.matmul(PS[pi][:, pb_, :], lhsT=Wb[:, c, 0 + ch * 128:0 + (ch + 1) * 128],
                                                  rhs=HT[:, c, nb * 512:(nb + 1) * 512], start=(c == 0), stop=(c == 7)),
                         reads=["Wb"] + [("HT", nb * 4 + j) for j in range(4)], writes=[psk(pi, pb_)])
                P.op("act", lambda e: e.activation(out=G[:, 1 + nb * 512:1 + (nb + 1) * 512], in_=PS[pi][:, pb_, :],
                                                   func=AF.Identity, bias=cst[:, 3:4], scale=1.0),
                     reads=[psk(pi, pb_)], writes=[("G", gq_, nb)])
                P.op("act", lambda e: e.activation(out=T[:, nb * 512:(nb + 1) * 512], in_=PS[pi][:, pb_, :],
                                                   func=AF.Identity, bias=cwx[:, ch, 3:4], scale=cwx[:, ch, 1:2]),
                     reads=[psk(pi, pb_), "cwx"], writes=[("T", gq_, nb)])

        def Sb(ch):
            G, T, gq_ = Gs[ch % 2], Ts[ch % 2], ch % 2
            gks = [("G", gq_, nb) for nb in range(4)]
            tks = [("T", gq_, nb) for nb in range(4)]
            P.op("dve", lambda e: e.scalar_tensor_tensor(out=T, in0=G[:, 0:S], scalar=cwx[:, ch, 0:1], in1=T, op0=ALU.mult, op1=ALU.add),
                 reads=gks + tks + [("Gl", gq_), "cwx"], writes=tks)
            P.op("dve", lambda e: e.scalar_tensor_tensor(out=UCT[:, ch, :], in0=G[:, 2:S + 2], scalar=cwx[:, ch, 2:3], in1=T, op0=ALU.mult, op1=ALU.add),
                 reads=gks + tks + [("Gr", gq_), "cwx"], writes=[("UCT", ch)])
        run_pipe(list(range(6)), [Sa, Sb])
        for tt in range(NT):
            i2 = tt % 2
            tsl = slice(tt * 128, (tt + 1) * 128)
            pst = PS[2 + i2][:, 0, :].bitcast(BF16).rearrange("p (a b) -> p a b", a=8)
            for ch in range(6):
                P.op("pe", lambda e: e.transpose(out=pst[:, ch, :], in_=UCT[:, ch, tsl], identity=ident[:]),
                     reads=[("UCT", ch), "ident"], writes=[psk(2 + i2, 0)])
            P.op("dve", lambda e: e.tensor_copy(out=V0[:, tt, :].rearrange("p (a b) -> p a b", a=2), in_=pst[:, 0:2, :]),
                 reads=[psk(2 + i2, 0)], writes=[("z0", tt)])
            P.op("dve", lambda e: e.tensor_copy(out=X12[:, :, tt, :].rearrange("p n (a b) -> p n a b", a=2),
                                                in_=pst[:, 2:6, :].rearrange("p (n a) b -> p n a b", n=2)),
                 reads=[psk(2 + i2, 0)], writes=[("X12", tt)])
        P.barrier()
        P.mark("hy_conv")
        A.off = mark2
        Z1 = A.alloc([NT, 256], BF16)
        PQ = A.alloc([16, 512], BF16)
        Yc = A.alloc([2, 16, 256], BF16)
        slab = [A.alloc([2, 16, 128], BF16) for _ in range(2)]
        AB = [A.alloc([512], F32) for _ in range(2)]
        tq = [A.alloc([4, 256], F32) for _ in range(2)]
        te = [A.alloc([256], F32) for _ in range(2)]
        sit = 0
        for n in range(2):
            zin = V0 if n == 0 else Z1
            zkf = (lambda tt: ("z0", tt)) if n == 0 else (lambda tt: ("z1", tt))
            P.dma("sp", PQ, PQ_b[l, n].rearrange("fc p x -> p fc x"), reads=[("PQ_b", l, n, fc) for fc in range(16)], writes=["PQ"])
            zks = [zkf(tt) for tt in range(NT)]
            for fc in range(16):
                sb = slab[sit % 2]
                sk_ = ("slab", sit % 2)
                sit += 1
                i2 = fc % 2
                for m in range(2):
                    P.dma("sp", sb[:, m], dft_f_d[m, fc], writes=[(sk_, m)])
                for m in range(2):
                    for tc in range(16):
                        P.op("pe", lambda e: e.matmul(PS[0][:, i2, m * 256:(m + 1) * 256], lhsT=sb[:, m, tc, :], rhs=zin[:, tc, :],
                                                      start=(tc == 0), stop=(tc == 15)), reads=[(sk_, m)] + zks, writes=[psk(0, i2)])
                ab = AB[i2]
                abk = ("AB", i2)
                P.op("act", lambda e: e.activation(out=ab, in_=PS[0][:, i2, :], func=AF.Identity, bias=cst[:, 3:4], scale=1.0),
                     reads=[psk(0, i2)], writes=[abk])
                q_ = tq[i2]
                qk_ = ("tq", i2)
                Aa, Bb = ab[:, 0:256], ab[:, 256:512]
                Pp, Qq = PQ[:, fc, 0:256], PQ[:, fc, 256:512]
                P.op("dve", lambda e: e.tensor_tensor(out=q_[:, 0, :], in0=Aa, in1=Pp, op=ALU.mult), reads=[abk, "PQ"], writes=[(qk_, 0)])
                P.op("pool", lambda e: e.tensor_tensor(out=q_[:, 1, :], in0=Bb, in1=Qq, op=ALU.mult), reads=[abk, "PQ"], writes=[(qk_, 1)])
                P.op("dve", lambda e: e.tensor_tensor(out=q_[:, 2, :], in0=Aa, in1=Qq, op=ALU.mult), reads=[abk, "PQ"], writes=[(qk_, 2)])
                P.op("pool", lambda e: e.tensor_tensor(out=q_[:, 3, :], in0=Bb, in1=Pp, op=ALU.mult), reads=[abk, "PQ"], writes=[(qk_, 3)])
                P.op("dve", lambda e: e.tensor_tensor(out=Yc[:, 0, fc, :], in0=q_[:, 0, :], in1=q_[:, 1, :], op=ALU.add),
                     reads=[(qk_, 0), (qk_, 1)], writes=[("Yc", fc)])
                P.op("pool", lambda e: e.tensor_tensor(out=Yc[:, 1, fc, :], in0=q_[:, 2, :], in1=q_[:, 3, :], op=ALU.subtract),
                     reads=[(qk_, 2), (qk_, 3)], writes=[("Yc", fc)])
            yks = [("Yc", fc) for fc in range(16)]
            for tcl in range(16):
                sb = slab[sit % 2]
                sk_ = ("slab", sit % 2)
                sit += 1
                i2 = tcl % 2
                for m in range(2):
                    P.dma("sp", sb[:, m], dft_i_d[m, tcl], writes=[(sk_, m)])
                py = PS[1][:, i2, 0:256]
                for m in range(2):
                    for fc in range(16):
                        P.op("pe", lambda e: e.matmul(py, lhsT=sb[:, m, fc, :], rhs=Yc[:, m, fc, :],
                                                      start=(m == 0 and fc == 0), stop=(m == 1 and fc == 15)),
                             reads=[(sk_, m)] + yks, writes=[psk(1, i2)])
                t_ = te[i2]
                tk = ("te", i2)
                P.op("pool", lambda e: e.tensor_tensor(out=t_, in0=zin[:, tcl, :], in1=hbias[:, n, :], op=ALU.mult),
                     reads=[zkf(tcl), "hbias"], writes=[tk])
                P.op("dve", lambda e: e.tensor_tensor(out=t_, in0=t_, in1=py, op=ALU.add), reads=[tk, psk(1, i2)], writes=[tk])
                if n == 0:
                    P.op("pool", lambda e: e.tensor_tensor(out=Z1[:, tcl, :], in0=t_, in1=X12[:, 0, tcl, :], op=ALU.mult),
                         reads=[tk, ("X12", tcl)], writes=[("z1", tcl)])
                else:
                    P.op("pool", lambda e: e.tensor_tensor(out=yB[:, tcl, :], in0=t_, in1=X12[:, 1, tcl, :], op=ALU.mult),
                         reads=[tk, ("X12", tcl)], writes=[("yB", tcl)])
        finish_group(1, YT, gmix, yB, lambda tt: [("yB", tt)])
        P.barrier()
        A.off = mark

    def stage_mix(l, b):
        P.barrier()
        A.reset()
        YT = A.alloc([8, S], BF16)
        gmix = A.alloc([D], F32)
        P.dma("sp", gmix, _bc(dram["mix_norm_g"][l]), writes=["gmix"])
        fns = {"a": mix_mla, "b": mix_hyena, "c": mix_swa, "d": mix_ssd}
        for g, nm in enumerate("abcd"):
            if nm in groups and nm in fns:
                fns[nm](l, b, YT, gmix)
            else:
                y_from_dbg(l, b, g, YT, gmix)
        return YT

    stop = dbg.get("stop")
    if "b" in groups:
        for l in range(nlayer):
            hyena_prologue(l)

    def dump_and_stop():
        P.barrier()
        P.dma("sp", out_d[0], Hf, writes=["outdump"])
        P.barrier()
        P.emit()
        return nc

    for b in range(nseq):
        stage_embed(b)
        if stop == "embed":
            return dump_and_stop()
        for l in range(nlayer):
            YT = stage_mix(l, b)
            stage_outproj(l, YT)
            if stop == "outproj":
                return dump_and_stop()
            stage_ffn(l)
            if stop == "ffn":
                return dump_and_stop()
            stage_ple(l, b, last=(l == nlayer - 1))
            if stop == "ple":
                return dump_and_stop()
    P.barrier()
    P.mark("end")
    P.emit()
    nc._marks = P.marks
    return nc


def host_consts():
    c = {}
    inv = 10000.0 ** (-np.arange(0, 32, 2, dtype=np.float32) / 32.0)
    ang = np.arange(S, dtype=np.float32)[:, None] * inv[None, :].astype(np.float32)
    c["rope_cs"] = np.concatenate([np.cos(ang), np.sin(ang)], axis=1).astype(np.float32)
    j = np.arange(128)[:, None, None, None]
    r = np.arange(3)[None, :, None, None]
    q = np.arange(128)[None, None, None, :]
    dist = np.abs(q - j - (r - 1) * 128).astype(np.float32)
    slopes = ((2.0 ** (-8.0 / 4)) ** np.arange(1, 5, dtype=np.float32))[None, None, :, None]
    c["swa_eb"] = np.where(dist <= 128, np.exp(-slopes * dist), 0.0).astype(np.float32)
    u = np.arange(128)[:, None]
    t = np.arange(128)[None, :]
    c["ssd_masks"] = np.stack([(u <= t), (u >= t), (u > t), (u < t), np.ones((128, 128), bool)], axis=1).astype(np.float32)
    th = 2.0 * np.pi / 4096.0
    f = np.arange(S, dtype=np.float64)[:, None] + 0.5
    t = np.arange(S, dtype=np.float64)[None, :]
    ang = th * f * t
    mats = [np.cos(ang), np.sin(ang)]
    dff = np.empty((2, 16, 128, 16, 128), dtype=ml_dtypes.bfloat16)
    dfi = np.empty((2, 16, 128, 16, 128), dtype=ml_dtypes.bfloat16)
    for m in range(2):
        M = mats[m].reshape(16, 128, 16, 128)
        dff[m] = M.transpose(0, 3, 2, 1).astype(np.float32)
        sgn = 1.0 if m == 0 else -1.0
        dfi[m] = (sgn / 2048.0 * M).transpose(2, 1, 0, 3).astype(np.float32)
    c["dft_f"] = dff
    c["dft_i"] = dfi
    tl = np.linspace(0.0, 1.0, S, dtype=np.float32)[:, None]
    ang2 = (2.0 * math.pi * np.arange(S, dtype=np.float32)[:, None] / S).astype(np.float32)
    bands = np.linspace(1e-4, 15.0, 16, dtype=np.float32)[None, :]
    feat = np.concatenate([tl, np.cos(bands * ang2), -np.sin(bands * ang2)], -1).astype(np.float32)
    c["hy_featT"] = np.ascontiguousarray(feat.T)
    max_decay = math.log(1e-2) / 0.3
    min_decay = math.log(1e-2) / 1.5
    deltas = np.linspace(min_decay, max_decay, 256, dtype=np.float32)
    c["hy_decay"] = np.exp(-tl * np.abs(deltas)[None, :]).astype(np.float32)
    return c


def kernel(**inputs):
    ncores = 8
    nseq = 32 // ncores
    nc = build(nseq=nseq, nlayer=2)
    in_maps = []
    consts = host_consts()
    for c in range(ncores):
        m = {"x": np.ascontiguousarray(inputs["x"][c * nseq:(c + 1) * nseq]),
             "p": np.ascontiguousarray(inputs["p"][:, c * nseq:(c + 1) * nseq])}
        for n in WNAMES:
            m[n] = np.ascontiguousarray(inputs[n])
        m.update(consts)
        in_maps.append(m)
    res = run_bass_kernel_spmd(nc, in_maps, core_ids=list(range(ncores)))
    return np.concatenate([r["out"] for r in res.results], axis=0)
```

```python
import contextlib
import math
import numpy as np
import ml_dtypes
import concourse.bass as bass
import concourse.mybir as mybir
from concourse.bass_utils import run_bass_kernel_spmd

F32 = mybir.dt.float32
BF16 = mybir.dt.bfloat16
AF = mybir.ActivationFunctionType
ALU = mybir.AluOpType
AX = mybir.AxisListType

S = 2048
D = 1024
NT = S // 128
DFF = 2816
NFC = DFF // 128
INW = 2728
ALPHA = (2.0 * 2) ** 0.25
LN_EPS = 1e-5
RMS_EPS = 1e-6
O_CQ, O_CKV, O_KR, O_HY, O_SQ, O_SK, O_SV, O_Z, O_XBC, O_DT = 0, 256, 384, 416, 1184, 1440, 1568, 1696, 1952, 2720

COMPUTE = ("pe", "act", "dve", "pool")
NDMASEM = 12


class _Cap:
    def __getattr__(self, name):
        def f(*a, **k):
            self.rec = (name, a, k)
            return self
        return f


class Prog:
    def __init__(self, nc):
        self.nc = nc
        self.ops = {e: [] for e in ("pe", "act", "dve", "pool", "sp")}
        self.cnt = {e: 0 for e in COMPUTE}
        self.dq_eng = {"sp": "sp", "act": "act", "pool": "pool"}
        self.dq_n = {q: 0 for q in self.dq_eng}
        self.dq_semcnt = {q: [0] * NDMASEM for q in self.dq_eng}
        self.last_w = {}
        self.readers = {}
        self.seen = {e: {} for e in self.ops}
        self.marks = []
        self.more_w = {}

    def mark(self, name):
        self.marks.append((name, dict(self.cnt)))

    def _deps(self, eng, reads, writes):
        deps = {}

        def add(tok):
            if tok is None:
                return
            s, v = tok
            if deps.get(s, 0) < v:
                deps[s] = v

        for r in reads:
            add(self.last_w.get(r))
            for t in self.more_w.get(r, ()):
                add(t)
            if isinstance(r, tuple) and r and r[0] == "ps":
                for t in self.readers.get(r, ()):
                    if t[0] != eng:
                        add(t)
        for w in writes:
            add(self.last_w.get(w))
            for t in self.more_w.get(w, ()):
                add(t)
            for t in self.readers.get(w, ()):
                add(t)
        out = []
        for s, v in deps.items():
            if s == "pe" and eng == "pe":
                continue
            if self.seen[eng].get(s, 0) >= v:
                continue
            self.seen[eng][s] = v
            out.append((s, v))
        return out

    def _commit(self, tok, reads, writes):
        for r in reads:
            self.readers.setdefault(r, []).append(tok)
        for w in writes:
            prev = self.last_w.get(w)
            if (prev is not None and tok[0].startswith("d_") and prev[0].startswith("d_") and not self.readers.get(w)):
                self.more_w.setdefault(w, []).append(prev)
            else:
                self.more_w.pop(w, None)
            self.last_w[w] = tok
            self.readers[w] = []

    def op(self, eng, fn, reads=(), writes=()):
        cap = _Cap()
        fn(cap)
        name, a, k = cap.rec
        fn = lambda e, name=name, a=a, k=k: getattr(e, name)(*a, **k)
        waits = self._deps(eng, reads, writes)
        self.cnt[eng] += 1
        tok = (eng, self.cnt[eng])
        self.ops[eng].append((fn, waits, (eng, 1)))
        self._commit(tok, reads, writes)

    def dma(self, q, out, in_, reads=(), writes=(), **kw):
        eng = self.dq_eng[q]
        n = self.dq_n[q]
        self.dq_n[q] += 1
        si = n % NDMASEM
        sname = f"d_{q}_{si}"
        waits = self._deps(eng, reads, writes)
        prev = self.dq_semcnt[q][si]
        if prev > 0 and self.seen[eng].get(sname, 0) < prev:
            self.seen[eng][sname] = prev
            waits.append((sname, prev))
        self.dq_semcnt[q][si] += 16
        tok = (sname, self.dq_semcnt[q][si])

        def fn(e, out=out, in_=in_, kw=kw):
            return e.dma_start(out=out, in_=in_, **kw)

        self.ops[eng].append((fn, waits, (sname, 16)))
        self._commit(tok, reads, writes)
        return tok

    def barrier(self, skip_q=()):
        toks = [(e, self.cnt[e]) for e in COMPUTE if self.cnt[e] > 0]
        for q in self.dq_eng:
            if q in skip_q:
                continue
            for i in range(NDMASEM):
                if self.dq_semcnt[q][i] > 0:
                    toks.append((f"d_{q}_{i}", self.dq_semcnt[q][i]))
        for eng in self.ops:
            waits = []
            for s, v in toks:
                if s == eng and eng == "pe":
                    continue
                if self.seen[eng].get(s, 0) < v:
                    self.seen[eng][s] = v
                    waits.append((s, v))
            if waits:
                self.ops[eng].append((None, waits, None))
        pref = tuple(f"d_{q}_" for q in skip_q)
        self.last_w = {k: v for k, v in self.last_w.items() if pref and v[0].startswith(pref)}
        self.more_w = {k: v for k, v in self.more_w.items() if k in self.last_w}
        self.readers = {}

    def emit(self):
        nc = self.nc
        semnames = list(COMPUTE) + [f"d_{q}_{i}" for q in self.dq_eng for i in range(NDMASEM)]
        sems = {}
        with contextlib.ExitStack() as st:
            for s in semnames:
                sems[s] = st.enter_context(nc.semaphore(s))
            block = st.enter_context(nc.Block())

            def run(engname):
                def body(e):
                    for fn, waits, inc in self.ops[engname]:
                        for s, v in waits:
                            e.wait_ge(sems[s], v)
                        if fn is not None:
                            fn(e).then_inc(sems[inc[0]], inc[1])
                return body

            block.tensor(run("pe"))
            block.scalar(run("act"))
            block.vector(run("dve"))
            block.gpsimd(run("pool"))
            block.sync(run("sp"))


class Arena:
    def __init__(self, tensor, nbytes):
        self.t = tensor
        self.nbytes = nbytes
        self.off = 0

    def reset(self):
        self.off = 0

    def alloc(self, shape, dtype, parts=128):
        esz = 4 if dtype == F32 else 2
        n = int(np.prod(shape))
        nb = n * esz
        self.off = (self.off + 31) // 32 * 32
        assert self.off + nb <= self.nbytes, f"arena overflow {self.off + nb} > {self.nbytes}"
        a = self.t[0:parts, self.off // 2:(self.off + nb) // 2]
        self.off += nb
        if dtype == F32:
            a = a.bitcast(F32)
        if len(shape) == 2:
            a = a.rearrange("p (a b) -> p a b", a=shape[0])
        elif len(shape) == 3:
            a = a.rearrange("p (a b c) -> p a b c", a=shape[0], b=shape[1])
        elif len(shape) == 4:
            a = a.rearrange("p (a b c d) -> p a b c d", a=shape[0], b=shape[1], c=shape[2])
        return a


def _bc(ap1d, parts=128):
    return ap1d.partition_broadcast(parts)


WNAMES = ["emb_ln_g", "emb_ln_b", "w_in", "mla_q_norm", "mla_kv_norm", "mla_w_uq", "mla_w_ukv",
          "hy_conv_w", "hy_conv_b", "hy_f_w1", "hy_f_b1", "hy_f_freq", "hy_f_w2", "hy_f_b2", "hy_f_w3", "hy_bias",
          "swa_sink", "ssd_conv_w", "ssd_conv_b", "ssd_dt_bias", "ssd_a_log", "ssd_d", "mix_norm_g", "w_out",
          "ln1_g", "ln1_b", "ffn_w_gate", "ffn_w_up", "ffn_conv_w", "ffn_conv_b", "ffn_w_down",
          "ln2_g", "ln2_b", "ple_w_proj", "ple_w_gate", "ple_b_gate", "ln3_g", "ln3_b"]

WSHAPES = {
    "emb_ln_g": [D], "emb_ln_b": [D], "w_in": [2, D, INW], "mla_q_norm": [2, 256], "mla_kv_norm": [2, 128],
    "mla_w_uq": [2, 256, 384], "mla_w_ukv": [2, 128, 512], "hy_conv_w": [2, 3, 768], "hy_conv_b": [2, 768],
    "hy_f_w1": [2, 33, 64], "hy_f_b1": [2, 64], "hy_f_freq": [2, 64], "hy_f_w2": [2, 64, 64], "hy_f_b2": [2, 64],
    "hy_f_w3": [2, 64, 1024], "hy_bias": [2, 2, 256], "swa_sink": [2, 4], "ssd_conv_w": [2, 3, 768],
    "ssd_conv_b": [2, 768], "ssd_dt_bias": [2, 2, 4], "ssd_a_log": [2, 2, 4], "ssd_d": [2, 2, 4],
    "mix_norm_g": [2, D], "w_out": [2, D, D], "ln1_g": [2, D], "ln1_b": [2, D], "ffn_w_gate": [2, D, DFF],
    "ffn_w_up": [2, D, DFF], "ffn_conv_w": [2, 3, DFF], "ffn_conv_b": [2, DFF], "ffn_w_down": [2, DFF, D],
    "ln2_g": [2, D], "ln2_b": [2, D], "ple_w_proj": [2, 256, D], "ple_w_gate": [2, D, D], "ple_b_gate": [2, D],
    "ln3_g": [2, D], "ln3_b": [2, D],
}


def build(nseq=4, nlayer=2, dbg=None):
    dbg = dbg or {}
    nc = bass.Bass("TRN2", target_bir_lowering=False)
    P = Prog(nc)
    dram = {}
    x_d = nc.dram_tensor("x", [nseq, S, D], F32, kind="ExternalInput").ap()
    p_d = nc.dram_tensor("p", [2, nseq, S, 256], F32, kind="ExternalInput").ap()
    for n in WNAMES:
        dram[n] = nc.dram_tensor(n, WSHAPES[n], F32, kind="ExternalInput").ap()
    out_d = nc.dram_tensor("out", [nseq, S, D], F32, kind="ExternalOutput").ap()
    ydbg_d = None
    if dbg.get("ydbg"):
        ydbg_d = nc.dram_tensor("ydbg", [nlayer, nseq, S, D], F32, kind="ExternalInput").ap()
    rope_d = nc.dram_tensor("rope_cs", [S, 32], F32, kind="ExternalInput").ap()
    swa_eb_d = nc.dram_tensor("swa_eb", [128, 3, 4, 128], F32, kind="ExternalInput").ap()
    ssd_masks_d = nc.dram_tensor("ssd_masks", [128, 5, 128], F32, kind="ExternalInput").ap()
    dft_f_d = nc.dram_tensor("dft_f", [2, 16, 128, 16, 128], BF16, kind="ExternalInput").ap()
    dft_i_d = nc.dram_tensor("dft_i", [2, 16, 128, 16, 128], BF16, kind="ExternalInput").ap()
    hy_featT_d = nc.dram_tensor("hy_featT", [33, S], F32, kind="ExternalInput").ap()
    hy_decay_d = nc.dram_tensor("hy_decay", [S, 256], F32, kind="ExternalInput").ap()
    PQ_b = nc.dram_tensor("PQ_b", [2, 2, 16, 128, 512], BF16, kind="Internal").ap()
    ydump_d = None
    if dbg.get("ydump"):
        ydump_d = nc.dram_tensor("ydump", [S, D], F32, kind="ExternalOutput").ap()
    groups = dbg.get("groups", "abcd")

    Hf = nc.dram_tensor("Hf", [S, D], F32, kind="Internal").ap()
    Wi_b = nc.dram_tensor("Wi_b", [2, D, INW], BF16, kind="Internal").ap()
    Wo_b = nc.dram_tensor("Wo_b", [2, D, D], BF16, kind="Internal").ap()
    Wg_b = nc.dram_tensor("Wg_b", [2, NFC, 128, 8, 128], BF16, kind="Internal").ap()
    Wu_b = nc.dram_tensor("Wu_b", [2, NFC, 128, 8, 128], BF16, kind="Internal").ap()
    Wd_b = nc.dram_tensor("Wd_b", [2, DFF, D], BF16, kind="Internal").ap()
    Wpp_b = nc.dram_tensor("Wpp_b", [2, 256, D], BF16, kind="Internal").ap()
    Wpg_b = nc.dram_tensor("Wpg_b", [2, D, D], BF16, kind="Internal").ap()
    Wuq_b = nc.dram_tensor("Wuq_b", [2, 256, 384], BF16, kind="Internal").ap()
    Wukv_b = nc.dram_tensor("Wukv_b", [2, 128, 512], BF16, kind="Internal").ap()

    HT = nc.alloc_sbuf_tensor("HT", [128, 8, S], BF16)
    ident = nc.alloc_sbuf_tensor("ident", [128, 128], BF16)
    ident_f = nc.alloc_sbuf_tensor("ident_f", [128, 128], F32)
    cst = nc.alloc_sbuf_tensor("cst", [128, 8], F32)
    lng = nc.alloc_sbuf_tensor("lng", [128, D], F32)
    lnb = nc.alloc_sbuf_tensor("lnb", [128, D], F32)
    ARENA_BYTES = dbg.get("arena", 160 * 1024)
    arena_t = nc.alloc_sbuf_tensor("arena", [128, ARENA_BYTES // 2], BF16)
    A = Arena(arena_t, ARENA_BYTES)
    PS = [nc.alloc_psum_tensor(f"ps{i}", [128, 2, 512], F32) for i in range(4)]

    def psk(i, j):
        return ("ps", i, j)

    P.op("pool", lambda e: e.memset(ident[:], 1.0), writes=["ident"])
    P.op("pool", lambda e: e.affine_select(out=ident[:], in_=ident[:], pattern=[[-1, 128]],
                                           compare_op=ALU.is_equal, fill=0.0, base=0, channel_multiplier=1),
         reads=["ident"], writes=["ident"])
    P.op("pool", lambda e: e.memset(ident_f[:], 1.0), writes=["ident_f"])
    P.op("pool", lambda e: e.affine_select(out=ident_f[:], in_=ident_f[:], pattern=[[-1, 128]],
                                           compare_op=ALU.is_equal, fill=0.0, base=0, channel_multiplier=1),
         reads=["ident_f"], writes=["ident_f"])
    for i, v in enumerate([LN_EPS, RMS_EPS, 1.0, 0.0, -math.pi]):
        P.op("pool", lambda e, i=i, v=v: e.memset(cst[:, i:i + 1], v), writes=[("cst", i)])

    def cast_rows(dst, src, nrows, key):
        for r0 in range(0, nrows, 256):
            r1 = min(nrows, r0 + 256)
            P.dma("pool", dst[r0:r1], src[r0:r1], writes=[key])

    for l in range(nlayer):
        cast_rows(Wi_b[l], dram["w_in"][l], D, ("Wi_b", l))
        cast_rows(Wo_b[l], dram["w_out"][l], D, ("Wo_b", l))
        cast_rows(Wd_b[l], dram["ffn_w_down"][l], DFF, ("Wd_b", l))
        cast_rows(Wpp_b[l], dram["ple_w_proj"][l], 256, ("Wpp_b", l))
        cast_rows(Wpg_b[l], dram["ple_w_gate"][l], D, ("Wpg_b", l))
        cast_rows(Wuq_b[l], dram["mla_w_uq"][l], 256, ("Wuq_b", l))
        cast_rows(Wukv_b[l], dram["mla_w_ukv"][l], 128, ("Wukv_b", l))
        for ch in range(NFC):
            for (dst, src, nm) in ((Wg_b, dram["ffn_w_gate"], "Wg_b"), (Wu_b, dram["ffn_w_up"], "Wu_b")):
                P.dma("pool", dst[l, ch], src[l][:, ch * 128:(ch + 1) * 128].rearrange("(c p) n -> p c n", p=128),
                      writes=[(nm, l)])

    def load_ln_params(gap, bap):
        P.dma("sp", lng[:], _bc(gap), writes=["lng"])
        P.dma("sp", lnb[:], _bc(bap), writes=["lnb"])

    def ln_tile(z, tt, st, zk, dst_dram, tagk):
        stk = ("st", zk)
        for hseg in range(2):
            P.op("dve", lambda e, hseg=hseg: e.bn_stats(out=st[:, hseg * 6:(hseg + 1) * 6],
                                                         in_=z[:, hseg * 512:(hseg + 1) * 512]),
                 reads=[zk], writes=[(stk, hseg)])
        P.op("dve", lambda e: e.bn_aggr(out=st[:, 12:14], in_=st[:, 0:12]),
             reads=[(stk, 0), (stk, 1)], writes=[(stk, 2)])
        P.op("act", lambda e: e.activation(out=st[:, 14:15], in_=st[:, 13:14], func=AF.Sqrt,
                                           bias=cst[:, 0:1], scale=1.0),
             reads=[(stk, 2), ("cst", 0)], writes=[(stk, 3)])
        P.op("dve", lambda e: e.reciprocal(out=st[:, 14:15], in_=st[:, 14:15]), reads=[(stk, 3)], writes=[(stk, 3)])
        P.op("dve", lambda e: e.scalar_tensor_tensor(out=st[:, 15:16], in0=st[:, 12:13], scalar=-1.0,
                                                     in1=st[:, 14:15], op0=ALU.mult, op1=ALU.mult),
             reads=[(stk, 2), (stk, 3)], writes=[(stk, 4)])
        P.op("act", lambda e: e.activation(out=z, in_=z, func=AF.Identity, bias=st[:, 15:16], scale=st[:, 14:15]),
             reads=[zk, (stk, 3), (stk, 4)], writes=[zk])
        P.op("pool", lambda e: e.tensor_tensor(out=z, in0=z, in1=lng[:], op=ALU.mult), reads=[zk, "lng"], writes=[zk])
        P.op("pool", lambda e: e.tensor_tensor(out=z, in0=z, in1=lnb[:], op=ALU.add), reads=[zk, "lnb"], writes=[zk])
        P.dma("sp", dst_dram, z, reads=[zk], writes=[tagk])
        return

    def to_HT(z, zk, tt, hb, hbk, psi):
        P.op("act", lambda e: e.activation(out=hb, in_=z, func=AF.Identity, bias=cst[:, 3:4], scale=1.0), reads=[zk], writes=[hbk])
        pst = PS[psi[0]][:, psi[1], :].bitcast(BF16).rearrange("p (a b) -> p a b", a=8)
        for c in range(8):
            P.op("pe", lambda e, c=c: e.transpose(out=pst[:, c, :], in_=hb[:, c * 128:(c + 1) * 128], identity=ident[:]),
                 reads=[hbk, "ident"], writes=[psk(*psi)])
        P.op("dve", lambda e: e.tensor_copy(out=HT[:, :, tt * 128:(tt + 1) * 128], in_=pst),
             reads=[psk(*psi)], writes=[("HT", tt)])

    def run_pipe(tiles, stages):
        n, K = len(tiles), len(stages)
        for step in range(n + K - 1):
            for k in range(K - 1, -1, -1):
                i = step - k
                if 0 <= i < n and stages[k] is not None:
                    stages[k](tiles[i])

    def ln_stages(zt, stt, hbs, dst_fn, do_ht=True, psi_fn=lambda tt: (2 + tt % 2, 0)):
        NBz, NBh = len(zt), len(hbs)

        def zk_(tt):
            return ("z", tt % NBz)

        def L1(tt):
            z, st, zk = zt[tt % NBz], stt[tt % NBz], zk_(tt)
            stk = ("st", zk)
            for hseg in range(2):
                P.op("dve", lambda e: e.bn_stats(out=st[:, hseg * 6:(hseg + 1) * 6], in_=z[:, hseg * 512:(hseg + 1) * 512]),
                     reads=[zk], writes=[(stk, hseg)])
            P.op("dve", lambda e: e.bn_aggr(out=st[:, 12:14], in_=st[:, 0:12]), reads=[(stk, 0), (stk, 1)], writes=[(stk, 2)])

        def L234(tt):
            z, st, zk = zt[tt % NBz], stt[tt % NBz], zk_(tt)
            stk = ("st", zk)
            P.op("act", lambda e: e.activation(out=st[:, 14:15], in_=st[:, 13:14], func=AF.Ln, bias=cst[:, 0:1], scale=1.0),
                 reads=[(stk, 2), ("cst", 0)], writes=[(stk, 3)])
            P.op("act", lambda e: e.activation(out=st[:, 14:15], in_=st[:, 14:15], func=AF.Exp, scale=-0.5),
                 reads=[(stk, 3)], writes=[(stk, 3)])
            P.op("dve", lambda e: e.scalar_tensor_tensor(out=st[:, 15:16], in0=st[:, 12:13], scalar=-1.0, in1=st[:, 14:15],
                                                         op0=ALU.mult, op1=ALU.mult), reads=[(stk, 2), (stk, 3)], writes=[(stk, 4)])
            P.op("act", lambda e: e.activation(out=z, in_=z, func=AF.Identity, bias=st[:, 15:16], scale=st[:, 14:15]),
                 reads=[zk, (stk, 3), (stk, 4)], writes=[zk])

        def L5(tt):
            z, zk = zt[tt % NBz], zk_(tt)
            P.op("dve", lambda e: e.tensor_tensor(out=z, in0=z, in1=lng[:], op=ALU.mult), reads=[zk, "lng"], writes=[zk])
            P.op("pool", lambda e: e.tensor_tensor(out=z, in0=z, in1=lnb[:], op=ALU.add), reads=[zk, "lnb"], writes=[zk])

        def L6(tt):
            z, zk = zt[tt % NBz], zk_(tt)
            dst, dk = dst_fn(tt)
            P.dma("sp", dst, z, reads=[zk], writes=[dk])
            if do_ht:
                hb, hbk = hbs[tt % NBh], ("hb", tt % NBh)
                P.op("act", lambda e: e.activation(out=hb, in_=z, func=AF.Identity, bias=cst[:, 3:4], scale=1.0), reads=[zk], writes=[hbk])

        def L7(tt):
            hb, hbk = hbs[tt % NBh], ("hb", tt % NBh)
            psi = psi_fn(tt)
            pst = PS[psi[0]][:, psi[1], :].bitcast(BF16).rearrange("p (a b) -> p a b", a=8)
            for c in range(8):
                P.op("pe", lambda e: e.transpose(out=pst[:, c, :], in_=hb[:, c * 128:(c + 1) * 128], identity=ident[:]),
                     reads=[hbk, "ident"], writes=[psk(*psi)])

        def L8(tt):
            psi = psi_fn(tt)
            pst = PS[psi[0]][:, psi[1], :].bitcast(BF16).rearrange("p (a b) -> p a b", a=8)
            P.op("act", lambda e: e.activation(out=HT[:, :, tt * 128:(tt + 1) * 128], in_=pst, func=AF.Identity, bias=cst[:, 3:4], scale=1.0),
                 reads=[psk(*psi)], writes=[("HT", tt)])

        if do_ht:
            return [L1, L234, L5, L6, L7, L8]
        return [L1, L234, L5, L6]

    def stage_embed(b):
        P.mark("stage_embed")
        P.barrier(skip_q=("pool",) if b == 0 else ())
        A.reset()
        zt = [A.alloc([D], F32) for _ in range(7)]
        hb = [A.alloc([D], BF16) for _ in range(3)]
        stt = [A.alloc([16], F32) for _ in range(7)]
        load_ln_params(dram["emb_ln_g"], dram["emb_ln_b"])

        def F0(tt):
            P.dma("sp", zt[tt % 7], x_d[b, tt * 128:(tt + 1) * 128, :], writes=[("z", tt % 7)])
        run_pipe(list(range(NT)), [F0, None] + ln_stages(zt, stt, hb, lambda tt: (Hf[tt * 128:(tt + 1) * 128, :], ("Hf", tt))))

    def emit_y(YT, gmix, g, tt, y, yk, scr, scrk, ybf, ybk, psi):
        sq = scr[:, 0:256]
        st = scr[:, 256:260]
        yks = yk if isinstance(yk, list) else [yk]
        P.op("act", lambda e: e.activation(out=sq, in_=y, func=AF.Square), reads=yks, writes=[scrk])
        P.op("dve", lambda e: e.reduce_sum(out=st[:, 0:1], in_=sq, axis=AX.X), reads=[scrk], writes=[scrk])
        P.op("act", lambda e: e.activation(out=st[:, 1:2], in_=st[:, 0:1], func=AF.Sqrt, bias=cst[:, 1:2],
                                           scale=1.0 / 256.0), reads=[scrk, ("cst", 1)], writes=[scrk])
        P.op("dve", lambda e: e.reciprocal(out=st[:, 1:2], in_=st[:, 1:2]), reads=[scrk], writes=[scrk])
        P.op("dve", lambda e: e.scalar_tensor_tensor(out=ybf, in0=y, scalar=st[:, 1:2],
                                                     in1=gmix[:, g * 256:(g + 1) * 256], op0=ALU.mult, op1=ALU.mult),
             reads=yks + [scrk, "gmix"], writes=[ybk])
        pst = PS[psi[0]][:, psi[1], :].bitcast(BF16).rearrange("p (a b) -> p a b", a=8)
        for c in range(2):
            P.op("pe", lambda e, c=c: e.transpose(out=pst[:, c, :], in_=ybf[:, c * 128:(c + 1) * 128], identity=ident[:]),
                 reads=[ybk, "ident"], writes=[psk(*psi)])
        P.op("dve", lambda e: e.tensor_copy(out=YT[:, 2 * g:2 * g + 2, tt * 128:(tt + 1) * 128], in_=pst[:, 0:2, :]),
             reads=[psk(*psi)], writes=[("YT", g, tt)])

    def stage_outproj(l, YT):
        P.mark("stage_outproj")
        Wo = A.alloc([8, D], BF16)
        for c in range(8):
            P.dma("sp", Wo[:, c, :], Wo_b[l, c * 128:(c + 1) * 128, :], reads=[("Wo_b", l)], writes=[("Wo", c)])
        zt = [A.alloc([D], F32) for _ in range(7)]
        hb = [A.alloc([D], BF16) for _ in range(3)]
        stt = [A.alloc([16], F32) for _ in range(7)]
        load_ln_params(dram["ln1_g"][l], dram["ln1_b"][l])

        def F0(tt):
            P.dma("sp", zt[tt % 7], Hf[tt * 128:(tt + 1) * 128, :], reads=[("Hf", tt)], writes=[("z", tt % 7)])

        def F2(tt):
            pi = tt % 2
            for half in range(2):
                for c in range(8):
                    P.op("pe", lambda e: e.matmul(PS[pi][:, half, :], lhsT=YT[:, c, tt * 128:(tt + 1) * 128],
                                                  rhs=Wo[:, c, half * 512:(half + 1) * 512], start=(c == 0), stop=(c == 7)),
                         reads=[("YT", c // 2, tt), ("Wo", c)], writes=[psk(pi, half)])

        lns = ln_stages(zt, stt, hb, lambda tt: (Hf[tt * 128:(tt + 1) * 128, :], ("Hf", tt)))

        def F3(tt):
            pi = tt % 2
            z, zk = zt[tt % 7], ("z", tt % 7)
            P.op("dve", lambda e: e.scalar_tensor_tensor(out=z, in0=z, scalar=ALPHA, in1=PS[pi][:, :, :].rearrange("p a b -> p (a b)"),
                                                         op0=ALU.mult, op1=ALU.add), reads=[zk, psk(pi, 0), psk(pi, 1)], writes=[zk])
            lns[0](tt)
        run_pipe(list(range(NT)), [F0, F2, F3] + lns[1:])

    def stage_ffn(l):
        P.mark("stage_ffn")
        P.barrier()
        A.reset()
        HW = S // 2
        actT = A.alloc([NFC, HW], BF16)
        Wd = A.alloc([NFC, D], BF16)
        wgu = [A.alloc([2, 8, 128], BF16) for _ in range(3)]
        G = [A.alloc([HW + 2], F32) for _ in range(2)]
        T1 = [A.alloc([HW], F32) for _ in range(2)]
        halo_h = A.alloc([8, 2], BF16)
        cw = A.alloc([NFC, 4], F32)
        NBZ = 7
        zt = [A.alloc([D], F32) for _ in range(NBZ)]
        hb = [A.alloc([D], BF16) for _ in range(3)]
        stt = [A.alloc([16], F32) for _ in range(NBZ)]
        for k in range(3):
            P.dma("sp", cw[:, :, k:k + 1], dram["ffn_conv_w"][l, k].rearrange("(c p o) -> p c o", p=128, o=1), writes=["cw"],
                  allow_slow_non_contiguous=True)
        P.dma("sp", cw[:, :, 3:4], dram["ffn_conv_b"][l].rearrange("(c p o) -> p c o", p=128, o=1), writes=["cw"],
              allow_slow_non_contiguous=True)
        load_ln_params(dram["ln2_g"][l], dram["ln2_b"][l])
        P.op("dve", lambda e: e.tensor_copy(out=halo_h, in_=HT[:, :, HW - 1:HW + 1]),
             reads=[("HT", NT // 2 - 1), ("HT", NT // 2)], writes=["halo_h"])
        it = 0
        for half in range(2):
            P.mark("ffn_ph1")
            t0 = half * HW
            tts = list(range(half * NT // 2, (half + 1) * NT // 2))
            for ch in range(NFC):
                wb = wgu[it % 3]
                wk = ("wgu", it % 3)
                gi = it % 2
                it += 1
                P.dma("sp", wb[:, 0], Wg_b[l, ch], reads=[("Wg_b", l)], writes=[(wk, 0)])
                P.dma("sp", wb[:, 1], Wu_b[l, ch], reads=[("Wu_b", l)], writes=[(wk, 1)])
                if half == 0 and 2 <= ch < 2 + 11:
                    c0 = (ch - 2) * 2
                    P.dma("pool", Wd[:, c0:c0 + 2, :], Wd_b[l, c0 * 128:(c0 + 2) * 128, :].rearrange("(c p) n -> p c n", p=128),
                          reads=[("Wd_b", l)], writes=[("Wd", c0), ("Wd", c0 + 1)])
                pg, pu = PS[gi * 2], PS[gi * 2 + 1]
                for (pp, wi, pidx) in ((pg, 0, gi * 2), (pu, 1, gi * 2 + 1)):
                    for nb in range(2):
                        for c in range(8):
                            P.op("pe", lambda e, pp=pp, wi=wi, nb=nb, c=c, wb=wb: e.matmul(
                                pp[:, nb, :], lhsT=wb[:, wi, c, :], rhs=HT[:, c, t0 + nb * 512:t0 + (nb + 1) * 512],
                                start=(c == 0), stop=(c == 7)),
                                reads=[(wk, wi)] + [("HT", t0 // 128 + nb * 4 + j) for j in range(4)],
                                writes=[psk(pidx, nb)])
                Gt = G[gi]
                gk = ("G", gi)
                P.op("act", lambda e, Gt=Gt, pg=pg: e.activation(out=Gt[:, 1:HW + 1], in_=pg[:, :, :].rearrange("p a b -> p (a b)"),
                                                                 func=AF.Identity, bias=cst[:, 3:4], scale=1.0),
                     reads=[psk(gi * 2, 0), psk(gi * 2, 1)], writes=[gk])
                hcol = 1 if half == 0 else 0
                for c in range(8):
                    P.op("pe", lambda e, pg=pg, c=c, wb=wb, hcol=hcol: e.matmul(
                        pg[:, 0, 0:1], lhsT=wb[:, 0, c, :], rhs=halo_h[:, c, hcol:hcol + 1],
                        start=(c == 0), stop=(c == 7)), reads=[(wk, 0), "halo_h", gk], writes=[psk(gi * 2, 0)])
                if half == 0:
                    P.op("pool", lambda e, Gt=Gt: e.memset(Gt[:, 0:1], 0.0), writes=[(gk, "l")])
                    P.op("act", lambda e, Gt=Gt, pg=pg: e.activation(out=Gt[:, HW + 1:HW + 2], in_=pg[:, 0, 0:1], func=AF.Identity, bias=cst[:, 3:4], scale=1.0),
                         reads=[psk(gi * 2, 0)], writes=[(gk, "r")])
                else:
                    P.op("pool", lambda e, Gt=Gt: e.memset(Gt[:, HW + 1:HW + 2], 0.0), writes=[(gk, "r")])
                    P.op("act", lambda e, Gt=Gt, pg=pg: e.activation(out=Gt[:, 0:1], in_=pg[:, 0, 0:1], func=AF.Identity, bias=cst[:, 3:4], scale=1.0),
                         reads=[psk(gi * 2, 0)], writes=[(gk, "l")])
                T = T1[gi]
                tk = ("T1", gi)
                P.op("dve", lambda e, T=T, Gt=Gt, ch=ch: e.tensor_scalar(
                    out=T, in0=Gt[:, 1:HW + 1], scalar1=cw[:, ch, 1:2], scalar2=cw[:, ch, 3:4], op0=ALU.mult, op1=ALU.add),
                    reads=[gk, "cw"], writes=[tk])
                P.op("dve", lambda e, T=T, Gt=Gt, ch=ch: e.scalar_tensor_tensor(
                    out=T, in0=Gt[:, 0:HW], scalar=cw[:, ch, 0:1], in1=T, op0=ALU.mult, op1=ALU.add),
                    reads=[gk, (gk, "l"), tk, "cw"], writes=[tk])
                P.op("dve", lambda e, T=T, Gt=Gt, ch=ch: e.scalar_tensor_tensor(
                    out=T, in0=Gt[:, 2:HW + 2], scalar=cw[:, ch, 2:3], in1=T, op0=ALU.mult, op1=ALU.add),
                    reads=[gk, (gk, "r"), tk, "cw"], writes=[tk])
                P.op("act", lambda e, T=T: e.activation(out=T, in_=T, func=AF.Silu), reads=[tk], writes=[tk])
                P.op("dve", lambda e, T=T, pu=pu, ch=ch: e.tensor_tensor(
                    out=actT[:, ch, :], in0=T, in1=pu[:, :, :].rearrange("p a b -> p (a b)"), op=ALU.mult),
                    reads=[tk, psk(gi * 2 + 1, 0), psk(gi * 2 + 1, 1)], writes=[("actT", ch)])
            P.mark("ffn_ph2")
            def F0(tt):
                P.dma("sp", zt[tt % NBZ], Hf[tt * 128:(tt + 1) * 128, :], reads=[("Hf", tt)], writes=[("z", tt % NBZ)])

            def F2(tt, tts=tts):
                pi = tt % 2
                tl = tt - tts[0]
                for hf in range(2):
                    for ch in range(NFC):
                        P.op("pe", lambda e: e.matmul(PS[pi][:, hf, :], lhsT=actT[:, ch, tl * 128:(tl + 1) * 128],
                                                      rhs=Wd[:, ch, hf * 512:(hf + 1) * 512], start=(ch == 0), stop=(ch == NFC - 1)),
                             reads=[("actT", ch), ("Wd", ch)], writes=[psk(pi, hf)])

            lns = ln_stages(zt, stt, hb, lambda tt: (Hf[tt * 128:(tt + 1) * 128, :], ("Hf", tt)))

            def F3(tt):
                pi = tt % 2
                z, zk = zt[tt % NBZ], ("z", tt % NBZ)
                P.op("dve", lambda e: e.scalar_tensor_tensor(out=z, in0=z, scalar=ALPHA, in1=PS[pi][:, :, :].rearrange("p a b -> p (a b)"),
                                                             op0=ALU.mult, op1=ALU.add), reads=[zk, psk(pi, 0), psk(pi, 1)], writes=[zk])
                lns[0](tt)
            run_pipe(tts, [F0, F2, F3] + lns[1:])

    def stage_ple(l, b, last):
        P.mark("stage_ple")
        P.barrier()
        A.reset()
        Wpg = A.alloc([8, D], BF16)
        Wpp = A.alloc([2, D], BF16)
        bg = A.alloc([D], F32)
        NBZ, NBS = 8, 5
        pt = [A.alloc([256], F32) for _ in range(3)]
        pb = [A.alloc([256], BF16) for _ in range(3)]
        pT = [A.alloc([2, 128], BF16) for _ in range(6)]
        sg = [A.alloc([D], F32) for _ in range(NBS)]
        zt = [A.alloc([D], F32) for _ in range(NBZ)]
        hb = [A.alloc([D], BF16) for _ in range(3)]
        stt = [A.alloc([16], F32) for _ in range(NBZ)]
        for c in range(8):
            P.dma("sp", Wpg[:, c, :], Wpg_b[l, c * 128:(c + 1) * 128, :], reads=[("Wpg_b", l)], writes=[("Wpg", c)])
        P.dma("sp", Wpp, Wpp_b[l].rearrange("(c p) n -> p c n", p=128), reads=[("Wpp_b", l)], writes=["Wpp"])
        P.dma("sp", bg, _bc(dram["ple_b_gate"][l]), writes=["bg"])
        load_ln_params(dram["ln3_g"][l], dram["ln3_b"][l])

        def G0(tt):
            P.dma("sp", pt[tt % 3], p_d[l, b, tt * 128:(tt + 1) * 128, :], writes=[("pt", tt % 3)])

        def G1(tt):
            P.op("act", lambda e: e.activation(out=pb[tt % 3], in_=pt[tt % 3], func=AF.Identity, bias=cst[:, 3:4], scale=1.0),
                 reads=[("pt", tt % 3)], writes=[("pb", tt % 3)])

        def G2(tt):
            i2 = tt % 2
            pst = PS[3][:, 1, :].bitcast(BF16).rearrange("p (a b) -> p a b", a=8)
            for c in range(2):
                P.op("pe", lambda e: e.transpose(out=pst[:, c, :], in_=pb[tt % 3][:, c * 128:(c + 1) * 128], identity=ident[:]),
                     reads=[("pb", tt % 3), "ident"], writes=[psk(3, 1)])
            for hf in range(2):
                for c in range(8):
                    P.op("pe", lambda e: e.matmul(PS[i2][:, hf, :], lhsT=HT[:, c, tt * 128:(tt + 1) * 128], rhs=Wpg[:, c, hf * 512:(hf + 1) * 512],
                                                  start=(c == 0), stop=(c == 7)), reads=[("HT", tt), ("Wpg", c)], writes=[psk(i2, hf)])

        def G3(tt):
            i2 = tt % 2
            pst = PS[3][:, 1, :].bitcast(BF16).rearrange("p (a b) -> p a b", a=8)
            P.op("dve", lambda e: e.tensor_copy(out=pT[tt % 6], in_=pst[:, 0:2, :]), reads=[psk(3, 1)], writes=[("pT", tt % 6)])
            s_, sk = sg[tt % NBS], ("sg", tt % NBS)
            P.op("dve", lambda e: e.tensor_tensor(out=s_, in0=PS[i2][:, :, :].rearrange("p a b -> p (a b)"), in1=bg, op=ALU.add),
                 reads=[psk(i2, 0), psk(i2, 1), "bg"], writes=[sk])

        def G4(tt):
            s_, sk = sg[tt % NBS], ("sg", tt % NBS)
            P.op("act", lambda e: e.activation(out=s_, in_=s_, func=AF.Sigmoid), reads=[sk], writes=[sk])
            i2 = tt % 2
            for hf in range(2):
                for c in range(2):
                    P.op("pe", lambda e: e.matmul(PS[2][:, hf, :], lhsT=pT[tt % 6][:, c, :], rhs=Wpp[:, c, hf * 512:(hf + 1) * 512],
                                                  start=(c == 0), stop=(c == 1)), reads=[("pT", tt % 6), "Wpp"], writes=[psk(2, hf)])
            P.dma("sp", zt[tt % NBZ], Hf[tt * 128:(tt + 1) * 128, :], reads=[("Hf", tt)], writes=[("z", tt % NBZ)])

        dstf = (lambda tt: (out_d[b, tt * 128:(tt + 1) * 128, :], ("out", b, tt))) if last else \
               (lambda tt: (Hf[tt * 128:(tt + 1) * 128, :], ("Hf", tt)))
        lns = ln_stages(zt, stt, hb, dstf, do_ht=not last, psi_fn=lambda tt: (3, 0))

        def G5(tt):
            i2 = tt % 2
            s_, sk = sg[tt % NBS], ("sg", tt % NBS)
            z, zk = zt[tt % NBZ], ("z", tt % NBZ)
            P.op("dve", lambda e: e.tensor_tensor(out=s_, in0=s_, in1=PS[2][:, :, :].rearrange("p a b -> p (a b)"), op=ALU.mult),
                 reads=[psk(2, 0), psk(2, 1), sk], writes=[sk])
            P.op("dve", lambda e: e.scalar_tensor_tensor(out=z, in0=z, scalar=ALPHA, in1=s_, op0=ALU.mult, op1=ALU.add),
                 reads=[zk, sk], writes=[zk])

        def G6(tt):
            lns[0](tt)
        run_pipe(list(range(NT)), [G0, G1, G2, G3, G4, G5, G6] + lns[1:])

    def stage_mix_dbg(l, b):
        P.barrier()
        A.reset()
        YT = A.alloc([8, S], BF16)
        gmix = A.alloc([D], F32)
        P.dma("sp", gmix, _bc(dram["mix_norm_g"][l]), writes=["gmix"])
        mark = A.off
        yt = [A.alloc([D], F32) for _ in range(2)]
        scr = [A.alloc([260], F32) for _ in range(2)]
        ybf = [A.alloc([256], BF16) for _ in range(2)]
        for tt in range(NT):
            i2 = tt % 2
            P.dma("sp", yt[i2], ydbg_d[l, b, tt * 128:(tt + 1) * 128, :], writes=[("yt", i2)])
            for g in range(4):
                emit_y(YT, gmix, g, tt, yt[i2][:, g * 256:(g + 1) * 256], ("yt", i2), scr[i2], ("scr", i2),
                       ybf[i2], ("ybf", i2), (3, 1))
        P.barrier()
        A.off = mark
        return YT

    def y_from_dbg(l, b, g, YT, gmix):
        mark = A.off
        yt = [A.alloc([256], F32) for _ in range(2)]
        scr = [A.alloc([260], F32) for _ in range(2)]
        ybf = [A.alloc([256], BF16) for _ in range(2)]
        for tt in range(NT):
            i2 = tt % 2
            P.dma("sp", yt[i2], ydbg_d[l, b, tt * 128:(tt + 1) * 128, g * 256:(g + 1) * 256], writes=[("yt", i2)])
            emit_y(YT, gmix, g, tt, yt[i2], ("yt", i2), scr[i2], ("scr", i2), ybf[i2], ("ybf", i2), (3, 1))
        P.barrier()
        A.off = mark

    def finish_group(g, YT, gmix, yG, ykeyfn):
        P.mark("finish_group")
        scr = [A.alloc([260], F32) for _ in range(4)]
        ybf = [A.alloc([256], BF16) for _ in range(3)]

        def E0(tt):
            s_, sk_ = scr[tt % 4], ("scr", tt % 4)
            if ydump_d is not None:
                P.dma("sp", ydump_d[tt * 128:(tt + 1) * 128, g * 256:(g + 1) * 256], yG[:, tt, :], reads=ykeyfn(tt),
                      writes=[("ydump", g, tt)])
            P.op("act", lambda e: e.activation(out=s_[:, 0:256], in_=yG[:, tt, :], func=AF.Square), reads=ykeyfn(tt), writes=[sk_])
            P.op("dve", lambda e: e.reduce_sum(out=s_[:, 256:257], in_=s_[:, 0:256], axis=AX.X), reads=[sk_], writes=[sk_])

        def E1(tt):
            s_, sk_ = scr[tt % 4], ("scr", tt % 4)
            st = s_[:, 256:260]
            P.op("act", lambda e: e.activation(out=st[:, 1:2], in_=st[:, 0:1], func=AF.Sqrt, bias=cst[:, 1:2], scale=1.0 / 256.0),
                 reads=[sk_, ("cst", 1)], writes=[sk_])
            P.op("dve", lambda e: e.reciprocal(out=st[:, 1:2], in_=st[:, 1:2]), reads=[sk_], writes=[sk_])
            P.op("dve", lambda e: e.scalar_tensor_tensor(out=ybf[tt % 3], in0=yG[:, tt, :], scalar=st[:, 1:2],
                                                         in1=gmix[:, g * 256:(g + 1) * 256], op0=ALU.mult, op1=ALU.mult),
                 reads=ykeyfn(tt) + [sk_, "gmix"], writes=[("ybf", tt % 3)])

        def E2(tt):
            pst = PS[3][:, 1, :].bitcast(BF16).rearrange("p (a b) -> p a b", a=8)
            for c in range(2):
                P.op("pe", lambda e: e.transpose(out=pst[:, (tt % 2) * 2 + c, :], in_=ybf[tt % 3][:, c * 128:(c + 1) * 128], identity=ident[:]),
                     reads=[("ybf", tt % 3), "ident"], writes=[psk(3, 1)])
            P.op("dve", lambda e: e.tensor_copy(out=YT[:, 2 * g:2 * g + 2, tt * 128:(tt + 1) * 128],
                                                in_=pst[:, (tt % 2) * 2:(tt % 2) * 2 + 2, :]),
                 reads=[psk(3, 1)], writes=[("YT", g, tt)])
        run_pipe(list(range(NT)), [E0, E1, E2])

    def mix_mla(l, b, YT, gmix):
        P.mark("mix_mla")
        mark = A.off
        Wa = A.alloc([8, 416], BF16)
        Wuq = A.alloc([2, 384], BF16)
        Wukv = A.alloc([512], BF16)
        gq = A.alloc([4], F32)
        CQT = A.alloc([3, S], BF16)
        SQ = A.alloc([3, S], BF16)
        QT = A.alloc([4, S], BF16)
        KT = A.alloc([4, S], BF16)
        V1 = A.alloc([NT, 4, 66], BF16)
        rope = A.alloc([NT, 32], F32)
        ones = A.alloc([2], BF16)
        yA = A.alloc([NT, 256], F32)
        Qb = [A.alloc([4, 96], BF16) for _ in range(5)]
        Kb = [A.alloc([4, 96], BF16) for _ in range(5)]
        R = [A.alloc([5, 32], F32) for _ in range(3)]
        Ro = [A.alloc([5, 32], F32) for _ in range(2)]
        T4 = [A.alloc([4, 5, 16], F32) for _ in range(3)]
        st = [A.alloc([4], F32) for _ in range(3)]
        PT = [A.alloc([1024], BF16) for _ in range(3)]
        rc = [A.alloc([4], F32) for _ in range(2)]
        OTs = [A.alloc([512], F32) for _ in range(2)]
        P.dma("sp", Wa, Wi_b[l][:, 0:416].rearrange("(c p) n -> p c n", p=128), reads=[("Wi_b", l)], writes=["Wa"])
        P.dma("sp", Wuq, Wuq_b[l].rearrange("(c p) n -> p c n", p=128), reads=[("Wuq_b", l)], writes=["Wuq"])
        P.dma("sp", Wukv, Wukv_b[l], reads=[("Wukv_b", l)], writes=["Wukv"])
        P.dma("sp", gq[:, 0:2], dram["mla_q_norm"][l].rearrange("(c p) -> p c", p=128), writes=["gq"],
              allow_slow_non_contiguous=True)
        P.dma("sp", gq[:, 2:3], dram["mla_kv_norm"][l].rearrange("(c p) -> p c", p=128), writes=["gq"],
              allow_slow_non_contiguous=True)
        P.dma("sp", rope, rope_d.rearrange("(t p) c -> p t c", p=128), writes=["rope"])
        P.op("pool", lambda e: e.memset(ones, 1.0), writes=["ones"])
        P.op("pool", lambda e: e.memset(V1[:, :, :, 64:65], 1.0), writes=["V1one"])
        it = 0
        for nb in range(4):
            for ci in range(3):
                pi, pb_ = (it % 4) // 2, it % 2
                it += 1
                for c in range(8):
                    P.op("pe", lambda e: e.matmul(PS[pi][:, pb_, :], lhsT=Wa[:, c, ci * 128:(ci + 1) * 128],
                                                  rhs=HT[:, c, nb * 512:(nb + 1) * 512], start=(c == 0), stop=(c == 7)),
                         reads=["Wa"] + [("HT", nb * 4 + j) for j in range(4)], writes=[psk(pi, pb_)])
                P.op("act", lambda e: e.activation(out=CQT[:, ci, nb * 512:(nb + 1) * 512], in_=PS[pi][:, pb_, :],
                                                   func=AF.Identity, bias=cst[:, 3:4], scale=gq[:, ci:ci + 1]),
                     reads=[psk(pi, pb_), "gq", ("cst", 3)], writes=[("CQT", nb)])
                P.op("act", lambda e: e.activation(out=SQ[:, ci, nb * 512:(nb + 1) * 512], in_=PS[pi][:, pb_, :],
                                                   func=AF.Square), reads=[psk(pi, pb_)], writes=[("SQ", nb)])
        P.mark("mla_A2")
        SCALE = 96.0 ** -0.5

        def bufs(tt):
            i3 = tt % 3
            b0 = PS[i3][:, 0, :]
            return i3, b0[:, 0:384], b0[:, 384:416], b0[:, 416:418], PS[i3][:, 1, :]

        def M0(tt):
            i3, PSq, PSkr, PSss, PSkv = bufs(tt)
            tsl = slice(tt * 128, (tt + 1) * 128)
            nb = tt // 4
            for c in range(2):
                P.op("pe", lambda e: e.matmul(PSq, lhsT=CQT[:, c, tsl], rhs=Wuq[:, c, :], start=(c == 0), stop=(c == 1)),
                     reads=[("CQT", nb), "Wuq"], writes=[psk(i3, 0)])
            for c in range(8):
                P.op("pe", lambda e: e.matmul(PSkr, lhsT=HT[:, c, tsl], rhs=Wa[:, c, 384:416], start=(c == 0), stop=(c == 7)),
                     reads=[("HT", tt), "Wa"], writes=[psk(i3, 0)])
            for c in range(2):
                P.op("pe", lambda e: e.matmul(PSss[:, 0:1], lhsT=SQ[:, c, tsl], rhs=ones[:, 0:1], start=(c == 0), stop=(c == 1)),
                     reads=[("SQ", nb), "ones"], writes=[psk(i3, 0)])
            P.op("pe", lambda e: e.matmul(PSss[:, 1:2], lhsT=SQ[:, 2, tsl], rhs=ones[:, 0:1], start=True, stop=True),
                 reads=[("SQ", nb), "ones"], writes=[psk(i3, 0)])
            P.op("pe", lambda e: e.matmul(PSkv, lhsT=CQT[:, 2, tsl], rhs=Wukv, start=True, stop=True),
                 reads=[("CQT", nb), "Wukv"], writes=[psk(i3, 1)])

        def M1(tt):
            i3, PSq, PSkr, PSss, PSkv = bufs(tt)
            s_, sk_ = st[tt % 3], ("mst", tt % 3)
            P.op("act", lambda e: e.activation(out=s_[:, 0:1], in_=PSss[:, 0:1], func=AF.Sqrt, bias=cst[:, 1:2], scale=1.0 / 256),
                 reads=[psk(i3, 0), ("cst", 1)], writes=[sk_])
            P.op("act", lambda e: e.activation(out=s_[:, 1:2], in_=PSss[:, 1:2], func=AF.Sqrt, bias=cst[:, 1:2], scale=1.0 / 128),
                 reads=[psk(i3, 0), ("cst", 1)], writes=[sk_])
            P.op("dve", lambda e: e.reciprocal(out=s_[:, 0:2], in_=s_[:, 0:2]), reads=[sk_], writes=[sk_])

        def M2(tt):
            i3, PSq, PSkr, PSss, PSkv = bufs(tt)
            s_, sk_ = st[tt % 3], ("mst", tt % 3)
            q3 = PSq.rearrange("p (h d) -> p h d", h=4)
            kv3 = PSkv.rearrange("p (h d) -> p h d", h=4)
            i5 = tt % 5
            qk, kk, rk = ("Qb", i5), ("Kb", i5), ("R", tt % 3)
            P.op("act", lambda e: e.activation(out=Qb[i5][:, :, 0:64], in_=q3[:, :, 0:64], func=AF.Identity,
                                               bias=cst[:, 3:4], scale=s_[:, 0:1]), reads=[psk(i3, 0), sk_, ("cst", 3)], writes=[qk])
            P.op("act", lambda e: e.activation(out=R[tt % 3][:, 0:4, :], in_=q3[:, :, 64:96], func=AF.Identity,
                                               bias=cst[:, 3:4], scale=s_[:, 0:1]), reads=[psk(i3, 0), sk_, ("cst", 3)], writes=[rk])
            P.op("act", lambda e: e.activation(out=R[tt % 3][:, 4, :], in_=PSkr, func=AF.Identity, bias=cst[:, 3:4], scale=1.0),
                 reads=[psk(i3, 0)], writes=[rk])
            P.op("act", lambda e: e.activation(out=Kb[i5][:, :, 0:64], in_=kv3[:, :, 0:64], func=AF.Identity,
                                               bias=cst[:, 3:4], scale=s_[:, 1:2]), reads=[psk(i3, 1), sk_, ("cst", 3)], writes=[kk])
            P.op("act", lambda e: e.activation(out=V1[:, tt, :, 0:64], in_=kv3[:, :, 64:128], func=AF.Identity,
                                               bias=cst[:, 3:4], scale=s_[:, 1:2]), reads=[psk(i3, 1), sk_, ("cst", 3)],
                 writes=[("V1", tt)])

        def M3(tt):
            cosb = rope[:, tt, 0:16].unsqueeze(1).broadcast_to([128, 5, 16])
            sinb = rope[:, tt, 16:32].unsqueeze(1).broadcast_to([128, 5, 16])
            Rr, T_ = R[tt % 3], T4[tt % 3]
            rk, tk_ = ("R", tt % 3), ("T4", tt % 3)
            P.op("pool", lambda e: e.tensor_tensor(out=T_[:, 0], in0=Rr[:, :, 0:16], in1=cosb, op=ALU.mult), reads=[rk, "rope"], writes=[(tk_, 0)])
            P.op("pool", lambda e: e.tensor_tensor(out=T_[:, 1], in0=Rr[:, :, 16:32], in1=sinb, op=ALU.mult), reads=[rk, "rope"], writes=[(tk_, 1)])
            P.op("pool", lambda e: e.tensor_tensor(out=T_[:, 2], in0=Rr[:, :, 16:32], in1=cosb, op=ALU.mult), reads=[rk, "rope"], writes=[(tk_, 2)])
            P.op("pool", lambda e: e.tensor_tensor(out=T_[:, 3], in0=Rr[:, :, 0:16], in1=sinb, op=ALU.mult), reads=[rk, "rope"], writes=[(tk_, 3)])

        def M4(tt):
            T_, tk_ = T4[tt % 3], ("T4", tt % 3)
            i5 = tt % 5
            qk, kk = ("Qb", i5), ("Kb", i5)
            ro, rok = Ro[tt % 2], ("Ro", tt % 2)
            P.op("dve", lambda e: e.tensor_tensor(out=ro[:, :, 0:16], in0=T_[:, 0], in1=T_[:, 1], op=ALU.subtract),
                 reads=[(tk_, 0), (tk_, 1)], writes=[(rok, 0)])
            P.op("dve", lambda e: e.tensor_tensor(out=ro[:, :, 16:32], in0=T_[:, 2], in1=T_[:, 3], op=ALU.add),
                 reads=[(tk_, 2), (tk_, 3)], writes=[(rok, 1)])
            P.op("dve", lambda e: e.tensor_copy(out=Qb[i5][:, :, 64:96], in_=ro[:, 0:4, :]), reads=[(rok, 0), (rok, 1)], writes=[(qk, "r")])
            P.op("dve", lambda e: e.tensor_copy(out=Kb[i5][:, :, 64:96], in_=ro[:, 4:5, :].broadcast_to([128, 4, 32])),
                 reads=[(rok, 0), (rok, 1)], writes=[(kk, "r")])

        def M5(tt):
            i5, i2 = tt % 5, tt % 2
            qk, kk = ("Qb", i5), ("Kb", i5)
            pst = PS[3][:, i2, :].bitcast(BF16).rearrange("p (a b) -> p a b", a=8)
            for h in range(4):
                P.op("pe", lambda e: e.transpose(out=pst[0:96, h, :], in_=Qb[i5][:, h, :], identity=ident[:]),
                     reads=[qk, (qk, "r"), "ident"], writes=[psk(3, i2)])
            for h in range(4):
                P.op("pe", lambda e: e.transpose(out=pst[0:96, 4 + h, :], in_=Kb[i5][:, h, :], identity=ident[:]),
                     reads=[kk, (kk, "r"), "ident"], writes=[psk(3, i2)])

        def M6(tt):
            i2 = tt % 2
            tsl = slice(tt * 128, (tt + 1) * 128)
            pst = PS[3][:, i2, :].bitcast(BF16).rearrange("p (a b) -> p a b", a=8)
            P.op("dve", lambda e: e.tensor_copy(out=QT[0:96, :, tsl], in_=pst[0:96, 0:4, :]), reads=[psk(3, i2)], writes=[("QT", tt)])
            P.op("act", lambda e: e.activation(out=KT[0:96, :, tsl], in_=pst[0:96, 4:8, :], func=AF.Identity, bias=cst[0:96, 3:4], scale=1.0),
                 reads=[psk(3, i2)], writes=[("KT", tt)])
        run_pipe(list(range(NT)), [M0, M1, M2, M3, M4, M5, M6])
        P.barrier()
        P.mark("mla_A3")
        items = [(h, qb, kp) for h in range(4) for qb in range(4) for kp in range(NT // 2)]

        def emit_S(i):
            h, qb, kp = items[i]
            j2 = i % 2
            for u in range(2):
                kt = 2 * kp + u
                P.op("pe", lambda e: e.matmul(PS[j2][:, u, :], lhsT=KT[0:96, h, kt * 128:(kt + 1) * 128],
                                              rhs=QT[0:96, h, qb * 512:(qb + 1) * 512], start=True, stop=True),
                     reads=[("KT", kt)] + [("QT", qb * 4 + j) for j in range(4)], writes=[psk(j2, u)])

        emit_S(0)
        NP2 = NT // 2
        for i, (h, qb, kp) in enumerate(items):
            grp = i // NP2
            oi = grp % 2
            POT = PS[2 + oi][0:65, 0, :]
            ok_ = psk(2 + oi, 0)
            if i + 1 < len(items):
                emit_S(i + 1)
            j2 = i % 2
            pt_ = PT[i % 3]
            ptk = ("PT", i % 3)
            P.op("act", lambda e: e.activation(out=pt_, in_=PS[j2][:, :, :].rearrange("p a b -> p (a b)"), func=AF.Exp, scale=SCALE),
                 reads=[psk(j2, 0), psk(j2, 1)], writes=[ptk])
            for u in range(2):
                kt = 2 * kp + u
                P.op("pe", lambda e: e.matmul(POT, lhsT=V1[:, kt, h, 0:65], rhs=pt_[:, u * 512:(u + 1) * 512],
                                              start=(kt == 0), stop=(kt == NT - 1)),
                     reads=[ptk, ("V1", kt), "V1one"], writes=[ok_])
            if kp == NP2 - 1:
                ots, otk = OTs[oi], ("OTs", oi)
                P.op("dve", lambda e: e.tensor_copy(out=ots[0:65, :], in_=POT), reads=[ok_], writes=[otk])
                PTR = PS[2 + oi][:, 1, 0:260].rearrange("p (j d) -> p j d", j=4)
                trk = psk(2 + oi, 1)
                for j in range(4):
                    P.op("pe", lambda e: e.transpose(out=PTR[:, j, :], in_=ots[0:65, j * 128:(j + 1) * 128], identity=ident_f[0:65, 0:65]),
                         reads=[otk, "ident_f"], writes=[trk])
                rck = ("rc", oi)
                P.op("dve", lambda e: e.reciprocal(out=rc[oi], in_=PTR[:, :, 64]), reads=[trk], writes=[rck])
                P.op("dve", lambda e: e.tensor_tensor(out=yA[:, qb * 4:(qb + 1) * 4, h * 64:(h + 1) * 64], in0=PTR[:, :, 0:64],
                                                      in1=rc[oi].unsqueeze(2).broadcast_to([128, 4, 64]), op=ALU.mult),
                     reads=[trk, rck], writes=[("yA", qb, h)])
        finish_group(0, YT, gmix, yA, lambda tt: [("yA", tt // 4, h) for h in range(4)])
        P.barrier()
        A.off = mark

    def mix_swa(l, b, YT, gmix):
        P.mark("mix_swa")
        mark = A.off
        Wc = A.alloc([8, 512], BF16)
        Wk2 = A.alloc([8, 2, 2, 64], BF16) if False else A.alloc([8, 256], BF16)
        SQT = A.alloc([2, NT, 256], BF16)
        SKT = A.alloc([2, S], BF16)
        V1s = A.alloc([NT, 2, 66], BF16)
        P.op("dve", lambda e: e.memset(SQT, 0.0), writes=["SQTz"])
        EBt = A.alloc([3, 4, 128], F32)
        esink = A.alloc([4], F32)
        yC = A.alloc([NT, 256], F32)
        E = [A.alloc([3, 256], F32) for _ in range(3)]
        Eb = [A.alloc([3, 256], BF16) for _ in range(3)]
        rc = [A.alloc([4], F32) for _ in range(2)]
        P.dma("sp", Wc, Wi_b[l][:, O_SQ:O_SQ + 512].rearrange("(c p) n -> p c n", p=128), reads=[("Wi_b", l)], writes=["Wc"])
        P.dma("sp", EBt, swa_eb_d, writes=["EBt"])
        P.dma("sp", esink, _bc(dram["swa_sink"][l]), writes=["esink"])
        P.op("act", lambda e: e.activation(out=esink, in_=esink, func=AF.Exp), reads=["esink"], writes=["esink"])
        P.op("pool", lambda e: e.memset(V1s[:, :, :, 64:65], 1.0), writes=["V1sone"])
        Wk2v = Wk2.rearrange("p c (k d e) -> p c k d e", k=2, d=2)
        for dup in range(2):
            P.op("pool", lambda e: e.tensor_copy(out=Wk2v[:, :, :, dup, :],
                                                 in_=Wc[:, :, 256:384].rearrange("p c (k e) -> p c k e", k=2)),
                 reads=["Wc"], writes=[("Wk2", dup)])
        it = 0
        for nb in range(4 if dbg.get("swa_stage", 9) >= 1 else 0):
            for (dst, is_k) in ((SQT, 0), (SKT, 1)):
                for p_ in range(2):
                    pi, pb_ = (it % 4) // 2, it % 2
                    it += 1
                    for c in range(8):
                        lw = Wk2[:, c, p_ * 128:(p_ + 1) * 128] if is_k else Wc[:, c, p_ * 128:(p_ + 1) * 128]
                        P.op("pe", lambda e: e.matmul(PS[pi][:, pb_, :], lhsT=lw, rhs=HT[:, c, nb * 512:(nb + 1) * 512],
                                                      start=(c == 0), stop=(c == 7)),
                             reads=["Wc", ("Wk2", 0), ("Wk2", 1)] + [("HT", nb * 4 + j) for j in range(4)], writes=[psk(pi, pb_)])
                    if is_k:
                        P.op("act", lambda e: e.activation(out=dst[:, p_, nb * 512:(nb + 1) * 512], in_=PS[pi][:, pb_, :],
                                                           func=AF.Identity, bias=cst[:, 3:4], scale=1.0),
                             reads=[psk(pi, pb_)], writes=[("SQK", is_k, nb)])
                    else:
                        for g in range(2):
                            P.op("act", lambda e: e.activation(
                                out=SQT[g * 64:(g + 1) * 64, p_, nb * 4:(nb + 1) * 4, g * 128:(g + 1) * 128],
                                in_=PS[pi][g * 64:(g + 1) * 64, pb_, :].rearrange("p (t q) -> p t q", t=4),
                                func=AF.Identity, bias=cst[g * 64:(g + 1) * 64, 3:4], scale=1.0),
                                reads=[psk(pi, pb_), "SQTz"], writes=[("SQK", is_k, nb)])
        P.mark("swa_V")
        for tt in range(NT if dbg.get("swa_stage", 9) >= 2 else 0):
            i2 = tt % 2
            pv = PS[2 + i2][:, 1, 0:128]
            for c in range(8):
                P.op("pe", lambda e: e.matmul(pv, lhsT=HT[:, c, tt * 128:(tt + 1) * 128], rhs=Wc[:, c, 384:512],
                                              start=(c == 0), stop=(c == 7)), reads=["Wc", ("HT", tt)], writes=[psk(2 + i2, 1)])
            P.op("act", lambda e: e.activation(out=V1s[:, tt, :, 0:64], in_=pv.rearrange("p (k d) -> p k d", k=2), func=AF.Identity,
                                               bias=cst[:, 3:4], scale=1.0),
                 reads=[psk(2 + i2, 1), ("cst", 3)], writes=[("V1s", tt)])
        P.mark("swa_attn")
        items = [(n, p_) for n in range(NT if dbg.get("swa_n") is None else dbg["swa_n"]) for p_ in range(2)]
        idx = {it_: i for i, it_ in enumerate(items)}

        def rels_of(n):
            return [r for r in range(3) if 0 <= n + r - 1 < NT]

        def W0(it_):
            n, p_ = it_
            i = idx[it_]
            PSs = PS[i % 2][:, :, :].rearrange("p a b -> p (a b)")
            for r in rels_of(n):
                kt = n + r - 1
                bankk = psk(i % 2, 0 if r < 2 else 1)
                P.op("pe", lambda e: e.matmul(PSs[:, r * 256:(r + 1) * 256], lhsT=SKT[:, p_, kt * 128:(kt + 1) * 128],
                                              rhs=SQT[:, p_, n, :], start=True, stop=True),
                     reads=[("SQK", 0, n // 4), ("SQK", 1, kt // 4)], writes=[bankk])

        def W1(it_):
            n, p_ = it_
            i = idx[it_]
            rels = rels_of(n)
            PSs = PS[i % 2][:, :, :].rearrange("p a b -> p (a b)")
            e_, eb_ = E[i % 3], Eb[i % 3]
            for r in rels:
                bankk = psk(i % 2, 0 if r < 2 else 1)
                P.op("act", lambda e: e.activation(out=e_[:, r, :], in_=PSs[:, r * 256:(r + 1) * 256], func=AF.Exp, scale=0.125),
                     reads=[bankk], writes=[("E", i % 3, r)])
            r0, r1 = rels[0], rels[-1] + 1
            P.op("pool", lambda e: e.tensor_tensor(out=eb_[:, r0:r1, :].rearrange("p r (g q) -> p r g q", g=2),
                                                   in0=e_[:, r0:r1, :].rearrange("p r (g q) -> p r g q", g=2),
                                                   in1=EBt[:, r0:r1, 2 * p_:2 * p_ + 2, :], op=ALU.mult),
                 reads=[("E", i % 3, r) for r in rels] + ["EBt"], writes=[("Eb", i % 3)])

        def W2(it_):
            n, p_ = it_
            i = idx[it_]
            i2 = n % 2
            rels = rels_of(n)
            eb_ = Eb[i % 3]
            PSo = PS[2 + i2][:, 0, 0:260].rearrange("p (h d) -> p h d", h=4)
            ok_ = psk(2 + i2, 0)
            for g in range(2):
                h = 2 * p_ + g
                for r in rels:
                    kt = n + r - 1
                    P.op("pe", lambda e: e.matmul(PSo[:, h, :], lhsT=eb_[:, r, g * 128:(g + 1) * 128], rhs=V1s[:, kt, p_, 0:65],
                                                  start=(r == rels[0]), stop=(r == rels[-1])),
                         reads=[("Eb", i % 3), ("V1s", kt), "V1sone"], writes=[ok_])
            if p_ == 1:
                rck = ("rc", i2)
                P.op("dve", lambda e: e.tensor_tensor(out=rc[i2], in0=PSo[:, :, 64], in1=esink, op=ALU.add), reads=[ok_, "esink"], writes=[rck])
                P.op("dve", lambda e: e.reciprocal(out=rc[i2], in_=rc[i2]), reads=[rck], writes=[rck])
                P.op("dve", lambda e: e.tensor_tensor(out=yC[:, n, :].rearrange("p (h d) -> p h d", h=4), in0=PSo[:, :, 0:64],
                                                      in1=rc[i2].unsqueeze(2).broadcast_to([128, 4, 64]), op=ALU.mult),
                     reads=[ok_, rck], writes=[("yC", n)])
        run_pipe(items, [W0, W1, W2])
        finish_group(2, YT, gmix, yC, lambda tt: [("yC", tt)])
        P.barrier()
        A.off = mark

    def mix_ssd(l, b, YT, gmix):
        P.mark("mix_ssd")
        mark = A.off
        Wd_ = A.alloc([8, 1032], BF16)
        XBCT = A.alloc([6, S], BF16)
        X = A.alloc([NT, 256], BF16)
        Bt = A.alloc([NT, 256], BF16)
        Zs = A.alloc([NT, 256], BF16)
        yD = A.alloc([NT, 256], F32)
        msk = A.alloc([5, 128], F32)
        cwx = A.alloc([6, 4], F32)
        prm = A.alloc([20], F32)
        dsk2 = A.alloc([8], F32)
        dta = A.alloc([NT, 16], F32)
        P.dma("sp", Wd_, Wi_b[l][:, O_Z:O_Z + 1032].rearrange("(c p) n -> p c n", p=128), reads=[("Wi_b", l)], writes=["Wd_"])
        P.dma("sp", msk, ssd_masks_d, writes=["msk"])
        for k in range(3):
            P.dma("sp", cwx[:, :, k:k + 1], dram["ssd_conv_w"][l, k].rearrange("(c p o) -> p c o", p=128, o=1), writes=["cwx"],
                  allow_slow_non_contiguous=True)
        P.dma("sp", cwx[:, :, 3:4], dram["ssd_conv_b"][l].rearrange("(c p o) -> p c o", p=128, o=1), writes=["cwx"],
              allow_slow_non_contiguous=True)
        P.dma("sp", prm[:, 0:8], _bc(dram["ssd_dt_bias"][l].rearrange("a b -> (a b)")), writes=["prm0"])
        P.dma("sp", prm[:, 8:16], _bc(dram["ssd_a_log"][l].rearrange("a b -> (a b)")), writes=["prm1"])
        P.dma("sp", dsk2, _bc(dram["ssd_d"][l].rearrange("a b -> (a b)")), writes=["dsk2"])
        P.op("act", lambda e: e.activation(out=prm[:, 8:16], in_=prm[:, 8:16], func=AF.Exp), reads=["prm1"], writes=["prm1"])
        P.op("dve", lambda e: e.tensor_scalar(out=prm[:, 8:16], in0=prm[:, 8:16], scalar1=-1.0, scalar2=None, op0=ALU.mult),
             reads=["prm1"], writes=["prm1"])
        P.op("dve", lambda e: e.tensor_tensor(out=prm[:, 16:20], in0=dsk2[:, 0:4], in1=dsk2[:, 4:8], op=ALU.add),
             reads=["dsk2"], writes=["prm2"])
        mark2 = A.off
        Gs = [A.alloc([S + 2], F32) for _ in range(2)]
        Ts = [A.alloc([S], F32) for _ in range(2)]
        for gi_ in range(2):
            P.op("pool", lambda e: e.memset(Gs[gi_][:, 0:1], 0.0), writes=[("Gl", gi_)])
            P.op("pool", lambda e: e.memset(Gs[gi_][:, S + 1:S + 2], 0.0), writes=[("Gr", gi_)])

        def Sa(ch):
            G, T, gq_ = Gs[ch % 2], Ts[ch % 2], ch % 2
            for nb in range(4):
                it = ch * 4 + nb
                pi, pb_ = (it % 4) // 2, it % 2
                for c in range(8):
                    P.op("pe", lambda e: e.matmul(PS[pi][:, pb_, :], lhsT=Wd_[:, c, 256 + ch * 128:256 + (ch + 1) * 128],
                                                  rhs=HT[:, c, nb * 512:(nb + 1) * 512], start=(c == 0), stop=(c == 7)),
                         reads=["Wd_"] + [("HT", nb * 4 + j) for j in range(4)], writes=[psk(pi, pb_)])
                P.op("act", lambda e: e.activation(out=G[:, 1 + nb * 512:1 + (nb + 1) * 512], in_=PS[pi][:, pb_, :],
                                                   func=AF.Identity, bias=cst[:, 3:4], scale=1.0),
                     reads=[psk(pi, pb_)], writes=[("G", gq_, nb)])
                P.op("act", lambda e: e.activation(out=T[:, nb * 512:(nb + 1) * 512], in_=PS[pi][:, pb_, :],
                                                   func=AF.Identity, bias=cwx[:, ch, 3:4], scale=cwx[:, ch, 1:2]),
                     reads=[psk(pi, pb_), "cwx"], writes=[("T", gq_, nb)])

        def Sb(ch):
            G, T, gq_ = Gs[ch % 2], Ts[ch % 2], ch % 2
            gks = [("G", gq_, nb) for nb in range(4)]
            tks = [("T", gq_, nb) for nb in range(4)]
            P.op("dve", lambda e: e.scalar_tensor_tensor(out=T, in0=G[:, 0:S], scalar=cwx[:, ch, 0:1], in1=T, op0=ALU.mult, op1=ALU.add),
                 reads=gks + tks + [("Gl", gq_), "cwx"], writes=tks)
            P.op("dve", lambda e: e.scalar_tensor_tensor(out=T, in0=G[:, 2:S + 2], scalar=cwx[:, ch, 2:3], in1=T, op0=ALU.mult, op1=ALU.add),
                 reads=gks + tks + [("Gr", gq_), "cwx"], writes=tks)

        def Sc(ch):
            T, gq_ = Ts[ch % 2], ch % 2
            P.op("act", lambda e: e.activation(out=XBCT[:, ch, :], in_=T, func=AF.Silu), reads=[("T", gq_, nb) for nb in range(4)],
                 writes=[("XBCT", ch)])
        run_pipe(list(range(6)), [Sa, Sb, Sc])
        P.barrier()
        A.off = mark2
        P.mark("ssd_D2")
        zt_ = [A.alloc([256], F32) for _ in range(2)]
        for tt in range(NT):
            i2 = tt % 2
            tsl = slice(tt * 128, (tt + 1) * 128)
            pst = PS[2 + i2][:, 0, :].bitcast(BF16).rearrange("p (a b) -> p a b", a=8)
            for ch in range(4):
                P.op("pe", lambda e: e.transpose(out=pst[:, ch, :], in_=XBCT[:, ch, tsl], identity=ident[:]),
                     reads=[("XBCT", ch), "ident"], writes=[psk(2 + i2, 0)])
            P.op("dve", lambda e: e.tensor_copy(out=X[:, tt, :].rearrange("p (a b) -> p a b", a=2), in_=pst[:, 0:2, :]),
                 reads=[psk(2 + i2, 0)], writes=[("X", tt)])
            P.op("dve", lambda e: e.tensor_copy(out=Bt[:, tt, :].rearrange("p (a b) -> p a b", a=2), in_=pst[:, 2:4, :]),
                 reads=[psk(2 + i2, 0)], writes=[("Bt", tt)])
            pz = PS[i2][:, 0, 0:256]
            for c in range(8):
                P.op("pe", lambda e: e.matmul(pz, lhsT=HT[:, c, tsl], rhs=Wd_[:, c, 0:256], start=(c == 0), stop=(c == 7)),
                     reads=[("HT", tt), "Wd_"], writes=[psk(i2, 0)])
            P.op("act", lambda e: e.activation(out=Zs[:, tt, :], in_=pz, func=AF.Silu), reads=[psk(i2, 0)], writes=[("Zs", tt)])
        for tt in range(NT):
            i2 = tt % 2
            tsl = slice(tt * 128, (tt + 1) * 128)
            pdt = PS[i2][:, 1, 0:8]
            for c in range(8):
                P.op("pe", lambda e: e.matmul(pdt, lhsT=HT[:, c, tsl], rhs=Wd_[:, c, 1024:1032], start=(c == 0), stop=(c == 7)),
                     reads=[("HT", tt), "Wd_"], writes=[psk(i2, 1)])
            dk = ("dta", tt)
            P.op("dve", lambda e: e.tensor_tensor(out=dta[:, tt, 0:8], in0=pdt, in1=prm[:, 0:8], op=ALU.add),
                 reads=[psk(i2, 1), "prm0"], writes=[dk])
            P.op("act", lambda e: e.activation(out=dta[:, tt, 0:8], in_=dta[:, tt, 0:8], func=AF.Exp), reads=[dk], writes=[dk])
            P.op("act", lambda e: e.activation(out=dta[:, tt, 0:8], in_=dta[:, tt, 0:8], func=AF.Ln, bias=cst[:, 2:3], scale=1.0),
                 reads=[dk, ("cst", 2)], writes=[dk])
            P.op("dve", lambda e: e.tensor_tensor(out=dta[:, tt, 8:16], in0=dta[:, tt, 0:8], in1=prm[:, 8:16], op=ALU.mult),
                 reads=[dk, "prm1"], writes=[dk])
        P.mark("ssd_D3")
        carry = A.alloc([4, 64], F32)
        prev = A.alloc([4, 64], BF16)
        Am = [A.alloc([4, 128], F32) for _ in range(3)]
        Lx = [A.alloc([4, 128], F32) for _ in range(3)]
        CBm = [A.alloc([2, 128], F32) for _ in range(3)]
        MT = [A.alloc([4, 128], BF16) for _ in range(3)]
        Xdt = [A.alloc([4, 64], BF16) for _ in range(4)]
        Xdd = [A.alloc([4, 64], BF16) for _ in range(4)]
        ex = [A.alloc([3, 4], F32) for _ in range(7)]
        ytmp = [A.alloc([4, 64], F32) for _ in range(2)]
        items = [(d, idx_) for d in range(2) for idx_ in range(NT)]
        inum = {it_: i for i, it_ in enumerate(items)}

        def info(it_):
            d, idx_ = it_
            tt = idx_ if d == 0 else NT - 1 - idx_
            tri = msk[:, 0, :] if d == 0 else msk[:, 1, :]
            mgl = msk[:, 2, :] if d == 0 else msk[:, 3, :]
            return d, idx_, tt, tri, mgl, inum[it_]

        def Q0(it_):
            d, idx_, tt, tri, mgl, i = info(it_)
            i2 = i % 2
            a_ = dta[:, tt, 8 + d * 4:12 + d * 4]
            dk = ("dta", tt)
            P.op("pe", lambda e: e.matmul(PS[0][:, i2, 0:4], lhsT=tri, rhs=a_, start=True, stop=True), reads=["msk", dk], writes=[psk(0, i2)])
            P.op("pe", lambda e: e.matmul(PS[0][:, i2, 4:8], lhsT=msk[:, 4, :], rhs=a_, start=True, stop=True), reads=["msk", dk], writes=[psk(0, i2)])
            P.op("pool", lambda e: e.tensor_tensor(out=Am[i % 3], in0=mgl.unsqueeze(1).broadcast_to([128, 4, 128]),
                                                   in1=a_.unsqueeze(2).broadcast_to([128, 4, 128]), op=ALU.mult),
                 reads=["msk", dk], writes=[("Am", i % 3)])

        def Q1(it_):
            d, idx_, tt, tri, mgl, i = info(it_)
            i2 = i % 2
            tsl = slice(tt * 128, (tt + 1) * 128)
            e_, exk = ex[i % 7], ("ex", i % 7)
            P.op("dve", lambda e: e.tensor_copy(out=e_[:, 1:3, :], in_=PS[0][:, i2, 0:8].rearrange("p (a b) -> p a b", a=2)),
                 reads=[psk(0, i2)], writes=[(exk, 1)])
            P.op("dve", lambda e: e.tensor_tensor(out=e_[:, 0, :], in0=e_[:, 2, :], in1=e_[:, 1, :], op=ALU.subtract),
                 reads=[(exk, 1)], writes=[exk])
            P.op("act", lambda e: e.activation(out=e_, in_=e_, func=AF.Exp), reads=[exk, (exk, 1)], writes=[exk])
            pseg = PS[1][:, i2, :].rearrange("p (h t) -> p h t", h=4)
            for h in range(4):
                P.op("pe", lambda e: e.matmul(pseg[:, h, :], lhsT=Am[i % 3][:, h, :], rhs=tri, start=True, stop=True),
                     reads=[("Am", i % 3), "msk"], writes=[psk(1, i2)])
            pcb = PS[2][:, i2, 0:256].rearrange("p (g t) -> p g t", g=2)
            for g in range(2):
                P.op("pe", lambda e: e.matmul(pcb[:, g, :], lhsT=XBCT[:, 2 + g, tsl], rhs=XBCT[:, 4 + g, tsl], start=True, stop=True),
                     reads=[("XBCT", 2 + g), ("XBCT", 4 + g)], writes=[psk(2, i2)])

        def Q2(it_):
            d, idx_, tt, tri, mgl, i = info(it_)
            i2 = i % 2
            e_, exk = ex[i % 7], ("ex", i % 7)
            dt_ = dta[:, tt, d * 4:d * 4 + 4]
            dk = ("dta", tt)
            pseg = PS[1][:, i2, :].rearrange("p (h t) -> p h t", h=4)
            pcb = PS[2][:, i2, 0:256].rearrange("p (g t) -> p g t", g=2)
            P.op("act", lambda e: e.activation(out=Lx[i % 3], in_=pseg, func=AF.Exp), reads=[psk(1, i2)], writes=[("Lx", i % 3)])
            P.op("dve", lambda e: e.tensor_tensor(out=CBm[i % 3], in0=pcb, in1=tri.unsqueeze(1).broadcast_to([128, 2, 128]), op=ALU.mult),
                 reads=[psk(2, i2), "msk"], writes=[("CBm", i % 3)])
            X4 = X[:, tt, :].rearrange("p (h d) -> p h d", h=4)
            P.op("dve", lambda e: e.tensor_tensor(out=Xdt[i % 4], in0=X4, in1=dt_.unsqueeze(2).broadcast_to([128, 4, 64]), op=ALU.mult),
                 reads=[("X", tt), dk], writes=[("Xdt", i % 4)])
            P.op("dve", lambda e: e.tensor_tensor(out=Xdd[i % 4], in0=Xdt[i % 4], in1=e_[:, 0, :].unsqueeze(2).broadcast_to([128, 4, 64]), op=ALU.mult),
                 reads=[("Xdt", i % 4), exk], writes=[("Xdd", i % 4)])

        def Q3(it_):
            d, idx_, tt, tri, mgl, i = info(it_)
            P.op("pool", lambda e: e.tensor_tensor(out=MT[i % 3].rearrange("p (g r) t -> p g r t", g=2),
                                                   in0=Lx[i % 3].rearrange("p (g r) t -> p g r t", g=2),
                                                   in1=CBm[i % 3].unsqueeze(2).broadcast_to([128, 2, 2, 128]), op=ALU.mult),
                 reads=[("Lx", i % 3), ("CBm", i % 3)], writes=[("MT", i % 3)])

        def Q4(it_):
            d, idx_, tt, tri, mgl, i = info(it_)
            i2 = i % 2
            tsl = slice(tt * 128, (tt + 1) * 128)
            pyd = PS[3][:, i2, 0:256].rearrange("p (h d) -> p h d", h=4)
            pyo = PS[3][:, i2, 256:512].rearrange("p (h d) -> p h d", h=4)
            pst_ = PS[0][:, i2, 256:512].rearrange("p (h d) -> p h d", h=4)
            for h in range(4):
                P.op("pe", lambda e: e.matmul(pyd[:, h, :], lhsT=MT[i % 3][:, h, :], rhs=Xdt[i % 4][:, h, :], start=True, stop=True),
                     reads=[("MT", i % 3), ("Xdt", i % 4)], writes=[psk(3, i2)])
            for h in range(4):
                P.op("pe", lambda e: e.matmul(pst_[:, h, :], lhsT=Bt[:, tt, (h // 2) * 128:(h // 2 + 1) * 128], rhs=Xdd[i % 4][:, h, :],
                                              start=True, stop=True), reads=[("Bt", tt), ("Xdd", i % 4)], writes=[psk(0, i2)])
            if idx_ != 0:
                for h in range(4):
                    P.op("pe", lambda e: e.matmul(pyo[:, h, :], lhsT=XBCT[:, 4 + h // 2, tsl], rhs=prev[:, h, :], start=True, stop=True),
                         reads=[("XBCT", 4 + h // 2), "prev"], writes=[psk(3, i2)])

        def Q5(it_):
            d, idx_, tt, tri, mgl, i = info(it_)
            i2 = i % 2
            e_, exk = ex[i % 7], ("ex", i % 7)
            pyd = PS[3][:, i2, 0:256].rearrange("p (h d) -> p h d", h=4)
            pyo = PS[3][:, i2, 256:512].rearrange("p (h d) -> p h d", h=4)
            pst_ = PS[0][:, i2, 256:512].rearrange("p (h d) -> p h d", h=4)
            first = (idx_ == 0)
            if first:
                P.op("dve", lambda e: e.tensor_copy(out=carry, in_=pst_), reads=[psk(0, i2)], writes=["carry"])
            else:
                P.op("dve", lambda e: e.tensor_tensor(out=carry, in0=carry, in1=e_[:, 2, :].unsqueeze(2).broadcast_to([128, 4, 64]), op=ALU.mult),
                     reads=["carry", exk], writes=["carry"])
                P.op("dve", lambda e: e.tensor_tensor(out=carry, in0=carry, in1=pst_, op=ALU.add), reads=["carry", psk(0, i2)], writes=["carry"])
            P.op("act", lambda e: e.activation(out=prev, in_=carry, func=AF.Identity, bias=cst[:, 3:4], scale=1.0),
                 reads=["carry"], writes=["prev"])
            yk_ = ("yD", tt)
            y4 = yD[:, tt, :].rearrange("p (h d) -> p h d", h=4)
            if d == 0:
                P.op("dve", lambda e: e.tensor_copy(out=y4, in_=pyd), reads=[psk(3, i2)], writes=[yk_])
            else:
                P.op("dve", lambda e: e.tensor_tensor(out=y4, in0=y4, in1=pyd, op=ALU.add), reads=[psk(3, i2), yk_], writes=[yk_])
            if not first:
                tk_ = ("ytmp", i2)
                P.op("dve", lambda e: e.tensor_tensor(out=ytmp[i2], in0=pyo, in1=e_[:, 1, :].unsqueeze(2).broadcast_to([128, 4, 64]), op=ALU.mult),
                     reads=[psk(3, i2), exk], writes=[tk_])
                P.op("pool", lambda e: e.tensor_tensor(out=y4, in0=y4, in1=ytmp[i2], op=ALU.add), reads=[tk_, yk_], writes=[yk_])

        run_pipe(items, [Q0, Q1, Q2, Q3, Q4, Q5])
        P.mark("ssd_D4")
        for tt in range(NT):
            i2 = tt % 2
            yk_ = ("yD", tt)
            y4 = yD[:, tt, :].rearrange("p (h d) -> p h d", h=4)
            X4 = X[:, tt, :].rearrange("p (h d) -> p h d", h=4)
            tk_ = ("ytmp", i2)
            P.op("pool", lambda e: e.tensor_tensor(out=ytmp[i2], in0=X4, in1=prm[:, 16:20].unsqueeze(2).broadcast_to([128, 4, 64]), op=ALU.mult),
                 reads=[("X", tt), "prm2"], writes=[tk_])
            P.op("pool", lambda e: e.tensor_tensor(out=y4, in0=y4, in1=ytmp[i2], op=ALU.add), reads=[tk_, yk_], writes=[yk_])
            P.op("dve", lambda e: e.tensor_tensor(out=yD[:, tt, :], in0=yD[:, tt, :], in1=Zs[:, tt, :], op=ALU.mult), reads=[yk_, ("Zs", tt)], writes=[yk_])
        finish_group(3, YT, gmix, yD, lambda tt: [("yD", tt)])
        P.barrier()
        A.off = mark

    def hyena_prologue(l):
        P.mark("hyena_prologue")
        P.barrier(skip_q=("pool",))
        A.reset()
        featT = A.alloc([S], F32)
        dec = A.alloc([NT, 256], F32)
        w1 = A.alloc([64], F32)
        w2 = A.alloc([64], F32)
        w3 = A.alloc([1024], F32)
        pr = A.alloc([6], F32)
        h1 = A.alloc([S], F32)
        h2 = A.alloc([S], F32)
        tmp = [A.alloc([512], F32) for _ in range(2)]
        tmpf = [A.alloc([512], F32) for _ in range(2)]
        tmpi = [A.alloc([512], F32).bitcast(mybir.dt.int32) for _ in range(2)]
        sd = A.alloc([2, 2, NT, 256], BF16)
        hfd = [A.alloc([2, 256], F32) for _ in range(2)]
        hbd = [A.alloc([2, 256], F32) for _ in range(2)]
        slab = [A.alloc([2, 16, 128], BF16) for _ in range(2)]
        pqs = [A.alloc([512], BF16) for _ in range(2)]
        P.dma("sp", featT[0:33, :], hy_featT_d, writes=["featT"])
        P.dma("sp", dec, hy_decay_d.rearrange("(t p) c -> p t c", p=128), writes=["dec"])
        P.dma("sp", w1[0:33, :], dram["hy_f_w1"][l], writes=["w1"])
        P.dma("sp", w2[0:64, :], dram["hy_f_w2"][l], writes=["w2"])
        P.dma("sp", w3[0:64, :], dram["hy_f_w3"][l], writes=["w3"])
        for i, nm in enumerate(["hy_f_b1", "hy_f_freq", "hy_f_b2"]):
            P.dma("sp", pr[0:64, i:i + 1], dram[nm][l].rearrange("(p o) -> p o", o=1), writes=["pr"], allow_slow_non_contiguous=True)

        def sin_layer(dst, wT, kdim, src, srck, bcol):
            for nb in range(4):
                i2 = nb % 2
                ps = PS[0][0:64, i2, :]
                P.op("pe", lambda e: e.matmul(ps, lhsT=wT[0:kdim, :], rhs=src[0:kdim, nb * 512:(nb + 1) * 512], start=True, stop=True),
                     reads=[srck, "w1", "w2"], writes=[psk(0, i2)])
                t_ = tmp[i2][0:64, :]
                tk = ("tmp", i2)
                P.op("dve", lambda e: e.tensor_scalar(out=t_, in0=ps, scalar1=pr[0:64, bcol:bcol + 1], scalar2=pr[0:64, 1:2],
                                                      op0=ALU.add, op1=ALU.mult), reads=[psk(0, i2), "pr"], writes=[tk])
                ti_ = tmpi[i2][0:64, :]
                tf_ = tmpf[i2][0:64, :]
                P.op("dve", lambda e: e.tensor_scalar(out=t_, in0=t_, scalar1=1.0 / (2 * math.pi), scalar2=None, op0=ALU.mult),
                     reads=[tk], writes=[tk])
                P.op("dve", lambda e: e.tensor_copy(out=ti_, in_=t_), reads=[tk], writes=[(tk, "i")])
                P.op("dve", lambda e: e.tensor_copy(out=tf_, in_=ti_), reads=[(tk, "i")], writes=[(tk, "f")])
                P.op("dve", lambda e: e.tensor_tensor(out=t_, in0=t_, in1=tf_, op=ALU.subtract), reads=[tk, (tk, "f")], writes=[tk])
                P.op("dve", lambda e: e.tensor_scalar(out=tf_, in0=t_, scalar1=0.5, scalar2=None, op0=ALU.is_gt), reads=[tk], writes=[(tk, "f")])
                P.op("dve", lambda e: e.tensor_tensor(out=t_, in0=t_, in1=tf_, op=ALU.subtract), reads=[tk, (tk, "f")], writes=[tk])
                P.op("dve", lambda e: e.tensor_scalar(out=tf_, in0=t_, scalar1=-0.5, scalar2=None, op0=ALU.is_lt), reads=[tk], writes=[(tk, "f")])
                P.op("dve", lambda e: e.tensor_tensor(out=t_, in0=t_, in1=tf_, op=ALU.add), reads=[tk, (tk, "f")], writes=[tk])
                P.op("act", lambda e: e.activation(out=dst[0:64, nb * 512:(nb + 1) * 512], in_=t_, func=AF.Sin,
                                                   bias=cst[0:64, 3:4], scale=2 * math.pi), reads=[tk, ("cst", 3)], writes=[(dst.tensor.name, id(dst))])
            return (dst.tensor.name, id(dst))

        k1 = sin_layer(h1, w1, 33, featT, "featT", 0)
        k2 = sin_layer(h2, w2, 64, h1, k1, 2)
        for tt in range(NT):
            i2 = tt % 2
            for n in range(2):
                P.op("pe", lambda e: e.matmul(PS[1][:, n, :], lhsT=h2[0:64, tt * 128:(tt + 1) * 128], rhs=w3[0:64, n * 512:(n + 1) * 512],
                                              start=True, stop=True), reads=[k2, "w3"], writes=[psk(1, n)])
            ps4 = PS[1][:, :, :].rearrange("p n (d c) -> p n d c", d=2)
            dbc = dec[:, tt, :].unsqueeze(1).broadcast_to([128, 2, 256])
            P.op("dve", lambda e: e.tensor_tensor(out=hfd[i2], in0=ps4[:, :, 0, :], in1=dbc, op=ALU.mult),
                 reads=[psk(1, 0), psk(1, 1), "dec"], writes=[("hfd", i2)])
            P.op("dve", lambda e: e.tensor_tensor(out=hbd[i2], in0=ps4[:, :, 1, :], in1=dbc, op=ALU.mult),
                 reads=[psk(1, 0), psk(1, 1), "dec"], writes=[("hbd", i2)])
            if tt == 0:
                P.op("dve", lambda e: e.memset(hbd[i2][0:1, :, :], 0.0), reads=[("hbd", i2)], writes=[("hbd", i2)])
            P.op("pool", lambda e: e.tensor_tensor(out=sd[:, :, 0, tt, :], in0=hfd[i2], in1=hbd[i2], op=ALU.add),
                 reads=[("hfd", i2), ("hbd", i2)], writes=[("sd", tt)])
            P.op("pool", lambda e: e.tensor_tensor(out=sd[:, :, 1, tt, :], in0=hbd[i2], in1=hfd[i2], op=ALU.subtract),
                 reads=[("hfd", i2), ("hbd", i2)], writes=[("sd", tt)])
        sdk = [("sd", tt) for tt in range(NT)]
        it = 0
        for fc in range(16):
            sb = slab[fc % 2]
            sk_ = ("slab", fc % 2)
            for m in range(2):
                P.dma("sp", sb[:, m], dft_f_d[m, fc], writes=[(sk_, m)])
            for n in range(2):
                i2 = it % 2
                it += 1
                for m in range(2):
                    for tc in range(16):
                        P.op("pe", lambda e: e.matmul(PS[2][:, i2, m * 256:(m + 1) * 256], lhsT=sb[:, m, tc, :], rhs=sd[:, n, m, tc, :],
                                                      start=(tc == 0), stop=(tc == 15)), reads=[(sk_, m)] + sdk, writes=[psk(2, i2)])
                P.op("act", lambda e: e.activation(out=pqs[i2], in_=PS[2][:, i2, :], func=AF.Identity, bias=cst[:, 3:4], scale=1.0),
                     reads=[psk(2, i2)], writes=[("pqs", i2)])
                P.dma("sp", PQ_b[l, n, fc], pqs[i2], reads=[("pqs", i2)], writes=[("PQ_b", l, n, fc)])

    def mix_hyena(l, b, YT, gmix):
        P.mark("mix_hyena")
        mark = A.off
        V0 = A.alloc([NT, 256], BF16)
        X12 = A.alloc([2, NT, 256], BF16)
        cwx = A.alloc([6, 4], F32)
        hbias = A.alloc([2, 256], F32)
        yB = A.alloc([NT, 256], F32)
        for k in range(3):
            P.dma("sp", cwx[:, :, k:k + 1], dram["hy_conv_w"][l, k].rearrange("(c p o) -> p c o", p=128, o=1), writes=["cwx"],
                  allow_slow_non_contiguous=True)
        P.dma("sp", cwx[:, :, 3:4], dram["hy_conv_b"][l].rearrange("(c p o) -> p c o", p=128, o=1), writes=["cwx"],
              allow_slow_non_contiguous=True)
        P.dma("sp", hbias, _bc(dram["hy_bias"][l].rearrange("a b -> (a b)")), writes=["hbias"])
        mark2 = A.off
        Wb = A.alloc([8, 768], BF16)
        P.dma("sp", Wb, Wi_b[l][:, O_HY:O_HY + 768].rearrange("(c p) n -> p c n", p=128), reads=[("Wi_b", l)], writes=["Wb"])
        UCT = A.alloc([6, S], BF16)
        Gs = [A.alloc([S + 2], F32) for _ in range(2)]
        Ts = [A.alloc([S], F32) for _ in range(2)]
        for gi_ in range(2):
            P.op("pool", lambda e: e.memset(Gs[gi_][:, 0:1], 0.0), writes=[("Gl", gi_)])
            P.op("pool", lambda e: e.memset(Gs[gi_][:, S + 1:S + 2], 0.0), writes=[("Gr", gi_)])

        def Sa(ch):
            G, T, gq_ = Gs[ch % 2], Ts[ch % 2], ch % 2
            for nb in range(4):
                it = ch * 4 + nb
                pi, pb_ = (it % 4) // 2, it % 2
                for c in range(8):
                    P.op("pe", lambda e: e.matmul(PS[pi][:, pb_, :], lhsT=Wb[:, c, 0 + ch * 128:0 + (ch + 1) * 128],
                                                  rhs=HT[:, c, nb * 512:(nb + 1) * 512], start=(c == 0), stop=(c == 7)),
                         reads=["Wb"] + [("HT", nb * 4 + j) for j in range(4)], writes=[psk(pi, pb_)])
                P.op("act", lambda e: e.activation(out=G[:, 1 + nb * 512:1 + (nb + 1) * 512], in_=PS[pi][:, pb_, :],
                                                   func=AF.Identity, bias=cst[:, 3:4], scale=1.0),
                     reads=[psk(pi, pb_)], writes=[("G", gq_, nb)])
                P.op("act", lambda e: e.activation(out=T[:, nb * 512:(nb + 1) * 512], in_=PS[pi][:, pb_, :],
                                                   func=AF.Identity, bias=cwx[:, ch, 3:4], scale=cwx[:, ch, 1:2]),
                     reads=[psk(pi, pb_), "cwx"], writes=[("T", gq_, nb)])

        def Sb(ch):
            G, T, gq_ = Gs[ch % 2], Ts[ch % 2], ch % 2
            gks = [("G", gq_, nb) for nb in range(4)]
            tks = [("T", gq_, nb) for nb in range(4)]
            P.op("dve", lambda e: e.scalar_tensor_tensor(out=T, in0=G[:, 0:S], scalar=cwx[:, ch, 0:1], in1=T, op0=ALU.mult, op1=ALU.add),
                 reads=gks + tks + [("Gl", gq_), "cwx"], writes=tks)
            P.op("dve", lambda e: e.scalar_tensor_tensor(out=UCT[:, ch, :], in0=G[:, 2:S + 2], scalar=cwx[:, ch, 2:3], in1=T, op0=ALU.mult, op1=ALU.add),
                 reads=gks + tks + [("Gr", gq_), "cwx"], writes=[("UCT", ch)])
        run_pipe(list(range(6)), [Sa, Sb])
        for tt in range(NT):
            i2 = tt % 2
            tsl = slice(tt * 128, (tt + 1) * 128)
            pst = PS[2 + i2][:, 0, :].bitcast(BF16).rearrange("p (a b) -> p a b", a=8)
            for ch in range(6):
                P.op("pe", lambda e: e.transpose(out=pst[:, ch, :], in_=UCT[:, ch, tsl], identity=ident[:]),
                     reads=[("UCT", ch), "ident"], writes=[psk(2 + i2, 0)])
            P.op("dve", lambda e: e.tensor_copy(out=V0[:, tt, :].rearrange("p (a b) -> p a b", a=2), in_=pst[:, 0:2, :]),
                 reads=[psk(2 + i2, 0)], writes=[("z0", tt)])
            P.op("dve", lambda e: e.tensor_copy(out=X12[:, :, tt, :].rearrange("p n (a b) -> p n a b", a=2),
                                                in_=pst[:, 2:6, :].rearrange("p (n a) b -> p n a b", n=2)),
                 reads=[psk(2 + i2, 0)], writes=[("X12", tt)])
        P.barrier()
        P.mark("hy_conv")
        A.off = mark2
        Z1 = A.alloc([NT, 256], BF16)
        PQ = A.alloc([16, 512], BF16)
        Yc = A.alloc([2, 16, 256], BF16)
        slab = [A.alloc([2, 16, 128], BF16) for _ in range(2)]
        AB = [A.alloc([512], F32) for _ in range(2)]
        tq = [A.alloc([4, 256], F32) for _ in range(2)]
        te = [A.alloc([256], F32) for _ in range(2)]
        sit = 0
        for n in range(2):
            zin = V0 if n == 0 else Z1
            zkf = (lambda tt: ("z0", tt)) if n == 0 else (lambda tt: ("z1", tt))
            P.dma("sp", PQ, PQ_b[l, n].rearrange("fc p x -> p fc x"), reads=[("PQ_b", l, n, fc) for fc in range(16)], writes=["PQ"])
            zks = [zkf(tt) for tt in range(NT)]
            for fc in range(16):
                sb = slab[sit % 2]
                sk_ = ("slab", sit % 2)
                sit += 1
                i2 = fc % 2
                for m in range(2):
                    P.dma("sp", sb[:, m], dft_f_d[m, fc], writes=[(sk_, m)])
                for m in range(2):
                    for tc in range(16):
                        P.op("pe", lambda e: e.matmul(PS[0][:, i2, m * 256:(m + 1) * 256], lhsT=sb[:, m, tc, :], rhs=zin[:, tc, :],
                                                      start=(tc == 0), stop=(tc == 15)), reads=[(sk_, m)] + zks, writes=[psk(0, i2)])
                ab = AB[i2]
                abk = ("AB", i2)
                P.op("act", lambda e: e.activation(out=ab, in_=PS[0][:, i2, :], func=AF.Identity, bias=cst[:, 3:4], scale=1.0),
                     reads=[psk(0, i2)], writes=[abk])
                q_ = tq[i2]
                qk_ = ("tq", i2)
                Aa, Bb = ab[:, 0:256], ab[:, 256:512]
                Pp, Qq = PQ[:, fc, 0:256], PQ[:, fc, 256:512]
                P.op("dve", lambda e: e.tensor_tensor(out=q_[:, 0, :], in0=Aa, in1=Pp, op=ALU.mult), reads=[abk, "PQ"], writes=[(qk_, 0)])
                P.op("pool", lambda e: e.tensor_tensor(out=q_[:, 1, :], in0=Bb, in1=Qq, op=ALU.mult), reads=[abk, "PQ"], writes=[(qk_, 1)])
                P.op("dve", lambda e: e.tensor_tensor(out=q_[:, 2, :], in0=Aa, in1=Qq, op=ALU.mult), reads=[abk, "PQ"], writes=[(qk_, 2)])
                P.op("pool", lambda e: e.tensor_tensor(out=q_[:, 3, :], in0=Bb, in1=Pp, op=ALU.mult), reads=[abk, "PQ"], writes=[(qk_, 3)])
                P.op("dve", lambda e: e.tensor_tensor(out=Yc[:, 0, fc, :], in0=q_[:, 0, :], in1=q_[:, 1, :], op=ALU.add),
                     reads=[(qk_, 0), (qk_, 1)], writes=[("Yc", fc)])
                P.op("pool", lambda e: e.tensor_tensor(out=Yc[:, 1, fc, :], in0=q_[:, 2, :], in1=q_[:, 3, :], op=ALU.subtract),
                     reads=[(qk_, 2), (qk_, 3)], writes=[("Yc", fc)])
            yks = [("Yc", fc) for fc in range(16)]
            for tcl in range(16):
                sb = slab[sit % 2]
                sk_ = ("slab", sit % 2)
                sit += 1
                i2 = tcl % 2
                for m in range(2):
                    P.dma("sp", sb[:, m], dft_i_d[m, tcl], writes=[(sk_, m)])
                py = PS[1][:, i2, 0:256]
                for m in range(2):
                    for fc in range(16):
                        P.op("pe", lambda e: e.matmul(py, lhsT=sb[:, m, fc, :], rhs=Yc[:, m, fc, :],
                                                      start=(m == 0 and fc == 0), stop=(m == 1 and fc == 15)),
                             reads=[(sk_, m)] + yks, writes=[psk(1, i2)])
                t_ = te[i2]
                tk = ("te", i2)
                P.op("pool", lambda e: e.tensor_tensor(out=t_, in0=zin[:, tcl, :], in1=hbias[:, n, :], op=ALU.mult),
                     reads=[zkf(tcl), "hbias"], writes=[tk])
                P.op("dve", lambda e: e.tensor_tensor(out=t_, in0=t_, in1=py, op=ALU.add), reads=[tk, psk(1, i2)], writes=[tk])
                if n == 0:
                    P.op("pool", lambda e: e.tensor_tensor(out=Z1[:, tcl, :], in0=t_, in1=X12[:, 0, tcl, :], op=ALU.mult),
                         reads=[tk, ("X12", tcl)], writes=[("z1", tcl)])
                else:
                    P.op("pool", lambda e: e.tensor_tensor(out=yB[:, tcl, :], in0=t_, in1=X12[:, 1, tcl, :], op=ALU.mult),
                         reads=[tk, ("X12", tcl)], writes=[("yB", tcl)])
        finish_group(1, YT, gmix, yB, lambda tt: [("yB", tt)])
        P.barrier()
        A.off = mark

    def stage_mix(l, b):
        P.barrier()
        A.reset()
        YT = A.alloc([8, S], BF16)
        gmix = A.alloc([D], F32)
        P.dma("sp", gmix, _bc(dram["mix_norm_g"][l]), writes=["gmix"])
        fns = {"a": mix_mla, "b": mix_hyena, "c": mix_swa, "d": mix_ssd}
        for g, nm in enumerate("abcd"):
            if nm in groups and nm in fns:
                fns[nm](l, b, YT, gmix)
            else:
                y_from_dbg(l, b, g, YT, gmix)
        return YT

    stop = dbg.get("stop")
    if "b" in groups:
        for l in range(nlayer):
            hyena_prologue(l)

    def dump_and_stop():
        P.barrier()
        P.dma("sp", out_d[0], Hf, writes=["outdump"])
        P.barrier()
        P.emit()
        return nc

    for b in range(nseq):
        stage_embed(b)
        if stop == "embed":
            return dump_and_stop()
        for l in range(nlayer):
            YT = stage_mix(l, b)
            stage_outproj(l, YT)
            if stop == "outproj":
                return dump_and_stop()
            stage_ffn(l)
            if stop == "ffn":
                return dump_and_stop()
            stage_ple(l, b, last=(l == nlayer - 1))
            if stop == "ple":
                return dump_and_stop()
    P.barrier()
    P.mark("end")
    P.emit()
    nc._marks = P.marks
    return nc


def host_consts():
    c = {}
    inv = 10000.0 ** (-np.arange(0, 32, 2, dtype=np.float32) / 32.0)
    ang = np.arange(S, dtype=np.float32)[:, None] * inv[None, :].astype(np.float32)
    c["rope_cs"] = np.concatenate([np.cos(ang), np.sin(ang)], axis=1).astype(np.float32)
    j = np.arange(128)[:, None, None, None]
    r = np.arange(3)[None, :, None, None]
    q = np.arange(128)[None, None, None, :]
    dist = np.abs(q - j - (r - 1) * 128).astype(np.float32)
    slopes = ((2.0 ** (-8.0 / 4)) ** np.arange(1, 5, dtype=np.float32))[None, None, :, None]
    c["swa_eb"] = np.where(dist <= 128, np.exp(-slopes * dist), 0.0).astype(np.float32)
    u = np.arange(128)[:, None]
    t = np.arange(128)[None, :]
    c["ssd_masks"] = np.stack([(u <= t), (u >= t), (u > t), (u < t), np.ones((128, 128), bool)], axis=1).astype(np.float32)
    th = 2.0 * np.pi / 4096.0
    f = np.arange(S, dtype=np.float64)[:, None] + 0.5
    t = np.arange(S, dtype=np.float64)[None, :]
    ang = th * f * t
    mats = [np.cos(ang), np.sin(ang)]
    dff = np.empty((2, 16, 128, 16, 128), dtype=ml_dtypes.bfloat16)
    dfi = np.empty((2, 16, 128, 16, 128), dtype=ml_dtypes.bfloat16)
    for m in range(2):
        M = mats[m].reshape(16, 128, 16, 128)
        dff[m] = M.transpose(0, 3, 2, 1).astype(np.float32)
        sgn = 1.0 if m == 0 else -1.0
        dfi[m] = (sgn / 2048.0 * M).transpose(2, 1, 0, 3).astype(np.float32)
    c["dft_f"] = dff
    c["dft_i"] = dfi
    tl = np.linspace(0.0, 1.0, S, dtype=np.float32)[:, None]
    ang2 = (2.0 * math.pi * np.arange(S, dtype=np.float32)[:, None] / S).astype(np.float32)
    bands = np.linspace(1e-4, 15.0, 16, dtype=np.float32)[None, :]
    feat = np.concatenate([tl, np.cos(bands * ang2), -np.sin(bands * ang2)], -1).astype(np.float32)
    c["hy_featT"] = np.ascontiguousarray(feat.T)
    max_decay = math.log(1e-2) / 0.3
    min_decay = math.log(1e-2) / 1.5
    deltas = np.linspace(min_decay, max_decay, 256, dtype=np.float32)
    c["hy_decay"] = np.exp(-tl * np.abs(deltas)[None, :]).astype(np.float32)
    return c


def kernel(**inputs):
    ncores = 8
    nseq = 32 // ncores
    nc = build(nseq=nseq, nlayer=2)
    in_maps = []
    consts = host_consts()
    for c in range(ncores):
        m = {"x": np.ascontiguousarray(inputs["x"][c * nseq:(c + 1) * nseq]),
             "p": np.ascontiguousarray(inputs["p"][:, c * nseq:(c + 1) * nseq])}
        for n in WNAMES:
            m[n] = np.ascontiguousarray(inputs[n])
        m.update(consts)
        in_maps.append(m)
    res = run_bass_kernel_spmd(nc, in_maps, core_ids=list(range(ncores)))
    return np.concatenate([r["out"] for r in res.results], axis=0)
```
